# Optimizing a Trainium2 kernel written in Bass

```python
import math
import jax, jax.numpy as jnp
from jax import lax
import numpy as np

D_MODEL = 1024
BATCH = 8
SEQ = 4096
DEPTH = 2

GRID_W = 64
CTX_LEN = 256
EPS = 1e-6
ROPE_THETA = 10000.0
Q_BLOCK = 128
SCAN_CHUNK = 128
FFN_HIDDEN = int(math.ceil(8 * D_MODEL / 3 / 256)) * 256

HEAD_DIM = 64
GQA_HQ = D_MODEL // (2 * HEAD_DIM)
GQA_HKV = GQA_HQ // 4
GQA_REP = GQA_HQ // GQA_HKV
GQA_DH = HEAD_DIM
MLA_H = D_MODEL // (2 * HEAD_DIM)
MLA_DNOPE = HEAD_DIM
MLA_DROPE = HEAD_DIM // 2
MLA_DQK = MLA_DNOPE + MLA_DROPE
MLA_DV = HEAD_DIM
MLA_Q_RANK = 3 * D_MODEL // 8
MLA_KV_RANK = D_MODEL // 4
ATTN_SPLITS = (GQA_HQ * GQA_DH, GQA_HKV * GQA_DH, GQA_HKV * GQA_DH, MLA_Q_RANK, MLA_KV_RANK, MLA_DROPE)
ATTN_IN = sum(ATTN_SPLITS)
ATTN_OUT = GQA_HQ * GQA_DH + MLA_H * MLA_DV
SSD_DINNER = D_MODEL
SSD_P = 64
SSD_H = SSD_DINNER // SSD_P
SSD_G = 2
SSD_HG = SSD_H // SSD_G
SSD_N = 128
SSD_CONV = 5
SSD_CONV_DIM = SSD_DINNER + 2 * SSD_G * SSD_N
RET_H = 8
RET_DK = D_MODEL // (2 * RET_H)
RET_DV = 2 * RET_DK
SSM_SPLITS = (SSD_DINNER, SSD_CONV_DIM, 2 * SSD_H, RET_H * RET_DK, RET_H * RET_DK, RET_H * RET_DV, RET_H * RET_DV)
SSM_IN = sum(SSM_SPLITS)
SSM_OUT = SSD_DINNER + RET_H * RET_DV

F32 = jnp.float32

kernel_name = 'hybrid_diffusion_prefix_backbone'


def split_cols(h, sizes):
    idx = np.cumsum(sizes)[:-1].tolist()
    return jnp.split(h, idx, axis=-1)


def rms_norm(x, w):
    xf = x.astype(F32)
    y = xf * lax.rsqrt(jnp.mean(xf * xf, axis=-1, keepdims=True) + EPS)
    return (y * w).astype(x.dtype)


def head_layer_norm(x, w):
    xf = x.astype(F32)
    mu = jnp.mean(xf, axis=-1, keepdims=True)
    var = jnp.mean(jnp.square(xf - mu), axis=-1, keepdims=True)
    return ((xf - mu) * lax.rsqrt(var + EPS) * w).astype(x.dtype)


def adaln(x, w, shift, scale):
    return rms_norm(x, w) * (1.0 + scale) + shift


def axial_rope_tables(seq_len, dim):
    rows = seq_len // GRID_W
    rr, cc = jnp.meshgrid(jnp.arange(rows, dtype=F32), jnp.arange(GRID_W, dtype=F32), indexing='ij')
    quarter = dim // 4
    inv = ROPE_THETA ** (-jnp.arange(quarter, dtype=F32) / quarter)
    ang = jnp.concatenate([rr.reshape(-1)[:, None] * inv, cc.reshape(-1)[:, None] * inv], axis=-1)
    cos = jnp.concatenate([jnp.ones((CTX_LEN, dim // 2), F32), jnp.cos(ang)], axis=0)
    sin = jnp.concatenate([jnp.zeros((CTX_LEN, dim // 2), F32), jnp.sin(ang)], axis=0)
    return cos, sin


def apply_rope(x, cos, sin):
    xf = x.astype(F32).reshape(x.shape[:-1] + (x.shape[-1] // 2, 2))
    x0, x1 = xf[..., 0], xf[..., 1]
    c = cos[None, :, None, :]
    s = sin[None, :, None, :]
    out = jnp.stack([x0 * c - x1 * s, x0 * s + x1 * c], axis=-1)
    return out.reshape(x.shape).astype(x.dtype)


def block_attention(q, k, v, scale):
    b, s, hk, r, dq = q.shape
    nb = s // Q_BLOCK
    qb = jnp.moveaxis(q.reshape(b, nb, Q_BLOCK, hk, r, dq), 1, 0)

    def attend(qi):
        sc = jnp.einsum('bqgrd,blgd->bgrql', qi, k).astype(F32) * scale
        p = jax.nn.softmax(sc, axis=-1).astype(v.dtype)
        return jnp.einsum('bgrql,blge->bqgre', p, v)

    o = lax.map(attend, qb)
    return jnp.moveaxis(o, 0, 1).reshape(b, s, hk, r, v.shape[-1])


def two_way_attention(q, k, v, scale, need_ctx):
    o_lat = block_attention(q[:, CTX_LEN:], k, v, scale)
    if not need_ctx:
        return o_lat
    o_ctx = block_attention(q[:, :CTX_LEN], k[:, :CTX_LEN], v[:, :CTX_LEN], scale)
    return jnp.concatenate([o_ctx, o_lat], axis=1)


def chunked_scan(q, k, v, log_a, state0):
    b, l, g, n = q.shape
    hg, p = v.shape[3], v.shape[4]
    nc = l // SCAN_CHUNK
    q = q.reshape(b, nc, SCAN_CHUNK, g, n)
    k = k.reshape(b, nc, SCAN_CHUNK, g, n)
    v = v.reshape(b, nc, SCAN_CHUNK, g, hg, p)
    a_cum = jnp.cumsum(log_a.astype(F32).reshape(b, nc, SCAN_CHUNK, g, hg), axis=2)
    lower = jnp.tril(jnp.ones((SCAN_CHUNK, SCAN_CHUNK), dtype=bool))[:, :, None, None]
    seg = a_cum[:, :, :, None] - a_cum[:, :, None, :]
    decay = jnp.exp(jnp.where(lower, seg, -jnp.inf))
    scores = jnp.einsum('bcign,bcjgn->bcijg', q, k).astype(F32)
    y_diag = jnp.einsum('bcijgh,bcjghp->bcighp', scores[..., None] * decay, v)
    decay_end = jnp.exp(a_cum[:, :, -1:] - a_cum)
    chunk_states = jnp.einsum('bcjgn,bcjgh,bcjghp->bcghpn', k, decay_end, v)
    chunk_decay = jnp.exp(a_cum[:, :, -1])

    def step(s, inp):
        st, dc = inp
        return s * dc[..., None, None] + st, s

    final, s_in = lax.scan(step, state0, (jnp.moveaxis(chunk_states, 1, 0), jnp.moveaxis(chunk_decay, 1, 0)))
    s_in = jnp.moveaxis(s_in, 0, 1)
    y_off = jnp.einsum('bcign,bcghpn,bcigh->bcighp', q, s_in, jnp.exp(a_cum))
    return (y_diag + y_off).reshape(b, l, g, hg, p), final


def bidir_scan(q, k, v_f, v_b, la_f, la_b):
    g, hg, p = v_f.shape[2], v_f.shape[3], v_f.shape[4]
    s0 = jnp.zeros((q.shape[0], g, hg, p, q.shape[-1]), F32)
    cl = CTX_LEN

    def rev(a):
        return jnp.flip(a, axis=1)

    yc_f, sc_f = chunked_scan(q[:, :cl], k[:, :cl], v_f[:, :cl], la_f[:, :cl], s0)
    yl_f, _ = chunked_scan(q[:, cl:], k[:, cl:], v_f[:, cl:], la_f[:, cl:], sc_f)
    yc_b, sc_b = chunked_scan(rev(q[:, :cl]), rev(k[:, :cl]), rev(v_b[:, :cl]), rev(la_b[:, :cl]), s0)
    yl_b, _ = chunked_scan(rev(q[:, cl:]), rev(k[:, cl:]), rev(v_b[:, cl:]), rev(la_b[:, cl:]), sc_b)
    return jnp.concatenate([yc_f + rev(yc_b), yl_f + rev(yl_b)], axis=1)


def dwconv(x, w, bias):
    kw = w.shape[0]
    y = lax.conv_general_dilated(x, w[:, None, :], window_strides=(1,), padding=[(kw // 2, kw // 2)],
                                 dimension_numbers=('NWC', 'WIO', 'NWC'), feature_group_count=x.shape[-1])
    return y + bias


def swiglu(h, w13, w2):
    gte, up = jnp.split(h @ w13, 2, axis=-1)
    return (jax.nn.silu(gte) * up) @ w2


def attn_mixer(u, w_in, gqa_qn, gqa_kn, mla_qa_norm, mla_wq_b, mla_kva_norm, mla_wkv_b, mla_qn, mla_kn, w_out,
               rope_gqa, rope_mla, need_ctx):
    b, n, _ = u.shape
    qa, ka, va, q_lora, kv_lora, k_rope = split_cols(u @ w_in, ATTN_SPLITS)
    qa = apply_rope(rms_norm(qa.reshape(b, n, GQA_HQ, GQA_DH), gqa_qn), *rope_gqa)
    ka = apply_rope(rms_norm(ka.reshape(b, n, GQA_HKV, GQA_DH), gqa_kn), *rope_gqa)
    va = va.reshape(b, n, GQA_HKV, GQA_DH)
    o_a = two_way_attention(qa.reshape(b, n, GQA_HKV, GQA_REP, GQA_DH), ka, va, GQA_DH ** -0.5, need_ctx)
    o_a = o_a.reshape(o_a.shape[0], o_a.shape[1], GQA_HQ * GQA_DH)
    qm = (rms_norm(q_lora, mla_qa_norm) @ mla_wq_b).reshape(b, n, MLA_H, MLA_DQK)
    kv = (rms_norm(kv_lora, mla_kva_norm) @ mla_wkv_b).reshape(b, n, MLA_H, MLA_DNOPE + MLA_DV)
    k_nope, vm = kv[..., :MLA_DNOPE], kv[..., MLA_DNOPE:]
    km = jnp.concatenate([k_nope, jnp.broadcast_to(k_rope[:, :, None, :], (b, n, MLA_H, MLA_DROPE))], axis=-1)
    qm = rms_norm(qm, mla_qn)
    km = rms_norm(km, mla_kn)
    qm = jnp.concatenate([qm[..., :MLA_DNOPE], apply_rope(qm[..., MLA_DNOPE:], *rope_mla)], axis=-1)
    km = jnp.concatenate([km[..., :MLA_DNOPE], apply_rope(km[..., MLA_DNOPE:], *rope_mla)], axis=-1)
    o_b = two_way_attention(qm[:, :, :, None, :], km, vm, MLA_DQK ** -0.5, need_ctx)
    o_b = o_b.reshape(o_b.shape[0], o_b.shape[1], MLA_H * MLA_DV)
    return jnp.concatenate([o_a, o_b], axis=-1) @ w_out


def ssm_mixer(u, w_in, conv_w, conv_b, dt_bias, a_log, d_skip, ssd_norm, ret_logit, ret_norm, w_out,
              rope_ret, need_ctx):
    b, n, _ = u.shape
    z, xbc, dt, rq, rk, rv, rg = split_cols(u @ w_in, SSM_SPLITS)
    xbc = jax.nn.silu(jnp.concatenate([dwconv(xbc[:, :CTX_LEN], conv_w, conv_b),
                                       dwconv(xbc[:, CTX_LEN:], conv_w, conv_b)], axis=1))
    xs, bm, cm = split_cols(xbc, (SSD_DINNER, SSD_G * SSD_N, SSD_G * SSD_N))
    xs = xs.reshape(b, n, SSD_G, SSD_HG, SSD_P)
    bm = bm.reshape(b, n, SSD_G, SSD_N)
    cm = cm.reshape(b, n, SSD_G, SSD_N)
    dt = jax.nn.softplus(dt.reshape(b, n, 2, SSD_H).astype(F32) + dt_bias).reshape(b, n, 2, SSD_G, SSD_HG)
    a = -jnp.exp(a_log.astype(F32)).reshape(2, SSD_G, SSD_HG)
    la = dt * a
    v_dir = xs[:, :, None] * dt[..., None]
    y = bidir_scan(cm, bm, v_dir[:, :, 0], v_dir[:, :, 1], la[:, :, 0], la[:, :, 1])
    y = y + d_skip.reshape(SSD_G, SSD_HG, 1) * xs
    y = y.reshape(b, n, SSD_G, SSD_HG * SSD_P) * jax.nn.silu(z.reshape(b, n, SSD_G, SSD_HG * SSD_P))
    y = rms_norm(y, ssd_norm.reshape(SSD_G, -1)).reshape(b, n, SSD_DINNER)
    rq = apply_rope(rq.reshape(b, n, RET_H, RET_DK), *rope_ret)
    rk = apply_rope(rk.reshape(b, n, RET_H, RET_DK), *rope_ret) * (RET_DK ** -0.5)
    rv = rv.reshape(b, n, RET_H, 1, RET_DV)
    la_r = jax.nn.log_sigmoid(ret_logit.astype(F32))
    la_f = jnp.broadcast_to(la_r[0][:, None], (b, n, RET_H, 1))
    la_b = jnp.broadcast_to(la_r[1][:, None], (b, n, RET_H, 1))
    yr = bidir_scan(rq, rk, rv, rv, la_f, la_b).reshape(b, n, RET_H, RET_DV)
    yr = head_layer_norm(yr, ret_norm.reshape(RET_H, RET_DV)).reshape(b, n, RET_H * RET_DV) * jax.nn.silu(rg)
    o = jnp.concatenate([y, yr.astype(y.dtype)], axis=-1)
    if not need_ctx:
        o = o[:, CTX_LEN:]
    return o @ w_out


def setup_inputs(seed: int = 0) -> dict:
    key = jax.random.key(seed)
    ks = iter(jax.random.split(key, 48))
    D = D_MODEL
    na = (DEPTH + 1) // 2
    ns = DEPTH // 2

    def nrm(shape, scale):
        return jax.random.normal(next(ks), shape, F32) * scale

    def gain(shape):
        return 1.0 + nrm(shape, 0.02)

    x = nrm((BATCH, SEQ, D), 1.0)
    c = nrm((BATCH, D), 1.0)
    ctx = nrm((BATCH, CTX_LEN, D), 1.0)
    c_ctx = nrm((D,), 1.0)
    mod_w = nrm((DEPTH, D, 6 * D), 0.5 * D ** -0.5)
    mod_b = nrm((DEPTH, 6 * D), 0.01)
    norm1_w = gain((DEPTH, D))
    norm2_w = gain((DEPTH, D))
    ffn_w13 = nrm((DEPTH, D, 2 * FFN_HIDDEN), D ** -0.5)
    ffn_w2 = nrm((DEPTH, FFN_HIDDEN, D), FFN_HIDDEN ** -0.5)
    attn_w_in = nrm((na, D, ATTN_IN), D ** -0.5)
    gqa_qn = gain((na, GQA_DH))
    gqa_kn = gain((na, GQA_DH))
    mla_qa_norm = gain((na, MLA_Q_RANK))
    mla_wq_b = nrm((na, MLA_Q_RANK, MLA_H * MLA_DQK), MLA_Q_RANK ** -0.5)
    mla_kva_norm = gain((na, MLA_KV_RANK))
    mla_wkv_b = nrm((na, MLA_KV_RANK, MLA_H * (MLA_DNOPE + MLA_DV)), MLA_KV_RANK ** -0.5)
    mla_qn = gain((na, MLA_DQK))
    mla_kn = gain((na, MLA_DQK))
    attn_w_out = nrm((na, ATTN_OUT, D), ATTN_OUT ** -0.5)
    ssm_w_in = nrm((ns, D, SSM_IN), D ** -0.5)
    ssd_conv_w = nrm((ns, SSD_CONV, SSD_CONV_DIM), SSD_CONV ** -0.5)
    ssd_conv_b = nrm((ns, SSD_CONV_DIM), 0.01)
    dt0 = jnp.exp(jax.random.uniform(next(ks), (ns, 2, SSD_H), F32, math.log(1e-3), math.log(1e-1)))
    ssd_dt_bias = dt0 + jnp.log(-jnp.expm1(-dt0))
    ssd_a_log = jnp.log(jax.random.uniform(next(ks), (ns, 2, SSD_H), F32, 1.0, 16.0))
    ssd_d = 1.0 + nrm((ns, SSD_H), 0.1)
    ssd_norm = gain((ns, SSD_DINNER))
    hidx = jnp.arange(RET_H, dtype=F32)
    base_logit = jnp.log1p(-jnp.power(2.0, -5.0 - hidx)) + (5.0 + hidx) * math.log(2.0)
    ret_decay_logit = base_logit + nrm((ns, 2, RET_H), 0.01)
    ret_norm = gain((ns, RET_H * RET_DV))
    ssm_w_out = nrm((ns, SSM_OUT, D), SSM_OUT ** -0.5)
    return {'x': x, 'c': c, 'ctx': ctx, 'c_ctx': c_ctx, 'mod_w': mod_w, 'mod_b': mod_b,
            'norm1_w': norm1_w, 'norm2_w': norm2_w, 'ffn_w13': ffn_w13, 'ffn_w2': ffn_w2,
            'attn_w_in': attn_w_in, 'gqa_qn': gqa_qn, 'gqa_kn': gqa_kn, 'mla_qa_norm': mla_qa_norm,
            'mla_wq_b': mla_wq_b, 'mla_kva_norm': mla_kva_norm, 'mla_wkv_b': mla_wkv_b, 'mla_qn': mla_qn,
            'mla_kn': mla_kn, 'attn_w_out': attn_w_out, 'ssm_w_in': ssm_w_in, 'ssd_conv_w': ssd_conv_w,
            'ssd_conv_b': ssd_conv_b, 'ssd_dt_bias': ssd_dt_bias, 'ssd_a_log': ssd_a_log, 'ssd_d': ssd_d,
            'ssd_norm': ssd_norm, 'ret_decay_logit': ret_decay_logit, 'ret_norm': ret_norm, 'ssm_w_out': ssm_w_out}


def reference(x, c, ctx, c_ctx, mod_w, mod_b, norm1_w, norm2_w, ffn_w13, ffn_w2,
              attn_w_in, gqa_qn, gqa_kn, mla_qa_norm, mla_wq_b, mla_kva_norm, mla_wkv_b, mla_qn, mla_kn, attn_w_out,
              ssm_w_in, ssd_conv_w, ssd_conv_b, ssd_dt_bias, ssd_a_log, ssd_d, ssd_norm, ret_decay_logit, ret_norm,
              ssm_w_out):
    seq_len = x.shape[1]
    rope_gqa = axial_rope_tables(seq_len, GQA_DH)
    rope_mla = axial_rope_tables(seq_len, MLA_DROPE)
    rope_ret = axial_rope_tables(seq_len, RET_DK)
    x_ctx, x_lat = ctx, x
    for i in range(DEPTH):
        need_ctx = i < DEPTH - 1
        j = i // 2
        m_lat = jax.nn.silu(c) @ mod_w[i] + mod_b[i]
        m_ctx = jax.nn.silu(c_ctx) @ mod_w[i] + mod_b[i]
        sh1, sc1, g1, sh2, sc2, g2 = [t[:, None] for t in jnp.split(m_lat, 6, axis=-1)]
        ch1, cs1, cg1, ch2, cs2, cg2 = jnp.split(m_ctx, 6, axis=-1)
        u = jnp.concatenate([adaln(x_ctx, norm1_w[i], ch1, cs1), adaln(x_lat, norm1_w[i], sh1, sc1)], axis=1)
        if i % 2 == 0:
            o = attn_mixer(u, attn_w_in[j], gqa_qn[j], gqa_kn[j], mla_qa_norm[j], mla_wq_b[j], mla_kva_norm[j],
                           mla_wkv_b[j], mla_qn[j], mla_kn[j], attn_w_out[j], rope_gqa, rope_mla, need_ctx)
        else:
            o = ssm_mixer(u, ssm_w_in[j], ssd_conv_w[j], ssd_conv_b[j], ssd_dt_bias[j], ssd_a_log[j], ssd_d[j],
                          ssd_norm[j], ret_decay_logit[j], ret_norm[j], ssm_w_out[j], rope_ret, need_ctx)
        x_lat = x_lat + g1 * o[:, o.shape[1] - seq_len:]
        f_lat_in = adaln(x_lat, norm2_w[i], sh2, sc2)
        if need_ctx:
            x_ctx = x_ctx + cg1 * o[:, :CTX_LEN]
            f = swiglu(jnp.concatenate([adaln(x_ctx, norm2_w[i], ch2, cs2), f_lat_in], axis=1), ffn_w13[i], ffn_w2[i])
            x_ctx = x_ctx + cg2 * f[:, :CTX_LEN]
            x_lat = x_lat + g2 * f[:, CTX_LEN:]
        else:
            x_lat = x_lat + g2 * swiglu(f_lat_in, ffn_w13[i], ffn_w2[i])
    return x_lat
```

```python
import math
from contextlib import ExitStack, contextmanager
import numpy as np
import concourse.bass as bass
import concourse.mybir as mybir
from concourse.bass_utils import run_bass_kernel_spmd

F32 = mybir.dt.float32
BF16 = mybir.dt.bfloat16
AF = mybir.ActivationFunctionType
ALU = mybir.AluOpType
AX = mybir.AxisListType

EPOCH = 30000
EPS = 1e-6
D = 1024
CTX = 256
FFH = 2816
GRID_W = 64
THETA = 10000.0


class Dep:
    __slots__ = ("w", "r")

    def __init__(self):
        self.w = None
        self.r = {}


class Eng:
    def __init__(self, em, name, h, self_sync=True):
        self.em, self.name, self.h, self.self_sync = em, name, h, self_sync
        self.sem = None
        self.cnt = 0
        self.waited = {}
        self.own = set()
        self.ninst = 0
        self.nwait = 0
        self.last = None

    def next_event(self):
        if self.sem is None or self.cnt >= EPOCH:
            self.sem = self.em.new_sem(self.name)
            self.own.add(self.sem.num)
            self.cnt = 0
        self.cnt += 1
        self.last = (self.sem.num, self.cnt)
        return self.last


class Emitter:
    def __init__(self, nc, es, n_dma_sp=24, n_dma_pool=16):
        self.nc, self.es = nc, es
        self.sems = {}
        self.nsem = 0
        self.pe = Eng(self, "pe", nc.tensor, self_sync=False)
        self.act = Eng(self, "act", nc.scalar)
        self.dve = Eng(self, "dve", nc.vector)
        self.pool = Eng(self, "pool", nc.gpsimd)
        self.sp = Eng(self, "sp", nc.sync)
        self.engines = [self.pe, self.act, self.dve, self.pool, self.sp]
        self.dma_pools = {}
        for e, n in ((self.sp, n_dma_sp), (self.pool, n_dma_pool)):
            self.dma_pools[e.name] = [[self.new_sem("d" + e.name), 0] for _ in range(n)]
        self.dma_rr = {k: 0 for k in self.dma_pools}
        self.out_events = []

    def new_sem(self, tag):
        s = self.es.enter_context(self.nc.semaphore("%s_%d" % (tag, self.nsem)))
        self.nsem += 1
        self.sems[s.num] = s
        return s

    def _wait(self, eng, evs):
        best = {}
        for (s, v) in evs:
            if v > best.get(s, 0):
                best[s] = v
        for s, v in best.items():
            if (not eng.self_sync) and s in eng.own:
                continue
            if eng.waited.get(s, 0) >= v:
                continue
            eng.h.wait_ge(self.sems[s], v)
            eng.waited[s] = v
            eng.nwait += 1

    @staticmethod
    def _collect(reads, writes):
        evs = []
        for d in reads:
            if d.w is not None:
                evs.append(d.w)
        for d in writes:
            if d.w is not None:
                evs.append(d.w)
            evs.extend(d.r.items())
        return evs

    @staticmethod
    def _commit(ev, reads, writes):
        for d in reads:
            if ev[1] > d.r.get(ev[0], 0):
                d.r[ev[0]] = ev[1]
        for d in writes:
            d.w = ev
            d.r = {}

    def op(self, eng, fn, reads=(), writes=()):
        self._wait(eng, self._collect(reads, writes))
        inst = fn()
        ev = eng.next_event()
        inst.then_inc(self.sems[ev[0]], 1)
        eng.ninst += 1
        self._commit(ev, reads, writes)
        return ev

    def dma(self, eng, out, in_, reads=(), writes=(), is_output=False, **kw):
        pool = self.dma_pools[eng.name]
        i = self.dma_rr[eng.name]
        self.dma_rr[eng.name] = (i + 1) % len(pool)
        slot = pool[i]
        evs = self._collect(reads, writes)
        if slot[1] > 0:
            evs.append((slot[0].num, slot[1]))
        self._wait(eng, evs)
        if slot[1] + 16 > EPOCH:
            slot[0] = self.new_sem("d" + eng.name)
            slot[1] = 0
        slot[1] += 16
        eng.h.dma_start(out=out, in_=in_, **kw).then_inc(slot[0], 16)
        ev = (slot[0].num, slot[1])
        eng.ninst += 1
        self._commit(ev, reads, writes)
        if is_output:
            self.out_events.append(ev)
        return ev

    def all_events(self):
        evs = []
        for e in self.engines:
            if e.last is not None:
                evs.append(e.last)
        for pool in self.dma_pools.values():
            for s, v in pool:
                if v > 0:
                    evs.append((s.num, v))
        return evs

    def barrier(self):
        evs = self.all_events()
        for e in self.engines:
            self._wait(e, evs)

    def finish(self):
        self._wait(self.sp, list(self.out_events) + self.all_events())

    def stats(self):
        return {e.name: (e.ninst, e.nwait) for e in self.engines}, self.nsem


class Buf:
    __slots__ = ("t", "d")

    def __init__(self, t):
        self.t = t
        self.d = Dep()


class KB:
    def __init__(self, L):
        self.L = L
        self.NT = CTX + L
        self.NCH = self.NT // 128
        self.nc = bass.Bass("TRN2", target_bir_lowering=False)
        self.es = ExitStack()
        self.em = Emitter(self.nc, self.es)
        self.cur = self.es
        self.uid = 0
        self.blocks = [(0, CTX, 1)] + [(CTX + i * 512, 512, 0) for i in range(L // 512)]
        self.PS = [Buf(self.es.enter_context(self.nc.psum_tensor("psb%d" % i, [128, 512], F32))) for i in range(8)]

    def sb(self, name, shape, dtype):
        self.uid += 1
        return Buf(self.cur.enter_context(self.nc.sbuf_tensor("s_%s_%d" % (name, self.uid), shape, dtype)))

    def gsb(self, name, shape, dtype):
        return Buf(self.es.enter_context(self.nc.sbuf_tensor("g_" + name, shape, dtype)))

    def dram(self, name, shape, dtype, kind="Internal"):
        return self.nc.dram_tensor(name, shape, dtype, kind=kind).ap()

    @contextmanager
    def phase(self):
        st = ExitStack()
        prev = self.cur
        self.cur = st
        try:
            yield
        finally:
            self.em.barrier()
            st.close()
            self.cur = prev

    def mm(self, out, lhsT, rhs, start, stop, R, W):
        nc = self.nc
        return self.em.op(self.em.pe, lambda: nc.tensor.matmul(out, lhsT=lhsT, rhs=rhs, start=start, stop=stop), R, W)

    def tr(self, out, in_, ident, R, W):
        nc = self.nc
        return self.em.op(self.em.pe, lambda: nc.tensor.transpose(out=out, in_=in_, identity=ident), R, W)

    def act(self, out, in_, func, R, W, scale=None, bias=None, accum_out=None):
        nc = self.nc
        kw = {}
        if scale is not None:
            kw["scale"] = scale
        if bias is not None:
            kw["bias"] = bias
        if accum_out is not None:
            kw["accum_out"] = accum_out
        return self.em.op(self.em.act, lambda: nc.scalar.activation(out=out, in_=in_, func=func, **kw), R, W)

    def _ve(self, eng):
        return self.nc.vector if eng is self.em.dve else self.nc.gpsimd

    def tt(self, eng, out, in0, in1, op, R, W):
        h = self._ve(eng)
        return self.em.op(eng, lambda: h.tensor_tensor(out=out, in0=in0, in1=in1, op=op), R, W)

    def ts(self, eng, out, in0, s1, op0, R, W, s2=None, op1=None):
        h = self._ve(eng)
        if op1 is None:
            return self.em.op(eng, lambda: h.tensor_scalar(out=out, in0=in0, scalar1=s1, scalar2=None, op0=op0), R, W)
        return self.em.op(eng, lambda: h.tensor_scalar(out=out, in0=in0, scalar1=s1, scalar2=s2, op0=op0, op1=op1), R, W)

    def stt(self, out, in0, scalar, in1, op0, op1, R, W):
        nc = self.nc
        return self.em.op(self.em.dve, lambda: nc.vector.scalar_tensor_tensor(out=out, in0=in0, scalar=scalar, in1=in1, op0=op0, op1=op1), R, W)

    def recip(self, out, in_, R, W):
        nc = self.nc
        return self.em.op(self.em.dve, lambda: nc.vector.reciprocal(out=out, in_=in_), R, W)

    def memset(self, eng, ap, val, W):
        h = self._ve(eng)
        return self.em.op(eng, lambda: h.memset(ap, val), (), W)

    def copy(self, eng, out, in_, R, W):
        h = self._ve(eng)
        return self.em.op(eng, lambda: h.tensor_copy(out=out, in_=in_), R, W)

    def ld(self, out, in_, W, R=(), **kw):
        return self.em.dma(self.em.sp, out, in_, reads=R, writes=W, **kw)

    def st(self, out, in_, R, W=(), **kw):
        return self.em.dma(self.em.pool, out, in_, reads=R, writes=W, **kw)

    def ldcast(self, out, in_, W, R=()):
        return self.em.dma(self.em.pool, out, in_, reads=R, writes=W, max_dma_last_dim=4096)


def rope_tables(L, dim):
    rows = L // GRID_W
    rr, cc = np.meshgrid(np.arange(rows, dtype=np.float32), np.arange(GRID_W, dtype=np.float32), indexing="ij")
    quarter = dim // 4
    inv = (np.float32(THETA) ** (-np.arange(quarter, dtype=np.float32) / np.float32(quarter))).astype(np.float32)
    ang = np.concatenate([rr.reshape(-1)[:, None] * inv, cc.reshape(-1)[:, None] * inv], axis=-1).astype(np.float32)
    cos = np.concatenate([np.ones((CTX, dim // 2), np.float32), np.cos(ang)], axis=0)
    sin = np.concatenate([np.zeros((CTX, dim // 2), np.float32), np.sin(ang)], axis=0)
    return cos.astype(np.float32), sin.astype(np.float32)


def host_consts(L):
    NT = CTX + L
    c64, s64 = rope_tables(L, 64)
    c32, s32 = rope_tables(L, 32)
    cos64 = np.repeat(c64.T, 2, axis=0)
    sin64 = np.repeat(s64.T, 2, axis=0)
    cos128 = np.concatenate([cos64, cos64], 0)
    sin128 = np.concatenate([sin64, sin64], 0)
    cosm = np.concatenate([np.ones((64, NT), np.float32), np.repeat(c32.T, 2, axis=0)], 0)
    sinm = np.concatenate([np.zeros((64, NT), np.float32), np.repeat(s32.T, 2, axis=0)], 0)
    rope = np.zeros((4, 128, NT), np.float32)
    rope[0], rope[1] = cos128, sin128
    rope[2, :96], rope[3, :96] = cosm, sinm
    ident = np.eye(128, dtype=np.float32)
    rot = np.zeros((128, 128), np.float32)
    for i in range(64):
        rot[2 * i + 1, 2 * i] = -1.0
        rot[2 * i, 2 * i + 1] = 1.0
    rotm = np.zeros((128, 128), np.float32)
    rotm[64:96, 64:96] = rot[64:96, 64:96]
    shift = np.zeros((128, 128), np.float32)
    for k in range(32):
        shift[k, 64 + k] = 1.0
    bd64 = np.zeros((128, 128), np.float32)
    bd64[:64, :64] = 1.0
    bd64[64:, 64:] = 1.0
    ones = np.ones((128, 128), np.float32)
    tri = np.triu(np.ones((128, 128), np.float32))
    sq = np.concatenate([ident, rot, rotm, shift, bd64, ones, tri, tri.T.copy()], axis=1)
    return rope, sq


CI_IDENT, CI_ROT, CI_ROTM, CI_SHIFT, CI_BD64, CI_ONES, CI_TRI, CI_TRIT = range(8)
NCONST = 8

V_GQ, V_GK, V_QA, V_KVA, V_MQ, V_MK = 0, 1, 2, 5, 7, 8
NV = 16
R_CONVB, R_DTB, R_ALOG, R_RLOG, R_DSK, R_SSDN, R_RETN = 0, 1536, 1568, 1600, 1616, 2640, 3664
NR1 = 4688


def build(L=4096, nlayers=2, debug=False):
    K = KB(L)
    nc, em = K.nc, K.em
    NT, NCH = K.NT, K.NCH
    PS = K.PS
    DVE, POOL = em.dve, em.pool

    xin = K.dram("xin", [NT, D], F32, "ExternalInput")
    ccd = K.dram("cc", [128, 8, 2], F32, "ExternalInput")
    mod_w = K.dram("mod_w", [2, D, 6 * D], F32, "ExternalInput")
    modb_col = K.dram("modb_col", [128, 2, 48], F32, "ExternalInput")
    mod_b = K.dram("mod_b", [2, 6 * D], F32, "ExternalInput")
    ncol = K.dram("ncol", [128, 2, 2, 8], F32, "ExternalInput")
    vecs = K.dram("vecs", [128, NV], F32, "ExternalInput")
    ropeD = K.dram("rope", [4, 128, NT], F32, "ExternalInput")
    sqc = K.dram("sqc", [128, NCONST * 128], F32, "ExternalInput")
    w13D = K.dram("ffn_w13", [2, D, 2 * FFH], F32, "ExternalInput")
    w2D = K.dram("ffn_w2", [2, FFH, D], F32, "ExternalInput")
    awin = K.dram("attn_w_in", [D, 1440], F32, "ExternalInput")
    wqb = K.dram("mla_wq_b", [384, 768], F32, "ExternalInput")
    wkvb = K.dram("mla_wkv_b", [256, 1024], F32, "ExternalInput")
    awout = K.dram("attn_w_out", [D, D], F32, "ExternalInput")
    outD = K.dram("out", [L, D], F32, "ExternalOutput")

    QA = K.dram("QA", [4, 128, NT], BF16)
    KA = K.dram("KA", [2, 128, NT], BF16)
    VA = K.dram("VA", [128, NCH, 130], BF16)
    QM = K.dram("QM", [8, 96, NT], BF16)
    KM = K.dram("KM", [8, 96, NT], BF16)
    VM = K.dram("VM", [128, NCH, 520], BF16)
    AO = K.dram("AO", [D, NT], BF16)
    X1 = K.dram("X1", [NT, D], F32)
    U2 = K.dram("U2", [D, NT], BF16)
    X2 = K.dram("X2", [NT, D], F32, "ExternalOutput" if debug else "Internal")
    MODROW = K.dram("MODROW", [2, 2, 2, D], F32)

    CS = K.gsb("consts", [128, NCONST * 128], F32)
    K.ld(CS.t[:, :], sqc[:, :], [CS.d])

    def cst(i, r=128, c=128):
        return CS.t[0:r, i * 128:i * 128 + c]

    CB = K.gsb("constsb", [128, 2 * 128], BF16)
    K.copy(DVE, CB.t[:, 0:128], cst(CI_BD64), [CS.d], [CB.d])
    K.copy(DVE, CB.t[:, 128:256], cst(CI_ONES), [CS.d], [CB.d])
    BD64b = CB.t[:, 0:128]
    ONESb = CB.t[:, 128:256]
    VEC = K.gsb("vecs", [128, NV], F32)
    K.ld(VEC.t[:, :], vecs[:, :], [VEC.d])
    NCOL = K.gsb("ncol", [128, 2, 2, 8], F32)
    K.ld(NCOL.t[:, :, :, :], ncol[:, :, :, :], [NCOL.d])
    MODT = K.gsb("modT", [128, 2, 48, 2], F32)
    MUL = K.gsb("mulT", [128, 2, 2, 8, 2], F32)

    with K.phase():
        cc = K.sb("cc", [128, 8, 2], F32)
        K.ld(cc.t[:, :, :], ccd[:, :, :], [cc.d])
        K.act(cc.t[:, :, :], cc.t[:, :, :], AF.Silu, [cc.d], [cc.d])
        modb = K.sb("modb", [128, 2, 48], F32)
        K.ld(modb.t[:, :, :], modb_col[:, :, :], [modb.d])
        mbrow = K.sb("mbrow", [2, 2, 6 * D], F32)
        K.ld(mbrow.t[:, :, :], mod_b.partition_broadcast(2), [mbrow.d])
        wbuf = [K.sb("mw", [128, 8, 1024], F32) for _ in range(2)]
        rowb = [K.sb("rowb", [2, 1024], F32) for _ in range(2)]
        it = 0
        for i in range(nlayers):
            for v in range(6):
                wb = wbuf[it % 2]
                K.ld(wb.t[:, :, :], mod_w[i, :, v * 1024:(v + 1) * 1024].rearrange("(k p) n -> p k n", p=128), [wb.d])
                ps = PS[it % 2]
                for j in range(8):
                    for k in range(8):
                        K.mm(ps.t[:, j * 2:(j + 1) * 2], wb.t[:, k, j * 128:(j + 1) * 128], cc.t[:, k, :], k == 0, k == 7,
                             [wb.d, cc.d], [ps.d])
                K.tt(DVE, MODT.t[:, i, v * 8:(v + 1) * 8, :], ps.t[:, 0:16].rearrange("p (j s) -> p j s", s=2),
                     modb.t[:, i, v * 8:(v + 1) * 8].unsqueeze(2).broadcast_to([128, 8, 2]), ALU.add,
                     [ps.d, modb.d], [MODT.d])
                if v in (2, 5):
                    gi = 0 if v == 2 else 1
                    rb = rowb[gi]
                    for half in range(2):
                        ps2 = PS[2 + half]
                        for k in range(8):
                            K.mm(ps2.t[0:2, :], cc.t[:, k, :], wb.t[:, k, half * 512:(half + 1) * 512], k == 0, k == 7,
                                 [wb.d, cc.d], [ps2.d])
                        K.tt(DVE, rb.t[:, half * 512:(half + 1) * 512], ps2.t[0:2, :],
                             mbrow.t[:, i, v * 1024 + half * 512: v * 1024 + (half + 1) * 512], ALU.add,
                             [ps2.d, mbrow.d], [rb.d])
                    K.st(MODROW[i, gi, :, :], rb.t[:, :], [rb.d])
                it += 1
            for nrm in range(2):
                K.stt(MUL.t[:, i, nrm, :, :], MODT.t[:, i, (3 * nrm + 1) * 8:(3 * nrm + 2) * 8, :], 1.0,
                      NCOL.t[:, i, nrm, :].unsqueeze(2).broadcast_to([128, 8, 2]), ALU.add, ALU.mult,
                      [MODT.d, NCOL.d], [MUL.d])

    def mul_ap(layer, nrm, k, s):
        return MUL.t[:, layer, nrm, k, s:s + 1]

    def add_ap(layer, nrm, k, s):
        return MODT.t[:, layer, 3 * nrm * 8 + k, s:s + 1]

    def norm_to_uT(xt, nch, T, layer, nrm, s, uT, scr, psbanks):
        junk, ss, xs = scr
        for c in range(nch):
            K.act(junk.t[:, :], xt.t[:, c, :], AF.Square, [xt.d], [junk.d, ss.d], accum_out=ss.t[:, c:c + 1])
        K.ts(DVE, ss.t[:, 0:nch], ss.t[:, 0:nch], 1.0 / D, ALU.mult, [ss.d], [ss.d], s2=EPS, op1=ALU.add)
        K.act(ss.t[:, 0:nch], ss.t[:, 0:nch], AF.Sqrt, [ss.d], [ss.d])
        K.recip(ss.t[:, 0:nch], ss.t[:, 0:nch], [ss.d], [ss.d])
        for c in range(nch):
            K.ts(DVE if c % 2 == 0 else POOL, xs.t[:, c, :], xt.t[:, c, :], ss.t[:, c:c + 1], ALU.mult, [xt.d, ss.d], [xs.d])
        for k in range(8):
            ps = psbanks[k % len(psbanks)]
            for c in range(nch):
                K.tr(ps.t[:, c * 128:(c + 1) * 128], xs.t[:, c, k * 128:(k + 1) * 128], cst(CI_IDENT), [xs.d, CS.d], [ps.d])
            K.act(uT.t[:, k, 0:T], ps.t[:, 0:T], AF.Identity, [ps.d, MUL.d, MODT.d], [uT.d],
                  scale=mul_ap(layer, nrm, k, s), bias=add_ap(layer, nrm, k, s))

    def l0_phaseA():
        with K.phase():
            W = K.sb("win", [128, 8, 1440], BF16)
            for k in range(8):
                K.ldcast(W.t[:, k, :], awin[k * 128:(k + 1) * 128, :], [W.d])
            Wkd = K.sb("wkd", [128, 8, 2, 128], BF16)
            for g in range(2):
                for dup in range(2):
                    K.ldcast(Wkd.t[:, :, g, dup * 64:(dup + 1) * 64],
                             awin[:, 512 + g * 64:512 + (g + 1) * 64].rearrange("(k p) n -> p k n", p=128), [Wkd.d])
            Wq = K.sb("wqb", [128, 3, 768], BF16)
            K.ldcast(Wq.t[:, :, :], wqb.rearrange("(j p) n -> p j n", p=128), [Wq.d])
            Wkp = K.sb("wkvp", [128, 2, 8, 96], BF16)
            K.memset(DVE, Wkp.t[:, :, :, :], 0.0, [Wkp.d])
            wkv4 = wkvb.rearrange("(j p) (h t e) -> p j h t e", p=128, t=2, e=64)
            for j in range(2):
                K.ldcast(Wkp.t[:, j, :, 0:64], wkv4[:, j, :, 0, :], [Wkp.d])
            Wv = K.sb("wkvv", [128, 2, 8, 64], BF16)
            for j in range(2):
                K.ldcast(Wv.t[:, j, :, :], wkv4[:, j, :, 1, :], [Wv.d])

            xb = [K.sb("xa", [128, 4, D], F32) for _ in range(2)]
            scr = (K.sb("junk", [128, D], BF16), K.sb("ss", [128, 4], F32), K.sb("xs", [128, 4, D], F32))
            uTb = [K.sb("uT", [128, 8, 512], BF16) for _ in range(2)]
            ropeb = [K.sb("rope", [128, 4, 512], F32) for _ in range(2)]
            sqb = [K.sb("sq", [128, 512], BF16) for _ in range(2)]
            r1b = [K.sb("r1", [128, 512], F32) for _ in range(2)]
            qnb = [K.sb("qn", [128, 512], F32) for _ in range(2)]
            t1b = [K.sb("t1", [128, 512], F32) for _ in range(2)]
            t2b = [K.sb("t2", [128, 512], F32) for _ in range(2)]
            outb = [K.sb("ob", [128, 512], BF16) for _ in range(3)]
            qln = K.sb("qln", [128, 3, 512], BF16)
            kvn = K.sb("kvn", [128, 2, 512], BF16)
            krp = K.sb("krp", [32, 512], F32)
            vat = [K.sb("vat", [128, 4, 130], BF16) for _ in range(2)]
            vmt = [K.sb("vmt", [128, 4, 520], BF16) for _ in range(2)]
            for b in vat + vmt:
                K.memset(DVE, b.t[:, :, :], 1.0, [b.d])
            cnt = {"n": 0, "o": 0, "p": 0}
            PROJ = [PS[2], PS[3], PS[4]]

            def proj_bank():
                cnt["p"] += 1
                return PROJ[cnt["p"] % 3]

            def post(ps_src, R_, T, gain, ss_lhsT, inv_n, rot_lhsT, ci, rp, dst):
                i = cnt["n"] % 2
                cnt["n"] += 1
                sq, r1, qn, t1, t2 = sqb[i], r1b[i], qnb[i], t1b[i], t2b[i]
                K.act(sq.t[0:R_, 0:T], ps_src.t[0:R_, 0:T], AF.Square, [ps_src.d], [sq.d])
                pss = PS[5]
                K.mm(pss.t[0:R_, 0:T], ss_lhsT, sq.t[0:R_, 0:T], True, True, [sq.d, CB.d], [pss.d])
                K.act(r1.t[0:R_, 0:T], pss.t[0:R_, 0:T], AF.Ln, [pss.d], [r1.d], scale=inv_n, bias=EPSB.t[0:R_, 0:1])
                K.act(r1.t[0:R_, 0:T], r1.t[0:R_, 0:T], AF.Exp, [r1.d], [r1.d], scale=-0.5)
                K.stt(qn.t[0:R_, 0:T], ps_src.t[0:R_, 0:T], gain, r1.t[0:R_, 0:T], ALU.mult, ALU.mult,
                      [ps_src.d, r1.d, VEC.d], [qn.d])
                psr = PS[6]
                K.mm(psr.t[0:R_, 0:T], rot_lhsT, qn.t[0:R_, 0:T], True, True, [qn.d, CS.d], [psr.d])
                K.tt(DVE, t1.t[0:R_, 0:T], qn.t[0:R_, 0:T], rp.t[0:R_, ci, 0:T], ALU.mult, [qn.d, rp.d], [t1.d])
                K.tt(DVE, t2.t[0:R_, 0:T], psr.t[0:R_, 0:T], rp.t[0:R_, ci + 1, 0:T], ALU.mult, [psr.d, rp.d], [t2.d])
                ob = outb[cnt["o"] % 3]
                cnt["o"] += 1
                K.tt(POOL, ob.t[0:R_, 0:T], t1.t[0:R_, 0:T], t2.t[0:R_, 0:T], ALU.add, [t1.d, t2.d], [ob.d])
                K.st(dst, ob.t[0:R_, 0:T], [ob.d])

            def lora_norm(ps_list, nj, gcol0, inv_n, dstb, T):
                i = cnt["n"] % 2
                cnt["n"] += 1
                r1 = r1b[i]
                pss = PS[5]
                for j in range(nj):
                    sq = sqb[(cnt["n"] + j) % 2]
                    K.act(sq.t[:, 0:T], ps_list[j].t[:, 0:T], AF.Square, [ps_list[j].d], [sq.d])
                    K.mm(pss.t[:, 0:T], ONESb, sq.t[:, 0:T], j == 0, j == nj - 1, [sq.d, CB.d], [pss.d])
                K.act(r1.t[:, 0:T], pss.t[:, 0:T], AF.Ln, [pss.d], [r1.d], scale=inv_n, bias=EPSB.t[:, 0:1])
                K.act(r1.t[:, 0:T], r1.t[:, 0:T], AF.Exp, [r1.d], [r1.d], scale=-0.5)
                for j in range(nj):
                    K.stt(dstb.t[:, j, 0:T], ps_list[j].t[:, 0:T], VEC.t[:, gcol0 + j:gcol0 + j + 1], r1.t[:, 0:T],
                          ALU.mult, ALU.mult, [ps_list[j].d, r1.d, VEC.d], [dstb.d])

            for bi, (t0, T, s) in enumerate(K.blocks):
                nch = T // 128
                c0 = t0 // 128
                xt = xb[bi % 2]
                K.ld(xt.t[:, 0:nch, :], xin[t0:t0 + T, :].rearrange("(c p) d -> p c d", p=128), [xt.d])
                rp = ropeb[bi % 2]
                K.ld(rp.t[:, :, 0:T], ropeD[:, :, t0:t0 + T].rearrange("f p t -> p f t"), [rp.d])
                uT = uTb[bi % 2]
                norm_to_uT(xt, nch, T, 0, 0, s, uT, scr, [PS[0], PS[1]])
                for ch in range(4):
                    ps = proj_bank()
                    for k in range(8):
                        K.mm(ps.t[:, 0:T], W.t[:, k, ch * 128:(ch + 1) * 128], uT.t[:, k, 0:T], k == 0, k == 7, [W.d, uT.d], [ps.d])
                    post(ps, 128, T, VEC.t[:, V_GQ:V_GQ + 1], BD64b, 1.0 / 64, cst(CI_ROT), 0, rp, QA[ch, :, t0:t0 + T])
                for g in range(2):
                    ps = proj_bank()
                    for k in range(8):
                        K.mm(ps.t[:, 0:T], Wkd.t[:, k, g, :], uT.t[:, k, 0:T], k == 0, k == 7, [Wkd.d, uT.d], [ps.d])
                    post(ps, 128, T, VEC.t[:, V_GK:V_GK + 1], BD64b, 1.0 / 64, cst(CI_ROT), 0, rp, KA[g, :, t0:t0 + T])
                va_t = vat[bi % 2]
                for c in range(nch):
                    ps = PS[7]
                    for k in range(8):
                        K.mm(ps.t[:, 0:128], uT.t[:, k, c * 128:(c + 1) * 128], W.t[:, k, 640:768], k == 0, k == 7, [W.d, uT.d], [ps.d])
                    K.act(va_t.t[:, c, :].rearrange("p (g e) -> p g e", e=65)[:, :, 0:64],
                          ps.t[:, 0:128].rearrange("p (g e) -> p g e", e=64), AF.Copy, [ps.d], [va_t.d])
                K.st(VA[:, c0:c0 + nch, :], va_t.t[:, 0:nch, :], [va_t.d])
                pl = []
                for j in range(3):
                    ps = proj_bank()
                    for k in range(8):
                        K.mm(ps.t[:, 0:T], W.t[:, k, 768 + j * 128:768 + (j + 1) * 128], uT.t[:, k, 0:T], k == 0, k == 7, [W.d, uT.d], [ps.d])
                    pl.append(ps)
                lora_norm(pl, 3, V_QA, 1.0 / 384, qln, T)
                for h in range(8):
                    ps = proj_bank()
                    for j in range(3):
                        K.mm(ps.t[0:96, 0:T], Wq.t[:, j, h * 96:(h + 1) * 96], qln.t[:, j, 0:T], j == 0, j == 2, [Wq.d, qln.d], [ps.d])
                    post(ps, 96, T, VEC.t[0:96, V_MQ:V_MQ + 1], ONESb[0:96, 0:96], 1.0 / 96, cst(CI_ROTM, 96, 96), 2, rp, QM[h, :, t0:t0 + T])
                pl = []
                for j in range(2):
                    ps = proj_bank()
                    for k in range(8):
                        K.mm(ps.t[:, 0:T], W.t[:, k, 1152 + j * 128:1152 + (j + 1) * 128], uT.t[:, k, 0:T], k == 0, k == 7, [W.d, uT.d], [ps.d])
                    pl.append(ps)
                lora_norm(pl, 2, V_KVA, 1.0 / 256, kvn, T)
                ps = proj_bank()
                for k in range(8):
                    K.mm(ps.t[0:32, 0:T], W.t[:, k, 1408:1440], uT.t[:, k, 0:T], k == 0, k == 7, [W.d, uT.d], [ps.d])
                K.act(krp.t[0:32, 0:T], ps.t[0:32, 0:T], AF.Copy, [ps.d], [krp.d])
                for h in range(8):
                    ps = proj_bank()
                    for j in range(2):
                        K.mm(ps.t[0:96, 0:T], Wkp.t[:, j, h, :], kvn.t[:, j, 0:T], j == 0, False, [Wkp.d, kvn.d], [ps.d])
                    K.mm(ps.t[0:96, 0:T], cst(CI_SHIFT, 32, 96), krp.t[0:32, 0:T], False, True, [krp.d, CS.d], [ps.d])
                    post(ps, 96, T, VEC.t[0:96, V_MK:V_MK + 1], ONESb[0:96, 0:96], 1.0 / 96, cst(CI_ROTM, 96, 96), 2, rp, KM[h, :, t0:t0 + T])
                vm_t = vmt[bi % 2]
                for c in range(nch):
                    ps = PS[7]
                    for j in range(2):
                        K.mm(ps.t[:, 0:512], kvn.t[:, j, c * 128:(c + 1) * 128], Wv.t[:, j, :, :].rearrange("p h e -> p (h e)"),
                             j == 0, j == 1, [Wv.d, kvn.d], [ps.d])
                    K.act(vm_t.t[:, c, :].rearrange("p (g e) -> p g e", e=65)[:, :, 0:64],
                          ps.t[:, 0:512].rearrange("p (g e) -> p g e", e=64), AF.Copy, [ps.d], [vm_t.d])
                K.st(VM[:, c0:c0 + nch, :], vm_t.t[:, 0:nch, :], [vm_t.d])

    def l0_phaseB():
        with K.phase():
            va = K.sb("va", [128, NCH, 130], BF16)
            vm = K.sb("vm", [128, NCH, 520], BF16)
            K.ld(va.t[:, :, :], VA[:, :, :], [va.d])
            K.ld(vm.t[:, :, :], VM[:, :, :], [vm.d])
            kb = [K.sb("kb", [128, NT], BF16) for _ in range(2)]
            qb = [K.sb("qb", [128, 512], BF16) for _ in range(3)]
            pb = [K.sb("pb", [128, 512], BF16) for _ in range(4)]
            osb = [K.sb("osb", [64, 512], F32) for _ in range(2)]
            rsb = [K.sb("rsb", [128, 512], F32) for _ in range(2)]
            aob = [K.sb("aob", [64, 512], BF16) for _ in range(2)]
            SB_ = [PS[0], PS[1], PS[2]]
            OB_ = [PS[3], PS[4]]
            BCB = PS[5]
            tasks = []
            groups = []
            ui = 0
            qi = 0
            units = [("a", c) for c in range(4)] + [("m", h) for h in range(8)]
            for kind, idx in units:
                kbf = kb[ui % 2]
                ui += 1
                if kind == "a":
                    kload = (kbf, KA[idx // 2, :, :], 128)
                else:
                    kload = (kbf, KM[idx, :, :], 96)
                first_in_unit = True
                for (t0, T, s) in K.blocks:
                    qbf = qb[qi % 3]
                    qi += 1
                    if kind == "a":
                        qload = (qbf, QA[idx, :, t0:t0 + T], 128, T)
                        subs = [(slice(0, 64), 2 * idx, va, idx // 2, 64 ** -0.5), (slice(64, 128), 2 * idx + 1, va, idx // 2, 64 ** -0.5)]
                    else:
                        qload = (qbf, QM[idx, :, t0:t0 + T], 96, T)
                        subs = [(slice(0, 96), 8 + idx, vm, idx, 96 ** -0.5)]
                    kcs = list(range(2)) if s == 1 else list(range(NCH))
                    for si, (rows, hg, vbuf, vcol, sc) in enumerate(subs):
                        g = len(groups)
                        groups.append((hg, t0, T))
                        for n, kc in enumerate(kcs):
                            tasks.append(dict(kb=kbf, rows=rows, kc=kc, qb=qbf, T=T, vb=vbuf, vcol=vcol, first=(n == 0),
                                              last=(n == len(kcs) - 1), grp=g, sc=sc,
                                              kload=kload if (first_in_unit and si == 0 and n == 0) else None,
                                              qload=qload if (si == 0 and n == 0) else None))
                    first_in_unit = False

            def emit_qk(i):
                t = tasks[i]
                if t["kload"] is not None:
                    kbf, src, R_ = t["kload"]
                    K.ld(kbf.t[0:R_, :], src[0:R_, :], [kbf.d])
                if t["qload"] is not None:
                    qbf, src, R_, T = t["qload"]
                    K.ld(qbf.t[0:R_, 0:T], src[0:R_, :], [qbf.d])
                ps = SB_[i % 3]
                kc, T = t["kc"], t["T"]
                K.mm(ps.t[:, 0:T], t["kb"].t[t["rows"], kc * 128:(kc + 1) * 128], t["qb"].t[t["rows"], 0:T], True, True,
                     [t["kb"].d, t["qb"].d], [ps.d])

            n = len(tasks)
            for i in range(min(2, n)):
                emit_qk(i)
            for i in range(n):
                t = tasks[i]
                T = t["T"]
                ps = SB_[i % 3]
                p = pb[i % 4]
                K.act(p.t[:, 0:T], ps.t[:, 0:T], AF.Exp, [ps.d], [p.d], scale=t["sc"])
                if i + 2 < n:
                    emit_qk(i + 2)
                po = OB_[t["grp"] % 2]
                vc = t["vcol"]
                K.mm(po.t[0:65, 0:T], t["vb"].t[:, t["kc"], vc * 65:(vc + 1) * 65], p.t[:, 0:T], t["first"], t["last"],
                     [t["vb"].d, p.d], [po.d])
                if t["last"]:
                    g = t["grp"]
                    hg, t0, _ = groups[g]
                    rs, osb_, ao_ = rsb[g % 2], osb[g % 2], aob[g % 2]
                    K.recip(rs.t[64:65, 0:T], po.t[64:65, 0:T], [po.d], [rs.d])
                    K.mm(BCB.t[0:64, 0:T], cst(CI_ONES)[64:65, 0:64], rs.t[64:65, 0:T], True, True, [rs.d, CS.d], [BCB.d])
                    K.act(osb_.t[0:64, 0:T], po.t[0:64, 0:T], AF.Copy, [po.d], [osb_.d])
                    K.tt(DVE, ao_.t[0:64, 0:T], osb_.t[0:64, 0:T], BCB.t[0:64, 0:T], ALU.mult, [osb_.d, BCB.d], [ao_.d])
                    K.st(AO[hg * 64:(hg + 1) * 64, t0:t0 + T], ao_.t[0:64, 0:T], [ao_.d])

    def phaseC1(layer, blocks, AOsrc, nK, woutD, Xsrc, xoff):
        with K.phase():
            wo = K.sb("wo", [128, nK, D], BF16)
            for k in range(nK):
                K.ldcast(wo.t[:, k, :], woutD[k * 128:(k + 1) * 128, :], [wo.d])
            gts = {}
            for s in set(b[2] for b in blocks):
                gts[s] = K.sb("gt", [128, D], F32)
                K.ld(gts[s].t[:, :], MODROW[layer, 0, s, :].partition_broadcast(128), [gts[s].d])
            aob = [K.sb("ao", [128, nK, 512], BF16) for _ in range(2)]
            xb = [K.sb("xc", [128, 4, D], F32) for _ in range(2)]
            tmpb = [K.sb("tmp", [128, 512], F32) for _ in range(2)]
            scr = (K.sb("junk", [128, D], BF16), K.sb("ss", [128, 4], F32), K.sb("xs", [128, 4, D], F32))
            uTb = [K.sb("uT", [128, 8, 512], BF16) for _ in range(2)]
            it = 0
            for bi, (t0, T, s) in enumerate(blocks):
                nch = T // 128
                ao = aob[bi % 2]
                K.ld(ao.t[:, :, 0:T], AOsrc[:, t0:t0 + T].rearrange("(k p) t -> p k t", p=128), [ao.d])
                xt = xb[bi % 2]
                K.ld(xt.t[:, 0:nch, :], Xsrc[t0 - xoff:t0 - xoff + T, :].rearrange("(c p) d -> p c d", p=128), [xt.d])
                for c in range(nch):
                    for n2 in range(2):
                        ps = PS[2 + it % 4]
                        tmp = tmpb[it % 2]
                        it += 1
                        for k in range(nK):
                            K.mm(ps.t[:, :], ao.t[:, k, c * 128:(c + 1) * 128], wo.t[:, k, n2 * 512:(n2 + 1) * 512], k == 0, k == nK - 1,
                                 [ao.d, wo.d], [ps.d])
                        K.tt(DVE, tmp.t[:, :], ps.t[:, :], gts[s].t[:, n2 * 512:(n2 + 1) * 512], ALU.mult, [ps.d, gts[s].d], [tmp.d])
                        K.tt(POOL, xt.t[:, c, n2 * 512:(n2 + 1) * 512], xt.t[:, c, n2 * 512:(n2 + 1) * 512], tmp.t[:, :], ALU.add,
                             [xt.d, tmp.d], [xt.d])
                K.st(X1[t0:t0 + T, :].rearrange("(c p) d -> p c d", p=128), xt.t[:, 0:nch, :], [xt.d])
                uT = uTb[bi % 2]
                norm_to_uT(xt, nch, T, layer, 1, s, uT, scr, [PS[0], PS[1]])
                K.st(U2[:, t0:t0 + T].rearrange("(k p) t -> p k t", p=128), uT.t[:, :, 0:T], [uT.d])

    def phaseC2(layer, blocks, Xdst, xoff, is_out):
        with K.phase():
            w13 = K.sb("w13", [128, 8, 2 * FFH], BF16)
            for k in range(8):
                for hh in range(2):
                    K.ldcast(w13.t[:, k, hh * FFH:(hh + 1) * FFH], w13D[layer, k * 128:(k + 1) * 128, hh * FFH:(hh + 1) * FFH], [w13.d])
            w2 = K.sb("w2", [128, 22, D], BF16)
            for j in range(22):
                K.ldcast(w2.t[:, j, :], w2D[layer, j * 128:(j + 1) * 128, :], [w2.d])
            gt1 = K.sb("gt", [128, D], F32)
            gts = {0: gt1, 1: gt1}
            ub = [K.sb("u", [128, 8, 512], BF16) for _ in range(1)]
            xb = [K.sb("xf", [128, 4, D], F32) for _ in range(1)]
            hb = K.sb("h", [128, 22, 512], BF16)
            sgb = [K.sb("sg", [128, 512], F32) for _ in range(2)]
            tmpb = [K.sb("tmp", [128, 512], F32) for _ in range(2)]
            it = 0
            for bi, (t0, T, s) in enumerate(blocks):
                nch = T // 128
                u = ub[0]
                K.ld(u.t[:, :, 0:T], U2[:, t0:t0 + T].rearrange("(k p) t -> p k t", p=128), [u.d])
                if bi == 0 or blocks[bi - 1][2] != s:
                    K.ld(gt1.t[:, :], MODROW[layer, 1, s, :].partition_broadcast(128), [gt1.d])
                xt = xb[0]
                K.ld(xt.t[:, 0:nch, :], X1[t0:t0 + T, :].rearrange("(c p) d -> p c d", p=128), [xt.d])
                for j in range(22):
                    psg = PS[(2 * j) % 4]
                    psu = PS[(2 * j + 1) % 4]
                    for k in range(8):
                        K.mm(psg.t[:, 0:T], w13.t[:, k, j * 128:(j + 1) * 128], u.t[:, k, 0:T], k == 0, k == 7, [w13.d, u.d], [psg.d])
                    for k in range(8):
                        K.mm(psu.t[:, 0:T], w13.t[:, k, FFH + j * 128:FFH + (j + 1) * 128], u.t[:, k, 0:T], k == 0, k == 7, [w13.d, u.d], [psu.d])
                    sg = sgb[j % 2]
                    K.act(sg.t[:, 0:T], psg.t[:, 0:T], AF.Silu, [psg.d], [sg.d])
                    K.tt(DVE, hb.t[:, j, 0:T], sg.t[:, 0:T], psu.t[:, 0:T], ALU.mult, [sg.d, psu.d], [hb.d])
                for c in range(nch):
                    for n2 in range(2):
                        ps = PS[4 + it % 4]
                        tmp = tmpb[it % 2]
                        it += 1
                        for j in range(22):
                            K.mm(ps.t[:, :], hb.t[:, j, c * 128:(c + 1) * 128], w2.t[:, j, n2 * 512:(n2 + 1) * 512], j == 0, j == 21,
                                 [hb.d, w2.d], [ps.d])
                        K.tt(DVE, tmp.t[:, :], ps.t[:, :], gts[s].t[:, n2 * 512:(n2 + 1) * 512], ALU.mult, [ps.d, gts[s].d], [tmp.d])
                        K.tt(POOL, xt.t[:, c, n2 * 512:(n2 + 1) * 512], xt.t[:, c, n2 * 512:(n2 + 1) * 512], tmp.t[:, :], ALU.add,
                             [xt.d, tmp.d], [xt.d])
                K.em.dma(em.pool, Xdst[t0 - xoff:t0 - xoff + T, :].rearrange("(c p) d -> p c d", p=128), xt.t[:, 0:nch, :],
                         reads=[xt.d], is_output=is_out)

    swin = K.dram("ssm_w_in", [D, 5664], F32, "ExternalInput")
    swout = K.dram("ssm_w_out", [2048, D], F32, "ExternalInput")
    convw_col = K.dram("convw_col", [128, 60], F32, "ExternalInput")
    convb_col = K.dram("convb_col", [128, 12], F32, "ExternalInput")
    rows1 = K.dram("rows1", [NR1], F32, "ExternalInput")
    sqc2 = K.dram("sqc2", [128, 2 * 128 + 4], F32, "ExternalInput")
    selc = K.dram("selc", [16, 2048], F32, "ExternalInput")
    SZ = K.dram("SZ", [NT, D], F32)
    SRG = K.dram("SRG", [NT, D], F32)
    DTs = K.dram("DTs", [NT, 32], F32)
    RVs = K.dram("RVs", [NT, D], BF16)
    XBC = K.dram("XBC", [1536, NT], BF16)
    RQ = K.dram("RQ", [4, 128, NT], BF16)
    RK = K.dram("RK", [4, 128, NT], BF16)
    RKT = K.dram("RKT", [NT, 512], BF16)
    XS = K.dram("XS", [NT, D], F32)
    BT = K.dram("BT", [NT, 256], BF16)
    BFs = K.dram("BFs", [256, NT], BF16)
    CFs = K.dram("CFs", [256, NT], BF16)
    YF = K.dram("YF", [NT, 2048], F32)
    YS = K.dram("YS", [2048, NT], BF16)
    ONEB = K.gsb("oneb", [128, 1], F32)
    K.memset(DVE, ONEB.t[:, :], 1.0, [ONEB.d])

    def softplus_small(x, tmp, R, W_):
        K.ts(DVE, tmp, x, -1.0, ALU.mult, R, W_)
        K.tt(DVE, tmp, tmp, x, ALU.max, R + W_, W_)
        K.act(tmp, tmp, AF.Exp, W_, W_, scale=-1.0)
        K.act(tmp, tmp, AF.Ln, W_, W_, bias=ONEB.t[0:x.shape[0], 0:1])
        K.stt(x, x, 0.0, tmp, ALU.max, ALU.add, R + W_, R)

    def l1_phaseA():
        with K.phase():
            W1 = K.sb("w1", [128, 8, 5664], BF16)
            for k in range(8):
                for (a, b) in ((0, 1888), (1888, 3776), (3776, 5664)):
                    K.ldcast(W1.t[:, k, a:b], swin[k * 128:(k + 1) * 128, a:b], [W1.d])
            xb = [K.sb("xa", [128, 4, D], F32)]
            scr = (K.sb("junk", [128, D], BF16), K.sb("ss", [128, 4], F32), K.sb("xs", [128, 4, D], F32))
            uTb = [K.sb("uT", [128, 8, 512], BF16) for _ in range(2)]
            ropeb = [K.sb("rope", [128, 2, 512], F32) for _ in range(2)]
            qnb = [K.sb("qn", [128, 512], F32) for _ in range(2)]
            t1b = [K.sb("t1", [128, 512], F32) for _ in range(2)]
            t2b = [K.sb("t2", [128, 512], F32) for _ in range(2)]
            ofb = [K.sb("of", [128, 512], F32) for _ in range(2)]
            obb = [K.sb("ob", [128, 512], BF16) for _ in range(3)]
            tmz = [K.sb("tmz", [128, D], F32) for _ in range(3)]
            rvt = [K.sb("rvt", [128, D], BF16) for _ in range(2)]
            rktb = [K.sb("rkt", [128, 512], BF16) for _ in range(2)]
            dtt = [K.sb("dtt", [128, 32], F32) for _ in range(2)]
            cnt = {"p": 0, "n": 0, "o": 0, "z": 0, "t": 0}

            def pbank(banks):
                cnt["p"] += 1
                return banks[cnt["p"] % len(banks)]

            for bi, (t0, T, s) in enumerate(K.blocks):
                nch = T // 128
                xt = xb[0]
                K.ld(xt.t[:, 0:nch, :], X2[t0:t0 + T, :].rearrange("(c p) d -> p c d", p=128), [xt.d])
                rp = ropeb[bi % 2]
                K.ld(rp.t[:, :, 0:T], ropeD[0:2, :, t0:t0 + T].rearrange("f p t -> p f t"), [rp.d])
                uT = uTb[bi % 2]
                norm_to_uT(xt, nch, T, 1, 0, s, uT, scr, [PS[0], PS[1]])
                for ch in range(12):
                    ps = pbank([PS[2], PS[3]])
                    for k in range(8):
                        K.mm(ps.t[:, 0:T], W1.t[:, k, 1024 + ch * 128:1024 + (ch + 1) * 128], uT.t[:, k, 0:T], k == 0, k == 7, [W1.d, uT.d], [ps.d])
                    ob = obb[cnt["o"] % 3]
                    cnt["o"] += 1
                    K.act(ob.t[:, 0:T], ps.t[:, 0:T], AF.Copy, [ps.d], [ob.d])
                    K.st(XBC[ch * 128:(ch + 1) * 128, t0:t0 + T], ob.t[:, 0:T], [ob.d])
                for kind in range(2):
                    for ch in range(4):
                        col0 = 2592 + kind * 512 + ch * 128
                        ps = pbank([PS[2], PS[3]])
                        for k in range(8):
                            K.mm(ps.t[:, 0:T], W1.t[:, k, col0:col0 + 128], uT.t[:, k, 0:T], k == 0, k == 7, [W1.d, uT.d], [ps.d])
                        i = cnt["n"] % 2
                        cnt["n"] += 1
                        qn, t1, t2, of = qnb[i], t1b[i], t2b[i], ofb[i]
                        K.act(qn.t[:, 0:T], ps.t[:, 0:T], AF.Copy, [ps.d], [qn.d], scale=(1.0 if kind == 0 else 0.125))
                        psr = PS[0]
                        K.mm(psr.t[:, 0:T], cst(CI_ROT), qn.t[:, 0:T], True, True, [qn.d, CS.d], [psr.d])
                        K.tt(DVE, t1.t[:, 0:T], qn.t[:, 0:T], rp.t[:, 0, 0:T], ALU.mult, [qn.d, rp.d], [t1.d])
                        K.tt(DVE, t2.t[:, 0:T], psr.t[:, 0:T], rp.t[:, 1, 0:T], ALU.mult, [psr.d, rp.d], [t2.d])
                        K.tt(POOL, of.t[:, 0:T], t1.t[:, 0:T], t2.t[:, 0:T], ALU.add, [t1.d, t2.d], [of.d])
                        ob = obb[cnt["o"] % 3]
                        cnt["o"] += 1
                        K.act(ob.t[:, 0:T], of.t[:, 0:T], AF.Copy, [of.d], [ob.d])
                        K.st((RQ if kind == 0 else RK)[ch, :, t0:t0 + T], ob.t[:, 0:T], [ob.d])
                        if kind == 1:
                            for c in range(nch):
                                K.tr(PS[4 + c].t[:, ch * 128:(ch + 1) * 128], of.t[:, c * 128:(c + 1) * 128], cst(CI_IDENT), [of.d, CS.d], [PS[4 + c].d])
                for c in range(nch):
                    rkt = rktb[c % 2]
                    K.copy(DVE, rkt.t[:, :], PS[4 + c].t[:, :], [PS[4 + c].d], [rkt.d])
                    K.st(RKT[t0 + c * 128:t0 + (c + 1) * 128, :], rkt.t[:, :], [rkt.d])
                TMB = [PS[2], PS[3], PS[4], PS[5], PS[6], PS[7]]
                for c in range(nch):
                    r0 = t0 + c * 128
                    for (col0, kind) in ((0, "z"), (4640, "g"), (3616, "v")):
                        if kind == "v":
                            dst = rvt[cnt["t"] % 2]
                            cnt["t"] += 1
                        else:
                            dst = tmz[cnt["z"] % 3]
                            cnt["z"] += 1
                        for half in range(2):
                            ps = pbank(TMB)
                            for k in range(8):
                                K.mm(ps.t[:, :], uT.t[:, k, c * 128:(c + 1) * 128], W1.t[:, k, col0 + half * 512:col0 + (half + 1) * 512],
                                     k == 0, k == 7, [W1.d, uT.d], [ps.d])
                            if kind == "v":
                                K.copy(DVE, dst.t[:, half * 512:(half + 1) * 512], ps.t[:, :], [ps.d], [dst.d])
                            else:
                                K.act(dst.t[:, half * 512:(half + 1) * 512], ps.t[:, :], AF.Silu, [ps.d], [dst.d])
                        K.st({"z": SZ, "g": SRG, "v": RVs}[kind][r0:r0 + 128, :], dst.t[:, :], [dst.d])
                    ps = pbank(TMB)
                    for k in range(8):
                        K.mm(ps.t[:, 0:32], uT.t[:, k, c * 128:(c + 1) * 128], W1.t[:, k, 2560:2592], k == 0, k == 7, [W1.d, uT.d], [ps.d])
                    dd = dtt[c % 2]
                    K.copy(DVE, dd.t[:, :], ps.t[:, 0:32], [ps.d], [dd.d])
                    K.st(DTs[r0:r0 + 128, :], dd.t[:, :], [dd.d])

    def l1_phaseV():
        with K.phase():
            cwc = K.sb("cwc", [128, 60], F32)
            K.ld(cwc.t[:, :], convw_col[:, :], [cwc.d])
            cbc = K.sb("cbc", [128, 12], F32)
            K.ld(cbc.t[:, :], convb_col[:, :], [cbc.d])
            cbr = K.sb("cbr", [1, 1536], F32)
            K.ld(cbr.t[:, :], rows1[R_CONVB:R_CONVB + 1536].partition_broadcast(1), [cbr.d])
            DG = K.sb("dg", [128, 60, 128], BF16)
            for idx in range(60):
                K.ts(DVE if idx % 2 == 0 else POOL, DG.t[:, idx, :], cst(CI_IDENT), cwc.t[:, idx:idx + 1], ALU.mult, [CS.d, cwc.d], [DG.d])
            xwb = [K.sb("xw", [128, 12, 516], BF16) for _ in range(2)]
            obb = [K.sb("ob", [128, 512], BF16) for _ in range(2)]
            xst = [K.sb("xst", [128, D], F32) for _ in range(2)]
            btt = [K.sb("btt", [128, 256], BF16) for _ in range(2)]
            ones_row = cst(CI_ONES)[0:1, 0:128]
            cnt = {"p": 0}
            BK = [PS[0], PS[1], PS[2], PS[3], PS[4], PS[5], PS[6], PS[7]]

            def pbank():
                cnt["p"] += 1
                return BK[cnt["p"] % 8]

            for bi, (t0, T, s) in enumerate(K.blocks):
                nch = T // 128
                seg0, seg1 = (0, CTX) if s == 1 else (CTX, NT)
                lo, hi = max(t0 - 2, seg0), min(t0 + T + 2, seg1)
                xw = xwb[bi % 2]
                K.memset(POOL, xw.t[:, :, :], 0.0, [xw.d])
                K.ld(xw.t[:, :, lo - (t0 - 2):hi - (t0 - 2)], XBC[:, lo:hi].rearrange("(c p) t -> p c t", p=128), [xw.d])
                for ch in range(8, 12):
                    ps = pbank()
                    for k in range(5):
                        K.mm(ps.t[:, 0:T], DG.t[:, k * 12 + ch, :], xw.t[:, ch, k:k + T], k == 0, k == 4, [DG.d, xw.d], [ps.d])
                    ob = obb[ch % 2]
                    K.act(ob.t[:, 0:T], ps.t[:, 0:T], AF.Silu, [ps.d, cbc.d], [ob.d], bias=cbc.t[:, ch:ch + 1])
                    dstD = BFs if ch < 10 else CFs
                    r = (ch - 8) % 2
                    K.st(dstD[r * 128:(r + 1) * 128, t0:t0 + T], ob.t[:, 0:T], [ob.d])
                for c in range(nch):
                    r0 = t0 + c * 128
                    banks = [pbank(), pbank(), pbank()]
                    for ch in range(10):
                        tgt = banks[ch // 4]
                        o_ap = tgt.t[:, (ch % 4) * 128:(ch % 4 + 1) * 128]
                        for k in range(5):
                            K.mm(o_ap, xw.t[:, ch, c * 128 + k:c * 128 + k + 128], DG.t[:, k * 12 + ch, :], k == 0, False, [DG.d, xw.d], [tgt.d])
                        K.mm(o_ap, ones_row, cbr.t[0:1, ch * 128:(ch + 1) * 128], False, True, [CS.d, cbr.d], [tgt.d])
                    xo = xst[c % 2]
                    for hh in range(2):
                        K.act(xo.t[:, hh * 512:(hh + 1) * 512], banks[hh].t[:, :], AF.Silu, [banks[hh].d], [xo.d])
                    K.st(XS[r0:r0 + 128, :], xo.t[:, :], [xo.d])
                    bo = btt[c % 2]
                    K.act(bo.t[:, :], banks[2].t[:, 0:256], AF.Silu, [banks[2].d], [bo.d])
                    K.st(BT[r0:r0 + 128, :], bo.t[:, :], [bo.d])

    def l1_scan(dirn):
        fin = dirn == 1
        with K.phase():
            C2 = K.sb("c2", [128, 2 * 128 + 4], F32)
            K.ld(C2.t[:, :], sqc2[:, :], [C2.d])
            SEL = K.sb("sel", [16, 2048], F32)
            K.ld(SEL.t[:, :], selc[:, :], [SEL.d])
            IDX = C2.t[:, dirn * 128:(dirn + 1) * 128]
            colA = C2.t[:, 256 + dirn:256 + dirn + 1]
            colE = C2.t[:, 258 + dirn:258 + dirn + 1]
            MASK = cst(CI_TRI) if dirn == 0 else cst(CI_TRIT)

            def brow(name, off, n):
                b = K.sb(name, [128, n], F32)
                K.ld(b.t[:, :], rows1[off:off + n].partition_broadcast(128), [b.d])
                return b

            DTB = brow("dtb", R_DTB + dirn * 16, 16)
            ANEG = brow("aneg", R_ALOG + dirn * 16, 16)
            K.act(ANEG.t[:, :], ANEG.t[:, :], AF.Exp, [ANEG.d], [ANEG.d])
            K.ts(DVE, ANEG.t[:, :], ANEG.t[:, :], -1.0, ALU.mult, [ANEG.d], [ANEG.d])
            LG = brow("lg", R_RLOG + dirn * 8, 8)
            lgt = K.sb("lgt", [128, 8], F32)
            K.ts(DVE, LG.t[:, :], LG.t[:, :], -1.0, ALU.mult, [LG.d], [LG.d])
            softplus_small(LG.t[:, :], lgt.t[:, :], [LG.d], [lgt.d])
            K.ts(DVE, LG.t[:, :], LG.t[:, :], -1.0, ALU.mult, [LG.d], [LG.d])
            LM = K.sb("lm", [128, 8, 128], F32)
            for h in range(8):
                K.ts(DVE, LM.t[:, h, :], IDX, LG.t[:, h:h + 1], ALU.mult, [C2.d, LG.d], [LM.d])
            K.act(LM.t[:, :, :], LM.t[:, :, :], AF.Exp, [LM.d], [LM.d])
            K.tt(DVE, LM.t[:, :, :], LM.t[:, :, :], MASK.unsqueeze(1).broadcast_to([128, 8, 128]), ALU.mult, [LM.d, CS.d], [LM.d])
            EAr = K.sb("ear", [128, 8], F32)
            K.ts(DVE, EAr.t[:, :], LG.t[:, :], colA, ALU.mult, [LG.d, C2.d], [EAr.d])
            K.act(EAr.t[:, :], EAr.t[:, :], AF.Exp, [EAr.d], [EAr.d])
            DEr = K.sb("der", [128, 8], F32)
            K.ts(DVE, DEr.t[:, :], LG.t[:, :], colE, ALU.mult, [LG.d, C2.d], [DEr.d])
            K.act(DEr.t[:, :], DEr.t[:, :], AF.Exp, [DEr.d], [DEr.d])
            CDR = K.sb("cdr", [128, 4], F32)
            lg2 = LG.t[:, :].rearrange("p (q two) -> p q two", two=2)
            K.ts(DVE, CDR.t[0:64, :], lg2[0:64, :, 0], 128.0, ALU.mult, [LG.d], [CDR.d])
            K.ts(DVE, CDR.t[64:128, :], lg2[64:128, :, 1], 128.0, ALU.mult, [LG.d], [CDR.d])
            K.act(CDR.t[:, :], CDR.t[:, :], AF.Exp, [CDR.d], [CDR.d])
            if fin:
                DSK = brow("dsk", R_DSK, 1024)
                WN = brow("wn", R_SSDN, 1024)
                RN = brow("rn", R_RETN, 1024)
            S = [K.sb("S", [128, 512], F32) for _ in range(2)]
            Sbf = [K.sb("Sbf", [128, 512], BF16) for _ in range(2)]
            SR = K.sb("SR", [128, 4, 128], F32)
            SRbf = K.sb("SRbf", [128, 4, 128], BF16)
            for b in S + Sbf:
                K.memset(DVE, b.t[:, :], 0.0, [b.d])
            K.memset(DVE, SR.t[:, :, :], 0.0, [SR.d])
            K.memset(DVE, SRbf.t[:, :, :], 0.0, [SRbf.d])
            NB = 2
            xsb = [K.sb("xs", [128, D], F32) for _ in range(NB)]
            btb = [K.sb("bt", [128, 256], BF16) for _ in range(NB)]
            bfb = [K.sb("bf", [128, 2, 128], BF16) for _ in range(NB)]
            cfb = [K.sb("cf", [128, 2, 128], BF16) for _ in range(NB)]
            dtb_ = [K.sb("dt", [128, 16], F32) for _ in range(NB)]
            rqb = [K.sb("rq", [128, 4, 128], BF16) for _ in range(NB)]
            rkb = [K.sb("rk", [128, 4, 128], BF16) for _ in range(NB)]
            rktb = [K.sb("rkt", [128, 512], BF16) for _ in range(NB)]
            rvb = [K.sb("rv", [128, D], BF16) for _ in range(NB)]
            if fin:
                yfb = [K.sb("yf", [128, 2048], F32) for _ in range(NB)]
                szb = [K.sb("sz", [128, D], F32) for _ in range(NB)]
                sgb = [K.sb("srg", [128, D], F32) for _ in range(NB)]
            sm = K.sb("sm", [128, 8, 16], F32)
            at = K.sb("at", [16, 256], F32)
            E = K.sb("E", [128, 16, 128], F32)
            Gm = K.sb("Gm", [128, 2, 128], F32)
            M = K.sb("M", [128, 16, 128], BF16)
            MR = K.sb("MR", [128, 8, 128], BF16)
            Vd = K.sb("Vd", [128, D], BF16)
            Vdec = K.sb("Vdec", [128, D], BF16)
            RVd = K.sb("RVd", [128, D], BF16)
            yo = K.sb("yo", [128, D], F32)
            ydb = [K.sb("yd", [128, 2048], F32) for _ in range(2)]
            if fin:
                junk = K.sb("junk", [128, D], F32)
                st8 = K.sb("st8", [128, 8, 8], F32)
                ynb = K.sb("yn", [128, 2048], F32)
                ysb = [K.sb("ys", [128, 4, 128], BF16) for _ in range(2)]
            cnt = {"p": 0}

            def pbank():
                cnt["p"] += 1
                return PS[cnt["p"] % 8]

            order = list(range(NCH)) if dirn == 0 else [1, 0] + list(range(NCH - 1, 1, -1))

            def loads(ci):
                c = order[ci]
                t0 = c * 128
                i = ci % NB
                K.ld(xsb[i].t[:, :], XS[t0:t0 + 128, :], [xsb[i].d])
                K.ld(btb[i].t[:, :], BT[t0:t0 + 128, :], [btb[i].d])
                K.ld(bfb[i].t[:, :, :], BFs[:, t0:t0 + 128].rearrange("(g n) t -> n g t", n=128), [bfb[i].d])
                K.ld(cfb[i].t[:, :, :], CFs[:, t0:t0 + 128].rearrange("(g n) t -> n g t", n=128), [cfb[i].d])
                K.ld(dtb_[i].t[:, :], DTs[t0:t0 + 128, dirn * 16:(dirn + 1) * 16], [dtb_[i].d])
                K.ld(rqb[i].t[:, :, :], RQ[:, :, t0:t0 + 128].rearrange("c p t -> p c t"), [rqb[i].d])
                K.ld(rkb[i].t[:, :, :], RK[:, :, t0:t0 + 128].rearrange("c p t -> p c t"), [rkb[i].d])
                K.ld(rktb[i].t[:, :], RKT[t0:t0 + 128, :], [rktb[i].d])
                K.ld(rvb[i].t[:, :], RVs[t0:t0 + 128, :], [rvb[i].d])
                if fin:
                    K.ld(yfb[i].t[:, :], YF[t0:t0 + 128, :], [yfb[i].d])
                    K.ld(szb[i].t[:, :], SZ[t0:t0 + 128, :], [szb[i].d])
                    K.ld(sgb[i].t[:, :], SRG[t0:t0 + 128, :], [sgb[i].d])

            def bc3(ap2, n):
                return ap2.unsqueeze(2).broadcast_to([128, ap2.shape[1], n])

            import os
            KSC = int(os.environ.get("KSCAN", "99"))
            loads(0)
            for ci in range(len(order)):
                if ci + 1 < len(order):
                    loads(ci + 1)
                c = order[ci]
                t0 = c * 128
                i = ci % NB
                xs_c, bt_c, bf_c, cf_c, dt_c = xsb[i], btb[i], bfb[i], cfb[i], dtb_[i]
                rq_c, rk_c, rkt_c, rv_c = rqb[i], rkb[i], rktb[i], rvb[i]
                yd = ydb[ci % 2]
                sp, tmpv, la, Acol, Atot, expA, dece, cd = [sm.t[:, j, :] for j in range(8)]
                smd = [sm.d]
                if KSC < 1:
                    continue
                K.tt(DVE, sp, dt_c.t[:, :], DTB.t[:, :], ALU.add, [dt_c.d, DTB.d], smd)
                softplus_small(sp, tmpv, smd, smd)
                K.tt(DVE, la, sp, ANEG.t[:, :], ALU.mult, smd + [ANEG.d], smd)
                psc = pbank()
                K.mm(psc.t[:, 0:16], MASK, la, True, True, smd + [CS.d], [psc.d])
                K.mm(psc.t[:, 16:32], cst(CI_ONES), la, True, True, smd + [CS.d], [psc.d])
                K.mm(psc.t[0:16, 128:256], la, MASK, True, True, smd + [CS.d], [psc.d])
                K.copy(DVE, sm.t[:, 3:5, :], psc.t[:, 0:32].rearrange("p (a b) -> p a b", b=16), [psc.d], smd)
                K.copy(DVE, at.t[:, 0:128], psc.t[0:16, 128:256], [psc.d], [at.d])
                K.ts(DVE, at.t[:, 128:256], at.t[:, 0:128], -1.0, ALU.mult, [at.d], [at.d])
                K.act(expA, Acol, AF.Exp, smd, smd)
                K.tt(DVE, dece, Atot, Acol, ALU.subtract, smd, smd)
                K.act(dece, dece, AF.Exp, smd, smd)
                K.act(cd, Atot, AF.Exp, smd, smd)
                K.tt(DVE, tmpv, sp, dece, ALU.mult, smd, smd)
                if KSC < 2:
                    continue
                xs3 = xs_c.t[:, :].rearrange("p (h e) -> p h e", e=64)
                K.tt(DVE, Vd.t[:, :].rearrange("p (h e) -> p h e", e=64), xs3, bc3(sp, 64), ALU.mult, [xs_c.d] + smd, [Vd.d])
                K.tt(POOL, Vdec.t[:, :].rearrange("p (h e) -> p h e", e=64), xs3, bc3(tmpv, 64), ALU.mult, [xs_c.d] + smd, [Vdec.d])
                if KSC < 3:
                    continue
                for q in range(4):
                    psd = pbank()
                    for hh in range(4):
                        h = q * 4 + hh
                        o_ap = psd.t[:, hh * 128:(hh + 1) * 128]
                        K.mm(o_ap, SEL.t[0:16, h * 128:(h + 1) * 128], at.t[0:16, 0:128], True, False, [SEL.d, at.d], [psd.d])
                        K.mm(o_ap, at.t[0:16, 128:256], SEL.t[0:16, h * 128:(h + 1) * 128], False, True, [SEL.d, at.d], [psd.d])
                    K.tt(DVE, E.t[:, q * 4:(q + 1) * 4, :], psd.t[:, :].rearrange("p (h i) -> p h i", i=128),
                         MASK.unsqueeze(1).broadcast_to([128, 4, 128]), ALU.mult, [psd.d, CS.d], [E.d])
                K.act(E.t[:, :, :], E.t[:, :, :], AF.Exp, [E.d], [E.d])
                if KSC < 4:
                    continue
                psg = pbank()
                for g in range(2):
                    K.mm(psg.t[:, g * 128:(g + 1) * 128], bf_c.t[:, g, :], cf_c.t[:, g, :], True, True, [bf_c.d, cf_c.d], [psg.d])
                K.tt(DVE, Gm.t[:, :, :], psg.t[:, 0:256].rearrange("p (g i) -> p g i", i=128),
                     MASK.unsqueeze(1).broadcast_to([128, 2, 128]), ALU.mult, [psg.d, CS.d], [Gm.d])
                for g in range(2):
                    K.tt(DVE if g == 0 else POOL, M.t[:, g * 8:(g + 1) * 8, :], E.t[:, g * 8:(g + 1) * 8, :],
                         Gm.t[:, g, :].unsqueeze(1).broadcast_to([128, 8, 128]), ALU.mult, [E.d, Gm.d], [M.d])
                for g in range(2):
                    pso = pbank()
                    K.mm(pso.t[:, :], cf_c.t[:, g, :], Sbf[g].t[:, :], True, True, [cf_c.d, Sbf[g].d], [pso.d])
                    K.tt(DVE, yo.t[:, g * 512:(g + 1) * 512].rearrange("p (h e) -> p h e", e=64),
                         pso.t[:, :].rearrange("p (h e) -> p h e", e=64), bc3(expA[:, g * 8:(g + 1) * 8], 64), ALU.mult,
                         [pso.d] + smd, [yo.d])
                for g in range(2):
                    psy = pbank()
                    for hl in range(8):
                        h = g * 8 + hl
                        K.mm(psy.t[:, hl * 64:(hl + 1) * 64], M.t[:, h, :], Vd.t[:, h * 64:(h + 1) * 64], True, True, [M.d, Vd.d], [psy.d])
                    K.tt(DVE, yd.t[:, g * 512:(g + 1) * 512], psy.t[:, :], yo.t[:, g * 512:(g + 1) * 512], ALU.add, [psy.d, yo.d], [yd.d])
                for g in range(2):
                    psd2 = pbank()
                    K.mm(psd2.t[:, :], bt_c.t[:, g * 128:(g + 1) * 128], Vdec.t[:, g * 512:(g + 1) * 512], True, True, [bt_c.d, Vdec.d], [psd2.d])
                    s3 = S[g].t[:, :].rearrange("p (h e) -> p h e", e=64)
                    K.tt(POOL, s3, s3, bc3(cd[:, g * 8:(g + 1) * 8], 64), ALU.mult, [S[g].d] + smd, [S[g].d])
                    K.tt(DVE, S[g].t[:, :], S[g].t[:, :], psd2.t[:, :], ALU.add, [S[g].d, psd2.d], [S[g].d])
                    K.act(Sbf[g].t[:, :], S[g].t[:, :], AF.Copy, [S[g].d], [Sbf[g].d])
                if KSC < 5:
                    continue
                psgr = [pbank(), pbank()]
                for h in range(8):
                    rows = slice((h % 2) * 64, (h % 2) * 64 + 64)
                    K.mm(psgr[h % 2].t[:, (h // 2) * 128:(h // 2 + 1) * 128], rk_c.t[rows, h // 2, :], rq_c.t[rows, h // 2, :], True, True,
                         [rk_c.d, rq_c.d], [psgr[h % 2].d])
                for par in range(2):
                    K.tt(DVE, MR.t[:, :, :].rearrange("p (b two) i -> p b two i", two=2)[:, :, par, :],
                         psgr[par].t[:, :].rearrange("p (h i) -> p h i", i=128),
                         LM.t[:, :, :].rearrange("p (b two) i -> p b two i", two=2)[:, :, par, :],
                         ALU.mult, [psgr[par].d, LM.d], [MR.d])
                if os.environ.get("KSUB") == "2":
                    continue
                K.tt(DVE, RVd.t[:, :].rearrange("p (h e) -> p h e", e=128), rv_c.t[:, :].rearrange("p (h e) -> p h e", e=128),
                     bc3(DEr.t[:, :], 128), ALU.mult, [rv_c.d, DEr.d], [RVd.d])
                if KSC < 6:
                    continue
                psro = [pbank(), pbank()]
                for h in range(8):
                    rows = slice((h % 2) * 64, (h % 2) * 64 + 64)
                    K.mm(psro[h % 2].t[:, (h // 2) * 128:(h // 2 + 1) * 128], rq_c.t[rows, h // 2, :], SRbf.t[rows, h // 2, :], True, True,
                         [rq_c.d, SRbf.d], [psro[h % 2].d])
                for par in range(2):
                    K.tt(DVE, yo.t[:, :].rearrange("p (b two e) -> p b two e", two=2, e=128)[:, :, par, :],
                         psro[par].t[:, :].rearrange("p (h e) -> p h e", e=128),
                         bc3(EAr.t[:, :].rearrange("p (b two) -> p b two", two=2)[:, :, par], 128), ALU.mult,
                         [psro[par].d, EAr.d], [yo.d])
                if KSC < 7:
                    continue
                psyr = [pbank(), pbank()]
                for h in range(8):
                    K.mm(psyr[h // 4].t[:, (h % 4) * 128:(h % 4 + 1) * 128], MR.t[:, h, :], rv_c.t[:, h * 128:(h + 1) * 128], True, True,
                         [MR.d, rv_c.d], [psyr[h // 4].d])
                for q in range(2):
                    K.tt(DVE, yd.t[:, 1024 + q * 512:1024 + (q + 1) * 512], psyr[q].t[:, :], yo.t[:, q * 512:(q + 1) * 512], ALU.add,
                         [psyr[q].d, yo.d], [yd.d])
                if KSC < 8:
                    continue
                psdr = [pbank(), pbank()]
                for h in range(8):
                    pr = h // 2
                    K.mm(psdr[h // 4].t[:, (h % 4) * 128:(h % 4 + 1) * 128], rkt_c.t[:, pr * 128:(pr + 1) * 128], RVd.t[:, h * 128:(h + 1) * 128],
                         True, True, [rkt_c.d, RVd.d], [psdr[h // 4].d])
                K.tt(POOL, SR.t[:, :, :], SR.t[:, :, :], bc3(CDR.t[:, :], 128), ALU.mult, [SR.d, CDR.d], [SR.d])
                for half in range(2):
                    rows = slice(half * 64, half * 64 + 64)
                    for q in range(2):
                        src = psdr[q].t[:, :].rearrange("p (pp two e) -> p pp two e", two=2, e=128)[rows, :, half, :]
                        K.tt(DVE, SR.t[rows, 2 * q:2 * q + 2, :], SR.t[rows, 2 * q:2 * q + 2, :], src, ALU.add, [SR.d, psdr[q].d], [SR.d])
                K.act(SRbf.t[:, :, :], SR.t[:, :, :], AF.Copy, [SR.d], [SRbf.d])
                if not fin:
                    K.st(YF[t0:t0 + 128, :], yd.t[:, :], [yd.d])
                    continue
                yf_c, sz_c, sg_c = yfb[i], szb[i], sgb[i]
                K.tt(POOL, yd.t[:, :], yd.t[:, :], yf_c.t[:, :], ALU.add, [yd.d, yf_c.d], [yd.d])
                K.tt(POOL, junk.t[:, :], xs_c.t[:, :], DSK.t[:, :], ALU.mult, [xs_c.d, DSK.d], [junk.d])
                K.tt(DVE, yd.t[:, 0:1024], yd.t[:, 0:1024], junk.t[:, :], ALU.add, [yd.d, junk.d], [yd.d])
                K.tt(DVE, yd.t[:, 0:1024], yd.t[:, 0:1024], sz_c.t[:, :], ALU.mult, [yd.d, sz_c.d], [yd.d])
                ssg = st8.t[:, 0, 0:2]
                for g in range(2):
                    K.act(junk.t[:, 0:512], yd.t[:, g * 512:(g + 1) * 512], AF.Square, [yd.d], [junk.d, st8.d], accum_out=st8.t[:, 0, g:g + 1])
                K.ts(DVE, ssg, ssg, 1.0 / 512, ALU.mult, [st8.d], [st8.d], s2=EPS, op1=ALU.add)
                K.act(ssg, ssg, AF.Sqrt, [st8.d], [st8.d])
                K.recip(ssg, ssg, [st8.d], [st8.d])
                for g in range(2):
                    K.stt(ynb.t[:, g * 512:(g + 1) * 512], yd.t[:, g * 512:(g + 1) * 512], st8.t[:, 0, g:g + 1], WN.t[:, g * 512:(g + 1) * 512],
                          ALU.mult, ALU.mult, [yd.d, st8.d, WN.d], [ynb.d])
                yr3 = yd.t[:, 1024:2048].rearrange("p (h e) -> p h e", e=128)
                s1, s2, mean, m2 = st8.t[:, 1, :], st8.t[:, 2, :], st8.t[:, 3, :], st8.t[:, 4, :]
                em.op(DVE, lambda: nc.vector.tensor_reduce(out=s1, in_=yr3, axis=AX.X, op=ALU.add), [yd.d], [st8.d])
                K.act(junk.t[:, :], yd.t[:, 1024:2048], AF.Square, [yd.d], [junk.d])
                em.op(DVE, lambda: nc.vector.tensor_reduce(out=s2, in_=junk.t[:, :].rearrange("p (h e) -> p h e", e=128), axis=AX.X, op=ALU.add),
                      [junk.d], [st8.d])
                K.ts(DVE, mean, s1, 1.0 / 128, ALU.mult, [st8.d], [st8.d])
                K.tt(DVE, m2, mean, mean, ALU.mult, [st8.d], [st8.d])
                K.stt(s2, s2, 1.0 / 128, m2, ALU.mult, ALU.subtract, [st8.d], [st8.d])
                K.ts(DVE, s2, s2, EPS, ALU.add, [st8.d], [st8.d])
                K.act(s2, s2, AF.Sqrt, [st8.d], [st8.d])
                K.recip(s2, s2, [st8.d], [st8.d])
                yn3 = ynb.t[:, 1024:2048].rearrange("p (h e) -> p h e", e=128)
                K.tt(DVE, yn3, yr3, bc3(mean, 128), ALU.subtract, [yd.d, st8.d], [ynb.d])
                K.tt(POOL, yn3, yn3, bc3(s2, 128), ALU.mult, [ynb.d, st8.d], [ynb.d])
                K.tt(DVE, ynb.t[:, 1024:2048], ynb.t[:, 1024:2048], RN.t[:, :], ALU.mult, [ynb.d, RN.d], [ynb.d])
                K.tt(POOL, ynb.t[:, 1024:2048], ynb.t[:, 1024:2048], sg_c.t[:, :], ALU.mult, [ynb.d, sg_c.d], [ynb.d])
                for b4 in range(4):
                    pst = pbank()
                    for kk in range(4):
                        k = b4 * 4 + kk
                        K.tr(pst.t[:, kk * 128:(kk + 1) * 128], ynb.t[:, k * 128:(k + 1) * 128], cst(CI_IDENT), [ynb.d, CS.d], [pst.d])
                    ys = ysb[b4 % 2]
                    K.act(ys.t[:, :, :], pst.t[:, :].rearrange("p (k t) -> p k t", t=128), AF.Copy, [pst.d], [ys.d])
                    K.st(YS[b4 * 512:(b4 + 1) * 512, t0:t0 + 128].rearrange("(k p) t -> p k t", p=128), ys.t[:, :, :], [ys.d])

    EPSB = K.gsb("epsb", [128, 1], F32)
    K.memset(DVE, EPSB.t[:, :], EPS, [EPSB.d])

    l0_phaseA()
    l0_phaseB()
    phaseC1(0, K.blocks, AO, 8, awout, xin, 0)
    if nlayers == 1:
        phaseC2(0, K.blocks, X2, 0, True)
    else:
        phaseC2(0, K.blocks, X2, 0, False)
        import os
        stop = int(os.environ.get("KSTOP", "99"))
        lat = [b for b in K.blocks if b[2] == 0]
        steps = [l1_phaseA, l1_phaseV, lambda: l1_scan(0), lambda: l1_scan(1),
                 lambda: phaseC1(1, lat, YS, 16, swout, X2, 0), lambda: phaseC2(1, lat, outD, CTX, True)]
        for si, fn in enumerate(steps):
            if si < stop:
                fn()
    em.finish()
    K.stats = em.stats()
    return K


def prep_inputs(inp, L):
    f = lambda a: np.ascontiguousarray(np.asarray(a, dtype=np.float32))
    rope, sq = host_consts(L)
    col = lambda v, n: f(v).reshape(n, 128).T
    shared = {
        "mod_w": f(inp["mod_w"]),
        "mod_b": f(inp["mod_b"]),
        "modb_col": np.ascontiguousarray(np.stack([col(inp["mod_b"][i], 48) for i in range(2)], axis=1)),
        "ncol": np.ascontiguousarray(np.stack([np.stack([col(inp["norm1_w"][i], 8), col(inp["norm2_w"][i], 8)], axis=1) for i in range(2)], axis=1)),
        "rope": rope, "sqc": sq,
        "ffn_w13": f(inp["ffn_w13"]), "ffn_w2": f(inp["ffn_w2"]),
        "attn_w_in": f(inp["attn_w_in"][0]), "mla_wq_b": f(inp["mla_wq_b"][0]), "mla_wkv_b": f(inp["mla_wkv_b"][0]),
        "attn_w_out": f(inp["attn_w_out"][0]),
    }
    if "ssm_w_in" in inp:
        shared["ssm_w_in"] = f(inp["ssm_w_in"][0])
        shared["ssm_w_out"] = f(inp["ssm_w_out"][0])
        cw = f(inp["ssd_conv_w"][0])
        shared["convw_col"] = np.ascontiguousarray(cw.reshape(5, 12, 128).transpose(2, 0, 1).reshape(128, 60))
        shared["convb_col"] = col(inp["ssd_conv_b"][0], 12)
        rows = np.zeros((NR1,), np.float32)
        rows[R_CONVB:R_CONVB + 1536] = f(inp["ssd_conv_b"][0])
        rows[R_DTB:R_DTB + 32] = f(inp["ssd_dt_bias"][0]).reshape(-1)
        rows[R_ALOG:R_ALOG + 32] = f(inp["ssd_a_log"][0]).reshape(-1)
        rows[R_RLOG:R_RLOG + 16] = f(inp["ret_decay_logit"][0]).reshape(-1)
        rows[R_DSK:R_DSK + 1024] = np.repeat(f(inp["ssd_d"][0]), 64)
        rows[R_SSDN:R_SSDN + 1024] = f(inp["ssd_norm"][0])
        rows[R_RETN:R_RETN + 1024] = f(inp["ret_norm"][0])
        shared["rows1"] = rows
        jj, ii = np.meshgrid(np.arange(128, dtype=np.float32), np.arange(128, dtype=np.float32), indexing="ij")
        c2 = np.zeros((128, 260), np.float32)
        c2[:, 0:128] = np.maximum(ii - jj, 0)
        c2[:, 128:256] = np.maximum(jj - ii, 0)
        j1 = np.arange(128, dtype=np.float32)
        c2[:, 256], c2[:, 257], c2[:, 258], c2[:, 259] = j1 + 1, 128 - j1, 127 - j1, j1
        shared["sqc2"] = c2
        sel = np.zeros((16, 16, 128), np.float32)
        for h in range(16):
            sel[h, h, :] = 1.0
        shared["selc"] = np.ascontiguousarray(sel.reshape(16, 2048))
    vec = np.zeros((128, NV), np.float32)
    vec[:, V_GQ] = np.tile(f(inp["gqa_qn"][0]), 2)
    vec[:, V_GK] = np.tile(f(inp["gqa_kn"][0]), 2)
    vec[:, V_QA:V_QA + 3] = col(inp["mla_qa_norm"][0], 3)
    vec[:, V_KVA:V_KVA + 2] = col(inp["mla_kva_norm"][0], 2)
    vec[:96, V_MQ] = f(inp["mla_qn"][0])
    vec[:96, V_MK] = f(inp["mla_kn"][0])
    shared["vecs"] = vec
    maps = []
    x, c, ctx, c_ctx = f(inp["x"]), f(inp["c"]), f(inp["ctx"]), f(inp["c_ctx"])
    for b in range(x.shape[0]):
        m = dict(shared)
        m["xin"] = np.ascontiguousarray(np.concatenate([ctx[b], x[b]], axis=0))
        cc = np.stack([col(c[b], 8), col(c_ctx, 8)], axis=2)
        m["cc"] = np.ascontiguousarray(cc)
        maps.append(m)
    return maps


_CACHE = {}


def kernel(**inputs):
    L = int(np.asarray(inputs["x"]).shape[1])
    B = int(np.asarray(inputs["x"]).shape[0])
    if L not in _CACHE:
        _CACHE[L] = build(L, 2)
    K = _CACHE[L]
    maps = prep_inputs(inputs, L)
    res = run_bass_kernel_spmd(K.nc, maps, core_ids=list(range(B)))
    return np.stack([np.asarray(res.results[b]["out"]) for b in range(B)], axis=0).astype(np.float32)
```

```python
import math
from contextlib import ExitStack, contextmanager
import numpy as np
import concourse.bass as bass
import concourse.mybir as mybir
from concourse.bass_utils import run_bass_kernel_spmd

F32 = mybir.dt.float32
BF16 = mybir.dt.bfloat16
AF = mybir.ActivationFunctionType
ALU = mybir.AluOpType
AX = mybir.AxisListType

EPOCH = 30000
EPS = 1e-6
D = 1024
CTX = 256
FFH = 2816
GRID_W = 64
THETA = 10000.0


class Dep:
    __slots__ = ("w", "r")

    def __init__(self):
        self.w = None
        self.r = {}


class Eng:
    def __init__(self, em, name, h, self_sync=True):
        self.em, self.name, self.h, self.self_sync = em, name, h, self_sync
        self.sem = None
        self.cnt = 0
        self.waited = {}
        self.own = set()
        self.ninst = 0
        self.nwait = 0
        self.last = None

    def next_event(self):
        if self.sem is None or self.cnt >= EPOCH:
            self.sem = self.em.new_sem(self.name)
            self.own.add(self.sem.num)
            self.cnt = 0
        self.cnt += 1
        self.last = (self.sem.num, self.cnt)
        return self.last


class Emitter:
    def __init__(self, nc, es, n_dma_sp=24, n_dma_pool=16):
        self.nc, self.es = nc, es
        self.sems = {}
        self.nsem = 0
        self.pe = Eng(self, "pe", nc.tensor, self_sync=False)
        self.act = Eng(self, "act", nc.scalar)
        self.dve = Eng(self, "dve", nc.vector)
        self.pool = Eng(self, "pool", nc.gpsimd)
        self.sp = Eng(self, "sp", nc.sync)
        self.engines = [self.pe, self.act, self.dve, self.pool, self.sp]
        self.dma_pools = {}
        for e, n in ((self.sp, n_dma_sp), (self.pool, n_dma_pool)):
            self.dma_pools[e.name] = [[self.new_sem("d" + e.name), 0] for _ in range(n)]
        self.dma_rr = {k: 0 for k in self.dma_pools}
        self.out_events = []

    def new_sem(self, tag):
        s = self.es.enter_context(self.nc.semaphore("%s_%d" % (tag, self.nsem)))
        self.nsem += 1
        self.sems[s.num] = s
        return s

    def _wait(self, eng, evs):
        best = {}
        for (s, v) in evs:
            if v > best.get(s, 0):
                best[s] = v
        for s, v in best.items():
            if (not eng.self_sync) and s in eng.own:
                continue
            if eng.waited.get(s, 0) >= v:
                continue
            eng.h.wait_ge(self.sems[s], v)
            eng.waited[s] = v
            eng.nwait += 1

    @staticmethod
    def _collect(reads, writes):
        evs = []
        for d in reads:
            if d.w is not None:
                evs.append(d.w)
        for d in writes:
            if d.w is not None:
                evs.append(d.w)
            evs.extend(d.r.items())
        return evs

    @staticmethod
    def _commit(ev, reads, writes):
        for d in reads:
            if ev[1] > d.r.get(ev[0], 0):
                d.r[ev[0]] = ev[1]
        for d in writes:
            d.w = ev
            d.r = {}

    def op(self, eng, fn, reads=(), writes=()):
        self._wait(eng, self._collect(reads, writes))
        inst = fn()
        ev = eng.next_event()
        inst.then_inc(self.sems[ev[0]], 1)
        eng.ninst += 1
        self._commit(ev, reads, writes)
        return ev

    def dma(self, eng, out, in_, reads=(), writes=(), is_output=False, **kw):
        pool = self.dma_pools[eng.name]
        i = self.dma_rr[eng.name]
        self.dma_rr[eng.name] = (i + 1) % len(pool)
        slot = pool[i]
        evs = self._collect(reads, writes)
        if slot[1] > 0:
            evs.append((slot[0].num, slot[1]))
        self._wait(eng, evs)
        if slot[1] + 16 > EPOCH:
            slot[0] = self.new_sem("d" + eng.name)
            slot[1] = 0
        slot[1] += 16
        eng.h.dma_start(out=out, in_=in_, **kw).then_inc(slot[0], 16)
        ev = (slot[0].num, slot[1])
        eng.ninst += 1
        self._commit(ev, reads, writes)
        if is_output:
            self.out_events.append(ev)
        return ev

    def all_events(self):
        evs = []
        for e in self.engines:
            if e.last is not None:
                evs.append(e.last)
        for pool in self.dma_pools.values():
            for s, v in pool:
                if v > 0:
                    evs.append((s.num, v))
        return evs

    def barrier(self):
        evs = self.all_events()
        for e in self.engines:
            self._wait(e, evs)

    def finish(self):
        self._wait(self.sp, list(self.out_events) + self.all_events())

    def stats(self):
        return {e.name: (e.ninst, e.nwait) for e in self.engines}, self.nsem


class Buf:
    __slots__ = ("t", "d")

    def __init__(self, t):
        self.t = t
        self.d = Dep()


class KB:
    def __init__(self, L):
        self.L = L
        self.NT = CTX + L
        self.NCH = self.NT // 128
        self.nc = bass.Bass("TRN2", target_bir_lowering=False)
        self.es = ExitStack()
        self.em = Emitter(self.nc, self.es)
        self.cur = self.es
        self.uid = 0
        self.blocks = [(0, CTX, 1)] + [(CTX + i * 512, 512, 0) for i in range(L // 512)]
        self.PS = [Buf(self.es.enter_context(self.nc.psum_tensor("psb%d" % i, [128, 512], F32))) for i in range(8)]

    def sb(self, name, shape, dtype):
        self.uid += 1
        return Buf(self.cur.enter_context(self.nc.sbuf_tensor("s_%s_%d" % (name, self.uid), shape, dtype)))

    def gsb(self, name, shape, dtype):
        return Buf(self.es.enter_context(self.nc.sbuf_tensor("g_" + name, shape, dtype)))

    def dram(self, name, shape, dtype, kind="Internal"):
        return self.nc.dram_tensor(name, shape, dtype, kind=kind).ap()

    @contextmanager
    def phase(self):
        st = ExitStack()
        prev = self.cur
        self.cur = st
        try:
            yield
        finally:
            self.em.barrier()
            st.close()
            self.cur = prev

    def mm(self, out, lhsT, rhs, start, stop, R, W):
        nc = self.nc
        return self.em.op(self.em.pe, lambda: nc.tensor.matmul(out, lhsT=lhsT, rhs=rhs, start=start, stop=stop), R, W)

    def tr(self, out, in_, ident, R, W):
        nc = self.nc
        return self.em.op(self.em.pe, lambda: nc.tensor.transpose(out=out, in_=in_, identity=ident), R, W)

    def act(self, out, in_, func, R, W, scale=None, bias=None, accum_out=None):
        nc = self.nc
        kw = {}
        if scale is not None:
            kw["scale"] = scale
        if bias is not None:
            kw["bias"] = bias
        if accum_out is not None:
            kw["accum_out"] = accum_out
        return self.em.op(self.em.act, lambda: nc.scalar.activation(out=out, in_=in_, func=func, **kw), R, W)

    def _ve(self, eng):
        return self.nc.vector if eng is self.em.dve else self.nc.gpsimd

    def tt(self, eng, out, in0, in1, op, R, W):
        h = self._ve(eng)
        return self.em.op(eng, lambda: h.tensor_tensor(out=out, in0=in0, in1=in1, op=op), R, W)

    def ts(self, eng, out, in0, s1, op0, R, W, s2=None, op1=None):
        h = self._ve(eng)
        if op1 is None:
            return self.em.op(eng, lambda: h.tensor_scalar(out=out, in0=in0, scalar1=s1, scalar2=None, op0=op0), R, W)
        return self.em.op(eng, lambda: h.tensor_scalar(out=out, in0=in0, scalar1=s1, scalar2=s2, op0=op0, op1=op1), R, W)

    def stt(self, out, in0, scalar, in1, op0, op1, R, W):
        nc = self.nc
        return self.em.op(self.em.dve, lambda: nc.vector.scalar_tensor_tensor(out=out, in0=in0, scalar=scalar, in1=in1, op0=op0, op1=op1), R, W)

    def recip(self, out, in_, R, W):
        nc = self.nc
        return self.em.op(self.em.dve, lambda: nc.vector.reciprocal(out=out, in_=in_), R, W)

    def memset(self, eng, ap, val, W):
        h = self._ve(eng)
        return self.em.op(eng, lambda: h.memset(ap, val), (), W)

    def copy(self, eng, out, in_, R, W):
        h = self._ve(eng)
        return self.em.op(eng, lambda: h.tensor_copy(out=out, in_=in_), R, W)

    def ld(self, out, in_, W, R=(), **kw):
        return self.em.dma(self.em.sp, out, in_, reads=R, writes=W, **kw)

    def st(self, out, in_, R, W=(), **kw):
        return self.em.dma(self.em.pool, out, in_, reads=R, writes=W, **kw)

    def ldcast(self, out, in_, W, R=()):
        return self.em.dma(self.em.pool, out, in_, reads=R, writes=W, max_dma_last_dim=8192)


def rope_tables(L, dim):
    rows = L // GRID_W
    rr, cc = np.meshgrid(np.arange(rows, dtype=np.float32), np.arange(GRID_W, dtype=np.float32), indexing="ij")
    quarter = dim // 4
    inv = (np.float32(THETA) ** (-np.arange(quarter, dtype=np.float32) / np.float32(quarter))).astype(np.float32)
    ang = np.concatenate([rr.reshape(-1)[:, None] * inv, cc.reshape(-1)[:, None] * inv], axis=-1).astype(np.float32)
    cos = np.concatenate([np.ones((CTX, dim // 2), np.float32), np.cos(ang)], axis=0)
    sin = np.concatenate([np.zeros((CTX, dim // 2), np.float32), np.sin(ang)], axis=0)
    return cos.astype(np.float32), sin.astype(np.float32)


def host_consts(L):
    NT = CTX + L
    c64, s64 = rope_tables(L, 64)
    c32, s32 = rope_tables(L, 32)
    cos64 = np.repeat(c64.T, 2, axis=0)
    sin64 = np.repeat(s64.T, 2, axis=0)
    cos128 = np.concatenate([cos64, cos64], 0)
    sin128 = np.concatenate([sin64, sin64], 0)
    cosm = np.concatenate([np.ones((64, NT), np.float32), np.repeat(c32.T, 2, axis=0)], 0)
    sinm = np.concatenate([np.zeros((64, NT), np.float32), np.repeat(s32.T, 2, axis=0)], 0)
    rope = np.zeros((4, 128, NT), np.float32)
    rope[0], rope[1] = cos128, sin128
    rope[2, :96], rope[3, :96] = cosm, sinm
    ident = np.eye(128, dtype=np.float32)
    rot = np.zeros((128, 128), np.float32)
    for i in range(64):
        rot[2 * i + 1, 2 * i] = -1.0
        rot[2 * i, 2 * i + 1] = 1.0
    rotm = np.zeros((128, 128), np.float32)
    rotm[64:96, 64:96] = rot[64:96, 64:96]
    shift = np.zeros((128, 128), np.float32)
    for k in range(32):
        shift[k, 64 + k] = 1.0
    bd64 = np.zeros((128, 128), np.float32)
    bd64[:64, :64] = 1.0
    bd64[64:, 64:] = 1.0
    ones = np.ones((128, 128), np.float32)
    tri = np.triu(np.ones((128, 128), np.float32))
    sq = np.concatenate([ident, rot, rotm, shift, bd64, ones, tri, tri.T.copy()], axis=1)
    return rope, sq


CI_IDENT, CI_ROT, CI_ROTM, CI_SHIFT, CI_BD64, CI_ONES, CI_TRI, CI_TRIT = range(8)
NCONST = 8

V_GQ, V_GK, V_QA, V_KVA, V_MQ, V_MK = 0, 1, 2, 5, 7, 8
NV = 16
R_CONVB, R_DTB, R_ALOG, R_RLOG, R_DSK, R_SSDN, R_RETN = 0, 1536, 1568, 1600, 1616, 2640, 3664
NR1 = 4688


def build(L=4096, nlayers=2, debug=False):
    K = KB(L)
    nc, em = K.nc, K.em
    NT, NCH = K.NT, K.NCH
    PS = K.PS
    DVE, POOL = em.dve, em.pool

    xin = K.dram("xin", [NT, D], F32, "ExternalInput")
    ccd = K.dram("cc", [128, 8, 2], F32, "ExternalInput")
    mod_w = K.dram("mod_w", [2, D, 6 * D], F32, "ExternalInput")
    modb_col = K.dram("modb_col", [128, 2, 48], F32, "ExternalInput")
    mod_b = K.dram("mod_b", [2, 6 * D], F32, "ExternalInput")
    ncol = K.dram("ncol", [128, 2, 2, 8], F32, "ExternalInput")
    vecs = K.dram("vecs", [128, NV], F32, "ExternalInput")
    ropeD = K.dram("rope", [4, 128, NT], F32, "ExternalInput")
    sqc = K.dram("sqc", [128, NCONST * 128], F32, "ExternalInput")
    w13D = K.dram("ffn_w13", [2, D, 2 * FFH], F32, "ExternalInput")
    w2D = K.dram("ffn_w2", [2, FFH, D], F32, "ExternalInput")
    awin = K.dram("attn_w_in", [D, 1440], F32, "ExternalInput")
    wqb = K.dram("mla_wq_b", [384, 768], F32, "ExternalInput")
    wkvb = K.dram("mla_wkv_b", [256, 1024], F32, "ExternalInput")
    awout = K.dram("attn_w_out", [D, D], F32, "ExternalInput")
    outD = K.dram("out", [L, D], F32, "ExternalOutput")

    QA = K.dram("QA", [4, 128, NT], BF16)
    KA = K.dram("KA", [2, 128, NT], BF16)
    VA = K.dram("VA", [128, NCH, 130], BF16)
    QM = K.dram("QM", [8, 96, NT], BF16)
    KM = K.dram("KM", [8, 96, NT], BF16)
    VM = K.dram("VM", [128, NCH, 520], BF16)
    AO = K.dram("AO", [D, NT], BF16)
    X1 = K.dram("X1", [NT, D], F32)
    U2 = K.dram("U2", [D, NT], BF16)
    X2 = K.dram("X2", [NT, D], F32, "ExternalOutput" if debug else "Internal")
    MODROW = K.dram("MODROW", [2, 2, 2, D], F32)

    CS = K.gsb("consts", [128, NCONST * 128], F32)
    K.ld(CS.t[:, :], sqc[:, :], [CS.d])

    def cst(i, r=128, c=128):
        return CS.t[0:r, i * 128:i * 128 + c]

    CB = K.gsb("constsb", [128, 2 * 128], BF16)
    K.copy(DVE, CB.t[:, 0:128], cst(CI_BD64), [CS.d], [CB.d])
    K.copy(DVE, CB.t[:, 128:256], cst(CI_ONES), [CS.d], [CB.d])
    BD64b = CB.t[:, 0:128]
    ONESb = CB.t[:, 128:256]
    VEC = K.gsb("vecs", [128, NV], F32)
    K.ld(VEC.t[:, :], vecs[:, :], [VEC.d])
    NCOL = K.gsb("ncol", [128, 2, 2, 8], F32)
    K.ld(NCOL.t[:, :, :, :], ncol[:, :, :, :], [NCOL.d])
    MODT = K.gsb("modT", [128, 2, 48, 2], F32)
    MUL = K.gsb("mulT", [128, 2, 2, 8, 2], F32)

    with K.phase():
        cc = K.sb("cc", [128, 8, 2], F32)
        K.ld(cc.t[:, :, :], ccd[:, :, :], [cc.d])
        K.act(cc.t[:, :, :], cc.t[:, :, :], AF.Silu, [cc.d], [cc.d])
        modb = K.sb("modb", [128, 2, 48], F32)
        K.ld(modb.t[:, :, :], modb_col[:, :, :], [modb.d])
        mbrow = K.sb("mbrow", [2, 2, 6 * D], F32)
        K.ld(mbrow.t[:, :, :], mod_b.partition_broadcast(2), [mbrow.d])
        wbuf = [K.sb("mw", [128, 8, 1024], F32) for _ in range(2)]
        rowb = [K.sb("rowb", [2, 1024], F32) for _ in range(2)]
        it = 0
        for i in range(nlayers):
            for v in range(6):
                wb = wbuf[it % 2]
                K.ld(wb.t[:, :, :], mod_w[i, :, v * 1024:(v + 1) * 1024].rearrange("(k p) n -> p k n", p=128), [wb.d])
                ps = PS[it % 2]
                for j in range(8):
                    for k in range(8):
                        K.mm(ps.t[:, j * 2:(j + 1) * 2], wb.t[:, k, j * 128:(j + 1) * 128], cc.t[:, k, :], k == 0, k == 7,
                             [wb.d, cc.d], [ps.d])
                K.tt(DVE, MODT.t[:, i, v * 8:(v + 1) * 8, :], ps.t[:, 0:16].rearrange("p (j s) -> p j s", s=2),
                     modb.t[:, i, v * 8:(v + 1) * 8].unsqueeze(2).broadcast_to([128, 8, 2]), ALU.add,
                     [ps.d, modb.d], [MODT.d])
                if v in (2, 5):
                    gi = 0 if v == 2 else 1
                    rb = rowb[gi]
                    for half in range(2):
                        ps2 = PS[2 + half]
                        for k in range(8):
                            K.mm(ps2.t[0:2, :], cc.t[:, k, :], wb.t[:, k, half * 512:(half + 1) * 512], k == 0, k == 7,
                                 [wb.d, cc.d], [ps2.d])
                        K.tt(DVE, rb.t[:, half * 512:(half + 1) * 512], ps2.t[0:2, :],
                             mbrow.t[:, i, v * 1024 + half * 512: v * 1024 + (half + 1) * 512], ALU.add,
                             [ps2.d, mbrow.d], [rb.d])
                    K.st(MODROW[i, gi, :, :], rb.t[:, :], [rb.d])
                it += 1
            for nrm in range(2):
                K.stt(MUL.t[:, i, nrm, :, :], MODT.t[:, i, (3 * nrm + 1) * 8:(3 * nrm + 2) * 8, :], 1.0,
                      NCOL.t[:, i, nrm, :].unsqueeze(2).broadcast_to([128, 8, 2]), ALU.add, ALU.mult,
                      [MODT.d, NCOL.d], [MUL.d])

    def mul_ap(layer, nrm, k, s):
        return MUL.t[:, layer, nrm, k, s:s + 1]

    def add_ap(layer, nrm, k, s):
        return MODT.t[:, layer, 3 * nrm * 8 + k, s:s + 1]

    def norm_to_uT(xt, nch, T, layer, nrm, s, uT, scr, psbanks):
        junk, ss, xs = scr
        for c in range(nch):
            K.act(junk.t[:, :], xt.t[:, c, :], AF.Square, [xt.d], [junk.d, ss.d], accum_out=ss.t[:, c:c + 1])
        K.ts(DVE, ss.t[:, 0:nch], ss.t[:, 0:nch], 1.0 / D, ALU.mult, [ss.d], [ss.d], s2=EPS, op1=ALU.add)
        K.act(ss.t[:, 0:nch], ss.t[:, 0:nch], AF.Sqrt, [ss.d], [ss.d])
        K.recip(ss.t[:, 0:nch], ss.t[:, 0:nch], [ss.d], [ss.d])
        for c in range(nch):
            K.ts(DVE if c % 2 == 0 else POOL, xs.t[:, c, :], xt.t[:, c, :], ss.t[:, c:c + 1], ALU.mult, [xt.d, ss.d], [xs.d])
        for k in range(8):
            ps = psbanks[k % len(psbanks)]
            for c in range(nch):
                K.tr(ps.t[:, c * 128:(c + 1) * 128], xs.t[:, c, k * 128:(k + 1) * 128], cst(CI_IDENT), [xs.d, CS.d], [ps.d])
            K.act(uT.t[:, k, 0:T], ps.t[:, 0:T], AF.Identity, [ps.d, MUL.d, MODT.d], [uT.d],
                  scale=mul_ap(layer, nrm, k, s), bias=add_ap(layer, nrm, k, s))

    def l0_phaseA():
        with K.phase():
            W = K.sb("win", [128, 8, 1440], BF16)
            for k in range(8):
                K.ldcast(W.t[:, k, :], awin[k * 128:(k + 1) * 128, :], [W.d])
            Wkd = K.sb("wkd", [128, 8, 2, 128], BF16)
            for g in range(2):
                for dup in range(2):
                    K.ldcast(Wkd.t[:, :, g, dup * 64:(dup + 1) * 64],
                             awin[:, 512 + g * 64:512 + (g + 1) * 64].rearrange("(k p) n -> p k n", p=128), [Wkd.d])
            Wq = K.sb("wqb", [128, 3, 768], BF16)
            K.ldcast(Wq.t[:, :, :], wqb.rearrange("(j p) n -> p j n", p=128), [Wq.d])
            Wkp = K.sb("wkvp", [128, 2, 8, 96], BF16)
            K.memset(DVE, Wkp.t[:, :, :, :], 0.0, [Wkp.d])
            wkv4 = wkvb.rearrange("(j p) (h t e) -> p j h t e", p=128, t=2, e=64)
            for j in range(2):
                K.ldcast(Wkp.t[:, j, :, 0:64], wkv4[:, j, :, 0, :], [Wkp.d])
            Wv = K.sb("wkvv", [128, 2, 8, 64], BF16)
            for j in range(2):
                K.ldcast(Wv.t[:, j, :, :], wkv4[:, j, :, 1, :], [Wv.d])

            xb = [K.sb("xa", [128, 4, D], F32) for _ in range(2)]
            scr = (K.sb("junk", [128, D], BF16), K.sb("ss", [128, 4], F32), K.sb("xs", [128, 4, D], F32))
            uTb = [K.sb("uT", [128, 8, 512], BF16) for _ in range(2)]
            ropeb = [K.sb("rope", [128, 4, 512], F32) for _ in range(2)]
            sqb = [K.sb("sq", [128, 512], BF16) for _ in range(2)]
            r1b = [K.sb("r1", [128, 512], F32) for _ in range(2)]
            qnb = [K.sb("qn", [128, 512], F32) for _ in range(2)]
            t1b = [K.sb("t1", [128, 512], F32) for _ in range(2)]
            t2b = [K.sb("t2", [128, 512], F32) for _ in range(2)]
            outb = [K.sb("ob", [128, 512], BF16) for _ in range(3)]
            qln = K.sb("qln", [128, 3, 512], BF16)
            kvn = K.sb("kvn", [128, 2, 512], BF16)
            krp = K.sb("krp", [32, 512], F32)
            vat = [K.sb("vat", [128, 4, 130], BF16) for _ in range(2)]
            vmt = [K.sb("vmt", [128, 4, 520], BF16) for _ in range(2)]
            for b in vat + vmt:
                K.memset(DVE, b.t[:, :, :], 1.0, [b.d])
            cnt = {"n": 0, "o": 0, "p": 0}
            PROJ = [PS[2], PS[3], PS[4]]

            def proj_bank():
                cnt["p"] += 1
                return PROJ[cnt["p"] % 3]

            pend = []
            SSB = [PS[5], PS[6]]
            ROTB = [PS[7], PS[0]]

            def flush():
                while pend:
                    pend.pop(0)()

            def post(ps_src, R_, T, gain, ss_lhsT, inv_n, rot_lhsT, ci, rp, dst):
                i = cnt["n"] % 2
                cnt["n"] += 1
                sq, r1, qn, t1, t2 = sqb[i], r1b[i], qnb[i], t1b[i], t2b[i]
                pss, psr = SSB[i], ROTB[i]
                K.act(sq.t[0:R_, 0:T], ps_src.t[0:R_, 0:T], AF.Square, [ps_src.d], [sq.d])
                K.mm(pss.t[0:R_, 0:T], ss_lhsT, sq.t[0:R_, 0:T], True, True, [sq.d, CB.d], [pss.d])

                def stage2():
                    K.act(r1.t[0:R_, 0:T], pss.t[0:R_, 0:T], AF.Ln, [pss.d], [r1.d], scale=inv_n, bias=EPSB.t[0:R_, 0:1])
                    K.act(r1.t[0:R_, 0:T], r1.t[0:R_, 0:T], AF.Exp, [r1.d], [r1.d], scale=-0.5)
                    K.stt(qn.t[0:R_, 0:T], ps_src.t[0:R_, 0:T], gain, r1.t[0:R_, 0:T], ALU.mult, ALU.mult,
                          [ps_src.d, r1.d, VEC.d], [qn.d])
                    K.mm(psr.t[0:R_, 0:T], rot_lhsT, qn.t[0:R_, 0:T], True, True, [qn.d, CS.d], [psr.d])
                    K.tt(DVE, t1.t[0:R_, 0:T], qn.t[0:R_, 0:T], rp.t[0:R_, ci, 0:T], ALU.mult, [qn.d, rp.d], [t1.d])
                    K.tt(DVE, t2.t[0:R_, 0:T], psr.t[0:R_, 0:T], rp.t[0:R_, ci + 1, 0:T], ALU.mult, [psr.d, rp.d], [t2.d])
                    ob = outb[cnt["o"] % 3]
                    cnt["o"] += 1
                    K.tt(POOL, ob.t[0:R_, 0:T], t1.t[0:R_, 0:T], t2.t[0:R_, 0:T], ALU.add, [t1.d, t2.d], [ob.d])
                    K.st(dst, ob.t[0:R_, 0:T], [ob.d])

                pend.append(stage2)
                if len(pend) > 1:
                    pend.pop(0)()

            def lora_norm(ps_list, nj, gcol0, inv_n, dstb, T):
                i = cnt["n"] % 2
                cnt["n"] += 1
                r1 = r1b[i]
                pss = PS[5]
                for j in range(nj):
                    sq = sqb[(cnt["n"] + j) % 2]
                    K.act(sq.t[:, 0:T], ps_list[j].t[:, 0:T], AF.Square, [ps_list[j].d], [sq.d])
                    K.mm(pss.t[:, 0:T], ONESb, sq.t[:, 0:T], j == 0, j == nj - 1, [sq.d, CB.d], [pss.d])
                K.act(r1.t[:, 0:T], pss.t[:, 0:T], AF.Ln, [pss.d], [r1.d], scale=inv_n, bias=EPSB.t[:, 0:1])
                K.act(r1.t[:, 0:T], r1.t[:, 0:T], AF.Exp, [r1.d], [r1.d], scale=-0.5)
                for j in range(nj):
                    K.stt(dstb.t[:, j, 0:T], ps_list[j].t[:, 0:T], VEC.t[:, gcol0 + j:gcol0 + j + 1], r1.t[:, 0:T],
                          ALU.mult, ALU.mult, [ps_list[j].d, r1.d, VEC.d], [dstb.d])

            for bi, (t0, T, s) in enumerate(K.blocks):
                nch = T // 128
                c0 = t0 // 128
                xt = xb[bi % 2]
                K.ld(xt.t[:, 0:nch, :], xin[t0:t0 + T, :].rearrange("(c p) d -> p c d", p=128), [xt.d])
                rp = ropeb[bi % 2]
                K.ld(rp.t[:, :, 0:T], ropeD[:, :, t0:t0 + T].rearrange("f p t -> p f t"), [rp.d])
                uT = uTb[bi % 2]
                norm_to_uT(xt, nch, T, 0, 0, s, uT, scr, [PS[0], PS[1]])
                for ch in range(4):
                    ps = proj_bank()
                    for k in range(8):
                        K.mm(ps.t[:, 0:T], W.t[:, k, ch * 128:(ch + 1) * 128], uT.t[:, k, 0:T], k == 0, k == 7, [W.d, uT.d], [ps.d])
                    post(ps, 128, T, VEC.t[:, V_GQ:V_GQ + 1], BD64b, 1.0 / 64, cst(CI_ROT), 0, rp, QA[ch, :, t0:t0 + T])
                for g in range(2):
                    ps = proj_bank()
                    for k in range(8):
                        K.mm(ps.t[:, 0:T], Wkd.t[:, k, g, :], uT.t[:, k, 0:T], k == 0, k == 7, [Wkd.d, uT.d], [ps.d])
                    post(ps, 128, T, VEC.t[:, V_GK:V_GK + 1], BD64b, 1.0 / 64, cst(CI_ROT), 0, rp, KA[g, :, t0:t0 + T])
                flush()
                va_t = vat[bi % 2]
                for c in range(nch):
                    ps = proj_bank()
                    for k in range(8):
                        K.mm(ps.t[:, 0:128], uT.t[:, k, c * 128:(c + 1) * 128], W.t[:, k, 640:768], k == 0, k == 7, [W.d, uT.d], [ps.d])
                    K.act(va_t.t[:, c, :].rearrange("p (g e) -> p g e", e=65)[:, :, 0:64],
                          ps.t[:, 0:128].rearrange("p (g e) -> p g e", e=64), AF.Copy, [ps.d], [va_t.d])
                K.st(VA[:, c0:c0 + nch, :], va_t.t[:, 0:nch, :], [va_t.d])
                pl = []
                for j in range(3):
                    ps = proj_bank()
                    for k in range(8):
                        K.mm(ps.t[:, 0:T], W.t[:, k, 768 + j * 128:768 + (j + 1) * 128], uT.t[:, k, 0:T], k == 0, k == 7, [W.d, uT.d], [ps.d])
                    pl.append(ps)
                lora_norm(pl, 3, V_QA, 1.0 / 384, qln, T)
                for h in range(8):
                    ps = proj_bank()
                    for j in range(3):
                        K.mm(ps.t[0:96, 0:T], Wq.t[:, j, h * 96:(h + 1) * 96], qln.t[:, j, 0:T], j == 0, j == 2, [Wq.d, qln.d], [ps.d])
                    post(ps, 96, T, VEC.t[0:96, V_MQ:V_MQ + 1], ONESb[0:96, 0:96], 1.0 / 96, cst(CI_ROTM, 96, 96), 2, rp, QM[h, :, t0:t0 + T])
                flush()
                pl = []
                for j in range(2):
                    ps = proj_bank()
                    for k in range(8):
                        K.mm(ps.t[:, 0:T], W.t[:, k, 1152 + j * 128:1152 + (j + 1) * 128], uT.t[:, k, 0:T], k == 0, k == 7, [W.d, uT.d], [ps.d])
                    pl.append(ps)
                lora_norm(pl, 2, V_KVA, 1.0 / 256, kvn, T)
                ps = proj_bank()
                for k in range(8):
                    K.mm(ps.t[0:32, 0:T], W.t[:, k, 1408:1440], uT.t[:, k, 0:T], k == 0, k == 7, [W.d, uT.d], [ps.d])
                K.act(krp.t[0:32, 0:T], ps.t[0:32, 0:T], AF.Copy, [ps.d], [krp.d])
                for h in range(8):
                    ps = proj_bank()
                    for j in range(2):
                        K.mm(ps.t[0:96, 0:T], Wkp.t[:, j, h, :], kvn.t[:, j, 0:T], j == 0, False, [Wkp.d, kvn.d], [ps.d])
                    K.mm(ps.t[0:96, 0:T], cst(CI_SHIFT, 32, 96), krp.t[0:32, 0:T], False, True, [krp.d, CS.d], [ps.d])
                    post(ps, 96, T, VEC.t[0:96, V_MK:V_MK + 1], ONESb[0:96, 0:96], 1.0 / 96, cst(CI_ROTM, 96, 96), 2, rp, KM[h, :, t0:t0 + T])
                flush()
                vm_t = vmt[bi % 2]
                for c in range(nch):
                    ps = proj_bank()
                    for j in range(2):
                        K.mm(ps.t[:, 0:512], kvn.t[:, j, c * 128:(c + 1) * 128], Wv.t[:, j, :, :].rearrange("p h e -> p (h e)"),
                             j == 0, j == 1, [Wv.d, kvn.d], [ps.d])
                    K.act(vm_t.t[:, c, :].rearrange("p (g e) -> p g e", e=65)[:, :, 0:64],
                          ps.t[:, 0:512].rearrange("p (g e) -> p g e", e=64), AF.Copy, [ps.d], [vm_t.d])
                K.st(VM[:, c0:c0 + nch, :], vm_t.t[:, 0:nch, :], [vm_t.d])

    def l0_phaseB():
        with K.phase():
            va = K.sb("va", [128, NCH, 130], BF16)
            vm = K.sb("vm", [128, NCH, 520], BF16)
            K.ld(va.t[:, :, :], VA[:, :, :], [va.d])
            K.ld(vm.t[:, :, :], VM[:, :, :], [vm.d])
            kb = [K.sb("kb", [128, NT], BF16) for _ in range(2)]
            kzT = [K.sb("kzT", [128, NT], BF16) for _ in range(2)]
            kzB = [K.sb("kzB", [128, NT], BF16) for _ in range(2)]
            for b in kzT:
                K.memset(DVE, b.t[64:128, :], 0.0, [b.d])
            for b in kzB:
                K.memset(DVE, b.t[0:64, :], 0.0, [b.d])
            qb = [K.sb("qb", [128, 512], BF16) for _ in range(3)]
            pb = [K.sb("pb", [128, 512], BF16) for _ in range(4)]
            osb = [K.sb("osb", [64, 512], F32) for _ in range(2)]
            rsb = [K.sb("rsb", [128, 512], F32) for _ in range(2)]
            aob = [K.sb("aob", [64, 512], BF16) for _ in range(2)]
            SB_ = [PS[0], PS[1], PS[2]]
            OB_ = [PS[3], PS[4]]
            BCB = PS[5]
            tasks = []
            groups = []
            ui = 0
            qi = 0
            units = [("a", c) for c in range(4)] + [("m", h) for h in range(8)]
            for kind, idx in units:
                if kind == "a":
                    kT, kB = kzT[ui % 2], kzB[ui % 2]
                    kloads = [(kT, KA[idx // 2, 0:64, :], slice(0, 64)), (kB, KA[idx // 2, 64:128, :], slice(64, 128))]
                else:
                    kbf = kb[ui % 2]
                    kloads = [(kbf, KM[idx, :, :], slice(0, 96))]
                ui += 1
                first_in_unit = True
                for (t0, T, s) in K.blocks:
                    qbf = qb[qi % 3]
                    qi += 1
                    if kind == "a":
                        qload = (qbf, QA[idx, :, t0:t0 + T], 128, T)
                        subs = [(kT, slice(0, 128), 2 * idx, va, idx // 2, 64 ** -0.5), (kB, slice(0, 128), 2 * idx + 1, va, idx // 2, 64 ** -0.5)]
                    else:
                        qload = (qbf, QM[idx, :, t0:t0 + T], 96, T)
                        subs = [(kbf, slice(0, 96), 8 + idx, vm, idx, 96 ** -0.5)]
                    kcs = list(range(2)) if s == 1 else list(range(NCH))
                    for si, (kbuf_, rows, hg, vbuf, vcol, sc) in enumerate(subs):
                        g = len(groups)
                        groups.append((hg, t0, T))
                        for n, kc in enumerate(kcs):
                            tasks.append(dict(kb=kbuf_, rows=rows, kc=kc, qb=qbf, T=T, vb=vbuf, vcol=vcol, first=(n == 0),
                                              last=(n == len(kcs) - 1), grp=g, sc=sc,
                                              kload=kloads if (first_in_unit and si == 0 and n == 0) else None,
                                              qload=qload if (si == 0 and n == 0) else None))
                    first_in_unit = False

            def emit_qk(i):
                t = tasks[i]
                if t["kload"] is not None:
                    for (kbf, src, rws) in t["kload"]:
                        K.ld(kbf.t[rws, :], src, [kbf.d])
                if t["qload"] is not None:
                    qbf, src, R_, T = t["qload"]
                    K.ld(qbf.t[0:R_, 0:T], src[0:R_, :], [qbf.d])
                ps = SB_[i % 3]
                kc, T = t["kc"], t["T"]
                K.mm(ps.t[:, 0:T], t["kb"].t[t["rows"], kc * 128:(kc + 1) * 128], t["qb"].t[t["rows"], 0:T], True, True,
                     [t["kb"].d, t["qb"].d], [ps.d])

            n = len(tasks)
            for i in range(min(2, n)):
                emit_qk(i)
            for i in range(n):
                t = tasks[i]
                T = t["T"]
                ps = SB_[i % 3]
                p = pb[i % 4]
                K.act(p.t[:, 0:T], ps.t[:, 0:T], AF.Exp, [ps.d], [p.d], scale=t["sc"])
                if i + 2 < n:
                    emit_qk(i + 2)
                po = OB_[t["grp"] % 2]
                vc = t["vcol"]
                K.mm(po.t[0:65, 0:T], t["vb"].t[:, t["kc"], vc * 65:(vc + 1) * 65], p.t[:, 0:T], t["first"], t["last"],
                     [t["vb"].d, p.d], [po.d])
                if t["last"]:
                    g = t["grp"]
                    hg, t0, _ = groups[g]
                    rs, osb_, ao_ = rsb[g % 2], osb[g % 2], aob[g % 2]
                    K.recip(rs.t[64:65, 0:T], po.t[64:65, 0:T], [po.d], [rs.d])
                    K.mm(BCB.t[0:64, 0:T], cst(CI_ONES)[64:65, 0:64], rs.t[64:65, 0:T], True, True, [rs.d, CS.d], [BCB.d])
                    K.act(osb_.t[0:64, 0:T], po.t[0:64, 0:T], AF.Copy, [po.d], [osb_.d])
                    K.tt(DVE, ao_.t[0:64, 0:T], osb_.t[0:64, 0:T], BCB.t[0:64, 0:T], ALU.mult, [osb_.d, BCB.d], [ao_.d])
                    K.st(AO[hg * 64:(hg + 1) * 64, t0:t0 + T], ao_.t[0:64, 0:T], [ao_.d])

    def phaseC1(layer, blocks, AOsrc, nK, woutD, Xsrc, xoff):
        with K.phase():
            wo = K.sb("wo", [128, nK, D], BF16)
            for k in range(nK):
                K.ldcast(wo.t[:, k, :], woutD[k * 128:(k + 1) * 128, :], [wo.d])
            gts = {}
            for s in set(b[2] for b in blocks):
                gts[s] = K.sb("gt", [128, D], F32)
                K.ld(gts[s].t[:, :], MODROW[layer, 0, s, :].partition_broadcast(128), [gts[s].d])
            aob = [K.sb("ao", [128, nK, 512], BF16) for _ in range(2)]
            xb = [K.sb("xc", [128, 4, D], F32) for _ in range(2)]
            tmpb = [K.sb("tmp", [128, 512], F32) for _ in range(2)]
            scr = (K.sb("junk", [128, D], BF16), K.sb("ss", [128, 4], F32), K.sb("xs", [128, 4, D], F32))
            uTb = [K.sb("uT", [128, 8, 512], BF16) for _ in range(2)]
            it = 0
            for bi, (t0, T, s) in enumerate(blocks):
                nch = T // 128
                ao = aob[bi % 2]
                K.ld(ao.t[:, :, 0:T], AOsrc[:, t0:t0 + T].rearrange("(k p) t -> p k t", p=128), [ao.d])
                xt = xb[bi % 2]
                K.ld(xt.t[:, 0:nch, :], Xsrc[t0 - xoff:t0 - xoff + T, :].rearrange("(c p) d -> p c d", p=128), [xt.d])
                for c in range(nch):
                    for n2 in range(2):
                        ps = PS[2 + it % 4]
                        tmp = tmpb[it % 2]
                        it += 1
                        for k in range(nK):
                            K.mm(ps.t[:, :], ao.t[:, k, c * 128:(c + 1) * 128], wo.t[:, k, n2 * 512:(n2 + 1) * 512], k == 0, k == nK - 1,
                                 [ao.d, wo.d], [ps.d])
                        K.tt(DVE, tmp.t[:, :], ps.t[:, :], gts[s].t[:, n2 * 512:(n2 + 1) * 512], ALU.mult, [ps.d, gts[s].d], [tmp.d])
                        K.tt(POOL, xt.t[:, c, n2 * 512:(n2 + 1) * 512], xt.t[:, c, n2 * 512:(n2 + 1) * 512], tmp.t[:, :], ALU.add,
                             [xt.d, tmp.d], [xt.d])
                K.st(X1[t0:t0 + T, :].rearrange("(c p) d -> p c d", p=128), xt.t[:, 0:nch, :], [xt.d])
                uT = uTb[bi % 2]
                norm_to_uT(xt, nch, T, layer, 1, s, uT, scr, [PS[0], PS[1]])
                K.st(U2[:, t0:t0 + T].rearrange("(k p) t -> p k t", p=128), uT.t[:, :, 0:T], [uT.d])

    def phaseC2(layer, blocks, Xdst, xoff, is_out):
        with K.phase():
            w13 = K.sb("w13", [128, 8, 2 * FFH], BF16)
            for k in range(8):
                for hh in range(2):
                    K.ldcast(w13.t[:, k, hh * FFH:(hh + 1) * FFH], w13D[layer, k * 128:(k + 1) * 128, hh * FFH:(hh + 1) * FFH], [w13.d])
            w2 = K.sb("w2", [128, 22, D], BF16)
            for j in range(22):
                K.ldcast(w2.t[:, j, :], w2D[layer, j * 128:(j + 1) * 128, :], [w2.d])
            gt1 = K.sb("gt", [128, D], F32)
            gts = {0: gt1, 1: gt1}
            ub = [K.sb("u", [128, 8, 512], BF16) for _ in range(1)]
            xb = [K.sb("xf", [128, 4, D], F32) for _ in range(1)]
            hb = K.sb("h", [128, 22, 512], BF16)
            sgb = [K.sb("sg", [128, 512], F32) for _ in range(2)]
            tmpb = [K.sb("tmp", [128, 512], F32) for _ in range(2)]
            it = 0
            for bi, (t0, T, s) in enumerate(blocks):
                nch = T // 128
                u = ub[0]
                K.ld(u.t[:, :, 0:T], U2[:, t0:t0 + T].rearrange("(k p) t -> p k t", p=128), [u.d])
                if bi == 0 or blocks[bi - 1][2] != s:
                    K.ld(gt1.t[:, :], MODROW[layer, 1, s, :].partition_broadcast(128), [gt1.d])
                xt = xb[0]
                K.ld(xt.t[:, 0:nch, :], X1[t0:t0 + T, :].rearrange("(c p) d -> p c d", p=128), [xt.d])
                for j in range(22):
                    psg = PS[(2 * j) % 4]
                    psu = PS[(2 * j + 1) % 4]
                    for k in range(8):
                        K.mm(psg.t[:, 0:T], w13.t[:, k, j * 128:(j + 1) * 128], u.t[:, k, 0:T], k == 0, k == 7, [w13.d, u.d], [psg.d])
                    for k in range(8):
                        K.mm(psu.t[:, 0:T], w13.t[:, k, FFH + j * 128:FFH + (j + 1) * 128], u.t[:, k, 0:T], k == 0, k == 7, [w13.d, u.d], [psu.d])
                    sg = sgb[j % 2]
                    K.act(sg.t[:, 0:T], psg.t[:, 0:T], AF.Silu, [psg.d], [sg.d])
                    K.tt(DVE, hb.t[:, j, 0:T], sg.t[:, 0:T], psu.t[:, 0:T], ALU.mult, [sg.d, psu.d], [hb.d])
                for c in range(nch):
                    for n2 in range(2):
                        ps = PS[4 + it % 4]
                        tmp = tmpb[it % 2]
                        it += 1
                        for j in range(22):
                            K.mm(ps.t[:, :], hb.t[:, j, c * 128:(c + 1) * 128], w2.t[:, j, n2 * 512:(n2 + 1) * 512], j == 0, j == 21,
                                 [hb.d, w2.d], [ps.d])
                        K.tt(DVE, tmp.t[:, :], ps.t[:, :], gts[s].t[:, n2 * 512:(n2 + 1) * 512], ALU.mult, [ps.d, gts[s].d], [tmp.d])
                        K.tt(POOL, xt.t[:, c, n2 * 512:(n2 + 1) * 512], xt.t[:, c, n2 * 512:(n2 + 1) * 512], tmp.t[:, :], ALU.add,
                             [xt.d, tmp.d], [xt.d])
                K.em.dma(em.pool, Xdst[t0 - xoff:t0 - xoff + T, :].rearrange("(c p) d -> p c d", p=128), xt.t[:, 0:nch, :],
                         reads=[xt.d], is_output=is_out)

    swin = K.dram("ssm_w_in", [D, 5664], F32, "ExternalInput")
    swout = K.dram("ssm_w_out", [2048, D], F32, "ExternalInput")
    convw_col = K.dram("convw_col", [128, 60], F32, "ExternalInput")
    convb_col = K.dram("convb_col", [128, 12], F32, "ExternalInput")
    rows1 = K.dram("rows1", [NR1], F32, "ExternalInput")
    sqc2 = K.dram("sqc2", [128, 2 * 128 + 4], F32, "ExternalInput")
    selc = K.dram("selc", [16, 2048], F32, "ExternalInput")
    SZ = K.dram("SZ", [NT, D], F32)
    SRG = K.dram("SRG", [NT, D], F32)
    DTs = K.dram("DTs", [NT, 32], F32)
    RVs = K.dram("RVs", [NT, D], BF16)
    XBC = K.dram("XBC", [1536, NT], BF16)
    RQ = K.dram("RQ", [4, 128, NT], BF16)
    RK = K.dram("RK", [4, 128, NT], BF16)
    RKT = K.dram("RKT", [NT, 512], BF16)
    XS = K.dram("XS", [NT, D], F32)
    BT = K.dram("BT", [NT, 256], BF16)
    BFs = K.dram("BFs", [256, NT], BF16)
    CFs = K.dram("CFs", [256, NT], BF16)
    YF = K.dram("YF", [NT, 2048], F32)
    YS = K.dram("YS", [2048, NT], BF16)
    ONEB = K.gsb("oneb", [128, 1], F32)
    K.memset(DVE, ONEB.t[:, :], 1.0, [ONEB.d])

    def softplus_small(x, tmp, R, W_):
        K.ts(DVE, tmp, x, -1.0, ALU.mult, R, W_)
        K.tt(DVE, tmp, tmp, x, ALU.max, R + W_, W_)
        K.act(tmp, tmp, AF.Exp, W_, W_, scale=-1.0)
        K.act(tmp, tmp, AF.Ln, W_, W_, bias=ONEB.t[0:x.shape[0], 0:1])
        K.stt(x, x, 0.0, tmp, ALU.max, ALU.add, R + W_, R)

    def l1_phaseA():
        with K.phase():
            W1 = K.sb("w1", [128, 8, 5664], BF16)
            for k in range(8):
                for (a, b) in ((0, 1888), (1888, 3776), (3776, 5664)):
                    K.ldcast(W1.t[:, k, a:b], swin[k * 128:(k + 1) * 128, a:b], [W1.d])
            xb = [K.sb("xa", [128, 4, D], F32)]
            scr = (K.sb("junk", [128, D], BF16), K.sb("ss", [128, 4], F32), K.sb("xs", [128, 4, D], F32))
            uTb = [K.sb("uT", [128, 8, 512], BF16) for _ in range(2)]
            ropeb = [K.sb("rope", [128, 2, 512], F32) for _ in range(2)]
            qnb = [K.sb("qn", [128, 512], F32) for _ in range(2)]
            t1b = [K.sb("t1", [128, 512], F32) for _ in range(2)]
            t2b = [K.sb("t2", [128, 512], F32) for _ in range(2)]
            ofb = [K.sb("of", [128, 512], F32) for _ in range(2)]
            obb = [K.sb("ob", [128, 512], BF16) for _ in range(3)]
            tmz = [K.sb("tmz", [128, D], F32) for _ in range(3)]
            rvt = [K.sb("rvt", [128, D], BF16) for _ in range(2)]
            rktb = [K.sb("rkt", [128, 512], BF16) for _ in range(2)]
            dtt = [K.sb("dtt", [128, 32], F32) for _ in range(2)]
            cnt = {"p": 0, "n": 0, "o": 0, "z": 0, "t": 0}

            def pbank(banks):
                cnt["p"] += 1
                return banks[cnt["p"] % len(banks)]

            for bi, (t0, T, s) in enumerate(K.blocks):
                nch = T // 128
                xt = xb[0]
                K.ld(xt.t[:, 0:nch, :], X2[t0:t0 + T, :].rearrange("(c p) d -> p c d", p=128), [xt.d])
                rp = ropeb[bi % 2]
                K.ld(rp.t[:, :, 0:T], ropeD[0:2, :, t0:t0 + T].rearrange("f p t -> p f t"), [rp.d])
                uT = uTb[bi % 2]
                norm_to_uT(xt, nch, T, 1, 0, s, uT, scr, [PS[0], PS[1]])
                for ch in range(12):
                    ps = pbank([PS[2], PS[3]])
                    for k in range(8):
                        K.mm(ps.t[:, 0:T], W1.t[:, k, 1024 + ch * 128:1024 + (ch + 1) * 128], uT.t[:, k, 0:T], k == 0, k == 7, [W1.d, uT.d], [ps.d])
                    ob = obb[cnt["o"] % 3]
                    cnt["o"] += 1
                    K.act(ob.t[:, 0:T], ps.t[:, 0:T], AF.Copy, [ps.d], [ob.d])
                    K.st(XBC[ch * 128:(ch + 1) * 128, t0:t0 + T], ob.t[:, 0:T], [ob.d])
                for kind in range(2):
                    for ch in range(4):
                        col0 = 2592 + kind * 512 + ch * 128
                        ps = pbank([PS[2], PS[3]])
                        for k in range(8):
                            K.mm(ps.t[:, 0:T], W1.t[:, k, col0:col0 + 128], uT.t[:, k, 0:T], k == 0, k == 7, [W1.d, uT.d], [ps.d])
                        i = cnt["n"] % 2
                        cnt["n"] += 1
                        qn, t1, t2, of = qnb[i], t1b[i], t2b[i], ofb[i]
                        K.act(qn.t[:, 0:T], ps.t[:, 0:T], AF.Copy, [ps.d], [qn.d], scale=(1.0 if kind == 0 else 0.125))
                        psr = PS[0]
                        K.mm(psr.t[:, 0:T], cst(CI_ROT), qn.t[:, 0:T], True, True, [qn.d, CS.d], [psr.d])
                        K.tt(DVE, t1.t[:, 0:T], qn.t[:, 0:T], rp.t[:, 0, 0:T], ALU.mult, [qn.d, rp.d], [t1.d])
                        K.tt(DVE, t2.t[:, 0:T], psr.t[:, 0:T], rp.t[:, 1, 0:T], ALU.mult, [psr.d, rp.d], [t2.d])
                        K.tt(POOL, of.t[:, 0:T], t1.t[:, 0:T], t2.t[:, 0:T], ALU.add, [t1.d, t2.d], [of.d])
                        ob = obb[cnt["o"] % 3]
                        cnt["o"] += 1
                        K.act(ob.t[:, 0:T], of.t[:, 0:T], AF.Copy, [of.d], [ob.d])
                        K.st((RQ if kind == 0 else RK)[ch, :, t0:t0 + T], ob.t[:, 0:T], [ob.d])
                        if kind == 1:
                            for c in range(nch):
                                K.tr(PS[4 + c].t[:, ch * 128:(ch + 1) * 128], of.t[:, c * 128:(c + 1) * 128], cst(CI_IDENT), [of.d, CS.d], [PS[4 + c].d])
                for c in range(nch):
                    rkt = rktb[c % 2]
                    K.copy(DVE, rkt.t[:, :], PS[4 + c].t[:, :], [PS[4 + c].d], [rkt.d])
                    K.st(RKT[t0 + c * 128:t0 + (c + 1) * 128, :], rkt.t[:, :], [rkt.d])
                TMB = [PS[2], PS[3], PS[4], PS[5], PS[6], PS[7]]
                for c in range(nch):
                    r0 = t0 + c * 128
                    for (col0, kind) in ((0, "z"), (4640, "g"), (3616, "v")):
                        if kind == "v":
                            dst = rvt[cnt["t"] % 2]
                            cnt["t"] += 1
                        else:
                            dst = tmz[cnt["z"] % 3]
                            cnt["z"] += 1
                        for half in range(2):
                            ps = pbank(TMB)
                            for k in range(8):
                                K.mm(ps.t[:, :], uT.t[:, k, c * 128:(c + 1) * 128], W1.t[:, k, col0 + half * 512:col0 + (half + 1) * 512],
                                     k == 0, k == 7, [W1.d, uT.d], [ps.d])
                            if kind == "v":
                                K.copy(DVE, dst.t[:, half * 512:(half + 1) * 512], ps.t[:, :], [ps.d], [dst.d])
                            else:
                                K.act(dst.t[:, half * 512:(half + 1) * 512], ps.t[:, :], AF.Silu, [ps.d], [dst.d])
                        K.st({"z": SZ, "g": SRG, "v": RVs}[kind][r0:r0 + 128, :], dst.t[:, :], [dst.d])
                    ps = pbank(TMB)
                    for k in range(8):
                        K.mm(ps.t[:, 0:32], uT.t[:, k, c * 128:(c + 1) * 128], W1.t[:, k, 2560:2592], k == 0, k == 7, [W1.d, uT.d], [ps.d])
                    dd = dtt[c % 2]
                    K.copy(DVE, dd.t[:, :], ps.t[:, 0:32], [ps.d], [dd.d])
                    K.st(DTs[r0:r0 + 128, :], dd.t[:, :], [dd.d])

    def l1_phaseV():
        with K.phase():
            cwc = K.sb("cwc", [128, 60], F32)
            K.ld(cwc.t[:, :], convw_col[:, :], [cwc.d])
            cbc = K.sb("cbc", [128, 12], F32)
            K.ld(cbc.t[:, :], convb_col[:, :], [cbc.d])
            cbr = K.sb("cbr", [1, 1536], F32)
            K.ld(cbr.t[:, :], rows1[R_CONVB:R_CONVB + 1536].partition_broadcast(1), [cbr.d])
            DG = K.sb("dg", [128, 60, 128], BF16)
            for idx in range(60):
                K.ts(DVE if idx % 2 == 0 else POOL, DG.t[:, idx, :], cst(CI_IDENT), cwc.t[:, idx:idx + 1], ALU.mult, [CS.d, cwc.d], [DG.d])
            xwb = [K.sb("xw", [128, 12, 516], BF16) for _ in range(2)]
            obb = [K.sb("ob", [128, 512], BF16) for _ in range(2)]
            xst = [K.sb("xst", [128, D], F32) for _ in range(2)]
            btt = [K.sb("btt", [128, 256], BF16) for _ in range(2)]
            ones_row = cst(CI_ONES)[0:1, 0:128]
            cnt = {"p": 0}
            BK = [PS[0], PS[1], PS[2], PS[3], PS[4], PS[5], PS[6], PS[7]]

            def pbank():
                cnt["p"] += 1
                return BK[cnt["p"] % 8]

            for bi, (t0, T, s) in enumerate(K.blocks):
                nch = T // 128
                seg0, seg1 = (0, CTX) if s == 1 else (CTX, NT)
                lo, hi = max(t0 - 2, seg0), min(t0 + T + 2, seg1)
                xw = xwb[bi % 2]
                K.memset(POOL, xw.t[:, :, :], 0.0, [xw.d])
                K.ld(xw.t[:, :, lo - (t0 - 2):hi - (t0 - 2)], XBC[:, lo:hi].rearrange("(c p) t -> p c t", p=128), [xw.d])
                for ch in range(8, 12):
                    ps = pbank()
                    for k in range(5):
                        K.mm(ps.t[:, 0:T], DG.t[:, k * 12 + ch, :], xw.t[:, ch, k:k + T], k == 0, k == 4, [DG.d, xw.d], [ps.d])
                    ob = obb[ch % 2]
                    K.act(ob.t[:, 0:T], ps.t[:, 0:T], AF.Silu, [ps.d, cbc.d], [ob.d], bias=cbc.t[:, ch:ch + 1])
                    dstD = BFs if ch < 10 else CFs
                    r = (ch - 8) % 2
                    K.st(dstD[r * 128:(r + 1) * 128, t0:t0 + T], ob.t[:, 0:T], [ob.d])
                for c in range(nch):
                    r0 = t0 + c * 128
                    banks = [pbank(), pbank(), pbank()]
                    for ch in range(10):
                        tgt = banks[ch // 4]
                        o_ap = tgt.t[:, (ch % 4) * 128:(ch % 4 + 1) * 128]
                        for k in range(5):
                            K.mm(o_ap, xw.t[:, ch, c * 128 + k:c * 128 + k + 128], DG.t[:, k * 12 + ch, :], k == 0, False, [DG.d, xw.d], [tgt.d])
                        K.mm(o_ap, ones_row, cbr.t[0:1, ch * 128:(ch + 1) * 128], False, True, [CS.d, cbr.d], [tgt.d])
                    xo = xst[c % 2]
                    for hh in range(2):
                        K.act(xo.t[:, hh * 512:(hh + 1) * 512], banks[hh].t[:, :], AF.Silu, [banks[hh].d], [xo.d])
                    K.st(XS[r0:r0 + 128, :], xo.t[:, :], [xo.d])
                    bo = btt[c % 2]
                    K.act(bo.t[:, :], banks[2].t[:, 0:256], AF.Silu, [banks[2].d], [bo.d])
                    K.st(BT[r0:r0 + 128, :], bo.t[:, :], [bo.d])

    def l1_scan(dirn):
        fin = dirn == 1
        with K.phase():
            C2 = K.sb("c2", [128, 2 * 128 + 4], F32)
            K.ld(C2.t[:, :], sqc2[:, :], [C2.d])
            SEL = K.sb("sel", [16, 2048], F32)
            K.ld(SEL.t[:, :], selc[:, :], [SEL.d])
            IDX = C2.t[:, dirn * 128:(dirn + 1) * 128]
            colA = C2.t[:, 256 + dirn:256 + dirn + 1]
            colE = C2.t[:, 258 + dirn:258 + dirn + 1]
            MASK = cst(CI_TRI) if dirn == 0 else cst(CI_TRIT)

            def brow(name, off, n):
                b = K.sb(name, [128, n], F32)
                K.ld(b.t[:, :], rows1[off:off + n].partition_broadcast(128), [b.d])
                return b

            DTB = brow("dtb", R_DTB + dirn * 16, 16)
            ANEG = brow("aneg", R_ALOG + dirn * 16, 16)
            K.act(ANEG.t[:, :], ANEG.t[:, :], AF.Exp, [ANEG.d], [ANEG.d])
            K.ts(DVE, ANEG.t[:, :], ANEG.t[:, :], -1.0, ALU.mult, [ANEG.d], [ANEG.d])
            LG = brow("lg", R_RLOG + dirn * 8, 8)
            lgt = K.sb("lgt", [128, 8], F32)
            K.ts(DVE, LG.t[:, :], LG.t[:, :], -1.0, ALU.mult, [LG.d], [LG.d])
            softplus_small(LG.t[:, :], lgt.t[:, :], [LG.d], [lgt.d])
            K.ts(DVE, LG.t[:, :], LG.t[:, :], -1.0, ALU.mult, [LG.d], [LG.d])
            LM = K.sb("lm", [128, 8, 128], F32)
            for h in range(8):
                K.ts(DVE, LM.t[:, h, :], IDX, LG.t[:, h:h + 1], ALU.mult, [C2.d, LG.d], [LM.d])
            K.act(LM.t[:, :, :], LM.t[:, :, :], AF.Exp, [LM.d], [LM.d])
            K.tt(DVE, LM.t[:, :, :], LM.t[:, :, :], MASK.unsqueeze(1).broadcast_to([128, 8, 128]), ALU.mult, [LM.d, CS.d], [LM.d])
            EAr = K.sb("ear", [128, 8], F32)
            K.ts(DVE, EAr.t[:, :], LG.t[:, :], colA, ALU.mult, [LG.d, C2.d], [EAr.d])
            K.act(EAr.t[:, :], EAr.t[:, :], AF.Exp, [EAr.d], [EAr.d])
            DEr = K.sb("der", [128, 8], F32)
            K.ts(DVE, DEr.t[:, :], LG.t[:, :], colE, ALU.mult, [LG.d, C2.d], [DEr.d])
            K.act(DEr.t[:, :], DEr.t[:, :], AF.Exp, [DEr.d], [DEr.d])
            CDR = K.sb("cdr", [128, 4], F32)
            lg2 = LG.t[:, :].rearrange("p (q two) -> p q two", two=2)
            K.ts(DVE, CDR.t[0:64, :], lg2[0:64, :, 0], 128.0, ALU.mult, [LG.d], [CDR.d])
            K.ts(DVE, CDR.t[64:128, :], lg2[64:128, :, 1], 128.0, ALU.mult, [LG.d], [CDR.d])
            K.act(CDR.t[:, :], CDR.t[:, :], AF.Exp, [CDR.d], [CDR.d])
            if fin:
                DSK = brow("dsk", R_DSK, 1024)
                WN = brow("wn", R_SSDN, 1024)
                RN = brow("rn", R_RETN, 1024)
            S = [K.sb("S", [128, 512], F32) for _ in range(2)]
            Sbf = [K.sb("Sbf", [128, 512], BF16) for _ in range(2)]
            SR = K.sb("SR", [128, 4, 128], F32)
            SRbf = K.sb("SRbf", [128, 4, 128], BF16)
            for b in S + Sbf:
                K.memset(DVE, b.t[:, :], 0.0, [b.d])
            K.memset(DVE, SR.t[:, :, :], 0.0, [SR.d])
            K.memset(DVE, SRbf.t[:, :, :], 0.0, [SRbf.d])
            NB = 2
            xsb = [K.sb("xs", [128, D], F32) for _ in range(NB)]
            btb = [K.sb("bt", [128, 256], BF16) for _ in range(NB)]
            bfb = [K.sb("bf", [128, 2, 128], BF16) for _ in range(NB)]
            cfb = [K.sb("cf", [128, 2, 128], BF16) for _ in range(NB)]
            dtb_ = [K.sb("dt", [128, 16], F32) for _ in range(NB)]
            rqb = [K.sb("rq", [128, 4, 128], BF16) for _ in range(NB)]
            rkb = [K.sb("rk", [128, 4, 128], BF16) for _ in range(NB)]
            rktb = [K.sb("rkt", [128, 512], BF16) for _ in range(NB)]
            rvb = [K.sb("rv", [128, D], BF16) for _ in range(NB)]
            if fin:
                yfb = [K.sb("yf", [128, 2048], F32) for _ in range(NB)]
                szb = [K.sb("sz", [128, D], F32) for _ in range(NB)]
                sgb = [K.sb("srg", [128, D], F32) for _ in range(NB)]
            sm = K.sb("sm", [128, 8, 16], F32)
            at = K.sb("at", [16, 256], F32)
            E = K.sb("E", [128, 16, 128], F32)
            Gm = K.sb("Gm", [128, 2, 128], F32)
            M = K.sb("M", [128, 16, 128], BF16)
            MR = K.sb("MR", [128, 8, 128], BF16)
            Vd = K.sb("Vd", [128, D], BF16)
            Vdec = K.sb("Vdec", [128, D], BF16)
            RVd = K.sb("RVd", [128, D], BF16)
            yo = K.sb("yo", [128, D], F32)
            ydb = [K.sb("yd", [128, 2048], F32) for _ in range(2)]
            if fin:
                junk = K.sb("junk", [128, D], F32)
                st8 = K.sb("st8", [128, 8, 8], F32)
                ynb = K.sb("yn", [128, 2048], F32)
                ysb = [K.sb("ys", [128, 4, 128], BF16) for _ in range(2)]
            cnt = {"p": 0}

            def pbank():
                cnt["p"] += 1
                return PS[cnt["p"] % 8]

            order = list(range(NCH)) if dirn == 0 else [1, 0] + list(range(NCH - 1, 1, -1))

            def loads(ci):
                c = order[ci]
                t0 = c * 128
                i = ci % NB
                K.ld(xsb[i].t[:, :], XS[t0:t0 + 128, :], [xsb[i].d])
                K.ld(btb[i].t[:, :], BT[t0:t0 + 128, :], [btb[i].d])
                K.ld(bfb[i].t[:, :, :], BFs[:, t0:t0 + 128].rearrange("(g n) t -> n g t", n=128), [bfb[i].d])
                K.ld(cfb[i].t[:, :, :], CFs[:, t0:t0 + 128].rearrange("(g n) t -> n g t", n=128), [cfb[i].d])
                K.ld(dtb_[i].t[:, :], DTs[t0:t0 + 128, dirn * 16:(dirn + 1) * 16], [dtb_[i].d])
                K.ld(rqb[i].t[:, :, :], RQ[:, :, t0:t0 + 128].rearrange("c p t -> p c t"), [rqb[i].d])
                K.ld(rkb[i].t[:, :, :], RK[:, :, t0:t0 + 128].rearrange("c p t -> p c t"), [rkb[i].d])
                K.ld(rktb[i].t[:, :], RKT[t0:t0 + 128, :], [rktb[i].d])
                K.ld(rvb[i].t[:, :], RVs[t0:t0 + 128, :], [rvb[i].d])
                if fin:
                    K.ld(yfb[i].t[:, :], YF[t0:t0 + 128, :], [yfb[i].d])
                    K.ld(szb[i].t[:, :], SZ[t0:t0 + 128, :], [szb[i].d])
                    K.ld(sgb[i].t[:, :], SRG[t0:t0 + 128, :], [sgb[i].d])

            def bc3(ap2, n):
                return ap2.unsqueeze(2).broadcast_to([128, ap2.shape[1], n])

            import os
            KSC = int(os.environ.get("KSCAN", "99"))
            loads(0)
            for ci in range(len(order)):
                if ci + 1 < len(order):
                    loads(ci + 1)
                c = order[ci]
                t0 = c * 128
                i = ci % NB
                xs_c, bt_c, bf_c, cf_c, dt_c = xsb[i], btb[i], bfb[i], cfb[i], dtb_[i]
                rq_c, rk_c, rkt_c, rv_c = rqb[i], rkb[i], rktb[i], rvb[i]
                yd = ydb[ci % 2]
                sp, tmpv, la, Acol, Atot, expA, dece, cd = [sm.t[:, j, :] for j in range(8)]
                smd = [sm.d]
                if KSC < 1:
                    continue
                K.tt(DVE, sp, dt_c.t[:, :], DTB.t[:, :], ALU.add, [dt_c.d, DTB.d], smd)
                softplus_small(sp, tmpv, smd, smd)
                K.tt(DVE, la, sp, ANEG.t[:, :], ALU.mult, smd + [ANEG.d], smd)
                psc = pbank()
                K.mm(psc.t[:, 0:16], MASK, la, True, True, smd + [CS.d], [psc.d])
                K.mm(psc.t[:, 16:32], cst(CI_ONES), la, True, True, smd + [CS.d], [psc.d])
                K.mm(psc.t[0:16, 128:256], la, MASK, True, True, smd + [CS.d], [psc.d])
                K.copy(DVE, sm.t[:, 3:5, :], psc.t[:, 0:32].rearrange("p (a b) -> p a b", b=16), [psc.d], smd)
                K.copy(DVE, at.t[:, 0:128], psc.t[0:16, 128:256], [psc.d], [at.d])
                K.ts(DVE, at.t[:, 128:256], at.t[:, 0:128], -1.0, ALU.mult, [at.d], [at.d])
                K.act(expA, Acol, AF.Exp, smd, smd)
                K.tt(DVE, dece, Atot, Acol, ALU.subtract, smd, smd)
                K.act(dece, dece, AF.Exp, smd, smd)
                K.act(cd, Atot, AF.Exp, smd, smd)
                K.tt(DVE, tmpv, sp, dece, ALU.mult, smd, smd)
                if KSC < 2:
                    continue
                xs3 = xs_c.t[:, :].rearrange("p (h e) -> p h e", e=64)
                K.tt(DVE, Vd.t[:, :].rearrange("p (h e) -> p h e", e=64), xs3, bc3(sp, 64), ALU.mult, [xs_c.d] + smd, [Vd.d])
                K.tt(POOL, Vdec.t[:, :].rearrange("p (h e) -> p h e", e=64), xs3, bc3(tmpv, 64), ALU.mult, [xs_c.d] + smd, [Vdec.d])
                if KSC < 3:
                    continue
                for q in range(4):
                    psd = pbank()
                    for hh in range(4):
                        h = q * 4 + hh
                        o_ap = psd.t[:, hh * 128:(hh + 1) * 128]
                        K.mm(o_ap, SEL.t[0:16, h * 128:(h + 1) * 128], at.t[0:16, 0:128], True, False, [SEL.d, at.d], [psd.d])
                        K.mm(o_ap, at.t[0:16, 128:256], SEL.t[0:16, h * 128:(h + 1) * 128], False, True, [SEL.d, at.d], [psd.d])
                    K.tt(DVE, E.t[:, q * 4:(q + 1) * 4, :], psd.t[:, :].rearrange("p (h i) -> p h i", i=128),
                         MASK.unsqueeze(1).broadcast_to([128, 4, 128]), ALU.mult, [psd.d, CS.d], [E.d])
                K.act(E.t[:, :, :], E.t[:, :, :], AF.Exp, [E.d], [E.d])
                if KSC < 4:
                    continue
                psg = pbank()
                for g in range(2):
                    K.mm(psg.t[:, g * 128:(g + 1) * 128], bf_c.t[:, g, :], cf_c.t[:, g, :], True, True, [bf_c.d, cf_c.d], [psg.d])
                K.tt(DVE, Gm.t[:, :, :], psg.t[:, 0:256].rearrange("p (g i) -> p g i", i=128),
                     MASK.unsqueeze(1).broadcast_to([128, 2, 128]), ALU.mult, [psg.d, CS.d], [Gm.d])
                for g in range(2):
                    K.tt(DVE if g == 0 else POOL, M.t[:, g * 8:(g + 1) * 8, :], E.t[:, g * 8:(g + 1) * 8, :],
                         Gm.t[:, g, :].unsqueeze(1).broadcast_to([128, 8, 128]), ALU.mult, [E.d, Gm.d], [M.d])
                for g in range(2):
                    pso = pbank()
                    K.mm(pso.t[:, :], cf_c.t[:, g, :], Sbf[g].t[:, :], True, True, [cf_c.d, Sbf[g].d], [pso.d])
                    K.tt(DVE, yo.t[:, g * 512:(g + 1) * 512].rearrange("p (h e) -> p h e", e=64),
                         pso.t[:, :].rearrange("p (h e) -> p h e", e=64), bc3(expA[:, g * 8:(g + 1) * 8], 64), ALU.mult,
                         [pso.d] + smd, [yo.d])
                for g in range(2):
                    psy = pbank()
                    for hl in range(8):
                        h = g * 8 + hl
                        K.mm(psy.t[:, hl * 64:(hl + 1) * 64], M.t[:, h, :], Vd.t[:, h * 64:(h + 1) * 64], True, True, [M.d, Vd.d], [psy.d])
                    K.tt(DVE, yd.t[:, g * 512:(g + 1) * 512], psy.t[:, :], yo.t[:, g * 512:(g + 1) * 512], ALU.add, [psy.d, yo.d], [yd.d])
                for g in range(2):
                    psd2 = pbank()
                    K.mm(psd2.t[:, :], bt_c.t[:, g * 128:(g + 1) * 128], Vdec.t[:, g * 512:(g + 1) * 512], True, True, [bt_c.d, Vdec.d], [psd2.d])
                    s3 = S[g].t[:, :].rearrange("p (h e) -> p h e", e=64)
                    K.tt(POOL, s3, s3, bc3(cd[:, g * 8:(g + 1) * 8], 64), ALU.mult, [S[g].d] + smd, [S[g].d])
                    K.tt(DVE, S[g].t[:, :], S[g].t[:, :], psd2.t[:, :], ALU.add, [S[g].d, psd2.d], [S[g].d])
                    K.act(Sbf[g].t[:, :], S[g].t[:, :], AF.Copy, [S[g].d], [Sbf[g].d])
                if KSC < 5:
                    continue
                psgr = [pbank(), pbank()]
                for h in range(8):
                    rows = slice((h % 2) * 64, (h % 2) * 64 + 64)
                    K.mm(psgr[h % 2].t[:, (h // 2) * 128:(h // 2 + 1) * 128], rk_c.t[rows, h // 2, :], rq_c.t[rows, h // 2, :], True, True,
                         [rk_c.d, rq_c.d], [psgr[h % 2].d])
                for par in range(2):
                    K.tt(DVE, MR.t[:, :, :].rearrange("p (b two) i -> p b two i", two=2)[:, :, par, :],
                         psgr[par].t[:, :].rearrange("p (h i) -> p h i", i=128),
                         LM.t[:, :, :].rearrange("p (b two) i -> p b two i", two=2)[:, :, par, :],
                         ALU.mult, [psgr[par].d, LM.d], [MR.d])
                if os.environ.get("KSUB") == "2":
                    continue
                K.tt(DVE, RVd.t[:, :].rearrange("p (h e) -> p h e", e=128), rv_c.t[:, :].rearrange("p (h e) -> p h e", e=128),
                     bc3(DEr.t[:, :], 128), ALU.mult, [rv_c.d, DEr.d], [RVd.d])
                if KSC < 6:
                    continue
                psro = [pbank(), pbank()]
                for h in range(8):
                    rows = slice((h % 2) * 64, (h % 2) * 64 + 64)
                    K.mm(psro[h % 2].t[:, (h // 2) * 128:(h // 2 + 1) * 128], rq_c.t[rows, h // 2, :], SRbf.t[rows, h // 2, :], True, True,
                         [rq_c.d, SRbf.d], [psro[h % 2].d])
                for par in range(2):
                    K.tt(DVE, yo.t[:, :].rearrange("p (b two e) -> p b two e", two=2, e=128)[:, :, par, :],
                         psro[par].t[:, :].rearrange("p (h e) -> p h e", e=128),
                         bc3(EAr.t[:, :].rearrange("p (b two) -> p b two", two=2)[:, :, par], 128), ALU.mult,
                         [psro[par].d, EAr.d], [yo.d])
                if KSC < 7:
                    continue
                psyr = [pbank(), pbank()]
                for h in range(8):
                    K.mm(psyr[h // 4].t[:, (h % 4) * 128:(h % 4 + 1) * 128], MR.t[:, h, :], rv_c.t[:, h * 128:(h + 1) * 128], True, True,
                         [MR.d, rv_c.d], [psyr[h // 4].d])
                for q in range(2):
                    K.tt(DVE, yd.t[:, 1024 + q * 512:1024 + (q + 1) * 512], psyr[q].t[:, :], yo.t[:, q * 512:(q + 1) * 512], ALU.add,
                         [psyr[q].d, yo.d], [yd.d])
                if KSC < 8:
                    continue
                psdr = [pbank(), pbank()]
                for h in range(8):
                    pr = h // 2
                    K.mm(psdr[h // 4].t[:, (h % 4) * 128:(h % 4 + 1) * 128], rkt_c.t[:, pr * 128:(pr + 1) * 128], RVd.t[:, h * 128:(h + 1) * 128],
                         True, True, [rkt_c.d, RVd.d], [psdr[h // 4].d])
                K.tt(POOL, SR.t[:, :, :], SR.t[:, :, :], bc3(CDR.t[:, :], 128), ALU.mult, [SR.d, CDR.d], [SR.d])
                for half in range(2):
                    rows = slice(half * 64, half * 64 + 64)
                    for q in range(2):
                        src = psdr[q].t[:, :].rearrange("p (pp two e) -> p pp two e", two=2, e=128)[rows, :, half, :]
                        K.tt(DVE, SR.t[rows, 2 * q:2 * q + 2, :], SR.t[rows, 2 * q:2 * q + 2, :], src, ALU.add, [SR.d, psdr[q].d], [SR.d])
                K.act(SRbf.t[:, :, :], SR.t[:, :, :], AF.Copy, [SR.d], [SRbf.d])
                if not fin:
                    K.st(YF[t0:t0 + 128, :], yd.t[:, :], [yd.d])
                    continue
                yf_c, sz_c, sg_c = yfb[i], szb[i], sgb[i]
                K.tt(POOL, yd.t[:, :], yd.t[:, :], yf_c.t[:, :], ALU.add, [yd.d, yf_c.d], [yd.d])
                K.tt(POOL, junk.t[:, :], xs_c.t[:, :], DSK.t[:, :], ALU.mult, [xs_c.d, DSK.d], [junk.d])
                K.tt(DVE, yd.t[:, 0:1024], yd.t[:, 0:1024], junk.t[:, :], ALU.add, [yd.d, junk.d], [yd.d])
                K.tt(DVE, yd.t[:, 0:1024], yd.t[:, 0:1024], sz_c.t[:, :], ALU.mult, [yd.d, sz_c.d], [yd.d])
                ssg = st8.t[:, 0, 0:2]
                for g in range(2):
                    K.act(junk.t[:, 0:512], yd.t[:, g * 512:(g + 1) * 512], AF.Square, [yd.d], [junk.d, st8.d], accum_out=st8.t[:, 0, g:g + 1])
                K.ts(DVE, ssg, ssg, 1.0 / 512, ALU.mult, [st8.d], [st8.d], s2=EPS, op1=ALU.add)
                K.act(ssg, ssg, AF.Sqrt, [st8.d], [st8.d])
                K.recip(ssg, ssg, [st8.d], [st8.d])
                for g in range(2):
                    K.stt(ynb.t[:, g * 512:(g + 1) * 512], yd.t[:, g * 512:(g + 1) * 512], st8.t[:, 0, g:g + 1], WN.t[:, g * 512:(g + 1) * 512],
                          ALU.mult, ALU.mult, [yd.d, st8.d, WN.d], [ynb.d])
                yr3 = yd.t[:, 1024:2048].rearrange("p (h e) -> p h e", e=128)
                s1, s2, mean, m2 = st8.t[:, 1, :], st8.t[:, 2, :], st8.t[:, 3, :], st8.t[:, 4, :]
                em.op(DVE, lambda: nc.vector.tensor_reduce(out=s1, in_=yr3, axis=AX.X, op=ALU.add), [yd.d], [st8.d])
                K.act(junk.t[:, :], yd.t[:, 1024:2048], AF.Square, [yd.d], [junk.d])
                em.op(DVE, lambda: nc.vector.tensor_reduce(out=s2, in_=junk.t[:, :].rearrange("p (h e) -> p h e", e=128), axis=AX.X, op=ALU.add),
                      [junk.d], [st8.d])
                K.ts(DVE, mean, s1, 1.0 / 128, ALU.mult, [st8.d], [st8.d])
                K.tt(DVE, m2, mean, mean, ALU.mult, [st8.d], [st8.d])
                K.stt(s2, s2, 1.0 / 128, m2, ALU.mult, ALU.subtract, [st8.d], [st8.d])
                K.ts(DVE, s2, s2, EPS, ALU.add, [st8.d], [st8.d])
                K.act(s2, s2, AF.Sqrt, [st8.d], [st8.d])
                K.recip(s2, s2, [st8.d], [st8.d])
                yn3 = ynb.t[:, 1024:2048].rearrange("p (h e) -> p h e", e=128)
                K.tt(DVE, yn3, yr3, bc3(mean, 128), ALU.subtract, [yd.d, st8.d], [ynb.d])
                K.tt(POOL, yn3, yn3, bc3(s2, 128), ALU.mult, [ynb.d, st8.d], [ynb.d])
                K.tt(DVE, ynb.t[:, 1024:2048], ynb.t[:, 1024:2048], RN.t[:, :], ALU.mult, [ynb.d, RN.d], [ynb.d])
                K.tt(POOL, ynb.t[:, 1024:2048], ynb.t[:, 1024:2048], sg_c.t[:, :], ALU.mult, [ynb.d, sg_c.d], [ynb.d])
                for b4 in range(4):
                    pst = pbank()
                    for kk in range(4):
                        k = b4 * 4 + kk
                        K.tr(pst.t[:, kk * 128:(kk + 1) * 128], ynb.t[:, k * 128:(k + 1) * 128], cst(CI_IDENT), [ynb.d, CS.d], [pst.d])
                    ys = ysb[b4 % 2]
                    K.act(ys.t[:, :, :], pst.t[:, :].rearrange("p (k t) -> p k t", t=128), AF.Copy, [pst.d], [ys.d])
                    K.st(YS[b4 * 512:(b4 + 1) * 512, t0:t0 + 128].rearrange("(k p) t -> p k t", p=128), ys.t[:, :, :], [ys.d])

    EPSB = K.gsb("epsb", [128, 1], F32)
    K.memset(DVE, EPSB.t[:, :], EPS, [EPSB.d])

    l0_phaseA()
    l0_phaseB()
    phaseC1(0, K.blocks, AO, 8, awout, xin, 0)
    if nlayers == 1:
        phaseC2(0, K.blocks, X2, 0, True)
    else:
        phaseC2(0, K.blocks, X2, 0, False)
        import os
        stop = int(os.environ.get("KSTOP", "99"))
        lat = [b for b in K.blocks if b[2] == 0]
        steps = [l1_phaseA, l1_phaseV, lambda: l1_scan(0), lambda: l1_scan(1),
                 lambda: phaseC1(1, lat, YS, 16, swout, X2, 0), lambda: phaseC2(1, lat, outD, CTX, True)]
        for si, fn in enumerate(steps):
            if si < stop:
                fn()
    em.finish()
    K.stats = em.stats()
    return K


def prep_inputs(inp, L):
    f = lambda a: np.ascontiguousarray(np.asarray(a, dtype=np.float32))
    rope, sq = host_consts(L)
    col = lambda v, n: f(v).reshape(n, 128).T
    shared = {
        "mod_w": f(inp["mod_w"]),
        "mod_b": f(inp["mod_b"]),
        "modb_col": np.ascontiguousarray(np.stack([col(inp["mod_b"][i], 48) for i in range(2)], axis=1)),
        "ncol": np.ascontiguousarray(np.stack([np.stack([col(inp["norm1_w"][i], 8), col(inp["norm2_w"][i], 8)], axis=1) for i in range(2)], axis=1)),
        "rope": rope, "sqc": sq,
        "ffn_w13": f(inp["ffn_w13"]), "ffn_w2": f(inp["ffn_w2"]),
        "attn_w_in": f(inp["attn_w_in"][0]), "mla_wq_b": f(inp["mla_wq_b"][0]), "mla_wkv_b": f(inp["mla_wkv_b"][0]),
        "attn_w_out": f(inp["attn_w_out"][0]),
    }
    if "ssm_w_in" in inp:
        shared["ssm_w_in"] = f(inp["ssm_w_in"][0])
        shared["ssm_w_out"] = f(inp["ssm_w_out"][0])
        cw = f(inp["ssd_conv_w"][0])
        shared["convw_col"] = np.ascontiguousarray(cw.reshape(5, 12, 128).transpose(2, 0, 1).reshape(128, 60))
        shared["convb_col"] = col(inp["ssd_conv_b"][0], 12)
        rows = np.zeros((NR1,), np.float32)
        rows[R_CONVB:R_CONVB + 1536] = f(inp["ssd_conv_b"][0])
        rows[R_DTB:R_DTB + 32] = f(inp["ssd_dt_bias"][0]).reshape(-1)
        rows[R_ALOG:R_ALOG + 32] = f(inp["ssd_a_log"][0]).reshape(-1)
        rows[R_RLOG:R_RLOG + 16] = f(inp["ret_decay_logit"][0]).reshape(-1)
        rows[R_DSK:R_DSK + 1024] = np.repeat(f(inp["ssd_d"][0]), 64)
        rows[R_SSDN:R_SSDN + 1024] = f(inp["ssd_norm"][0])
        rows[R_RETN:R_RETN + 1024] = f(inp["ret_norm"][0])
        shared["rows1"] = rows
        jj, ii = np.meshgrid(np.arange(128, dtype=np.float32), np.arange(128, dtype=np.float32), indexing="ij")
        c2 = np.zeros((128, 260), np.float32)
        c2[:, 0:128] = np.maximum(ii - jj, 0)
        c2[:, 128:256] = np.maximum(jj - ii, 0)
        j1 = np.arange(128, dtype=np.float32)
        c2[:, 256], c2[:, 257], c2[:, 258], c2[:, 259] = j1 + 1, 128 - j1, 127 - j1, j1
        shared["sqc2"] = c2
        sel = np.zeros((16, 16, 128), np.float32)
        for h in range(16):
            sel[h, h, :] = 1.0
        shared["selc"] = np.ascontiguousarray(sel.reshape(16, 2048))
    vec = np.zeros((128, NV), np.float32)
    vec[:, V_GQ] = np.tile(f(inp["gqa_qn"][0]), 2)
    vec[:, V_GK] = np.tile(f(inp["gqa_kn"][0]), 2)
    vec[:, V_QA:V_QA + 3] = col(inp["mla_qa_norm"][0], 3)
    vec[:, V_KVA:V_KVA + 2] = col(inp["mla_kva_norm"][0], 2)
    vec[:96, V_MQ] = f(inp["mla_qn"][0])
    vec[:96, V_MK] = f(inp["mla_kn"][0])
    shared["vecs"] = vec
    maps = []
    x, c, ctx, c_ctx = f(inp["x"]), f(inp["c"]), f(inp["ctx"]), f(inp["c_ctx"])
    for b in range(x.shape[0]):
        m = dict(shared)
        m["xin"] = np.ascontiguousarray(np.concatenate([ctx[b], x[b]], axis=0))
        cc = np.stack([col(c[b], 8), col(c_ctx, 8)], axis=2)
        m["cc"] = np.ascontiguousarray(cc)
        maps.append(m)
    return maps


_CACHE = {}


def kernel(**inputs):
    L = int(np.asarray(inputs["x"]).shape[1])
    B = int(np.asarray(inputs["x"]).shape[0])
    if L not in _CACHE:
        _CACHE[L] = build(L, 2)
    K = _CACHE[L]
    maps = prep_inputs(inputs, L)
    res = run_bass_kernel_spmd(K.nc, maps, core_ids=list(range(B)))
    return np.stack([np.asarray(res.results[b]["out"]) for b in range(B)], axis=0).astype(np.float32)
```

```python
import math
from contextlib import ExitStack, contextmanager
import numpy as np
import concourse.bass as bass
import concourse.mybir as mybir
from concourse.bass_utils import run_bass_kernel_spmd

F32 = mybir.dt.float32
BF16 = mybir.dt.bfloat16
AF = mybir.ActivationFunctionType
ALU = mybir.AluOpType
AX = mybir.AxisListType

EPOCH = 30000
EPS = 1e-6
D = 1024
CTX = 256
FFH = 2816
GRID_W = 64
THETA = 10000.0


class Dep:
    __slots__ = ("w", "r")

    def __init__(self):
        self.w = None
        self.r = {}


class Eng:
    def __init__(self, em, name, h, self_sync=True):
        self.em, self.name, self.h, self.self_sync = em, name, h, self_sync
        self.sem = None
        self.cnt = 0
        self.waited = {}
        self.own = set()
        self.ninst = 0
        self.nwait = 0
        self.last = None

    def next_event(self):
        if self.sem is None or self.cnt >= EPOCH:
            self.sem = self.em.new_sem(self.name)
            self.own.add(self.sem.num)
            self.cnt = 0
        self.cnt += 1
        self.last = (self.sem.num, self.cnt)
        return self.last


class Emitter:
    def __init__(self, nc, es, n_dma_sp=24, n_dma_pool=16):
        self.nc, self.es = nc, es
        self.sems = {}
        self.nsem = 0
        self.pe = Eng(self, "pe", nc.tensor, self_sync=False)
        self.act = Eng(self, "act", nc.scalar)
        self.dve = Eng(self, "dve", nc.vector)
        self.pool = Eng(self, "pool", nc.gpsimd)
        self.sp = Eng(self, "sp", nc.sync)
        self.engines = [self.pe, self.act, self.dve, self.pool, self.sp]
        self.dma_pools = {}
        for e, n in ((self.sp, n_dma_sp), (self.pool, n_dma_pool)):
            self.dma_pools[e.name] = [[self.new_sem("d" + e.name), 0] for _ in range(n)]
        self.dma_rr = {k: 0 for k in self.dma_pools}
        self.out_events = []

    def new_sem(self, tag):
        s = self.es.enter_context(self.nc.semaphore("%s_%d" % (tag, self.nsem)))
        self.nsem += 1
        self.sems[s.num] = s
        return s

    def _wait(self, eng, evs):
        best = {}
        for (s, v) in evs:
            if v > best.get(s, 0):
                best[s] = v
        for s, v in best.items():
            if (not eng.self_sync) and s in eng.own:
                continue
            if eng.waited.get(s, 0) >= v:
                continue
            eng.h.wait_ge(self.sems[s], v)
            eng.waited[s] = v
            eng.nwait += 1

    @staticmethod
    def _collect(reads, writes):
        evs = []
        for d in reads:
            if d.w is not None:
                evs.append(d.w)
        for d in writes:
            if d.w is not None:
                evs.append(d.w)
            evs.extend(d.r.items())
        return evs

    @staticmethod
    def _commit(ev, reads, writes):
        for d in reads:
            if ev[1] > d.r.get(ev[0], 0):
                d.r[ev[0]] = ev[1]
        for d in writes:
            d.w = ev
            d.r = {}

    def op(self, eng, fn, reads=(), writes=()):
        self._wait(eng, self._collect(reads, writes))
        inst = fn()
        ev = eng.next_event()
        inst.then_inc(self.sems[ev[0]], 1)
        eng.ninst += 1
        self._commit(ev, reads, writes)
        return ev

    def dma(self, eng, out, in_, reads=(), writes=(), is_output=False, **kw):
        pool = self.dma_pools[eng.name]
        i = self.dma_rr[eng.name]
        self.dma_rr[eng.name] = (i + 1) % len(pool)
        slot = pool[i]
        evs = self._collect(reads, writes)
        if slot[1] > 0:
            evs.append((slot[0].num, slot[1]))
        self._wait(eng, evs)
        if slot[1] + 16 > EPOCH:
            slot[0] = self.new_sem("d" + eng.name)
            slot[1] = 0
        slot[1] += 16
        eng.h.dma_start(out=out, in_=in_, **kw).then_inc(slot[0], 16)
        ev = (slot[0].num, slot[1])
        eng.ninst += 1
        self._commit(ev, reads, writes)
        if is_output:
            self.out_events.append(ev)
        return ev

    def all_events(self):
        evs = []
        for e in self.engines:
            if e.last is not None:
                evs.append(e.last)
        for pool in self.dma_pools.values():
            for s, v in pool:
                if v > 0:
                    evs.append((s.num, v))
        return evs

    def barrier(self):
        evs = self.all_events()
        for e in self.engines:
            self._wait(e, evs)

    def finish(self):
        self._wait(self.sp, list(self.out_events) + self.all_events())

    def stats(self):
        return {e.name: (e.ninst, e.nwait) for e in self.engines}, self.nsem


class Buf:
    __slots__ = ("t", "d")

    def __init__(self, t):
        self.t = t
        self.d = Dep()


class KB:
    def __init__(self, L):
        self.L = L
        self.NT = CTX + L
        self.NCH = self.NT // 128
        self.nc = bass.Bass("TRN2", target_bir_lowering=False)
        self.es = ExitStack()
        self.em = Emitter(self.nc, self.es)
        self.cur = self.es
        self.uid = 0
        self.blocks = [(0, CTX, 1)] + [(CTX + i * 512, 512, 0) for i in range(L // 512)]
        self.PS = [Buf(self.es.enter_context(self.nc.psum_tensor("psb%d" % i, [128, 512], F32))) for i in range(8)]

    def sb(self, name, shape, dtype):
        self.uid += 1
        return Buf(self.cur.enter_context(self.nc.sbuf_tensor("s_%s_%d" % (name, self.uid), shape, dtype)))

    def gsb(self, name, shape, dtype):
        return Buf(self.es.enter_context(self.nc.sbuf_tensor("g_" + name, shape, dtype)))

    def dram(self, name, shape, dtype, kind="Internal"):
        return self.nc.dram_tensor(name, shape, dtype, kind=kind).ap()

    @contextmanager
    def phase(self):
        st = ExitStack()
        prev = self.cur
        self.cur = st
        try:
            yield
        finally:
            self.em.barrier()
            st.close()
            self.cur = prev

    def mm(self, out, lhsT, rhs, start, stop, R, W):
        nc = self.nc
        return self.em.op(self.em.pe, lambda: nc.tensor.matmul(out, lhsT=lhsT, rhs=rhs, start=start, stop=stop), R, W)

    def tr(self, out, in_, ident, R, W):
        nc = self.nc
        return self.em.op(self.em.pe, lambda: nc.tensor.transpose(out=out, in_=in_, identity=ident), R, W)

    def act(self, out, in_, func, R, W, scale=None, bias=None, accum_out=None):
        nc = self.nc
        kw = {}
        if scale is not None:
            kw["scale"] = scale
        if bias is not None:
            kw["bias"] = bias
        if accum_out is not None:
            kw["accum_out"] = accum_out
        return self.em.op(self.em.act, lambda: nc.scalar.activation(out=out, in_=in_, func=func, **kw), R, W)

    def _ve(self, eng):
        return self.nc.vector if eng is self.em.dve else self.nc.gpsimd

    def tt(self, eng, out, in0, in1, op, R, W):
        h = self._ve(eng)
        return self.em.op(eng, lambda: h.tensor_tensor(out=out, in0=in0, in1=in1, op=op), R, W)

    def ts(self, eng, out, in0, s1, op0, R, W, s2=None, op1=None):
        h = self._ve(eng)
        if op1 is None:
            return self.em.op(eng, lambda: h.tensor_scalar(out=out, in0=in0, scalar1=s1, scalar2=None, op0=op0), R, W)
        return self.em.op(eng, lambda: h.tensor_scalar(out=out, in0=in0, scalar1=s1, scalar2=s2, op0=op0, op1=op1), R, W)

    def stt(self, out, in0, scalar, in1, op0, op1, R, W):
        nc = self.nc
        return self.em.op(self.em.dve, lambda: nc.vector.scalar_tensor_tensor(out=out, in0=in0, scalar=scalar, in1=in1, op0=op0, op1=op1), R, W)

    def recip(self, out, in_, R, W):
        nc = self.nc
        return self.em.op(self.em.dve, lambda: nc.vector.reciprocal(out=out, in_=in_), R, W)

    def memset(self, eng, ap, val, W):
        h = self._ve(eng)
        return self.em.op(eng, lambda: h.memset(ap, val), (), W)

    def copy(self, eng, out, in_, R, W):
        h = self._ve(eng)
        return self.em.op(eng, lambda: h.tensor_copy(out=out, in_=in_), R, W)

    def ld(self, out, in_, W, R=(), **kw):
        return self.em.dma(self.em.sp, out, in_, reads=R, writes=W, **kw)

    def st(self, out, in_, R, W=(), **kw):
        return self.em.dma(self.em.pool, out, in_, reads=R, writes=W, **kw)

    def ldcast(self, out, in_, W, R=()):
        return self.em.dma(self.em.pool, out, in_, reads=R, writes=W, max_dma_last_dim=8192)


def rope_tables(L, dim):
    rows = L // GRID_W
    rr, cc = np.meshgrid(np.arange(rows, dtype=np.float32), np.arange(GRID_W, dtype=np.float32), indexing="ij")
    quarter = dim // 4
    inv = (np.float32(THETA) ** (-np.arange(quarter, dtype=np.float32) / np.float32(quarter))).astype(np.float32)
    ang = np.concatenate([rr.reshape(-1)[:, None] * inv, cc.reshape(-1)[:, None] * inv], axis=-1).astype(np.float32)
    cos = np.concatenate([np.ones((CTX, dim // 2), np.float32), np.cos(ang)], axis=0)
    sin = np.concatenate([np.zeros((CTX, dim // 2), np.float32), np.sin(ang)], axis=0)
    return cos.astype(np.float32), sin.astype(np.float32)


def host_consts(L):
    NT = CTX + L
    c64, s64 = rope_tables(L, 64)
    c32, s32 = rope_tables(L, 32)
    cos64 = np.repeat(c64.T, 2, axis=0)
    sin64 = np.repeat(s64.T, 2, axis=0)
    cos128 = np.concatenate([cos64, cos64], 0)
    sin128 = np.concatenate([sin64, sin64], 0)
    cosm = np.concatenate([np.ones((64, NT), np.float32), np.repeat(c32.T, 2, axis=0)], 0)
    sinm = np.concatenate([np.zeros((64, NT), np.float32), np.repeat(s32.T, 2, axis=0)], 0)
    rope = np.zeros((4, 128, NT), np.float32)
    rope[0], rope[1] = cos128, sin128
    rope[2, :96], rope[3, :96] = cosm, sinm
    ident = np.eye(128, dtype=np.float32)
    rot = np.zeros((128, 128), np.float32)
    for i in range(64):
        rot[2 * i + 1, 2 * i] = -1.0
        rot[2 * i, 2 * i + 1] = 1.0
    rotm = np.zeros((128, 128), np.float32)
    rotm[64:96, 64:96] = rot[64:96, 64:96]
    shift = np.zeros((128, 128), np.float32)
    for k in range(32):
        shift[k, 64 + k] = 1.0
    bd64 = np.zeros((128, 128), np.float32)
    bd64[:64, :64] = 1.0
    bd64[64:, 64:] = 1.0
    ones = np.ones((128, 128), np.float32)
    tri = np.triu(np.ones((128, 128), np.float32))
    sq = np.concatenate([ident, rot, rotm, shift, bd64, ones, tri, tri.T.copy()], axis=1)
    return rope, sq


CI_IDENT, CI_ROT, CI_ROTM, CI_SHIFT, CI_BD64, CI_ONES, CI_TRI, CI_TRIT = range(8)
NCONST = 8

V_GQ, V_GK, V_QA, V_KVA, V_MQ, V_MK = 0, 1, 2, 5, 7, 8
NV = 16
R_CONVB, R_DTB, R_ALOG, R_RLOG, R_DSK, R_SSDN, R_RETN = 0, 1536, 1568, 1600, 1616, 2640, 3664
NR1 = 4688


def build(L=4096, nlayers=2, debug=False):
    K = KB(L)
    nc, em = K.nc, K.em
    NT, NCH = K.NT, K.NCH
    PS = K.PS
    DVE, POOL = em.dve, em.pool

    xin = K.dram("xin", [NT, D], F32, "ExternalInput")
    ccd = K.dram("cc", [128, 8, 2], F32, "ExternalInput")
    mod_w = K.dram("mod_w", [2, D, 6 * D], F32, "ExternalInput")
    modb_col = K.dram("modb_col", [128, 2, 48], F32, "ExternalInput")
    mod_b = K.dram("mod_b", [2, 6 * D], F32, "ExternalInput")
    ncol = K.dram("ncol", [128, 2, 2, 8], F32, "ExternalInput")
    vecs = K.dram("vecs", [128, NV], F32, "ExternalInput")
    ropeD = K.dram("rope", [4, 128, NT], F32, "ExternalInput")
    sqc = K.dram("sqc", [128, NCONST * 128], F32, "ExternalInput")
    w13D = K.dram("ffn_w13", [2, D, 2 * FFH], F32, "ExternalInput")
    w2D = K.dram("ffn_w2", [2, FFH, D], F32, "ExternalInput")
    awin = K.dram("attn_w_in", [D, 1440], F32, "ExternalInput")
    wqb = K.dram("mla_wq_b", [384, 768], F32, "ExternalInput")
    wkvb = K.dram("mla_wkv_b", [256, 1024], F32, "ExternalInput")
    awout = K.dram("attn_w_out", [D, D], F32, "ExternalInput")
    outD = K.dram("out", [L, D], F32, "ExternalOutput")

    QA = K.dram("QA", [4, 128, NT], BF16)
    KA = K.dram("KA", [2, 128, NT], BF16)
    VA = K.dram("VA", [128, NCH, 130], BF16)
    QM = K.dram("QM", [8, 96, NT], BF16)
    KM = K.dram("KM", [8, 96, NT], BF16)
    VM = K.dram("VM", [128, NCH, 520], BF16)
    AO = K.dram("AO", [D, NT], BF16)
    X1 = K.dram("X1", [NT, D], F32)
    U2 = K.dram("U2", [D, NT], BF16)
    X2 = K.dram("X2", [NT, D], F32, "ExternalOutput" if debug else "Internal")
    MODROW = K.dram("MODROW", [2, 2, 2, D], F32)

    CS = K.gsb("consts", [128, NCONST * 128], F32)
    K.ld(CS.t[:, :], sqc[:, :], [CS.d])

    def cst(i, r=128, c=128):
        return CS.t[0:r, i * 128:i * 128 + c]

    CB = K.gsb("constsb", [128, 2 * 128], BF16)
    K.copy(DVE, CB.t[:, 0:128], cst(CI_BD64), [CS.d], [CB.d])
    K.copy(DVE, CB.t[:, 128:256], cst(CI_ONES), [CS.d], [CB.d])
    BD64b = CB.t[:, 0:128]
    ONESb = CB.t[:, 128:256]
    VEC = K.gsb("vecs", [128, NV], F32)
    K.ld(VEC.t[:, :], vecs[:, :], [VEC.d])
    NCOL = K.gsb("ncol", [128, 2, 2, 8], F32)
    K.ld(NCOL.t[:, :, :, :], ncol[:, :, :, :], [NCOL.d])
    MODT = K.gsb("modT", [128, 2, 48, 2], F32)
    MUL = K.gsb("mulT", [128, 2, 2, 8, 2], F32)

    with K.phase():
        cc = K.sb("cc", [128, 8, 2], F32)
        K.ld(cc.t[:, :, :], ccd[:, :, :], [cc.d])
        K.act(cc.t[:, :, :], cc.t[:, :, :], AF.Silu, [cc.d], [cc.d])
        modb = K.sb("modb", [128, 2, 48], F32)
        K.ld(modb.t[:, :, :], modb_col[:, :, :], [modb.d])
        mbrow = K.sb("mbrow", [2, 2, 6 * D], F32)
        K.ld(mbrow.t[:, :, :], mod_b.partition_broadcast(2), [mbrow.d])
        wbuf = [K.sb("mw", [128, 8, 1024], F32) for _ in range(2)]
        rowb = [K.sb("rowb", [2, 1024], F32) for _ in range(2)]
        it = 0
        for i in range(nlayers):
            for v in range(6):
                wb = wbuf[it % 2]
                K.ld(wb.t[:, :, :], mod_w[i, :, v * 1024:(v + 1) * 1024].rearrange("(k p) n -> p k n", p=128), [wb.d])
                ps = PS[it % 2]
                for j in range(8):
                    for k in range(8):
                        K.mm(ps.t[:, j * 2:(j + 1) * 2], wb.t[:, k, j * 128:(j + 1) * 128], cc.t[:, k, :], k == 0, k == 7,
                             [wb.d, cc.d], [ps.d])
                K.tt(DVE, MODT.t[:, i, v * 8:(v + 1) * 8, :], ps.t[:, 0:16].rearrange("p (j s) -> p j s", s=2),
                     modb.t[:, i, v * 8:(v + 1) * 8].unsqueeze(2).broadcast_to([128, 8, 2]), ALU.add,
                     [ps.d, modb.d], [MODT.d])
                if v in (2, 5):
                    gi = 0 if v == 2 else 1
                    rb = rowb[gi]
                    for half in range(2):
                        ps2 = PS[2 + half]
                        for k in range(8):
                            K.mm(ps2.t[0:2, :], cc.t[:, k, :], wb.t[:, k, half * 512:(half + 1) * 512], k == 0, k == 7,
                                 [wb.d, cc.d], [ps2.d])
                        K.tt(DVE, rb.t[:, half * 512:(half + 1) * 512], ps2.t[0:2, :],
                             mbrow.t[:, i, v * 1024 + half * 512: v * 1024 + (half + 1) * 512], ALU.add,
                             [ps2.d, mbrow.d], [rb.d])
                    K.st(MODROW[i, gi, :, :], rb.t[:, :], [rb.d])
                it += 1
            for nrm in range(2):
                K.stt(MUL.t[:, i, nrm, :, :], MODT.t[:, i, (3 * nrm + 1) * 8:(3 * nrm + 2) * 8, :], 1.0,
                      NCOL.t[:, i, nrm, :].unsqueeze(2).broadcast_to([128, 8, 2]), ALU.add, ALU.mult,
                      [MODT.d, NCOL.d], [MUL.d])

    def mul_ap(layer, nrm, k, s):
        return MUL.t[:, layer, nrm, k, s:s + 1]

    def add_ap(layer, nrm, k, s):
        return MODT.t[:, layer, 3 * nrm * 8 + k, s:s + 1]

    def norm_to_uT(xt, nch, T, layer, nrm, s, uT, scr, psbanks):
        junk, ss, xs = scr
        for c in range(nch):
            K.act(junk.t[:, :], xt.t[:, c, :], AF.Square, [xt.d], [junk.d, ss.d], accum_out=ss.t[:, c:c + 1])
        K.ts(DVE, ss.t[:, 0:nch], ss.t[:, 0:nch], 1.0 / D, ALU.mult, [ss.d], [ss.d], s2=EPS, op1=ALU.add)
        K.act(ss.t[:, 0:nch], ss.t[:, 0:nch], AF.Sqrt, [ss.d], [ss.d])
        K.recip(ss.t[:, 0:nch], ss.t[:, 0:nch], [ss.d], [ss.d])
        for c in range(nch):
            K.ts(DVE if c % 2 == 0 else POOL, xs.t[:, c, :], xt.t[:, c, :], ss.t[:, c:c + 1], ALU.mult, [xt.d, ss.d], [xs.d])
        for k in range(8):
            ps = psbanks[k % len(psbanks)]
            for c in range(nch):
                K.tr(ps.t[:, c * 128:(c + 1) * 128], xs.t[:, c, k * 128:(k + 1) * 128], cst(CI_IDENT), [xs.d, CS.d], [ps.d])
            K.act(uT.t[:, k, 0:T], ps.t[:, 0:T], AF.Identity, [ps.d, MUL.d, MODT.d], [uT.d],
                  scale=mul_ap(layer, nrm, k, s), bias=add_ap(layer, nrm, k, s))

    def l0_phaseA():
        with K.phase():
            W = K.sb("win", [128, 8, 1440], BF16)
            for k in range(8):
                K.ldcast(W.t[:, k, :], awin[k * 128:(k + 1) * 128, :], [W.d])
            Wkd = K.sb("wkd", [128, 8, 2, 128], BF16)
            for g in range(2):
                for dup in range(2):
                    K.ldcast(Wkd.t[:, :, g, dup * 64:(dup + 1) * 64],
                             awin[:, 512 + g * 64:512 + (g + 1) * 64].rearrange("(k p) n -> p k n", p=128), [Wkd.d])
            Wq = K.sb("wqb", [128, 3, 768], BF16)
            K.ldcast(Wq.t[:, :, :], wqb.rearrange("(j p) n -> p j n", p=128), [Wq.d])
            Wkp = K.sb("wkvp", [128, 2, 8, 96], BF16)
            K.memset(DVE, Wkp.t[:, :, :, :], 0.0, [Wkp.d])
            wkv4 = wkvb.rearrange("(j p) (h t e) -> p j h t e", p=128, t=2, e=64)
            for j in range(2):
                K.ldcast(Wkp.t[:, j, :, 0:64], wkv4[:, j, :, 0, :], [Wkp.d])
            Wv = K.sb("wkvv", [128, 2, 8, 64], BF16)
            for j in range(2):
                K.ldcast(Wv.t[:, j, :, :], wkv4[:, j, :, 1, :], [Wv.d])

            xb = [K.sb("xa", [128, 4, D], F32) for _ in range(2)]
            scr = (K.sb("junk", [128, D], BF16), K.sb("ss", [128, 4], F32), K.sb("xs", [128, 4, D], F32))
            uTb = [K.sb("uT", [128, 8, 512], BF16) for _ in range(2)]
            ropeb = [K.sb("rope", [128, 4, 512], F32) for _ in range(2)]
            sqb = [K.sb("sq", [128, 512], BF16) for _ in range(2)]
            r1b = [K.sb("r1", [128, 512], F32) for _ in range(2)]
            qnb = [K.sb("qn", [128, 512], F32) for _ in range(2)]
            t1b = [K.sb("t1", [128, 512], F32) for _ in range(2)]
            t2b = [K.sb("t2", [128, 512], F32) for _ in range(2)]
            outb = [K.sb("ob", [128, 512], BF16) for _ in range(3)]
            qln = K.sb("qln", [128, 3, 512], BF16)
            kvn = K.sb("kvn", [128, 2, 512], BF16)
            krp = K.sb("krp", [32, 512], F32)
            vat = [K.sb("vat", [128, 4, 130], BF16) for _ in range(2)]
            vmt = [K.sb("vmt", [128, 4, 520], BF16) for _ in range(2)]
            for b in vat + vmt:
                K.memset(DVE, b.t[:, :, :], 1.0, [b.d])
            cnt = {"n": 0, "o": 0, "p": 0}
            PROJ = [PS[2], PS[3], PS[4]]

            def proj_bank():
                cnt["p"] += 1
                return PROJ[cnt["p"] % 3]

            pend = []
            SSB = [PS[5], PS[6]]
            ROTB = [PS[7], PS[0]]

            def flush():
                while pend:
                    pend.pop(0)()

            def post(ps_src, R_, T, gain, ss_lhsT, inv_n, rot_lhsT, ci, rp, dst):
                i = cnt["n"] % 2
                cnt["n"] += 1
                sq, r1, qn, t1, t2 = sqb[i], r1b[i], qnb[i], t1b[i], t2b[i]
                pss, psr = SSB[i], ROTB[i]
                K.act(sq.t[0:R_, 0:T], ps_src.t[0:R_, 0:T], AF.Square, [ps_src.d], [sq.d])
                K.mm(pss.t[0:R_, 0:T], ss_lhsT, sq.t[0:R_, 0:T], True, True, [sq.d, CB.d], [pss.d])

                def stage2():
                    K.act(r1.t[0:R_, 0:T], pss.t[0:R_, 0:T], AF.Ln, [pss.d], [r1.d], scale=inv_n, bias=EPSB.t[0:R_, 0:1])
                    K.act(r1.t[0:R_, 0:T], r1.t[0:R_, 0:T], AF.Exp, [r1.d], [r1.d], scale=-0.5)
                    K.stt(qn.t[0:R_, 0:T], ps_src.t[0:R_, 0:T], gain, r1.t[0:R_, 0:T], ALU.mult, ALU.mult,
                          [ps_src.d, r1.d, VEC.d], [qn.d])
                    K.mm(psr.t[0:R_, 0:T], rot_lhsT, qn.t[0:R_, 0:T], True, True, [qn.d, CS.d], [psr.d])
                    K.tt(DVE, t1.t[0:R_, 0:T], qn.t[0:R_, 0:T], rp.t[0:R_, ci, 0:T], ALU.mult, [qn.d, rp.d], [t1.d])
                    K.tt(DVE, t2.t[0:R_, 0:T], psr.t[0:R_, 0:T], rp.t[0:R_, ci + 1, 0:T], ALU.mult, [psr.d, rp.d], [t2.d])
                    ob = outb[cnt["o"] % 3]
                    cnt["o"] += 1
                    K.tt(POOL, ob.t[0:R_, 0:T], t1.t[0:R_, 0:T], t2.t[0:R_, 0:T], ALU.add, [t1.d, t2.d], [ob.d])
                    K.st(dst, ob.t[0:R_, 0:T], [ob.d])

                pend.append(stage2)
                if len(pend) > 1:
                    pend.pop(0)()

            def lora_norm(ps_list, nj, gcol0, inv_n, dstb, T):
                i = cnt["n"] % 2
                cnt["n"] += 1
                r1 = r1b[i]
                pss = PS[5]
                for j in range(nj):
                    sq = sqb[(cnt["n"] + j) % 2]
                    K.act(sq.t[:, 0:T], ps_list[j].t[:, 0:T], AF.Square, [ps_list[j].d], [sq.d])
                    K.mm(pss.t[:, 0:T], ONESb, sq.t[:, 0:T], j == 0, j == nj - 1, [sq.d, CB.d], [pss.d])
                K.act(r1.t[:, 0:T], pss.t[:, 0:T], AF.Ln, [pss.d], [r1.d], scale=inv_n, bias=EPSB.t[:, 0:1])
                K.act(r1.t[:, 0:T], r1.t[:, 0:T], AF.Exp, [r1.d], [r1.d], scale=-0.5)
                for j in range(nj):
                    K.stt(dstb.t[:, j, 0:T], ps_list[j].t[:, 0:T], VEC.t[:, gcol0 + j:gcol0 + j + 1], r1.t[:, 0:T],
                          ALU.mult, ALU.mult, [ps_list[j].d, r1.d, VEC.d], [dstb.d])

            for bi, (t0, T, s) in enumerate(K.blocks):
                nch = T // 128
                c0 = t0 // 128
                xt = xb[bi % 2]
                K.ld(xt.t[:, 0:nch, :], xin[t0:t0 + T, :].rearrange("(c p) d -> p c d", p=128), [xt.d])
                rp = ropeb[bi % 2]
                K.ld(rp.t[:, :, 0:T], ropeD[:, :, t0:t0 + T].rearrange("f p t -> p f t"), [rp.d])
                uT = uTb[bi % 2]
                norm_to_uT(xt, nch, T, 0, 0, s, uT, scr, [PS[0], PS[1]])
                for ch in range(4):
                    ps = proj_bank()
                    for k in range(8):
                        K.mm(ps.t[:, 0:T], W.t[:, k, ch * 128:(ch + 1) * 128], uT.t[:, k, 0:T], k == 0, k == 7, [W.d, uT.d], [ps.d])
                    post(ps, 128, T, VEC.t[:, V_GQ:V_GQ + 1], BD64b, 1.0 / 64, cst(CI_ROT), 0, rp, QA[ch, :, t0:t0 + T])
                for g in range(2):
                    ps = proj_bank()
                    for k in range(8):
                        K.mm(ps.t[:, 0:T], Wkd.t[:, k, g, :], uT.t[:, k, 0:T], k == 0, k == 7, [Wkd.d, uT.d], [ps.d])
                    post(ps, 128, T, VEC.t[:, V_GK:V_GK + 1], BD64b, 1.0 / 64, cst(CI_ROT), 0, rp, KA[g, :, t0:t0 + T])
                flush()
                va_t = vat[bi % 2]
                for c in range(nch):
                    ps = proj_bank()
                    for k in range(8):
                        K.mm(ps.t[:, 0:128], uT.t[:, k, c * 128:(c + 1) * 128], W.t[:, k, 640:768], k == 0, k == 7, [W.d, uT.d], [ps.d])
                    K.act(va_t.t[:, c, :].rearrange("p (g e) -> p g e", e=65)[:, :, 0:64],
                          ps.t[:, 0:128].rearrange("p (g e) -> p g e", e=64), AF.Copy, [ps.d], [va_t.d])
                K.st(VA[:, c0:c0 + nch, :], va_t.t[:, 0:nch, :], [va_t.d])
                pl = []
                for j in range(3):
                    ps = proj_bank()
                    for k in range(8):
                        K.mm(ps.t[:, 0:T], W.t[:, k, 768 + j * 128:768 + (j + 1) * 128], uT.t[:, k, 0:T], k == 0, k == 7, [W.d, uT.d], [ps.d])
                    pl.append(ps)
                lora_norm(pl, 3, V_QA, 1.0 / 384, qln, T)
                for h in range(8):
                    ps = proj_bank()
                    for j in range(3):
                        K.mm(ps.t[0:96, 0:T], Wq.t[:, j, h * 96:(h + 1) * 96], qln.t[:, j, 0:T], j == 0, j == 2, [Wq.d, qln.d], [ps.d])
                    post(ps, 96, T, VEC.t[0:96, V_MQ:V_MQ + 1], ONESb[0:96, 0:96], 1.0 / 96, cst(CI_ROTM, 96, 96), 2, rp, QM[h, :, t0:t0 + T])
                flush()
                pl = []
                for j in range(2):
                    ps = proj_bank()
                    for k in range(8):
                        K.mm(ps.t[:, 0:T], W.t[:, k, 1152 + j * 128:1152 + (j + 1) * 128], uT.t[:, k, 0:T], k == 0, k == 7, [W.d, uT.d], [ps.d])
                    pl.append(ps)
                lora_norm(pl, 2, V_KVA, 1.0 / 256, kvn, T)
                ps = proj_bank()
                for k in range(8):
                    K.mm(ps.t[0:32, 0:T], W.t[:, k, 1408:1440], uT.t[:, k, 0:T], k == 0, k == 7, [W.d, uT.d], [ps.d])
                K.act(krp.t[0:32, 0:T], ps.t[0:32, 0:T], AF.Copy, [ps.d], [krp.d])
                for h in range(8):
                    ps = proj_bank()
                    for j in range(2):
                        K.mm(ps.t[0:96, 0:T], Wkp.t[:, j, h, :], kvn.t[:, j, 0:T], j == 0, False, [Wkp.d, kvn.d], [ps.d])
                    K.mm(ps.t[0:96, 0:T], cst(CI_SHIFT, 32, 96), krp.t[0:32, 0:T], False, True, [krp.d, CS.d], [ps.d])
                    post(ps, 96, T, VEC.t[0:96, V_MK:V_MK + 1], ONESb[0:96, 0:96], 1.0 / 96, cst(CI_ROTM, 96, 96), 2, rp, KM[h, :, t0:t0 + T])
                flush()
                vm_t = vmt[bi % 2]
                for c in range(nch):
                    ps = proj_bank()
                    for j in range(2):
                        K.mm(ps.t[:, 0:512], kvn.t[:, j, c * 128:(c + 1) * 128], Wv.t[:, j, :, :].rearrange("p h e -> p (h e)"),
                             j == 0, j == 1, [Wv.d, kvn.d], [ps.d])
                    K.act(vm_t.t[:, c, :].rearrange("p (g e) -> p g e", e=65)[:, :, 0:64],
                          ps.t[:, 0:512].rearrange("p (g e) -> p g e", e=64), AF.Copy, [ps.d], [vm_t.d])
                K.st(VM[:, c0:c0 + nch, :], vm_t.t[:, 0:nch, :], [vm_t.d])

    def l0_phaseB():
        with K.phase():
            va = K.sb("va", [128, NCH, 130], BF16)
            vm = K.sb("vm", [128, NCH, 520], BF16)
            K.ld(va.t[:, :, :], VA[:, :, :], [va.d])
            K.ld(vm.t[:, :, :], VM[:, :, :], [vm.d])
            kb = [K.sb("kb", [128, NT], BF16) for _ in range(2)]
            kzT = [K.sb("kzT", [128, NT], BF16) for _ in range(2)]
            kzB = [K.sb("kzB", [128, NT], BF16) for _ in range(2)]
            for b in kzT:
                K.memset(DVE, b.t[64:128, :], 0.0, [b.d])
            for b in kzB:
                K.memset(DVE, b.t[0:64, :], 0.0, [b.d])
            qb = [K.sb("qb", [128, 512], BF16) for _ in range(3)]
            pb = [K.sb("pb", [128, 512], BF16) for _ in range(4)]
            osb = [K.sb("osb", [64, 512], F32) for _ in range(2)]
            rsb = [K.sb("rsb", [128, 512], F32) for _ in range(2)]
            aob = [K.sb("aob", [64, 512], BF16) for _ in range(2)]
            SB_ = [PS[0], PS[1], PS[2]]
            OB_ = [PS[3], PS[4]]
            BCB = PS[5]
            tasks = []
            groups = []
            ui = 0
            qi = 0
            units = [("a", c) for c in range(4)] + [("m", h) for h in range(8)]
            for kind, idx in units:
                if kind == "a":
                    kT, kB = kzT[ui % 2], kzB[ui % 2]
                    kloads = [(kT, KA[idx // 2, 0:64, :], slice(0, 64)), (kB, KA[idx // 2, 64:128, :], slice(64, 128))]
                else:
                    kbf = kb[ui % 2]
                    kloads = [(kbf, KM[idx, :, :], slice(0, 96))]
                ui += 1
                first_in_unit = True
                for (t0, T, s) in K.blocks:
                    qbf = qb[qi % 3]
                    qi += 1
                    if kind == "a":
                        qload = (qbf, QA[idx, :, t0:t0 + T], 128, T)
                        subs = [(kT, slice(0, 128), 2 * idx, va, idx // 2, 64 ** -0.5), (kB, slice(0, 128), 2 * idx + 1, va, idx // 2, 64 ** -0.5)]
                    else:
                        qload = (qbf, QM[idx, :, t0:t0 + T], 96, T)
                        subs = [(kbf, slice(0, 96), 8 + idx, vm, idx, 96 ** -0.5)]
                    kcs = list(range(2)) if s == 1 else list(range(NCH))
                    for si, (kbuf_, rows, hg, vbuf, vcol, sc) in enumerate(subs):
                        g = len(groups)
                        groups.append((hg, t0, T))
                        for n, kc in enumerate(kcs):
                            tasks.append(dict(kb=kbuf_, rows=rows, kc=kc, qb=qbf, T=T, vb=vbuf, vcol=vcol, first=(n == 0),
                                              last=(n == len(kcs) - 1), grp=g, sc=sc,
                                              kload=kloads if (first_in_unit and si == 0 and n == 0) else None,
                                              qload=qload if (si == 0 and n == 0) else None))
                    first_in_unit = False

            def emit_qk(i):
                t = tasks[i]
                if t["kload"] is not None:
                    for (kbf, src, rws) in t["kload"]:
                        K.ld(kbf.t[rws, :], src, [kbf.d])
                if t["qload"] is not None:
                    qbf, src, R_, T = t["qload"]
                    K.ld(qbf.t[0:R_, 0:T], src[0:R_, :], [qbf.d])
                ps = SB_[i % 3]
                kc, T = t["kc"], t["T"]
                K.mm(ps.t[:, 0:T], t["kb"].t[t["rows"], kc * 128:(kc + 1) * 128], t["qb"].t[t["rows"], 0:T], True, True,
                     [t["kb"].d, t["qb"].d], [ps.d])

            n = len(tasks)
            for i in range(min(2, n)):
                emit_qk(i)
            for i in range(n):
                t = tasks[i]
                T = t["T"]
                ps = SB_[i % 3]
                p = pb[i % 4]
                K.act(p.t[:, 0:T], ps.t[:, 0:T], AF.Exp, [ps.d], [p.d], scale=t["sc"])
                if i + 2 < n:
                    emit_qk(i + 2)
                po = OB_[t["grp"] % 2]
                vc = t["vcol"]
                K.mm(po.t[0:65, 0:T], t["vb"].t[:, t["kc"], vc * 65:(vc + 1) * 65], p.t[:, 0:T], t["first"], t["last"],
                     [t["vb"].d, p.d], [po.d])
                if t["last"]:
                    g = t["grp"]
                    hg, t0, _ = groups[g]
                    rs, osb_, ao_ = rsb[g % 2], osb[g % 2], aob[g % 2]
                    K.recip(rs.t[64:65, 0:T], po.t[64:65, 0:T], [po.d], [rs.d])
                    K.mm(BCB.t[0:64, 0:T], cst(CI_ONES)[64:65, 0:64], rs.t[64:65, 0:T], True, True, [rs.d, CS.d], [BCB.d])
                    K.act(osb_.t[0:64, 0:T], po.t[0:64, 0:T], AF.Copy, [po.d], [osb_.d])
                    K.tt(DVE, ao_.t[0:64, 0:T], osb_.t[0:64, 0:T], BCB.t[0:64, 0:T], ALU.mult, [osb_.d, BCB.d], [ao_.d])
                    K.st(AO[hg * 64:(hg + 1) * 64, t0:t0 + T], ao_.t[0:64, 0:T], [ao_.d])

    def phaseC1(layer, blocks, AOsrc, nK, woutD, Xsrc, xoff):
        with K.phase():
            wo = K.sb("wo", [128, nK, D], BF16)
            for k in range(nK):
                K.ldcast(wo.t[:, k, :], woutD[k * 128:(k + 1) * 128, :], [wo.d])
            gts = {}
            for s in set(b[2] for b in blocks):
                gts[s] = K.sb("gt", [128, D], F32)
                K.ld(gts[s].t[:, :], MODROW[layer, 0, s, :].partition_broadcast(128), [gts[s].d])
            aob = [K.sb("ao", [128, nK, 512], BF16) for _ in range(2)]
            xb = [K.sb("xc", [128, 4, D], F32) for _ in range(2)]
            tmpb = [K.sb("tmp", [128, 512], F32) for _ in range(2)]
            scr = (K.sb("junk", [128, D], BF16), K.sb("ss", [128, 4], F32), K.sb("xs", [128, 4, D], F32))
            uTb = [K.sb("uT", [128, 8, 512], BF16) for _ in range(2)]
            it = 0
            for bi, (t0, T, s) in enumerate(blocks):
                nch = T // 128
                ao = aob[bi % 2]
                K.ld(ao.t[:, :, 0:T], AOsrc[:, t0:t0 + T].rearrange("(k p) t -> p k t", p=128), [ao.d])
                xt = xb[bi % 2]
                K.ld(xt.t[:, 0:nch, :], Xsrc[t0 - xoff:t0 - xoff + T, :].rearrange("(c p) d -> p c d", p=128), [xt.d])
                for c in range(nch):
                    for n2 in range(2):
                        ps = PS[2 + it % 4]
                        tmp = tmpb[it % 2]
                        it += 1
                        for k in range(nK):
                            K.mm(ps.t[:, :], ao.t[:, k, c * 128:(c + 1) * 128], wo.t[:, k, n2 * 512:(n2 + 1) * 512], k == 0, k == nK - 1,
                                 [ao.d, wo.d], [ps.d])
                        K.tt(DVE, tmp.t[:, :], ps.t[:, :], gts[s].t[:, n2 * 512:(n2 + 1) * 512], ALU.mult, [ps.d, gts[s].d], [tmp.d])
                        K.tt(POOL, xt.t[:, c, n2 * 512:(n2 + 1) * 512], xt.t[:, c, n2 * 512:(n2 + 1) * 512], tmp.t[:, :], ALU.add,
                             [xt.d, tmp.d], [xt.d])
                K.st(X1[t0:t0 + T, :].rearrange("(c p) d -> p c d", p=128), xt.t[:, 0:nch, :], [xt.d])
                uT = uTb[bi % 2]
                norm_to_uT(xt, nch, T, layer, 1, s, uT, scr, [PS[0], PS[1]])
                K.st(U2[:, t0:t0 + T].rearrange("(k p) t -> p k t", p=128), uT.t[:, :, 0:T], [uT.d])

    def phaseC2(layer, blocks, Xdst, xoff, is_out):
        with K.phase():
            w13 = K.sb("w13", [128, 8, 2 * FFH], BF16)
            for k in range(8):
                for hh in range(2):
                    K.ldcast(w13.t[:, k, hh * FFH:(hh + 1) * FFH], w13D[layer, k * 128:(k + 1) * 128, hh * FFH:(hh + 1) * FFH], [w13.d])
            w2 = K.sb("w2", [128, 22, D], BF16)
            for j in range(22):
                K.ldcast(w2.t[:, j, :], w2D[layer, j * 128:(j + 1) * 128, :], [w2.d])
            gt1 = K.sb("gt", [128, D], F32)
            gts = {0: gt1, 1: gt1}
            ub = [K.sb("u", [128, 8, 512], BF16) for _ in range(1)]
            xb = [K.sb("xf", [128, 4, D], F32) for _ in range(1)]
            hb = K.sb("h", [128, 22, 512], BF16)
            sgb = [K.sb("sg", [128, 512], F32) for _ in range(2)]
            tmpb = [K.sb("tmp", [128, 512], F32) for _ in range(2)]
            it = 0
            for bi, (t0, T, s) in enumerate(blocks):
                nch = T // 128
                u = ub[0]
                K.ld(u.t[:, :, 0:T], U2[:, t0:t0 + T].rearrange("(k p) t -> p k t", p=128), [u.d])
                if bi == 0 or blocks[bi - 1][2] != s:
                    K.ld(gt1.t[:, :], MODROW[layer, 1, s, :].partition_broadcast(128), [gt1.d])
                xt = xb[0]
                K.ld(xt.t[:, 0:nch, :], X1[t0:t0 + T, :].rearrange("(c p) d -> p c d", p=128), [xt.d])
                for j in range(22):
                    psg = PS[(2 * j) % 4]
                    psu = PS[(2 * j + 1) % 4]
                    for k in range(8):
                        K.mm(psg.t[:, 0:T], w13.t[:, k, j * 128:(j + 1) * 128], u.t[:, k, 0:T], k == 0, k == 7, [w13.d, u.d], [psg.d])
                    for k in range(8):
                        K.mm(psu.t[:, 0:T], w13.t[:, k, FFH + j * 128:FFH + (j + 1) * 128], u.t[:, k, 0:T], k == 0, k == 7, [w13.d, u.d], [psu.d])
                    sg = sgb[j % 2]
                    K.act(sg.t[:, 0:T], psg.t[:, 0:T], AF.Silu, [psg.d], [sg.d])
                    K.tt(DVE, hb.t[:, j, 0:T], sg.t[:, 0:T], psu.t[:, 0:T], ALU.mult, [sg.d, psu.d], [hb.d])
                for c in range(nch):
                    for n2 in range(2):
                        ps = PS[4 + it % 4]
                        tmp = tmpb[it % 2]
                        it += 1
                        for j in range(22):
                            K.mm(ps.t[:, :], hb.t[:, j, c * 128:(c + 1) * 128], w2.t[:, j, n2 * 512:(n2 + 1) * 512], j == 0, j == 21,
                                 [hb.d, w2.d], [ps.d])
                        K.tt(DVE, tmp.t[:, :], ps.t[:, :], gts[s].t[:, n2 * 512:(n2 + 1) * 512], ALU.mult, [ps.d, gts[s].d], [tmp.d])
                        K.tt(POOL, xt.t[:, c, n2 * 512:(n2 + 1) * 512], xt.t[:, c, n2 * 512:(n2 + 1) * 512], tmp.t[:, :], ALU.add,
                             [xt.d, tmp.d], [xt.d])
                K.em.dma(em.pool, Xdst[t0 - xoff:t0 - xoff + T, :].rearrange("(c p) d -> p c d", p=128), xt.t[:, 0:nch, :],
                         reads=[xt.d], is_output=is_out)

    swin = K.dram("ssm_w_in", [D, 5664], F32, "ExternalInput")
    swout = K.dram("ssm_w_out", [2048, D], F32, "ExternalInput")
    convw_col = K.dram("convw_col", [128, 60], F32, "ExternalInput")
    convb_col = K.dram("convb_col", [128, 12], F32, "ExternalInput")
    rows1 = K.dram("rows1", [NR1], F32, "ExternalInput")
    sqc2 = K.dram("sqc2", [128, 2 * 128 + 4], F32, "ExternalInput")
    selc = K.dram("selc", [16, 2048], F32, "ExternalInput")
    SZ = K.dram("SZ", [NT, D], F32)
    SRG = K.dram("SRG", [NT, D], F32)
    DTs = K.dram("DTs", [NT, 32], F32)
    RVs = K.dram("RVs", [NT, D], BF16)
    XBC = K.dram("XBC", [1536, NT], BF16)
    RQ = K.dram("RQ", [4, 128, NT], BF16)
    RK = K.dram("RK", [4, 128, NT], BF16)
    RKT = K.dram("RKT", [NT, 512], BF16)
    XS = K.dram("XS", [NT, D], F32)
    BT = K.dram("BT", [NT, 256], BF16)
    BFs = K.dram("BFs", [256, NT], BF16)
    CFs = K.dram("CFs", [256, NT], BF16)
    YF = K.dram("YF", [NT, 2048], F32)
    YS = K.dram("YS", [2048, NT], BF16)
    ONEB = K.gsb("oneb", [128, 1], F32)
    K.memset(DVE, ONEB.t[:, :], 1.0, [ONEB.d])

    def softplus_small(x, tmp, R, W_):
        K.ts(DVE, tmp, x, -1.0, ALU.mult, R, W_)
        K.tt(DVE, tmp, tmp, x, ALU.max, R + W_, W_)
        K.act(tmp, tmp, AF.Exp, W_, W_, scale=-1.0)
        K.act(tmp, tmp, AF.Ln, W_, W_, bias=ONEB.t[0:x.shape[0], 0:1])
        K.stt(x, x, 0.0, tmp, ALU.max, ALU.add, R + W_, R)

    def l1_phaseA():
        with K.phase():
            W1 = K.sb("w1", [128, 8, 5664], BF16)
            for k in range(8):
                for (a, b) in ((0, 1888), (1888, 3776), (3776, 5664)):
                    K.ldcast(W1.t[:, k, a:b], swin[k * 128:(k + 1) * 128, a:b], [W1.d])
            xb = [K.sb("xa", [128, 4, D], F32)]
            scr = (K.sb("junk", [128, D], BF16), K.sb("ss", [128, 4], F32), K.sb("xs", [128, 4, D], F32))
            uTb = [K.sb("uT", [128, 8, 512], BF16) for _ in range(2)]
            ropeb = [K.sb("rope", [128, 2, 512], F32) for _ in range(2)]
            qnb = [K.sb("qn", [128, 512], F32) for _ in range(2)]
            t1b = [K.sb("t1", [128, 512], F32) for _ in range(2)]
            t2b = [K.sb("t2", [128, 512], F32) for _ in range(2)]
            ofb = [K.sb("of", [128, 512], F32) for _ in range(2)]
            obb = [K.sb("ob", [128, 512], BF16) for _ in range(3)]
            tmz = [K.sb("tmz", [128, D], F32) for _ in range(3)]
            rvt = [K.sb("rvt", [128, D], BF16) for _ in range(2)]
            rktb = [K.sb("rkt", [128, 512], BF16) for _ in range(2)]
            dtt = [K.sb("dtt", [128, 32], F32) for _ in range(2)]
            cnt = {"p": 0, "n": 0, "o": 0, "z": 0, "t": 0}

            def pbank(banks):
                cnt["p"] += 1
                return banks[cnt["p"] % len(banks)]

            for bi, (t0, T, s) in enumerate(K.blocks):
                nch = T // 128
                xt = xb[0]
                K.ld(xt.t[:, 0:nch, :], X2[t0:t0 + T, :].rearrange("(c p) d -> p c d", p=128), [xt.d])
                rp = ropeb[bi % 2]
                K.ld(rp.t[:, :, 0:T], ropeD[0:2, :, t0:t0 + T].rearrange("f p t -> p f t"), [rp.d])
                uT = uTb[bi % 2]
                norm_to_uT(xt, nch, T, 1, 0, s, uT, scr, [PS[0], PS[1]])
                for ch in range(12):
                    ps = pbank([PS[2], PS[3]])
                    for k in range(8):
                        K.mm(ps.t[:, 0:T], W1.t[:, k, 1024 + ch * 128:1024 + (ch + 1) * 128], uT.t[:, k, 0:T], k == 0, k == 7, [W1.d, uT.d], [ps.d])
                    ob = obb[cnt["o"] % 3]
                    cnt["o"] += 1
                    K.act(ob.t[:, 0:T], ps.t[:, 0:T], AF.Copy, [ps.d], [ob.d])
                    K.st(XBC[ch * 128:(ch + 1) * 128, t0:t0 + T], ob.t[:, 0:T], [ob.d])
                for kind in range(2):
                    for ch in range(4):
                        col0 = 2592 + kind * 512 + ch * 128
                        ps = pbank([PS[2], PS[3]])
                        for k in range(8):
                            K.mm(ps.t[:, 0:T], W1.t[:, k, col0:col0 + 128], uT.t[:, k, 0:T], k == 0, k == 7, [W1.d, uT.d], [ps.d])
                        i = cnt["n"] % 2
                        cnt["n"] += 1
                        qn, t1, t2, of = qnb[i], t1b[i], t2b[i], ofb[i]
                        K.act(qn.t[:, 0:T], ps.t[:, 0:T], AF.Copy, [ps.d], [qn.d], scale=(1.0 if kind == 0 else 0.125))
                        psr = PS[0]
                        K.mm(psr.t[:, 0:T], cst(CI_ROT), qn.t[:, 0:T], True, True, [qn.d, CS.d], [psr.d])
                        K.tt(DVE, t1.t[:, 0:T], qn.t[:, 0:T], rp.t[:, 0, 0:T], ALU.mult, [qn.d, rp.d], [t1.d])
                        K.tt(DVE, t2.t[:, 0:T], psr.t[:, 0:T], rp.t[:, 1, 0:T], ALU.mult, [psr.d, rp.d], [t2.d])
                        K.tt(POOL, of.t[:, 0:T], t1.t[:, 0:T], t2.t[:, 0:T], ALU.add, [t1.d, t2.d], [of.d])
                        ob = obb[cnt["o"] % 3]
                        cnt["o"] += 1
                        K.act(ob.t[:, 0:T], of.t[:, 0:T], AF.Copy, [of.d], [ob.d])
                        K.st((RQ if kind == 0 else RK)[ch, :, t0:t0 + T], ob.t[:, 0:T], [ob.d])
                        if kind == 1:
                            for c in range(nch):
                                K.tr(PS[4 + c].t[:, ch * 128:(ch + 1) * 128], of.t[:, c * 128:(c + 1) * 128], cst(CI_IDENT), [of.d, CS.d], [PS[4 + c].d])
                for c in range(nch):
                    rkt = rktb[c % 2]
                    K.copy(DVE, rkt.t[:, :], PS[4 + c].t[:, :], [PS[4 + c].d], [rkt.d])
                    K.st(RKT[t0 + c * 128:t0 + (c + 1) * 128, :], rkt.t[:, :], [rkt.d])
                TMB = [PS[2], PS[3], PS[4], PS[5], PS[6], PS[7]]
                for c in range(nch):
                    r0 = t0 + c * 128
                    for (col0, kind) in ((0, "z"), (4640, "g"), (3616, "v")):
                        if kind == "v":
                            dst = rvt[cnt["t"] % 2]
                            cnt["t"] += 1
                        else:
                            dst = tmz[cnt["z"] % 3]
                            cnt["z"] += 1
                        for half in range(2):
                            ps = pbank(TMB)
                            for k in range(8):
                                K.mm(ps.t[:, :], uT.t[:, k, c * 128:(c + 1) * 128], W1.t[:, k, col0 + half * 512:col0 + (half + 1) * 512],
                                     k == 0, k == 7, [W1.d, uT.d], [ps.d])
                            if kind == "v":
                                K.copy(DVE, dst.t[:, half * 512:(half + 1) * 512], ps.t[:, :], [ps.d], [dst.d])
                            else:
                                K.act(dst.t[:, half * 512:(half + 1) * 512], ps.t[:, :], AF.Silu, [ps.d], [dst.d])
                        K.st({"z": SZ, "g": SRG, "v": RVs}[kind][r0:r0 + 128, :], dst.t[:, :], [dst.d])
                    ps = pbank(TMB)
                    for k in range(8):
                        K.mm(ps.t[:, 0:32], uT.t[:, k, c * 128:(c + 1) * 128], W1.t[:, k, 2560:2592], k == 0, k == 7, [W1.d, uT.d], [ps.d])
                    dd = dtt[c % 2]
                    K.copy(DVE, dd.t[:, :], ps.t[:, 0:32], [ps.d], [dd.d])
                    K.st(DTs[r0:r0 + 128, :], dd.t[:, :], [dd.d])

    def l1_phaseV():
        with K.phase():
            cwc = K.sb("cwc", [128, 60], F32)
            K.ld(cwc.t[:, :], convw_col[:, :], [cwc.d])
            cbc = K.sb("cbc", [128, 12], F32)
            K.ld(cbc.t[:, :], convb_col[:, :], [cbc.d])
            cbr = K.sb("cbr", [1, 1536], F32)
            K.ld(cbr.t[:, :], rows1[R_CONVB:R_CONVB + 1536].partition_broadcast(1), [cbr.d])
            DG = K.sb("dg", [128, 60, 128], BF16)
            for idx in range(60):
                K.ts(DVE if idx % 2 == 0 else POOL, DG.t[:, idx, :], cst(CI_IDENT), cwc.t[:, idx:idx + 1], ALU.mult, [CS.d, cwc.d], [DG.d])
            xwb = [K.sb("xw", [128, 12, 516], BF16) for _ in range(2)]
            obb = [K.sb("ob", [128, 512], BF16) for _ in range(2)]
            xst = [K.sb("xst", [128, D], F32) for _ in range(2)]
            btt = [K.sb("btt", [128, 256], BF16) for _ in range(2)]
            ones_row = cst(CI_ONES)[0:1, 0:128]
            cnt = {"p": 0}
            BK = [PS[0], PS[1], PS[2], PS[3], PS[4], PS[5], PS[6], PS[7]]

            def pbank():
                cnt["p"] += 1
                return BK[cnt["p"] % 8]

            for bi, (t0, T, s) in enumerate(K.blocks):
                nch = T // 128
                seg0, seg1 = (0, CTX) if s == 1 else (CTX, NT)
                lo, hi = max(t0 - 2, seg0), min(t0 + T + 2, seg1)
                xw = xwb[bi % 2]
                K.memset(POOL, xw.t[:, :, :], 0.0, [xw.d])
                K.ld(xw.t[:, :, lo - (t0 - 2):hi - (t0 - 2)], XBC[:, lo:hi].rearrange("(c p) t -> p c t", p=128), [xw.d])
                for ch in range(8, 12):
                    ps = pbank()
                    for k in range(5):
                        K.mm(ps.t[:, 0:T], DG.t[:, k * 12 + ch, :], xw.t[:, ch, k:k + T], k == 0, k == 4, [DG.d, xw.d], [ps.d])
                    ob = obb[ch % 2]
                    K.act(ob.t[:, 0:T], ps.t[:, 0:T], AF.Silu, [ps.d, cbc.d], [ob.d], bias=cbc.t[:, ch:ch + 1])
                    dstD = BFs if ch < 10 else CFs
                    r = (ch - 8) % 2
                    K.st(dstD[r * 128:(r + 1) * 128, t0:t0 + T], ob.t[:, 0:T], [ob.d])
                for c in range(nch):
                    r0 = t0 + c * 128
                    banks = [pbank(), pbank(), pbank()]
                    for ch in range(10):
                        tgt = banks[ch // 4]
                        o_ap = tgt.t[:, (ch % 4) * 128:(ch % 4 + 1) * 128]
                        for k in range(5):
                            K.mm(o_ap, xw.t[:, ch, c * 128 + k:c * 128 + k + 128], DG.t[:, k * 12 + ch, :], k == 0, False, [DG.d, xw.d], [tgt.d])
                        K.mm(o_ap, ones_row, cbr.t[0:1, ch * 128:(ch + 1) * 128], False, True, [CS.d, cbr.d], [tgt.d])
                    xo = xst[c % 2]
                    for hh in range(2):
                        K.act(xo.t[:, hh * 512:(hh + 1) * 512], banks[hh].t[:, :], AF.Silu, [banks[hh].d], [xo.d])
                    K.st(XS[r0:r0 + 128, :], xo.t[:, :], [xo.d])
                    bo = btt[c % 2]
                    K.act(bo.t[:, :], banks[2].t[:, 0:256], AF.Silu, [banks[2].d], [bo.d])
                    K.st(BT[r0:r0 + 128, :], bo.t[:, :], [bo.d])

    def l1_scan(dirn):
        fin = dirn == 1
        with K.phase():
            C2 = K.sb("c2", [128, 2 * 128 + 4], F32)
            K.ld(C2.t[:, :], sqc2[:, :], [C2.d])
            SEL = K.sb("sel", [16, 2048], F32)
            K.ld(SEL.t[:, :], selc[:, :], [SEL.d])
            IDX = C2.t[:, dirn * 128:(dirn + 1) * 128]
            colA = C2.t[:, 256 + dirn:256 + dirn + 1]
            colE = C2.t[:, 258 + dirn:258 + dirn + 1]
            MASK = cst(CI_TRI) if dirn == 0 else cst(CI_TRIT)

            def brow(name, off, n):
                b = K.sb(name, [128, n], F32)
                K.ld(b.t[:, :], rows1[off:off + n].partition_broadcast(128), [b.d])
                return b

            DTB = brow("dtb", R_DTB + dirn * 16, 16)
            ANEG = brow("aneg", R_ALOG + dirn * 16, 16)
            K.act(ANEG.t[:, :], ANEG.t[:, :], AF.Exp, [ANEG.d], [ANEG.d])
            K.ts(DVE, ANEG.t[:, :], ANEG.t[:, :], -1.0, ALU.mult, [ANEG.d], [ANEG.d])
            LG = brow("lg", R_RLOG + dirn * 8, 8)
            lgt = K.sb("lgt", [128, 8], F32)
            K.ts(DVE, LG.t[:, :], LG.t[:, :], -1.0, ALU.mult, [LG.d], [LG.d])
            softplus_small(LG.t[:, :], lgt.t[:, :], [LG.d], [lgt.d])
            K.ts(DVE, LG.t[:, :], LG.t[:, :], -1.0, ALU.mult, [LG.d], [LG.d])
            LM = K.sb("lm", [128, 8, 128], F32)
            for h in range(8):
                K.ts(DVE, LM.t[:, h, :], IDX, LG.t[:, h:h + 1], ALU.mult, [C2.d, LG.d], [LM.d])
            K.act(LM.t[:, :, :], LM.t[:, :, :], AF.Exp, [LM.d], [LM.d])
            K.tt(DVE, LM.t[:, :, :], LM.t[:, :, :], MASK.unsqueeze(1).broadcast_to([128, 8, 128]), ALU.mult, [LM.d, CS.d], [LM.d])
            EAr = K.sb("ear", [128, 8], F32)
            K.ts(DVE, EAr.t[:, :], LG.t[:, :], colA, ALU.mult, [LG.d, C2.d], [EAr.d])
            K.act(EAr.t[:, :], EAr.t[:, :], AF.Exp, [EAr.d], [EAr.d])
            DEr = K.sb("der", [128, 8], F32)
            K.ts(DVE, DEr.t[:, :], LG.t[:, :], colE, ALU.mult, [LG.d, C2.d], [DEr.d])
            K.act(DEr.t[:, :], DEr.t[:, :], AF.Exp, [DEr.d], [DEr.d])
            CDR = K.sb("cdr", [128, 4], F32)
            lg2 = LG.t[:, :].rearrange("p (q two) -> p q two", two=2)
            K.ts(DVE, CDR.t[0:64, :], lg2[0:64, :, 0], 128.0, ALU.mult, [LG.d], [CDR.d])
            K.ts(DVE, CDR.t[64:128, :], lg2[64:128, :, 1], 128.0, ALU.mult, [LG.d], [CDR.d])
            K.act(CDR.t[:, :], CDR.t[:, :], AF.Exp, [CDR.d], [CDR.d])
            if fin:
                DSK = brow("dsk", R_DSK, 1024)
                WN = brow("wn", R_SSDN, 1024)
                RN = brow("rn", R_RETN, 1024)
            S = [K.sb("S", [128, 512], F32) for _ in range(2)]
            Sbf = [K.sb("Sbf", [128, 512], BF16) for _ in range(2)]
            SR = K.sb("SR", [128, 4, 128], F32)
            SRbf = K.sb("SRbf", [128, 4, 128], BF16)
            for b in S + Sbf:
                K.memset(DVE, b.t[:, :], 0.0, [b.d])
            K.memset(DVE, SR.t[:, :, :], 0.0, [SR.d])
            K.memset(DVE, SRbf.t[:, :, :], 0.0, [SRbf.d])
            NB = 3
            NF = 2
            xsb = [K.sb("xs", [128, D], F32) for _ in range(NB)]
            btb = [K.sb("bt", [128, 256], BF16) for _ in range(NB)]
            bfb = [K.sb("bf", [128, 2, 128], BF16) for _ in range(NB)]
            cfb = [K.sb("cf", [128, 2, 128], BF16) for _ in range(NB)]
            dtb_ = [K.sb("dt", [128, 16], F32) for _ in range(NB)]
            rqb = [K.sb("rq", [128, 4, 128], BF16) for _ in range(NB)]
            rkb = [K.sb("rk", [128, 4, 128], BF16) for _ in range(NB)]
            rktb = [K.sb("rkt", [128, 512], BF16) for _ in range(NB)]
            rvb = [K.sb("rv", [128, D], BF16) for _ in range(NB)]
            if fin:
                yfb = [K.sb("yf", [128, 2048], F32) for _ in range(NF)]
                szb = [K.sb("sz", [128, D], F32) for _ in range(NF)]
                sgb = [K.sb("srg", [128, D], F32) for _ in range(NF)]
            smb = [K.sb("sm", [128, 8, 16], F32) for _ in range(2)]
            at = K.sb("at", [16, 256], F32)
            E = K.sb("E", [128, 16, 128], F32)
            Gm = K.sb("Gm", [128, 2, 128], F32)
            Mb = [K.sb("M", [128, 16, 128], BF16) for _ in range(2)]
            MRb = [K.sb("MR", [128, 8, 128], BF16) for _ in range(2)]
            Vdb = [K.sb("Vd", [128, D], BF16) for _ in range(2)]
            Vdecb = [K.sb("Vdec", [128, D], BF16) for _ in range(2)]
            RVdb = [K.sb("RVd", [128, D], BF16) for _ in range(2)]
            yo = K.sb("yo", [128, D], F32)
            ydb = [K.sb("yd", [128, 2048], F32) for _ in range(2)]
            if fin:
                junk = K.sb("junk", [128, D], F32)
                st8 = K.sb("st8", [128, 8, 8], F32)
                ynb = K.sb("yn", [128, 2048], F32)
                ysb = [K.sb("ys", [128, 4, 128], BF16) for _ in range(2)]
            cnt = {"p": 0}

            def pbank():
                cnt["p"] += 1
                return PS[cnt["p"] % 8]

            order = list(range(NCH)) if dirn == 0 else [1, 0] + list(range(NCH - 1, 1, -1))
            NO = len(order)

            def loads(ci):
                c = order[ci]
                t0 = c * 128
                i = ci % NB
                K.ld(xsb[i].t[:, :], XS[t0:t0 + 128, :], [xsb[i].d])
                K.ld(btb[i].t[:, :], BT[t0:t0 + 128, :], [btb[i].d])
                K.ld(bfb[i].t[:, :, :], BFs[:, t0:t0 + 128].rearrange("(g n) t -> n g t", n=128), [bfb[i].d])
                K.ld(cfb[i].t[:, :, :], CFs[:, t0:t0 + 128].rearrange("(g n) t -> n g t", n=128), [cfb[i].d])
                K.ld(dtb_[i].t[:, :], DTs[t0:t0 + 128, dirn * 16:(dirn + 1) * 16], [dtb_[i].d])
                K.ld(rqb[i].t[:, :, :], RQ[:, :, t0:t0 + 128].rearrange("c p t -> p c t"), [rqb[i].d])
                K.ld(rkb[i].t[:, :, :], RK[:, :, t0:t0 + 128].rearrange("c p t -> p c t"), [rkb[i].d])
                K.ld(rktb[i].t[:, :], RKT[t0:t0 + 128, :], [rktb[i].d])
                K.ld(rvb[i].t[:, :], RVs[t0:t0 + 128, :], [rvb[i].d])

            def loads_fin(ci):
                c = order[ci]
                t0 = c * 128
                i = ci % NF
                K.ld(yfb[i].t[:, :], YF[t0:t0 + 128, :], [yfb[i].d])
                K.ld(szb[i].t[:, :], SZ[t0:t0 + 128, :], [szb[i].d])
                K.ld(sgb[i].t[:, :], SRG[t0:t0 + 128, :], [sgb[i].d])

            def bc3(ap2, n):
                return ap2.unsqueeze(2).broadcast_to([128, ap2.shape[1], n])

            def partA(ci):
                if ci + 1 < NO:
                    loads(ci + 1)
                i = ci % NB
                j2 = ci % 2
                xs_c, bf_c, cf_c, dt_c = xsb[i], bfb[i], cfb[i], dtb_[i]
                rq_c, rk_c, rv_c = rqb[i], rkb[i], rvb[i]
                sm = smb[j2]
                M, MR, Vd, Vdec, RVd = Mb[j2], MRb[j2], Vdb[j2], Vdecb[j2], RVdb[j2]
                sp, tmpv, la, Acol, Atot, expA, dece, cd = [sm.t[:, j, :] for j in range(8)]
                smd = [sm.d]
                K.tt(DVE, sp, dt_c.t[:, :], DTB.t[:, :], ALU.add, [dt_c.d, DTB.d], smd)
                softplus_small(sp, tmpv, smd, smd)
                K.tt(DVE, la, sp, ANEG.t[:, :], ALU.mult, smd + [ANEG.d], smd)
                psc = pbank()
                K.mm(psc.t[:, 0:16], MASK, la, True, True, smd + [CS.d], [psc.d])
                K.mm(psc.t[:, 16:32], cst(CI_ONES), la, True, True, smd + [CS.d], [psc.d])
                K.mm(psc.t[0:16, 128:256], la, MASK, True, True, smd + [CS.d], [psc.d])
                K.copy(DVE, sm.t[:, 3:5, :], psc.t[:, 0:32].rearrange("p (a b) -> p a b", b=16), [psc.d], smd)
                K.copy(DVE, at.t[:, 0:128], psc.t[0:16, 128:256], [psc.d], [at.d])
                K.ts(DVE, at.t[:, 128:256], at.t[:, 0:128], -1.0, ALU.mult, [at.d], [at.d])
                K.act(expA, Acol, AF.Exp, smd, smd)
                K.tt(DVE, dece, Atot, Acol, ALU.subtract, smd, smd)
                K.act(dece, dece, AF.Exp, smd, smd)
                K.act(cd, Atot, AF.Exp, smd, smd)
                K.tt(DVE, tmpv, sp, dece, ALU.mult, smd, smd)
                xs3 = xs_c.t[:, :].rearrange("p (h e) -> p h e", e=64)
                K.tt(DVE, Vd.t[:, :].rearrange("p (h e) -> p h e", e=64), xs3, bc3(sp, 64), ALU.mult, [xs_c.d] + smd, [Vd.d])
                K.tt(POOL, Vdec.t[:, :].rearrange("p (h e) -> p h e", e=64), xs3, bc3(tmpv, 64), ALU.mult, [xs_c.d] + smd, [Vdec.d])
                for q in range(4):
                    psd = pbank()
                    for hh in range(4):
                        h = q * 4 + hh
                        o_ap = psd.t[:, hh * 128:(hh + 1) * 128]
                        K.mm(o_ap, SEL.t[0:16, h * 128:(h + 1) * 128], at.t[0:16, 0:128], True, False, [SEL.d, at.d], [psd.d])
                        K.mm(o_ap, at.t[0:16, 128:256], SEL.t[0:16, h * 128:(h + 1) * 128], False, True, [SEL.d, at.d], [psd.d])
                    K.tt(DVE, E.t[:, q * 4:(q + 1) * 4, :], psd.t[:, :].rearrange("p (h i) -> p h i", i=128),
                         MASK.unsqueeze(1).broadcast_to([128, 4, 128]), ALU.mult, [psd.d, CS.d], [E.d])
                K.act(E.t[:, :, :], E.t[:, :, :], AF.Exp, [E.d], [E.d])
                psg = pbank()
                for g in range(2):
                    K.mm(psg.t[:, g * 128:(g + 1) * 128], bf_c.t[:, g, :], cf_c.t[:, g, :], True, True, [bf_c.d, cf_c.d], [psg.d])
                K.tt(DVE, Gm.t[:, :, :], psg.t[:, 0:256].rearrange("p (g i) -> p g i", i=128),
                     MASK.unsqueeze(1).broadcast_to([128, 2, 128]), ALU.mult, [psg.d, CS.d], [Gm.d])
                for g in range(2):
                    K.tt(DVE if g == 0 else POOL, M.t[:, g * 8:(g + 1) * 8, :], E.t[:, g * 8:(g + 1) * 8, :],
                         Gm.t[:, g, :].unsqueeze(1).broadcast_to([128, 8, 128]), ALU.mult, [E.d, Gm.d], [M.d])
                psgr = [pbank(), pbank()]
                for h in range(8):
                    rows = slice((h % 2) * 64, (h % 2) * 64 + 64)
                    K.mm(psgr[h % 2].t[:, (h // 2) * 128:(h // 2 + 1) * 128], rk_c.t[rows, h // 2, :], rq_c.t[rows, h // 2, :], True, True,
                         [rk_c.d, rq_c.d], [psgr[h % 2].d])
                for par in range(2):
                    K.tt(DVE, MR.t[:, :, :].rearrange("p (b two) i -> p b two i", two=2)[:, :, par, :],
                         psgr[par].t[:, :].rearrange("p (h i) -> p h i", i=128),
                         LM.t[:, :, :].rearrange("p (b two) i -> p b two i", two=2)[:, :, par, :],
                         ALU.mult, [psgr[par].d, LM.d], [MR.d])
                K.tt(DVE, RVd.t[:, :].rearrange("p (h e) -> p h e", e=128), rv_c.t[:, :].rearrange("p (h e) -> p h e", e=128),
                     bc3(DEr.t[:, :], 128), ALU.mult, [rv_c.d, DEr.d], [RVd.d])

            def partB(ci):
                if fin and ci + 1 < NO:
                    loads_fin(ci + 1)
                c = order[ci]
                t0 = c * 128
                i = ci % NB
                j2 = ci % 2
                xs_c, bt_c, cf_c = xsb[i], btb[i], cfb[i]
                rq_c, rkt_c, rv_c = rqb[i], rktb[i], rvb[i]
                sm = smb[j2]
                M, MR, Vd, Vdec, RVd = Mb[j2], MRb[j2], Vdb[j2], Vdecb[j2], RVdb[j2]
                sp, tmpv, la, Acol, Atot, expA, dece, cd = [sm.t[:, j, :] for j in range(8)]
                smd = [sm.d]
                yd = ydb[ci % 2]
                for g in range(2):
                    pso = pbank()
                    K.mm(pso.t[:, :], cf_c.t[:, g, :], Sbf[g].t[:, :], True, True, [cf_c.d, Sbf[g].d], [pso.d])
                    K.tt(DVE, yo.t[:, g * 512:(g + 1) * 512].rearrange("p (h e) -> p h e", e=64),
                         pso.t[:, :].rearrange("p (h e) -> p h e", e=64), bc3(expA[:, g * 8:(g + 1) * 8], 64), ALU.mult,
                         [pso.d] + smd, [yo.d])
                for g in range(2):
                    psy = pbank()
                    for hl in range(8):
                        h = g * 8 + hl
                        K.mm(psy.t[:, hl * 64:(hl + 1) * 64], M.t[:, h, :], Vd.t[:, h * 64:(h + 1) * 64], True, True, [M.d, Vd.d], [psy.d])
                    K.tt(DVE, yd.t[:, g * 512:(g + 1) * 512], psy.t[:, :], yo.t[:, g * 512:(g + 1) * 512], ALU.add, [psy.d, yo.d], [yd.d])
                for g in range(2):
                    psd2 = pbank()
                    K.mm(psd2.t[:, :], bt_c.t[:, g * 128:(g + 1) * 128], Vdec.t[:, g * 512:(g + 1) * 512], True, True, [bt_c.d, Vdec.d], [psd2.d])
                    s3 = S[g].t[:, :].rearrange("p (h e) -> p h e", e=64)
                    K.tt(POOL, s3, s3, bc3(cd[:, g * 8:(g + 1) * 8], 64), ALU.mult, [S[g].d] + smd, [S[g].d])
                    K.tt(DVE, S[g].t[:, :], S[g].t[:, :], psd2.t[:, :], ALU.add, [S[g].d, psd2.d], [S[g].d])
                    K.act(Sbf[g].t[:, :], S[g].t[:, :], AF.Copy, [S[g].d], [Sbf[g].d])
                psro = [pbank(), pbank()]
                for h in range(8):
                    rows = slice((h % 2) * 64, (h % 2) * 64 + 64)
                    K.mm(psro[h % 2].t[:, (h // 2) * 128:(h // 2 + 1) * 128], rq_c.t[rows, h // 2, :], SRbf.t[rows, h // 2, :], True, True,
                         [rq_c.d, SRbf.d], [psro[h % 2].d])
                for par in range(2):
                    K.tt(DVE, yo.t[:, :].rearrange("p (b two e) -> p b two e", two=2, e=128)[:, :, par, :],
                         psro[par].t[:, :].rearrange("p (h e) -> p h e", e=128),
                         bc3(EAr.t[:, :].rearrange("p (b two) -> p b two", two=2)[:, :, par], 128), ALU.mult,
                         [psro[par].d, EAr.d], [yo.d])
                psyr = [pbank(), pbank()]
                for h in range(8):
                    K.mm(psyr[h // 4].t[:, (h % 4) * 128:(h % 4 + 1) * 128], MR.t[:, h, :], rv_c.t[:, h * 128:(h + 1) * 128], True, True,
                         [MR.d, rv_c.d], [psyr[h // 4].d])
                for q in range(2):
                    K.tt(DVE, yd.t[:, 1024 + q * 512:1024 + (q + 1) * 512], psyr[q].t[:, :], yo.t[:, q * 512:(q + 1) * 512], ALU.add,
                         [psyr[q].d, yo.d], [yd.d])
                psdr = [pbank(), pbank()]
                for h in range(8):
                    pr = h // 2
                    K.mm(psdr[h // 4].t[:, (h % 4) * 128:(h % 4 + 1) * 128], rkt_c.t[:, pr * 128:(pr + 1) * 128], RVd.t[:, h * 128:(h + 1) * 128],
                         True, True, [rkt_c.d, RVd.d], [psdr[h // 4].d])
                K.tt(POOL, SR.t[:, :, :], SR.t[:, :, :], bc3(CDR.t[:, :], 128), ALU.mult, [SR.d, CDR.d], [SR.d])
                for half in range(2):
                    rows = slice(half * 64, half * 64 + 64)
                    for q in range(2):
                        src = psdr[q].t[:, :].rearrange("p (pp two e) -> p pp two e", two=2, e=128)[rows, :, half, :]
                        K.tt(DVE, SR.t[rows, 2 * q:2 * q + 2, :], SR.t[rows, 2 * q:2 * q + 2, :], src, ALU.add, [SR.d, psdr[q].d], [SR.d])
                K.act(SRbf.t[:, :, :], SR.t[:, :, :], AF.Copy, [SR.d], [SRbf.d])
                if not fin:
                    K.st(YF[t0:t0 + 128, :], yd.t[:, :], [yd.d])
                    return
                yf_c, sz_c, sg_c = yfb[ci % NF], szb[ci % NF], sgb[ci % NF]
                K.tt(POOL, yd.t[:, :], yd.t[:, :], yf_c.t[:, :], ALU.add, [yd.d, yf_c.d], [yd.d])
                K.tt(POOL, junk.t[:, :], xs_c.t[:, :], DSK.t[:, :], ALU.mult, [xs_c.d, DSK.d], [junk.d])
                K.tt(DVE, yd.t[:, 0:1024], yd.t[:, 0:1024], junk.t[:, :], ALU.add, [yd.d, junk.d], [yd.d])
                K.tt(DVE, yd.t[:, 0:1024], yd.t[:, 0:1024], sz_c.t[:, :], ALU.mult, [yd.d, sz_c.d], [yd.d])
                ssg = st8.t[:, 0, 0:2]
                for g in range(2):
                    K.act(junk.t[:, 0:512], yd.t[:, g * 512:(g + 1) * 512], AF.Square, [yd.d], [junk.d, st8.d], accum_out=st8.t[:, 0, g:g + 1])
                K.ts(DVE, ssg, ssg, 1.0 / 512, ALU.mult, [st8.d], [st8.d], s2=EPS, op1=ALU.add)
                K.act(ssg, ssg, AF.Sqrt, [st8.d], [st8.d])
                K.recip(ssg, ssg, [st8.d], [st8.d])
                for g in range(2):
                    K.stt(ynb.t[:, g * 512:(g + 1) * 512], yd.t[:, g * 512:(g + 1) * 512], st8.t[:, 0, g:g + 1], WN.t[:, g * 512:(g + 1) * 512],
                          ALU.mult, ALU.mult, [yd.d, st8.d, WN.d], [ynb.d])
                yr3 = yd.t[:, 1024:2048].rearrange("p (h e) -> p h e", e=128)
                s1, s2, mean, m2 = st8.t[:, 1, :], st8.t[:, 2, :], st8.t[:, 3, :], st8.t[:, 4, :]
                em.op(DVE, lambda: nc.vector.tensor_reduce(out=s1, in_=yr3, axis=AX.X, op=ALU.add), [yd.d], [st8.d])
                K.act(junk.t[:, :], yd.t[:, 1024:2048], AF.Square, [yd.d], [junk.d])
                em.op(DVE, lambda: nc.vector.tensor_reduce(out=s2, in_=junk.t[:, :].rearrange("p (h e) -> p h e", e=128), axis=AX.X, op=ALU.add),
                      [junk.d], [st8.d])
                K.ts(DVE, mean, s1, 1.0 / 128, ALU.mult, [st8.d], [st8.d])
                K.tt(DVE, m2, mean, mean, ALU.mult, [st8.d], [st8.d])
                K.stt(s2, s2, 1.0 / 128, m2, ALU.mult, ALU.subtract, [st8.d], [st8.d])
                K.ts(DVE, s2, s2, EPS, ALU.add, [st8.d], [st8.d])
                K.act(s2, s2, AF.Sqrt, [st8.d], [st8.d])
                K.recip(s2, s2, [st8.d], [st8.d])
                yn3 = ynb.t[:, 1024:2048].rearrange("p (h e) -> p h e", e=128)
                K.tt(DVE, yn3, yr3, bc3(mean, 128), ALU.subtract, [yd.d, st8.d], [ynb.d])
                K.tt(POOL, yn3, yn3, bc3(s2, 128), ALU.mult, [ynb.d, st8.d], [ynb.d])
                K.tt(DVE, ynb.t[:, 1024:2048], ynb.t[:, 1024:2048], RN.t[:, :], ALU.mult, [ynb.d, RN.d], [ynb.d])
                K.tt(POOL, ynb.t[:, 1024:2048], ynb.t[:, 1024:2048], sg_c.t[:, :], ALU.mult, [ynb.d, sg_c.d], [ynb.d])
                for b4 in range(4):
                    pst = pbank()
                    for kk in range(4):
                        k = b4 * 4 + kk
                        K.tr(pst.t[:, kk * 128:(kk + 1) * 128], ynb.t[:, k * 128:(k + 1) * 128], cst(CI_IDENT), [ynb.d, CS.d], [pst.d])
                    ys = ysb[b4 % 2]
                    K.act(ys.t[:, :, :], pst.t[:, :].rearrange("p (k t) -> p k t", t=128), AF.Copy, [pst.d], [ys.d])
                    K.st(YS[b4 * 512:(b4 + 1) * 512, t0:t0 + 128].rearrange("(k p) t -> p k t", p=128), ys.t[:, :, :], [ys.d])

            loads(0)
            if fin:
                loads_fin(0)
            partA(0)
            for ci in range(NO):
                if ci + 1 < NO:
                    partA(ci + 1)
                partB(ci)

    EPSB = K.gsb("epsb", [128, 1], F32)
    K.memset(DVE, EPSB.t[:, :], EPS, [EPSB.d])

    l0_phaseA()
    l0_phaseB()
    phaseC1(0, K.blocks, AO, 8, awout, xin, 0)
    if nlayers == 1:
        phaseC2(0, K.blocks, X2, 0, True)
    else:
        phaseC2(0, K.blocks, X2, 0, False)
        import os
        stop = int(os.environ.get("KSTOP", "99"))
        lat = [b for b in K.blocks if b[2] == 0]
        steps = [l1_phaseA, l1_phaseV, lambda: l1_scan(0), lambda: l1_scan(1),
                 lambda: phaseC1(1, lat, YS, 16, swout, X2, 0), lambda: phaseC2(1, lat, outD, CTX, True)]
        for si, fn in enumerate(steps):
            if si < stop:
                fn()
    em.finish()
    K.stats = em.stats()
    return K


def prep_inputs(inp, L):
    f = lambda a: np.ascontiguousarray(np.asarray(a, dtype=np.float32))
    rope, sq = host_consts(L)
    col = lambda v, n: f(v).reshape(n, 128).T
    shared = {
        "mod_w": f(inp["mod_w"]),
        "mod_b": f(inp["mod_b"]),
        "modb_col": np.ascontiguousarray(np.stack([col(inp["mod_b"][i], 48) for i in range(2)], axis=1)),
        "ncol": np.ascontiguousarray(np.stack([np.stack([col(inp["norm1_w"][i], 8), col(inp["norm2_w"][i], 8)], axis=1) for i in range(2)], axis=1)),
        "rope": rope, "sqc": sq,
        "ffn_w13": f(inp["ffn_w13"]), "ffn_w2": f(inp["ffn_w2"]),
        "attn_w_in": f(inp["attn_w_in"][0]), "mla_wq_b": f(inp["mla_wq_b"][0]), "mla_wkv_b": f(inp["mla_wkv_b"][0]),
        "attn_w_out": f(inp["attn_w_out"][0]),
    }
    if "ssm_w_in" in inp:
        shared["ssm_w_in"] = f(inp["ssm_w_in"][0])
        shared["ssm_w_out"] = f(inp["ssm_w_out"][0])
        cw = f(inp["ssd_conv_w"][0])
        shared["convw_col"] = np.ascontiguousarray(cw.reshape(5, 12, 128).transpose(2, 0, 1).reshape(128, 60))
        shared["convb_col"] = col(inp["ssd_conv_b"][0], 12)
        rows = np.zeros((NR1,), np.float32)
        rows[R_CONVB:R_CONVB + 1536] = f(inp["ssd_conv_b"][0])
        rows[R_DTB:R_DTB + 32] = f(inp["ssd_dt_bias"][0]).reshape(-1)
        rows[R_ALOG:R_ALOG + 32] = f(inp["ssd_a_log"][0]).reshape(-1)
        rows[R_RLOG:R_RLOG + 16] = f(inp["ret_decay_logit"][0]).reshape(-1)
        rows[R_DSK:R_DSK + 1024] = np.repeat(f(inp["ssd_d"][0]), 64)
        rows[R_SSDN:R_SSDN + 1024] = f(inp["ssd_norm"][0])
        rows[R_RETN:R_RETN + 1024] = f(inp["ret_norm"][0])
        shared["rows1"] = rows
        jj, ii = np.meshgrid(np.arange(128, dtype=np.float32), np.arange(128, dtype=np.float32), indexing="ij")
        c2 = np.zeros((128, 260), np.float32)
        c2[:, 0:128] = np.maximum(ii - jj, 0)
        c2[:, 128:256] = np.maximum(jj - ii, 0)
        j1 = np.arange(128, dtype=np.float32)
        c2[:, 256], c2[:, 257], c2[:, 258], c2[:, 259] = j1 + 1, 128 - j1, 127 - j1, j1
        shared["sqc2"] = c2
        sel = np.zeros((16, 16, 128), np.float32)
        for h in range(16):
            sel[h, h, :] = 1.0
        shared["selc"] = np.ascontiguousarray(sel.reshape(16, 2048))
    vec = np.zeros((128, NV), np.float32)
    vec[:, V_GQ] = np.tile(f(inp["gqa_qn"][0]), 2)
    vec[:, V_GK] = np.tile(f(inp["gqa_kn"][0]), 2)
    vec[:, V_QA:V_QA + 3] = col(inp["mla_qa_norm"][0], 3)
    vec[:, V_KVA:V_KVA + 2] = col(inp["mla_kva_norm"][0], 2)
    vec[:96, V_MQ] = f(inp["mla_qn"][0])
    vec[:96, V_MK] = f(inp["mla_kn"][0])
    shared["vecs"] = vec
    maps = []
    x, c, ctx, c_ctx = f(inp["x"]), f(inp["c"]), f(inp["ctx"]), f(inp["c_ctx"])
    for b in range(x.shape[0]):
        m = dict(shared)
        m["xin"] = np.ascontiguousarray(np.concatenate([ctx[b], x[b]], axis=0))
        cc = np.stack([col(c[b], 8), col(c_ctx, 8)], axis=2)
        m["cc"] = np.ascontiguousarray(cc)
        maps.append(m)
    return maps


_CACHE = {}


def kernel(**inputs):
    L = int(np.asarray(inputs["x"]).shape[1])
    B = int(np.asarray(inputs["x"]).shape[0])
    if L not in _CACHE:
        _CACHE[L] = build(L, 2)
    K = _CACHE[L]
    maps = prep_inputs(inputs, L)
    res = run_bass_kernel_spmd(K.nc, maps, core_ids=list(range(B)))
    return np.stack([np.asarray(res.results[b]["out"]) for b in range(B)], axis=0).astype(np.float32)
```

```python
import math
from contextlib import ExitStack, contextmanager
import numpy as np
import concourse.bass as bass
import concourse.mybir as mybir
from concourse.bass_utils import run_bass_kernel_spmd

F32 = mybir.dt.float32
BF16 = mybir.dt.bfloat16
AF = mybir.ActivationFunctionType
ALU = mybir.AluOpType
AX = mybir.AxisListType

EPOCH = 30000
EPS = 1e-6
D = 1024
CTX = 256
FFH = 2816
GRID_W = 64
THETA = 10000.0


class Dep:
    __slots__ = ("w", "r")

    def __init__(self):
        self.w = None
        self.r = {}


class Eng:
    def __init__(self, em, name, h, self_sync=True):
        self.em, self.name, self.h, self.self_sync = em, name, h, self_sync
        self.sem = None
        self.cnt = 0
        self.waited = {}
        self.own = set()
        self.ninst = 0
        self.nwait = 0
        self.last = None

    def next_event(self):
        if self.sem is None or self.cnt >= EPOCH:
            self.sem = self.em.new_sem(self.name)
            self.own.add(self.sem.num)
            self.cnt = 0
        self.cnt += 1
        self.last = (self.sem.num, self.cnt)
        return self.last


class Emitter:
    def __init__(self, nc, es, n_dma_sp=24, n_dma_pool=16):
        self.nc, self.es = nc, es
        self.sems = {}
        self.nsem = 0
        self.pe = Eng(self, "pe", nc.tensor, self_sync=False)
        self.act = Eng(self, "act", nc.scalar)
        self.dve = Eng(self, "dve", nc.vector)
        self.pool = Eng(self, "pool", nc.gpsimd)
        self.sp = Eng(self, "sp", nc.sync)
        self.engines = [self.pe, self.act, self.dve, self.pool, self.sp]
        self.dma_pools = {}
        for e, n in ((self.sp, n_dma_sp), (self.pool, n_dma_pool)):
            self.dma_pools[e.name] = [[self.new_sem("d" + e.name), 0] for _ in range(n)]
        self.dma_rr = {k: 0 for k in self.dma_pools}
        self.out_events = []

    def new_sem(self, tag):
        s = self.es.enter_context(self.nc.semaphore("%s_%d" % (tag, self.nsem)))
        self.nsem += 1
        self.sems[s.num] = s
        return s

    def _wait(self, eng, evs):
        best = {}
        for (s, v) in evs:
            if v > best.get(s, 0):
                best[s] = v
        for s, v in best.items():
            if (not eng.self_sync) and s in eng.own:
                continue
            if eng.waited.get(s, 0) >= v:
                continue
            eng.h.wait_ge(self.sems[s], v)
            eng.waited[s] = v
            eng.nwait += 1

    @staticmethod
    def _collect(reads, writes):
        evs = []
        for d in reads:
            if d.w is not None:
                evs.append(d.w)
        for d in writes:
            if d.w is not None:
                evs.append(d.w)
            evs.extend(d.r.items())
        return evs

    @staticmethod
    def _commit(ev, reads, writes):
        for d in reads:
            if ev[1] > d.r.get(ev[0], 0):
                d.r[ev[0]] = ev[1]
        for d in writes:
            d.w = ev
            d.r = {}

    def op(self, eng, fn, reads=(), writes=()):
        self._wait(eng, self._collect(reads, writes))
        inst = fn()
        ev = eng.next_event()
        inst.then_inc(self.sems[ev[0]], 1)
        eng.ninst += 1
        self._commit(ev, reads, writes)
        return ev

    def dma(self, eng, out, in_, reads=(), writes=(), is_output=False, **kw):
        pool = self.dma_pools[eng.name]
        i = self.dma_rr[eng.name]
        self.dma_rr[eng.name] = (i + 1) % len(pool)
        slot = pool[i]
        evs = self._collect(reads, writes)
        if slot[1] > 0:
            evs.append((slot[0].num, slot[1]))
        self._wait(eng, evs)
        if slot[1] + 16 > EPOCH:
            slot[0] = self.new_sem("d" + eng.name)
            slot[1] = 0
        slot[1] += 16
        eng.h.dma_start(out=out, in_=in_, **kw).then_inc(slot[0], 16)
        ev = (slot[0].num, slot[1])
        eng.ninst += 1
        self._commit(ev, reads, writes)
        if is_output:
            self.out_events.append(ev)
        return ev

    def all_events(self):
        evs = []
        for e in self.engines:
            if e.last is not None:
                evs.append(e.last)
        for pool in self.dma_pools.values():
            for s, v in pool:
                if v > 0:
                    evs.append((s.num, v))
        return evs

    def barrier(self):
        evs = self.all_events()
        for e in self.engines:
            self._wait(e, evs)

    def finish(self):
        self._wait(self.sp, list(self.out_events) + self.all_events())

    def stats(self):
        return {e.name: (e.ninst, e.nwait) for e in self.engines}, self.nsem


class Buf:
    __slots__ = ("t", "d")

    def __init__(self, t):
        self.t = t
        self.d = Dep()


class KB:
    def __init__(self, L):
        self.L = L
        self.NT = CTX + L
        self.NCH = self.NT // 128
        self.nc = bass.Bass("TRN2", target_bir_lowering=False)
        self.es = ExitStack()
        self.em = Emitter(self.nc, self.es)
        self.cur = self.es
        self.uid = 0
        self.blocks = [(0, CTX, 1)] + [(CTX + i * 512, 512, 0) for i in range(L // 512)]
        self.PS = [Buf(self.es.enter_context(self.nc.psum_tensor("psb%d" % i, [128, 512], F32))) for i in range(8)]

    def sb(self, name, shape, dtype):
        self.uid += 1
        return Buf(self.cur.enter_context(self.nc.sbuf_tensor("s_%s_%d" % (name, self.uid), shape, dtype)))

    def gsb(self, name, shape, dtype):
        return Buf(self.es.enter_context(self.nc.sbuf_tensor("g_" + name, shape, dtype)))

    def dram(self, name, shape, dtype, kind="Internal"):
        return self.nc.dram_tensor(name, shape, dtype, kind=kind).ap()

    @contextmanager
    def phase(self):
        st = ExitStack()
        prev = self.cur
        self.cur = st
        try:
            yield
        finally:
            self.em.barrier()
            st.close()
            self.cur = prev

    def mm(self, out, lhsT, rhs, start, stop, R, W):
        nc = self.nc
        return self.em.op(self.em.pe, lambda: nc.tensor.matmul(out, lhsT=lhsT, rhs=rhs, start=start, stop=stop), R, W)

    def tr(self, out, in_, ident, R, W):
        nc = self.nc
        return self.em.op(self.em.pe, lambda: nc.tensor.transpose(out=out, in_=in_, identity=ident), R, W)

    def act(self, out, in_, func, R, W, scale=None, bias=None, accum_out=None):
        nc = self.nc
        kw = {}
        if scale is not None:
            kw["scale"] = scale
        if bias is not None:
            kw["bias"] = bias
        if accum_out is not None:
            kw["accum_out"] = accum_out
        return self.em.op(self.em.act, lambda: nc.scalar.activation(out=out, in_=in_, func=func, **kw), R, W)

    def _ve(self, eng):
        return self.nc.vector if eng is self.em.dve else self.nc.gpsimd

    def tt(self, eng, out, in0, in1, op, R, W):
        h = self._ve(eng)
        return self.em.op(eng, lambda: h.tensor_tensor(out=out, in0=in0, in1=in1, op=op), R, W)

    def ts(self, eng, out, in0, s1, op0, R, W, s2=None, op1=None):
        h = self._ve(eng)
        if op1 is None:
            return self.em.op(eng, lambda: h.tensor_scalar(out=out, in0=in0, scalar1=s1, scalar2=None, op0=op0), R, W)
        return self.em.op(eng, lambda: h.tensor_scalar(out=out, in0=in0, scalar1=s1, scalar2=s2, op0=op0, op1=op1), R, W)

    def stt(self, out, in0, scalar, in1, op0, op1, R, W):
        nc = self.nc
        return self.em.op(self.em.dve, lambda: nc.vector.scalar_tensor_tensor(out=out, in0=in0, scalar=scalar, in1=in1, op0=op0, op1=op1), R, W)

    def recip(self, out, in_, R, W):
        nc = self.nc
        return self.em.op(self.em.dve, lambda: nc.vector.reciprocal(out=out, in_=in_), R, W)

    def memset(self, eng, ap, val, W):
        h = self._ve(eng)
        return self.em.op(eng, lambda: h.memset(ap, val), (), W)

    def copy(self, eng, out, in_, R, W):
        h = self._ve(eng)
        return self.em.op(eng, lambda: h.tensor_copy(out=out, in_=in_), R, W)

    def ld(self, out, in_, W, R=(), **kw):
        return self.em.dma(self.em.sp, out, in_, reads=R, writes=W, **kw)

    def st(self, out, in_, R, W=(), **kw):
        return self.em.dma(self.em.pool, out, in_, reads=R, writes=W, **kw)

    def ldcast(self, out, in_, W, R=()):
        return self.em.dma(self.em.pool, out, in_, reads=R, writes=W, max_dma_last_dim=8192)


def rope_tables(L, dim):
    rows = L // GRID_W
    rr, cc = np.meshgrid(np.arange(rows, dtype=np.float32), np.arange(GRID_W, dtype=np.float32), indexing="ij")
    quarter = dim // 4
    inv = (np.float32(THETA) ** (-np.arange(quarter, dtype=np.float32) / np.float32(quarter))).astype(np.float32)
    ang = np.concatenate([rr.reshape(-1)[:, None] * inv, cc.reshape(-1)[:, None] * inv], axis=-1).astype(np.float32)
    cos = np.concatenate([np.ones((CTX, dim // 2), np.float32), np.cos(ang)], axis=0)
    sin = np.concatenate([np.zeros((CTX, dim // 2), np.float32), np.sin(ang)], axis=0)
    return cos.astype(np.float32), sin.astype(np.float32)


def host_consts(L):
    NT = CTX + L
    c64, s64 = rope_tables(L, 64)
    c32, s32 = rope_tables(L, 32)
    cos64 = np.repeat(c64.T, 2, axis=0)
    sin64 = np.repeat(s64.T, 2, axis=0)
    cos128 = np.concatenate([cos64, cos64], 0)
    sin128 = np.concatenate([sin64, sin64], 0)
    cosm = np.concatenate([np.ones((64, NT), np.float32), np.repeat(c32.T, 2, axis=0)], 0)
    sinm = np.concatenate([np.zeros((64, NT), np.float32), np.repeat(s32.T, 2, axis=0)], 0)
    rope = np.zeros((4, 128, NT), np.float32)
    rope[0], rope[1] = cos128, sin128
    rope[2, :96], rope[3, :96] = cosm, sinm
    ident = np.eye(128, dtype=np.float32)
    rot = np.zeros((128, 128), np.float32)
    for i in range(64):
        rot[2 * i + 1, 2 * i] = -1.0
        rot[2 * i, 2 * i + 1] = 1.0
    rotm = np.zeros((128, 128), np.float32)
    rotm[64:96, 64:96] = rot[64:96, 64:96]
    shift = np.zeros((128, 128), np.float32)
    for k in range(32):
        shift[k, 64 + k] = 1.0
    bd64 = np.zeros((128, 128), np.float32)
    bd64[:64, :64] = 1.0
    bd64[64:, 64:] = 1.0
    ones = np.ones((128, 128), np.float32)
    tri = np.triu(np.ones((128, 128), np.float32))
    sq = np.concatenate([ident, rot, rotm, shift, bd64, ones, tri, tri.T.copy()], axis=1)
    return rope, sq


CI_IDENT, CI_ROT, CI_ROTM, CI_SHIFT, CI_BD64, CI_ONES, CI_TRI, CI_TRIT = range(8)
NCONST = 8

V_GQ, V_GK, V_QA, V_KVA, V_MQ, V_MK = 0, 1, 2, 5, 7, 8
NV = 16
R_CONVB, R_DTB, R_ALOG, R_RLOG, R_DSK, R_SSDN, R_RETN = 0, 1536, 1568, 1600, 1616, 2640, 3664
NR1 = 4688


def build(L=4096, nlayers=2, debug=False):
    K = KB(L)
    nc, em = K.nc, K.em
    NT, NCH = K.NT, K.NCH
    PS = K.PS
    DVE, POOL = em.dve, em.pool

    xin = K.dram("xin", [NT, D], F32, "ExternalInput")
    ccd = K.dram("cc", [128, 8, 2], F32, "ExternalInput")
    mod_w = K.dram("mod_w", [2, D, 6 * D], F32, "ExternalInput")
    modb_col = K.dram("modb_col", [128, 2, 48], F32, "ExternalInput")
    mod_b = K.dram("mod_b", [2, 6 * D], F32, "ExternalInput")
    ncol = K.dram("ncol", [128, 2, 2, 8], F32, "ExternalInput")
    vecs = K.dram("vecs", [128, NV], F32, "ExternalInput")
    ropeD = K.dram("rope", [4, 128, NT], F32, "ExternalInput")
    sqc = K.dram("sqc", [128, NCONST * 128], F32, "ExternalInput")
    w13D = K.dram("ffn_w13", [2, D, 2 * FFH], F32, "ExternalInput")
    w2D = K.dram("ffn_w2", [2, FFH, D], F32, "ExternalInput")
    awin = K.dram("attn_w_in", [D, 1440], F32, "ExternalInput")
    wqb = K.dram("mla_wq_b", [384, 768], F32, "ExternalInput")
    wkvb = K.dram("mla_wkv_b", [256, 1024], F32, "ExternalInput")
    awout = K.dram("attn_w_out", [D, D], F32, "ExternalInput")
    outD = K.dram("out", [L, D], F32, "ExternalOutput")

    QA = K.dram("QA", [4, 128, NT], BF16)
    KA = K.dram("KA", [2, 128, NT], BF16)
    VA = K.dram("VA", [128, NCH, 130], BF16)
    QM = K.dram("QM", [8, 96, NT], BF16)
    KM = K.dram("KM", [8, 96, NT], BF16)
    VM = K.dram("VM", [128, NCH, 520], BF16)
    AO = K.dram("AO", [D, NT], BF16)
    X1 = K.dram("X1", [NT, D], F32)
    U2 = K.dram("U2", [D, NT], BF16)
    X2 = K.dram("X2", [NT, D], F32, "ExternalOutput" if debug else "Internal")
    MODROW = K.dram("MODROW", [2, 2, 2, D], F32)

    CS = K.gsb("consts", [128, NCONST * 128], F32)
    K.ld(CS.t[:, :], sqc[:, :], [CS.d])

    def cst(i, r=128, c=128):
        return CS.t[0:r, i * 128:i * 128 + c]

    CB = K.gsb("constsb", [128, 2 * 128], BF16)
    K.copy(DVE, CB.t[:, 0:128], cst(CI_BD64), [CS.d], [CB.d])
    K.copy(DVE, CB.t[:, 128:256], cst(CI_ONES), [CS.d], [CB.d])
    BD64b = CB.t[:, 0:128]
    ONESb = CB.t[:, 128:256]
    VEC = K.gsb("vecs", [128, NV], F32)
    K.ld(VEC.t[:, :], vecs[:, :], [VEC.d])
    NCOL = K.gsb("ncol", [128, 2, 2, 8], F32)
    K.ld(NCOL.t[:, :, :, :], ncol[:, :, :, :], [NCOL.d])
    MODT = K.gsb("modT", [128, 2, 48, 2], F32)
    MUL = K.gsb("mulT", [128, 2, 2, 8, 2], F32)

    with K.phase():
        cc = K.sb("cc", [128, 8, 2], F32)
        K.ld(cc.t[:, :, :], ccd[:, :, :], [cc.d])
        K.act(cc.t[:, :, :], cc.t[:, :, :], AF.Silu, [cc.d], [cc.d])
        modb = K.sb("modb", [128, 2, 48], F32)
        K.ld(modb.t[:, :, :], modb_col[:, :, :], [modb.d])
        mbrow = K.sb("mbrow", [2, 2, 6 * D], F32)
        K.ld(mbrow.t[:, :, :], mod_b.partition_broadcast(2), [mbrow.d])
        wbuf = [K.sb("mw", [128, 8, 1024], F32) for _ in range(2)]
        rowb = [K.sb("rowb", [2, 1024], F32) for _ in range(2)]
        it = 0
        for i in range(nlayers):
            for v in range(6):
                wb = wbuf[it % 2]
                K.ld(wb.t[:, :, :], mod_w[i, :, v * 1024:(v + 1) * 1024].rearrange("(k p) n -> p k n", p=128), [wb.d])
                ps = PS[it % 2]
                for j in range(8):
                    for k in range(8):
                        K.mm(ps.t[:, j * 2:(j + 1) * 2], wb.t[:, k, j * 128:(j + 1) * 128], cc.t[:, k, :], k == 0, k == 7,
                             [wb.d, cc.d], [ps.d])
                K.tt(DVE, MODT.t[:, i, v * 8:(v + 1) * 8, :], ps.t[:, 0:16].rearrange("p (j s) -> p j s", s=2),
                     modb.t[:, i, v * 8:(v + 1) * 8].unsqueeze(2).broadcast_to([128, 8, 2]), ALU.add,
                     [ps.d, modb.d], [MODT.d])
                if v in (2, 5):
                    gi = 0 if v == 2 else 1
                    rb = rowb[gi]
                    for half in range(2):
                        ps2 = PS[2 + half]
                        for k in range(8):
                            K.mm(ps2.t[0:2, :], cc.t[:, k, :], wb.t[:, k, half * 512:(half + 1) * 512], k == 0, k == 7,
                                 [wb.d, cc.d], [ps2.d])
                        K.tt(DVE, rb.t[:, half * 512:(half + 1) * 512], ps2.t[0:2, :],
                             mbrow.t[:, i, v * 1024 + half * 512: v * 1024 + (half + 1) * 512], ALU.add,
                             [ps2.d, mbrow.d], [rb.d])
                    K.st(MODROW[i, gi, :, :], rb.t[:, :], [rb.d])
                it += 1
            for nrm in range(2):
                K.stt(MUL.t[:, i, nrm, :, :], MODT.t[:, i, (3 * nrm + 1) * 8:(3 * nrm + 2) * 8, :], 1.0,
                      NCOL.t[:, i, nrm, :].unsqueeze(2).broadcast_to([128, 8, 2]), ALU.add, ALU.mult,
                      [MODT.d, NCOL.d], [MUL.d])

    def mul_ap(layer, nrm, k, s):
        return MUL.t[:, layer, nrm, k, s:s + 1]

    def add_ap(layer, nrm, k, s):
        return MODT.t[:, layer, 3 * nrm * 8 + k, s:s + 1]

    def norm_to_uT(xt, nch, T, layer, nrm, s, uT, scr, psbanks, use_pool=True):
        junk, ss, xs = scr
        for c in range(nch):
            K.act(junk.t[:, :], xt.t[:, c, :], AF.Square, [xt.d], [junk.d, ss.d], accum_out=ss.t[:, c:c + 1])
        K.ts(DVE, ss.t[:, 0:nch], ss.t[:, 0:nch], 1.0 / D, ALU.mult, [ss.d], [ss.d], s2=EPS, op1=ALU.add)
        K.act(ss.t[:, 0:nch], ss.t[:, 0:nch], AF.Sqrt, [ss.d], [ss.d])
        K.recip(ss.t[:, 0:nch], ss.t[:, 0:nch], [ss.d], [ss.d])
        for c in range(nch):
            K.ts(DVE if (c % 2 == 0 or not use_pool) else POOL, xs.t[:, c, :], xt.t[:, c, :], ss.t[:, c:c + 1], ALU.mult, [xt.d, ss.d], [xs.d])
        for k in range(8):
            ps = psbanks[k % len(psbanks)]
            for c in range(nch):
                K.tr(ps.t[:, c * 128:(c + 1) * 128], xs.t[:, c, k * 128:(k + 1) * 128], cst(CI_IDENT), [xs.d, CS.d], [ps.d])
            K.act(uT.t[:, k, 0:T], ps.t[:, 0:T], AF.Identity, [ps.d, MUL.d, MODT.d], [uT.d],
                  scale=mul_ap(layer, nrm, k, s), bias=add_ap(layer, nrm, k, s))

    def l0_phaseA():
        with K.phase():
            W = K.sb("win", [128, 8, 1440], BF16)
            for k in range(8):
                K.ldcast(W.t[:, k, :], awin[k * 128:(k + 1) * 128, :], [W.d])
            Wkd = K.sb("wkd", [128, 8, 2, 128], BF16)
            for g in range(2):
                for dup in range(2):
                    K.ldcast(Wkd.t[:, :, g, dup * 64:(dup + 1) * 64],
                             awin[:, 512 + g * 64:512 + (g + 1) * 64].rearrange("(k p) n -> p k n", p=128), [Wkd.d])
            Wq = K.sb("wqb", [128, 3, 768], BF16)
            K.ldcast(Wq.t[:, :, :], wqb.rearrange("(j p) n -> p j n", p=128), [Wq.d])
            Wkp = K.sb("wkvp", [128, 2, 8, 96], BF16)
            K.memset(DVE, Wkp.t[:, :, :, :], 0.0, [Wkp.d])
            wkv4 = wkvb.rearrange("(j p) (h t e) -> p j h t e", p=128, t=2, e=64)
            for j in range(2):
                K.ldcast(Wkp.t[:, j, :, 0:64], wkv4[:, j, :, 0, :], [Wkp.d])
            Wv = K.sb("wkvv", [128, 2, 8, 64], BF16)
            for j in range(2):
                K.ldcast(Wv.t[:, j, :, :], wkv4[:, j, :, 1, :], [Wv.d])

            xb = [K.sb("xa", [128, 4, D], F32) for _ in range(2)]
            scr = (K.sb("junk", [128, D], BF16), K.sb("ss", [128, 4], F32), K.sb("xs", [128, 4, D], F32))
            uTb = [K.sb("uT", [128, 8, 512], BF16) for _ in range(2)]
            ropeb = [K.sb("rope", [128, 4, 512], F32) for _ in range(2)]
            sqb = [K.sb("sq", [128, 512], BF16) for _ in range(2)]
            r1b = [K.sb("r1", [128, 512], F32) for _ in range(2)]
            qnb = [K.sb("qn", [128, 512], F32) for _ in range(2)]
            t1b = [K.sb("t1", [128, 512], F32) for _ in range(2)]
            t2b = [K.sb("t2", [128, 512], F32) for _ in range(2)]
            outb = [K.sb("ob", [128, 512], BF16) for _ in range(3)]
            qln = K.sb("qln", [128, 3, 512], BF16)
            kvn = K.sb("kvn", [128, 2, 512], BF16)
            krp = K.sb("krp", [32, 512], F32)
            vat = [K.sb("vat", [128, 4, 130], BF16) for _ in range(2)]
            vmt = [K.sb("vmt", [128, 4, 520], BF16) for _ in range(2)]
            for b in vat + vmt:
                K.memset(DVE, b.t[:, :, :], 1.0, [b.d])
            cnt = {"n": 0, "o": 0, "p": 0}
            PROJ = [PS[2], PS[3], PS[4]]

            def proj_bank():
                cnt["p"] += 1
                return PROJ[cnt["p"] % 3]

            pq = {"a": None, "b": None}
            SSB = [PS[5], PS[6]]
            ROTB = [PS[7], PS[0]]

            def flush():
                a, b = pq["a"], pq["b"]
                if a is not None:
                    a[0]()
                if b is not None:
                    b[1]()
                if a is not None:
                    a[1]()
                pq["a"] = pq["b"] = None

            def post(ps_src, R_, T, gain, ss_lhsT, inv_n, rot_lhsT, ci, rp, dst):
                i = cnt["n"] % 2
                cnt["n"] += 1
                sq, r1, qn, t1, t2 = sqb[i], r1b[i], qnb[i], t1b[i], t2b[i]
                pss, psr = SSB[i], ROTB[i]
                K.act(sq.t[0:R_, 0:T], ps_src.t[0:R_, 0:T], AF.Square, [ps_src.d], [sq.d])
                K.mm(pss.t[0:R_, 0:T], ss_lhsT, sq.t[0:R_, 0:T], True, True, [sq.d, CB.d], [pss.d])

                def s2a():
                    K.act(r1.t[0:R_, 0:T], pss.t[0:R_, 0:T], AF.Ln, [pss.d], [r1.d], scale=inv_n, bias=EPSB.t[0:R_, 0:1])
                    K.act(r1.t[0:R_, 0:T], r1.t[0:R_, 0:T], AF.Exp, [r1.d], [r1.d], scale=-0.5)
                    K.stt(qn.t[0:R_, 0:T], ps_src.t[0:R_, 0:T], gain, r1.t[0:R_, 0:T], ALU.mult, ALU.mult,
                          [ps_src.d, r1.d, VEC.d], [qn.d])
                    K.mm(psr.t[0:R_, 0:T], rot_lhsT, qn.t[0:R_, 0:T], True, True, [qn.d, CS.d], [psr.d])

                def s2b():
                    K.tt(DVE, t1.t[0:R_, 0:T], qn.t[0:R_, 0:T], rp.t[0:R_, ci, 0:T], ALU.mult, [qn.d, rp.d], [t1.d])
                    K.tt(DVE, t2.t[0:R_, 0:T], psr.t[0:R_, 0:T], rp.t[0:R_, ci + 1, 0:T], ALU.mult, [psr.d, rp.d], [t2.d])
                    ob = outb[cnt["o"] % 3]
                    cnt["o"] += 1
                    K.tt(POOL, ob.t[0:R_, 0:T], t1.t[0:R_, 0:T], t2.t[0:R_, 0:T], ALU.add, [t1.d, t2.d], [ob.d])
                    K.st(dst, ob.t[0:R_, 0:T], [ob.d])

                a, b = pq["a"], pq["b"]
                if a is not None:
                    a[0]()
                if b is not None:
                    b[1]()
                pq["b"] = a
                pq["a"] = (s2a, s2b)

            def lora_norm(ps_list, nj, gcol0, inv_n, dstb, T):
                i = cnt["n"] % 2
                cnt["n"] += 1
                r1 = r1b[i]
                pss = PS[5]
                for j in range(nj):
                    sq = sqb[(cnt["n"] + j) % 2]
                    K.act(sq.t[:, 0:T], ps_list[j].t[:, 0:T], AF.Square, [ps_list[j].d], [sq.d])
                    K.mm(pss.t[:, 0:T], ONESb, sq.t[:, 0:T], j == 0, j == nj - 1, [sq.d, CB.d], [pss.d])
                K.act(r1.t[:, 0:T], pss.t[:, 0:T], AF.Ln, [pss.d], [r1.d], scale=inv_n, bias=EPSB.t[:, 0:1])
                K.act(r1.t[:, 0:T], r1.t[:, 0:T], AF.Exp, [r1.d], [r1.d], scale=-0.5)
                for j in range(nj):
                    K.stt(dstb.t[:, j, 0:T], ps_list[j].t[:, 0:T], VEC.t[:, gcol0 + j:gcol0 + j + 1], r1.t[:, 0:T],
                          ALU.mult, ALU.mult, [ps_list[j].d, r1.d, VEC.d], [dstb.d])

            for bi, (t0, T, s) in enumerate(K.blocks):
                nch = T // 128
                c0 = t0 // 128
                xt = xb[bi % 2]
                K.ld(xt.t[:, 0:nch, :], xin[t0:t0 + T, :].rearrange("(c p) d -> p c d", p=128), [xt.d])
                rp = ropeb[bi % 2]
                K.ld(rp.t[:, :, 0:T], ropeD[:, :, t0:t0 + T].rearrange("f p t -> p f t"), [rp.d])
                uT = uTb[bi % 2]
                norm_to_uT(xt, nch, T, 0, 0, s, uT, scr, [PS[0], PS[1]])
                for ch in range(4):
                    ps = proj_bank()
                    for k in range(8):
                        K.mm(ps.t[:, 0:T], W.t[:, k, ch * 128:(ch + 1) * 128], uT.t[:, k, 0:T], k == 0, k == 7, [W.d, uT.d], [ps.d])
                    post(ps, 128, T, VEC.t[:, V_GQ:V_GQ + 1], BD64b, 1.0 / 64, cst(CI_ROT), 0, rp, QA[ch, :, t0:t0 + T])
                for g in range(2):
                    ps = proj_bank()
                    for k in range(8):
                        K.mm(ps.t[:, 0:T], Wkd.t[:, k, g, :], uT.t[:, k, 0:T], k == 0, k == 7, [Wkd.d, uT.d], [ps.d])
                    post(ps, 128, T, VEC.t[:, V_GK:V_GK + 1], BD64b, 1.0 / 64, cst(CI_ROT), 0, rp, KA[g, :, t0:t0 + T])
                flush()
                va_t = vat[bi % 2]
                for c in range(nch):
                    ps = proj_bank()
                    for k in range(8):
                        K.mm(ps.t[:, 0:128], uT.t[:, k, c * 128:(c + 1) * 128], W.t[:, k, 640:768], k == 0, k == 7, [W.d, uT.d], [ps.d])
                    K.act(va_t.t[:, c, :].rearrange("p (g e) -> p g e", e=65)[:, :, 0:64],
                          ps.t[:, 0:128].rearrange("p (g e) -> p g e", e=64), AF.Copy, [ps.d], [va_t.d])
                K.st(VA[:, c0:c0 + nch, :], va_t.t[:, 0:nch, :], [va_t.d])
                pl = []
                for j in range(3):
                    ps = proj_bank()
                    for k in range(8):
                        K.mm(ps.t[:, 0:T], W.t[:, k, 768 + j * 128:768 + (j + 1) * 128], uT.t[:, k, 0:T], k == 0, k == 7, [W.d, uT.d], [ps.d])
                    pl.append(ps)
                lora_norm(pl, 3, V_QA, 1.0 / 384, qln, T)
                for h in range(8):
                    ps = proj_bank()
                    for j in range(3):
                        K.mm(ps.t[0:96, 0:T], Wq.t[:, j, h * 96:(h + 1) * 96], qln.t[:, j, 0:T], j == 0, j == 2, [Wq.d, qln.d], [ps.d])
                    post(ps, 96, T, VEC.t[0:96, V_MQ:V_MQ + 1], ONESb[0:96, 0:96], 1.0 / 96, cst(CI_ROTM, 96, 96), 2, rp, QM[h, :, t0:t0 + T])
                flush()
                pl = []
                for j in range(2):
                    ps = proj_bank()
                    for k in range(8):
                        K.mm(ps.t[:, 0:T], W.t[:, k, 1152 + j * 128:1152 + (j + 1) * 128], uT.t[:, k, 0:T], k == 0, k == 7, [W.d, uT.d], [ps.d])
                    pl.append(ps)
                lora_norm(pl, 2, V_KVA, 1.0 / 256, kvn, T)
                ps = proj_bank()
                for k in range(8):
                    K.mm(ps.t[0:32, 0:T], W.t[:, k, 1408:1440], uT.t[:, k, 0:T], k == 0, k == 7, [W.d, uT.d], [ps.d])
                K.act(krp.t[0:32, 0:T], ps.t[0:32, 0:T], AF.Copy, [ps.d], [krp.d])
                for h in range(8):
                    ps = proj_bank()
                    for j in range(2):
                        K.mm(ps.t[0:96, 0:T], Wkp.t[:, j, h, :], kvn.t[:, j, 0:T], j == 0, False, [Wkp.d, kvn.d], [ps.d])
                    K.mm(ps.t[0:96, 0:T], cst(CI_SHIFT, 32, 96), krp.t[0:32, 0:T], False, True, [krp.d, CS.d], [ps.d])
                    post(ps, 96, T, VEC.t[0:96, V_MK:V_MK + 1], ONESb[0:96, 0:96], 1.0 / 96, cst(CI_ROTM, 96, 96), 2, rp, KM[h, :, t0:t0 + T])
                flush()
                vm_t = vmt[bi % 2]
                for c in range(nch):
                    ps = proj_bank()
                    for j in range(2):
                        K.mm(ps.t[:, 0:512], kvn.t[:, j, c * 128:(c + 1) * 128], Wv.t[:, j, :, :].rearrange("p h e -> p (h e)"),
                             j == 0, j == 1, [Wv.d, kvn.d], [ps.d])
                    K.act(vm_t.t[:, c, :].rearrange("p (g e) -> p g e", e=65)[:, :, 0:64],
                          ps.t[:, 0:512].rearrange("p (g e) -> p g e", e=64), AF.Copy, [ps.d], [vm_t.d])
                K.st(VM[:, c0:c0 + nch, :], vm_t.t[:, 0:nch, :], [vm_t.d])

    def l0_phaseB():
        with K.phase():
            va = K.sb("va", [128, NCH, 130], BF16)
            vm = K.sb("vm", [128, NCH, 520], BF16)
            K.ld(va.t[:, :, :], VA[:, :, :], [va.d])
            K.ld(vm.t[:, :, :], VM[:, :, :], [vm.d])
            kb = [K.sb("kb", [128, NT], BF16) for _ in range(2)]
            kzT = [K.sb("kzT", [128, NT], BF16) for _ in range(2)]
            kzB = [K.sb("kzB", [128, NT], BF16) for _ in range(2)]
            for b in kzT:
                K.memset(DVE, b.t[64:128, :], 0.0, [b.d])
            for b in kzB:
                K.memset(DVE, b.t[0:64, :], 0.0, [b.d])
            qb = [K.sb("qb", [128, 512], BF16) for _ in range(3)]
            pb = [K.sb("pb", [128, 512], BF16) for _ in range(4)]
            osb = [K.sb("osb", [64, 512], F32) for _ in range(2)]
            rsb = [K.sb("rsb", [128, 512], F32) for _ in range(2)]
            aob = [K.sb("aob", [64, 512], BF16) for _ in range(2)]
            SB_ = [PS[0], PS[1], PS[2]]
            OB_ = [PS[3], PS[4]]
            BCB = PS[5]
            tasks = []
            groups = []
            ui = 0
            qi = 0
            units = [("a", c) for c in range(4)] + [("m", h) for h in range(8)]
            for kind, idx in units:
                if kind == "a":
                    kT, kB = kzT[ui % 2], kzB[ui % 2]
                    kloads = [(kT, KA[idx // 2, 0:64, :], slice(0, 64)), (kB, KA[idx // 2, 64:128, :], slice(64, 128))]
                else:
                    kbf = kb[ui % 2]
                    kloads = [(kbf, KM[idx, :, :], slice(0, 96))]
                ui += 1
                first_in_unit = True
                for (t0, T, s) in K.blocks:
                    qbf = qb[qi % 3]
                    qi += 1
                    if kind == "a":
                        qload = (qbf, QA[idx, :, t0:t0 + T], 128, T)
                        subs = [(kT, slice(0, 128), 2 * idx, va, idx // 2, 64 ** -0.5), (kB, slice(0, 128), 2 * idx + 1, va, idx // 2, 64 ** -0.5)]
                    else:
                        qload = (qbf, QM[idx, :, t0:t0 + T], 96, T)
                        subs = [(kbf, slice(0, 96), 8 + idx, vm, idx, 96 ** -0.5)]
                    kcs = list(range(2)) if s == 1 else list(range(NCH))
                    for si, (kbuf_, rows, hg, vbuf, vcol, sc) in enumerate(subs):
                        g = len(groups)
                        groups.append((hg, t0, T))
                        for n, kc in enumerate(kcs):
                            tasks.append(dict(kb=kbuf_, rows=rows, kc=kc, qb=qbf, T=T, vb=vbuf, vcol=vcol, first=(n == 0),
                                              last=(n == len(kcs) - 1), grp=g, sc=sc,
                                              kload=kloads if (first_in_unit and si == 0 and n == 0) else None,
                                              qload=qload if (si == 0 and n == 0) else None))
                    first_in_unit = False

            def emit_qk(i):
                t = tasks[i]
                if t["kload"] is not None:
                    for (kbf, src, rws) in t["kload"]:
                        K.ld(kbf.t[rws, :], src, [kbf.d])
                if t["qload"] is not None:
                    qbf, src, R_, T = t["qload"]
                    K.ld(qbf.t[0:R_, 0:T], src[0:R_, :], [qbf.d])
                ps = SB_[i % 3]
                kc, T = t["kc"], t["T"]
                K.mm(ps.t[:, 0:T], t["kb"].t[t["rows"], kc * 128:(kc + 1) * 128], t["qb"].t[t["rows"], 0:T], True, True,
                     [t["kb"].d, t["qb"].d], [ps.d])

            n = len(tasks)
            for i in range(min(2, n)):
                emit_qk(i)
            for i in range(n):
                t = tasks[i]
                T = t["T"]
                ps = SB_[i % 3]
                p = pb[i % 4]
                K.act(p.t[:, 0:T], ps.t[:, 0:T], AF.Exp, [ps.d], [p.d], scale=t["sc"])
                if i + 2 < n:
                    emit_qk(i + 2)
                po = OB_[t["grp"] % 2]
                vc = t["vcol"]
                K.mm(po.t[0:65, 0:T], t["vb"].t[:, t["kc"], vc * 65:(vc + 1) * 65], p.t[:, 0:T], t["first"], t["last"],
                     [t["vb"].d, p.d], [po.d])
                if t["last"]:
                    g = t["grp"]
                    hg, t0, _ = groups[g]
                    rs, osb_, ao_ = rsb[g % 2], osb[g % 2], aob[g % 2]
                    K.recip(rs.t[64:65, 0:T], po.t[64:65, 0:T], [po.d], [rs.d])
                    K.mm(BCB.t[0:64, 0:T], cst(CI_ONES)[64:65, 0:64], rs.t[64:65, 0:T], True, True, [rs.d, CS.d], [BCB.d])
                    K.act(osb_.t[0:64, 0:T], po.t[0:64, 0:T], AF.Copy, [po.d], [osb_.d])
                    K.tt(DVE, ao_.t[0:64, 0:T], osb_.t[0:64, 0:T], BCB.t[0:64, 0:T], ALU.mult, [osb_.d, BCB.d], [ao_.d])
                    K.st(AO[hg * 64:(hg + 1) * 64, t0:t0 + T], ao_.t[0:64, 0:T], [ao_.d])

    def load_w13(layer, w13=None):
        if w13 is None:
            w13 = K.sb("w13", [128, 8, 2 * FFH], BF16)
        for k in range(8):
            for hh in range(2):
                K.ldcast(w13.t[:, k, hh * FFH:(hh + 1) * FFH], w13D[layer, k * 128:(k + 1) * 128, hh * FFH:(hh + 1) * FFH], [w13.d])
        return w13

    def phaseC1(layer, blocks, AOsrc, nK, woutD, Xsrc, xoff):
        with K.phase():
            w13 = K.sb("w13", [128, 8, 2 * FFH], BF16)
            with K.phase():
                wo = K.sb("wo", [128, nK, D], BF16)
                for k in range(nK):
                    K.ldcast(wo.t[:, k, :], woutD[k * 128:(k + 1) * 128, :], [wo.d])
                load_w13(layer, w13)
                phaseC1_body(layer, blocks, AOsrc, nK, wo, Xsrc, xoff)
            phaseC2(layer, K.c2_args[0], K.c2_args[1], K.c2_args[2], K.c2_args[3], w13)

    def phaseC1_body(layer, blocks, AOsrc, nK, wo, Xsrc, xoff):
        if True:
            gts = {}
            for s in set(b[2] for b in blocks):
                gts[s] = K.sb("gt", [128, D], F32)
                K.ld(gts[s].t[:, :], MODROW[layer, 0, s, :].partition_broadcast(128), [gts[s].d])
            aob = [K.sb("ao", [128, nK, 512], BF16) for _ in range(1)]
            xb = [K.sb("xc", [128, 4, D], F32) for _ in range(1)]
            tmpb = [K.sb("tmp", [128, 512], F32) for _ in range(2)]
            scr = (K.sb("junk", [128, D], BF16), K.sb("ss", [128, 4], F32), K.sb("xs", [128, 4, D], F32))
            uTb = [K.sb("uT", [128, 8, 512], BF16) for _ in range(1)]
            it = 0
            for bi, (t0, T, s) in enumerate(blocks):
                nch = T // 128
                ao = aob[0]
                K.ld(ao.t[:, :, 0:T], AOsrc[:, t0:t0 + T].rearrange("(k p) t -> p k t", p=128), [ao.d])
                xt = xb[0]
                K.ld(xt.t[:, 0:nch, :], Xsrc[t0 - xoff:t0 - xoff + T, :].rearrange("(c p) d -> p c d", p=128), [xt.d])
                for c in range(nch):
                    for n2 in range(2):
                        ps = PS[2 + it % 4]
                        tmp = tmpb[it % 2]
                        it += 1
                        for k in range(nK):
                            K.mm(ps.t[:, :], ao.t[:, k, c * 128:(c + 1) * 128], wo.t[:, k, n2 * 512:(n2 + 1) * 512], k == 0, k == nK - 1,
                                 [ao.d, wo.d], [ps.d])
                        K.tt(DVE, tmp.t[:, :], ps.t[:, :], gts[s].t[:, n2 * 512:(n2 + 1) * 512], ALU.mult, [ps.d, gts[s].d], [tmp.d])
                        K.tt(DVE, xt.t[:, c, n2 * 512:(n2 + 1) * 512], xt.t[:, c, n2 * 512:(n2 + 1) * 512], tmp.t[:, :], ALU.add,
                             [xt.d, tmp.d], [xt.d])
                K.ld(X1[t0:t0 + T, :].rearrange("(c p) d -> p c d", p=128), xt.t[:, 0:nch, :], (), R=[xt.d])
                uT = uTb[0]
                norm_to_uT(xt, nch, T, layer, 1, s, uT, scr, [PS[0], PS[1]], use_pool=False)
                K.ld(U2[:, t0:t0 + T].rearrange("(k p) t -> p k t", p=128), uT.t[:, :, 0:T], (), R=[uT.d])

    def phaseC2(layer, blocks, Xdst, xoff, is_out, w13=None):
        with K.phase():
            if w13 is None:
                w13 = load_w13(layer)
            w2 = K.sb("w2", [128, 22, D], BF16)
            for j in range(22):
                K.ldcast(w2.t[:, j, :], w2D[layer, j * 128:(j + 1) * 128, :], [w2.d])
            gt1 = K.sb("gt", [128, D], F32)
            gts = {0: gt1, 1: gt1}
            ub = [K.sb("u", [128, 8, 512], BF16) for _ in range(1)]
            xb = [K.sb("xf", [128, 4, D], F32) for _ in range(1)]
            hb = K.sb("h", [128, 22, 512], BF16)
            sgb = [K.sb("sg", [128, 512], F32) for _ in range(2)]
            tmpb = [K.sb("tmp", [128, 512], F32) for _ in range(2)]
            it = 0
            for bi, (t0, T, s) in enumerate(blocks):
                nch = T // 128
                u = ub[0]
                K.ld(u.t[:, :, 0:T], U2[:, t0:t0 + T].rearrange("(k p) t -> p k t", p=128), [u.d])
                if bi == 0 or blocks[bi - 1][2] != s:
                    K.ld(gt1.t[:, :], MODROW[layer, 1, s, :].partition_broadcast(128), [gt1.d])
                xt = xb[0]
                K.ld(xt.t[:, 0:nch, :], X1[t0:t0 + T, :].rearrange("(c p) d -> p c d", p=128), [xt.d])
                for j in range(22):
                    psg = PS[(2 * j) % 4]
                    psu = PS[(2 * j + 1) % 4]
                    for k in range(8):
                        K.mm(psg.t[:, 0:T], w13.t[:, k, j * 128:(j + 1) * 128], u.t[:, k, 0:T], k == 0, k == 7, [w13.d, u.d], [psg.d])
                    for k in range(8):
                        K.mm(psu.t[:, 0:T], w13.t[:, k, FFH + j * 128:FFH + (j + 1) * 128], u.t[:, k, 0:T], k == 0, k == 7, [w13.d, u.d], [psu.d])
                    sg = sgb[j % 2]
                    K.act(sg.t[:, 0:T], psg.t[:, 0:T], AF.Silu, [psg.d], [sg.d])
                    K.tt(DVE, hb.t[:, j, 0:T], sg.t[:, 0:T], psu.t[:, 0:T], ALU.mult, [sg.d, psu.d], [hb.d])
                for c in range(nch):
                    for n2 in range(2):
                        ps = PS[4 + it % 4]
                        tmp = tmpb[it % 2]
                        it += 1
                        for j in range(22):
                            K.mm(ps.t[:, :], hb.t[:, j, c * 128:(c + 1) * 128], w2.t[:, j, n2 * 512:(n2 + 1) * 512], j == 0, j == 21,
                                 [hb.d, w2.d], [ps.d])
                        K.tt(DVE, tmp.t[:, :], ps.t[:, :], gts[s].t[:, n2 * 512:(n2 + 1) * 512], ALU.mult, [ps.d, gts[s].d], [tmp.d])
                        K.tt(POOL, xt.t[:, c, n2 * 512:(n2 + 1) * 512], xt.t[:, c, n2 * 512:(n2 + 1) * 512], tmp.t[:, :], ALU.add,
                             [xt.d, tmp.d], [xt.d])
                K.em.dma(em.pool, Xdst[t0 - xoff:t0 - xoff + T, :].rearrange("(c p) d -> p c d", p=128), xt.t[:, 0:nch, :],
                         reads=[xt.d], is_output=is_out)

    swin = K.dram("ssm_w_in", [D, 5664], F32, "ExternalInput")
    swout = K.dram("ssm_w_out", [2048, D], F32, "ExternalInput")
    convw_col = K.dram("convw_col", [128, 60], F32, "ExternalInput")
    convb_col = K.dram("convb_col", [128, 12], F32, "ExternalInput")
    rows1 = K.dram("rows1", [NR1], F32, "ExternalInput")
    sqc2 = K.dram("sqc2", [128, 2 * 128 + 4], F32, "ExternalInput")
    selc = K.dram("selc", [16, 2048], F32, "ExternalInput")
    SZ = K.dram("SZ", [NT, D], F32)
    SRG = K.dram("SRG", [NT, D], F32)
    DTs = K.dram("DTs", [NT, 32], F32)
    RVs = K.dram("RVs", [NT, D], BF16)
    XBC = K.dram("XBC", [1536, NT], BF16)
    RQ = K.dram("RQ", [4, 128, NT], BF16)
    RK = K.dram("RK", [4, 128, NT], BF16)
    RKT = K.dram("RKT", [NT, 512], BF16)
    XS = K.dram("XS", [NT, D], F32)
    BT = K.dram("BT", [NT, 256], BF16)
    BFs = K.dram("BFs", [256, NT], BF16)
    CFs = K.dram("CFs", [256, NT], BF16)
    YF = K.dram("YF", [NT, 2048], F32)
    YS = K.dram("YS", [2048, NT], BF16)
    ONEB = K.gsb("oneb", [128, 1], F32)
    K.memset(DVE, ONEB.t[:, :], 1.0, [ONEB.d])

    def softplus_small(x, tmp, R, W_):
        K.ts(DVE, tmp, x, -1.0, ALU.mult, R, W_)
        K.tt(DVE, tmp, tmp, x, ALU.max, R + W_, W_)
        K.act(tmp, tmp, AF.Exp, W_, W_, scale=-1.0)
        K.act(tmp, tmp, AF.Ln, W_, W_, bias=ONEB.t[0:x.shape[0], 0:1])
        K.stt(x, x, 0.0, tmp, ALU.max, ALU.add, R + W_, R)

    def l1_phaseA():
        with K.phase():
            W1 = K.sb("w1", [128, 8, 5664], BF16)
            for k in range(8):
                for (a, b) in ((0, 1888), (1888, 3776), (3776, 5664)):
                    K.ldcast(W1.t[:, k, a:b], swin[k * 128:(k + 1) * 128, a:b], [W1.d])
            xb = [K.sb("xa", [128, 4, D], F32)]
            scr = (K.sb("junk", [128, D], BF16), K.sb("ss", [128, 4], F32), K.sb("xs", [128, 4, D], F32))
            uTb = [K.sb("uT", [128, 8, 512], BF16) for _ in range(2)]
            ropeb = [K.sb("rope", [128, 2, 512], F32) for _ in range(2)]
            qnb = [K.sb("qn", [128, 512], F32) for _ in range(2)]
            t1b = [K.sb("t1", [128, 512], F32) for _ in range(2)]
            t2b = [K.sb("t2", [128, 512], F32) for _ in range(2)]
            ofb = [K.sb("of", [128, 512], F32) for _ in range(2)]
            obb = [K.sb("ob", [128, 512], BF16) for _ in range(3)]
            tmz = [K.sb("tmz", [128, D], F32) for _ in range(3)]
            rvt = [K.sb("rvt", [128, D], BF16) for _ in range(2)]
            rktb = [K.sb("rkt", [128, 512], BF16) for _ in range(2)]
            dtt = [K.sb("dtt", [128, 32], F32) for _ in range(2)]
            cnt = {"p": 0, "n": 0, "o": 0, "z": 0, "t": 0}

            def pbank(banks):
                cnt["p"] += 1
                return banks[cnt["p"] % len(banks)]

            for bi, (t0, T, s) in enumerate(K.blocks):
                nch = T // 128
                xt = xb[0]
                K.ld(xt.t[:, 0:nch, :], X2[t0:t0 + T, :].rearrange("(c p) d -> p c d", p=128), [xt.d])
                rp = ropeb[bi % 2]
                K.ld(rp.t[:, :, 0:T], ropeD[0:2, :, t0:t0 + T].rearrange("f p t -> p f t"), [rp.d])
                uT = uTb[bi % 2]
                norm_to_uT(xt, nch, T, 1, 0, s, uT, scr, [PS[0], PS[1]])
                for ch in range(12):
                    ps = pbank([PS[2], PS[3]])
                    for k in range(8):
                        K.mm(ps.t[:, 0:T], W1.t[:, k, 1024 + ch * 128:1024 + (ch + 1) * 128], uT.t[:, k, 0:T], k == 0, k == 7, [W1.d, uT.d], [ps.d])
                    ob = obb[cnt["o"] % 3]
                    cnt["o"] += 1
                    K.act(ob.t[:, 0:T], ps.t[:, 0:T], AF.Copy, [ps.d], [ob.d])
                    K.st(XBC[ch * 128:(ch + 1) * 128, t0:t0 + T], ob.t[:, 0:T], [ob.d])
                for kind in range(2):
                    for ch in range(4):
                        col0 = 2592 + kind * 512 + ch * 128
                        ps = pbank([PS[2], PS[3]])
                        for k in range(8):
                            K.mm(ps.t[:, 0:T], W1.t[:, k, col0:col0 + 128], uT.t[:, k, 0:T], k == 0, k == 7, [W1.d, uT.d], [ps.d])
                        i = cnt["n"] % 2
                        cnt["n"] += 1
                        qn, t1, t2, of = qnb[i], t1b[i], t2b[i], ofb[i]
                        K.act(qn.t[:, 0:T], ps.t[:, 0:T], AF.Copy, [ps.d], [qn.d], scale=(1.0 if kind == 0 else 0.125))
                        psr = PS[0]
                        K.mm(psr.t[:, 0:T], cst(CI_ROT), qn.t[:, 0:T], True, True, [qn.d, CS.d], [psr.d])
                        K.tt(DVE, t1.t[:, 0:T], qn.t[:, 0:T], rp.t[:, 0, 0:T], ALU.mult, [qn.d, rp.d], [t1.d])
                        K.tt(DVE, t2.t[:, 0:T], psr.t[:, 0:T], rp.t[:, 1, 0:T], ALU.mult, [psr.d, rp.d], [t2.d])
                        K.tt(POOL, of.t[:, 0:T], t1.t[:, 0:T], t2.t[:, 0:T], ALU.add, [t1.d, t2.d], [of.d])
                        ob = obb[cnt["o"] % 3]
                        cnt["o"] += 1
                        K.act(ob.t[:, 0:T], of.t[:, 0:T], AF.Copy, [of.d], [ob.d])
                        K.st((RQ if kind == 0 else RK)[ch, :, t0:t0 + T], ob.t[:, 0:T], [ob.d])
                        if kind == 1:
                            for c in range(nch):
                                K.tr(PS[4 + c].t[:, ch * 128:(ch + 1) * 128], of.t[:, c * 128:(c + 1) * 128], cst(CI_IDENT), [of.d, CS.d], [PS[4 + c].d])
                for c in range(nch):
                    rkt = rktb[c % 2]
                    K.copy(DVE, rkt.t[:, :], PS[4 + c].t[:, :], [PS[4 + c].d], [rkt.d])
                    K.st(RKT[t0 + c * 128:t0 + (c + 1) * 128, :], rkt.t[:, :], [rkt.d])
                TMB = [PS[2], PS[3], PS[4], PS[5], PS[6], PS[7]]
                for c in range(nch):
                    r0 = t0 + c * 128
                    for (col0, kind) in ((0, "z"), (4640, "g"), (3616, "v")):
                        if kind == "v":
                            dst = rvt[cnt["t"] % 2]
                            cnt["t"] += 1
                        else:
                            dst = tmz[cnt["z"] % 3]
                            cnt["z"] += 1
                        for half in range(2):
                            ps = pbank(TMB)
                            for k in range(8):
                                K.mm(ps.t[:, :], uT.t[:, k, c * 128:(c + 1) * 128], W1.t[:, k, col0 + half * 512:col0 + (half + 1) * 512],
                                     k == 0, k == 7, [W1.d, uT.d], [ps.d])
                            if kind == "v":
                                K.copy(DVE, dst.t[:, half * 512:(half + 1) * 512], ps.t[:, :], [ps.d], [dst.d])
                            else:
                                K.act(dst.t[:, half * 512:(half + 1) * 512], ps.t[:, :], AF.Silu, [ps.d], [dst.d])
                        K.st({"z": SZ, "g": SRG, "v": RVs}[kind][r0:r0 + 128, :], dst.t[:, :], [dst.d])
                    ps = pbank(TMB)
                    for k in range(8):
                        K.mm(ps.t[:, 0:32], uT.t[:, k, c * 128:(c + 1) * 128], W1.t[:, k, 2560:2592], k == 0, k == 7, [W1.d, uT.d], [ps.d])
                    dd = dtt[c % 2]
                    K.copy(DVE, dd.t[:, :], ps.t[:, 0:32], [ps.d], [dd.d])
                    K.st(DTs[r0:r0 + 128, :], dd.t[:, :], [dd.d])

    def l1_phaseV():
        with K.phase():
            cwc = K.sb("cwc", [128, 60], F32)
            K.ld(cwc.t[:, :], convw_col[:, :], [cwc.d])
            cbc = K.sb("cbc", [128, 12], F32)
            K.ld(cbc.t[:, :], convb_col[:, :], [cbc.d])
            cbr = K.sb("cbr", [1, 1536], F32)
            K.ld(cbr.t[:, :], rows1[R_CONVB:R_CONVB + 1536].partition_broadcast(1), [cbr.d])
            DG = K.sb("dg", [128, 60, 128], BF16)
            for idx in range(60):
                K.ts(DVE if idx % 2 == 0 else POOL, DG.t[:, idx, :], cst(CI_IDENT), cwc.t[:, idx:idx + 1], ALU.mult, [CS.d, cwc.d], [DG.d])
            xwb = [K.sb("xw", [128, 12, 516], BF16) for _ in range(2)]
            obb = [K.sb("ob", [128, 512], BF16) for _ in range(2)]
            xst = [K.sb("xst", [128, D], F32) for _ in range(2)]
            btt = [K.sb("btt", [128, 256], BF16) for _ in range(2)]
            ones_row = cst(CI_ONES)[0:1, 0:128]
            cnt = {"p": 0}
            BK = [PS[0], PS[1], PS[2], PS[3], PS[4], PS[5], PS[6], PS[7]]

            def pbank():
                cnt["p"] += 1
                return BK[cnt["p"] % 8]

            for bi, (t0, T, s) in enumerate(K.blocks):
                nch = T // 128
                seg0, seg1 = (0, CTX) if s == 1 else (CTX, NT)
                lo, hi = max(t0 - 2, seg0), min(t0 + T + 2, seg1)
                xw = xwb[bi % 2]
                K.memset(POOL, xw.t[:, :, :], 0.0, [xw.d])
                K.ld(xw.t[:, :, lo - (t0 - 2):hi - (t0 - 2)], XBC[:, lo:hi].rearrange("(c p) t -> p c t", p=128), [xw.d])
                for ch in range(8, 12):
                    ps = pbank()
                    for k in range(5):
                        K.mm(ps.t[:, 0:T], DG.t[:, k * 12 + ch, :], xw.t[:, ch, k:k + T], k == 0, k == 4, [DG.d, xw.d], [ps.d])
                    ob = obb[ch % 2]
                    K.act(ob.t[:, 0:T], ps.t[:, 0:T], AF.Silu, [ps.d, cbc.d], [ob.d], bias=cbc.t[:, ch:ch + 1])
                    dstD = BFs if ch < 10 else CFs
                    r = (ch - 8) % 2
                    K.st(dstD[r * 128:(r + 1) * 128, t0:t0 + T], ob.t[:, 0:T], [ob.d])
                for c in range(nch):
                    r0 = t0 + c * 128
                    banks = [pbank(), pbank(), pbank()]
                    for ch in range(10):
                        tgt = banks[ch // 4]
                        o_ap = tgt.t[:, (ch % 4) * 128:(ch % 4 + 1) * 128]
                        for k in range(5):
                            K.mm(o_ap, xw.t[:, ch, c * 128 + k:c * 128 + k + 128], DG.t[:, k * 12 + ch, :], k == 0, False, [DG.d, xw.d], [tgt.d])
                        K.mm(o_ap, ones_row, cbr.t[0:1, ch * 128:(ch + 1) * 128], False, True, [CS.d, cbr.d], [tgt.d])
                    xo = xst[c % 2]
                    for hh in range(2):
                        K.act(xo.t[:, hh * 512:(hh + 1) * 512], banks[hh].t[:, :], AF.Silu, [banks[hh].d], [xo.d])
                    K.st(XS[r0:r0 + 128, :], xo.t[:, :], [xo.d])
                    bo = btt[c % 2]
                    K.act(bo.t[:, :], banks[2].t[:, 0:256], AF.Silu, [banks[2].d], [bo.d])
                    K.st(BT[r0:r0 + 128, :], bo.t[:, :], [bo.d])

    def l1_scan(dirn):
        fin = dirn == 1
        with K.phase():
            C2 = K.sb("c2", [128, 2 * 128 + 4], F32)
            K.ld(C2.t[:, :], sqc2[:, :], [C2.d])
            SEL = K.sb("sel", [16, 2048], F32)
            K.ld(SEL.t[:, :], selc[:, :], [SEL.d])
            IDX = C2.t[:, dirn * 128:(dirn + 1) * 128]
            colA = C2.t[:, 256 + dirn:256 + dirn + 1]
            colE = C2.t[:, 258 + dirn:258 + dirn + 1]
            MASK = cst(CI_TRI) if dirn == 0 else cst(CI_TRIT)

            def brow(name, off, n):
                b = K.sb(name, [128, n], F32)
                K.ld(b.t[:, :], rows1[off:off + n].partition_broadcast(128), [b.d])
                return b

            DTB = brow("dtb", R_DTB + dirn * 16, 16)
            ANEG = brow("aneg", R_ALOG + dirn * 16, 16)
            K.act(ANEG.t[:, :], ANEG.t[:, :], AF.Exp, [ANEG.d], [ANEG.d])
            K.ts(DVE, ANEG.t[:, :], ANEG.t[:, :], -1.0, ALU.mult, [ANEG.d], [ANEG.d])
            LG = brow("lg", R_RLOG + dirn * 8, 8)
            lgt = K.sb("lgt", [128, 8], F32)
            K.ts(DVE, LG.t[:, :], LG.t[:, :], -1.0, ALU.mult, [LG.d], [LG.d])
            softplus_small(LG.t[:, :], lgt.t[:, :], [LG.d], [lgt.d])
            K.ts(DVE, LG.t[:, :], LG.t[:, :], -1.0, ALU.mult, [LG.d], [LG.d])
            LM = K.sb("lm", [128, 8, 128], F32)
            for h in range(8):
                K.ts(DVE, LM.t[:, h, :], IDX, LG.t[:, h:h + 1], ALU.mult, [C2.d, LG.d], [LM.d])
            K.act(LM.t[:, :, :], LM.t[:, :, :], AF.Exp, [LM.d], [LM.d])
            K.tt(DVE, LM.t[:, :, :], LM.t[:, :, :], MASK.unsqueeze(1).broadcast_to([128, 8, 128]), ALU.mult, [LM.d, CS.d], [LM.d])
            EAr = K.sb("ear", [128, 8], F32)
            K.ts(DVE, EAr.t[:, :], LG.t[:, :], colA, ALU.mult, [LG.d, C2.d], [EAr.d])
            K.act(EAr.t[:, :], EAr.t[:, :], AF.Exp, [EAr.d], [EAr.d])
            DEr = K.sb("der", [128, 8], F32)
            K.ts(DVE, DEr.t[:, :], LG.t[:, :], colE, ALU.mult, [LG.d, C2.d], [DEr.d])
            K.act(DEr.t[:, :], DEr.t[:, :], AF.Exp, [DEr.d], [DEr.d])
            CDR = K.sb("cdr", [128, 4], F32)
            lg2 = LG.t[:, :].rearrange("p (q two) -> p q two", two=2)
            K.ts(DVE, CDR.t[0:64, :], lg2[0:64, :, 0], 128.0, ALU.mult, [LG.d], [CDR.d])
            K.ts(DVE, CDR.t[64:128, :], lg2[64:128, :, 1], 128.0, ALU.mult, [LG.d], [CDR.d])
            K.act(CDR.t[:, :], CDR.t[:, :], AF.Exp, [CDR.d], [CDR.d])
            if fin:
                DSK = brow("dsk", R_DSK, 1024)
                WN = brow("wn", R_SSDN, 1024)
                RN = brow("rn", R_RETN, 1024)
            S = [K.sb("S", [128, 512], F32) for _ in range(2)]
            Sbf = [K.sb("Sbf", [128, 512], BF16) for _ in range(2)]
            SR = K.sb("SR", [128, 4, 128], F32)
            SRbf = K.sb("SRbf", [128, 4, 128], BF16)
            for b in S + Sbf:
                K.memset(DVE, b.t[:, :], 0.0, [b.d])
            K.memset(DVE, SR.t[:, :, :], 0.0, [SR.d])
            K.memset(DVE, SRbf.t[:, :, :], 0.0, [SRbf.d])
            NB = 3
            NF = 2
            xsb = [K.sb("xs", [128, D], F32) for _ in range(NB)]
            btb = [K.sb("bt", [128, 256], BF16) for _ in range(NB)]
            bfb = [K.sb("bf", [128, 2, 128], BF16) for _ in range(NB)]
            cfb = [K.sb("cf", [128, 2, 128], BF16) for _ in range(NB)]
            dtb_ = [K.sb("dt", [128, 16], F32) for _ in range(NB)]
            rqb = [K.sb("rq", [128, 4, 128], BF16) for _ in range(NB)]
            rkb = [K.sb("rk", [128, 4, 128], BF16) for _ in range(NB)]
            rktb = [K.sb("rkt", [128, 512], BF16) for _ in range(NB)]
            rvb = [K.sb("rv", [128, D], BF16) for _ in range(NB)]
            if fin:
                yfb = [K.sb("yf", [128, 2048], F32) for _ in range(NF)]
                szb = [K.sb("sz", [128, D], F32) for _ in range(NF)]
                sgb = [K.sb("srg", [128, D], F32) for _ in range(NF)]
            smb = [K.sb("sm", [128, 8, 16], F32) for _ in range(2)]
            at = K.sb("at", [16, 256], F32)
            E = K.sb("E", [128, 16, 128], F32)
            Gm = K.sb("Gm", [128, 2, 128], F32)
            Mb = [K.sb("M", [128, 16, 128], BF16) for _ in range(2)]
            MRb = [K.sb("MR", [128, 8, 128], BF16) for _ in range(2)]
            Vdb = [K.sb("Vd", [128, D], BF16) for _ in range(2)]
            Vdecb = [K.sb("Vdec", [128, D], BF16) for _ in range(2)]
            RVdb = [K.sb("RVd", [128, D], BF16) for _ in range(2)]
            yo = K.sb("yo", [128, D], F32)
            ydb = [K.sb("yd", [128, 2048], F32) for _ in range(2)]
            if fin:
                junk = K.sb("junk", [128, D], F32)
                st8 = K.sb("st8", [128, 8, 8], F32)
                ynb = K.sb("yn", [128, 2048], F32)
                ysb = [K.sb("ys", [128, 4, 128], BF16) for _ in range(2)]
            cnt = {"p": 0}

            def pbank():
                cnt["p"] += 1
                return PS[cnt["p"] % 8]

            order = list(range(NCH)) if dirn == 0 else [1, 0] + list(range(NCH - 1, 1, -1))
            NO = len(order)

            def loads(ci):
                c = order[ci]
                t0 = c * 128
                i = ci % NB
                K.ld(xsb[i].t[:, :], XS[t0:t0 + 128, :], [xsb[i].d])
                K.ld(btb[i].t[:, :], BT[t0:t0 + 128, :], [btb[i].d])
                K.ld(bfb[i].t[:, :, :], BFs[:, t0:t0 + 128].rearrange("(g n) t -> n g t", n=128), [bfb[i].d])
                K.ld(cfb[i].t[:, :, :], CFs[:, t0:t0 + 128].rearrange("(g n) t -> n g t", n=128), [cfb[i].d])
                K.ld(dtb_[i].t[:, :], DTs[t0:t0 + 128, dirn * 16:(dirn + 1) * 16], [dtb_[i].d])
                K.ld(rqb[i].t[:, :, :], RQ[:, :, t0:t0 + 128].rearrange("c p t -> p c t"), [rqb[i].d])
                K.ld(rkb[i].t[:, :, :], RK[:, :, t0:t0 + 128].rearrange("c p t -> p c t"), [rkb[i].d])
                K.ld(rktb[i].t[:, :], RKT[t0:t0 + 128, :], [rktb[i].d])
                K.ld(rvb[i].t[:, :], RVs[t0:t0 + 128, :], [rvb[i].d])

            def loads_fin(ci):
                c = order[ci]
                t0 = c * 128
                i = ci % NF
                K.ld(yfb[i].t[:, :], YF[t0:t0 + 128, :], [yfb[i].d])
                K.ld(szb[i].t[:, :], SZ[t0:t0 + 128, :], [szb[i].d])
                K.ld(sgb[i].t[:, :], SRG[t0:t0 + 128, :], [sgb[i].d])

            def bc3(ap2, n):
                return ap2.unsqueeze(2).broadcast_to([128, ap2.shape[1], n])

            def partA(ci):
                if ci + 1 < NO:
                    loads(ci + 1)
                i = ci % NB
                j2 = ci % 2
                xs_c, bf_c, cf_c, dt_c = xsb[i], bfb[i], cfb[i], dtb_[i]
                rq_c, rk_c, rv_c = rqb[i], rkb[i], rvb[i]
                sm = smb[j2]
                M, MR, Vd, Vdec, RVd = Mb[j2], MRb[j2], Vdb[j2], Vdecb[j2], RVdb[j2]
                sp, tmpv, la, Acol, Atot, expA, dece, cd = [sm.t[:, j, :] for j in range(8)]
                smd = [sm.d]
                K.tt(DVE, sp, dt_c.t[:, :], DTB.t[:, :], ALU.add, [dt_c.d, DTB.d], smd)
                softplus_small(sp, tmpv, smd, smd)
                K.tt(DVE, la, sp, ANEG.t[:, :], ALU.mult, smd + [ANEG.d], smd)
                psc = pbank()
                K.mm(psc.t[:, 0:16], MASK, la, True, True, smd + [CS.d], [psc.d])
                K.mm(psc.t[:, 16:32], cst(CI_ONES), la, True, True, smd + [CS.d], [psc.d])
                K.mm(psc.t[0:16, 128:256], la, MASK, True, True, smd + [CS.d], [psc.d])
                K.copy(DVE, sm.t[:, 3:5, :], psc.t[:, 0:32].rearrange("p (a b) -> p a b", b=16), [psc.d], smd)
                K.copy(DVE, at.t[:, 0:128], psc.t[0:16, 128:256], [psc.d], [at.d])
                K.ts(DVE, at.t[:, 128:256], at.t[:, 0:128], -1.0, ALU.mult, [at.d], [at.d])
                K.act(expA, Acol, AF.Exp, smd, smd)
                K.tt(DVE, dece, Atot, Acol, ALU.subtract, smd, smd)
                K.act(dece, dece, AF.Exp, smd, smd)
                K.act(cd, Atot, AF.Exp, smd, smd)
                K.tt(DVE, tmpv, sp, dece, ALU.mult, smd, smd)
                xs3 = xs_c.t[:, :].rearrange("p (h e) -> p h e", e=64)
                K.tt(DVE, Vd.t[:, :].rearrange("p (h e) -> p h e", e=64), xs3, bc3(sp, 64), ALU.mult, [xs_c.d] + smd, [Vd.d])
                K.tt(POOL, Vdec.t[:, :].rearrange("p (h e) -> p h e", e=64), xs3, bc3(tmpv, 64), ALU.mult, [xs_c.d] + smd, [Vdec.d])
                for q in range(4):
                    psd = pbank()
                    for hh in range(4):
                        h = q * 4 + hh
                        o_ap = psd.t[:, hh * 128:(hh + 1) * 128]
                        K.mm(o_ap, SEL.t[0:16, h * 128:(h + 1) * 128], at.t[0:16, 0:128], True, False, [SEL.d, at.d], [psd.d])
                        K.mm(o_ap, at.t[0:16, 128:256], SEL.t[0:16, h * 128:(h + 1) * 128], False, True, [SEL.d, at.d], [psd.d])
                    K.tt(DVE, E.t[:, q * 4:(q + 1) * 4, :], psd.t[:, :].rearrange("p (h i) -> p h i", i=128),
                         MASK.unsqueeze(1).broadcast_to([128, 4, 128]), ALU.mult, [psd.d, CS.d], [E.d])
                K.act(E.t[:, :, :], E.t[:, :, :], AF.Exp, [E.d], [E.d])
                psg = pbank()
                for g in range(2):
                    K.mm(psg.t[:, g * 128:(g + 1) * 128], bf_c.t[:, g, :], cf_c.t[:, g, :], True, True, [bf_c.d, cf_c.d], [psg.d])
                K.tt(DVE, Gm.t[:, :, :], psg.t[:, 0:256].rearrange("p (g i) -> p g i", i=128),
                     MASK.unsqueeze(1).broadcast_to([128, 2, 128]), ALU.mult, [psg.d, CS.d], [Gm.d])
                for g in range(2):
                    K.tt(DVE if g == 0 else POOL, M.t[:, g * 8:(g + 1) * 8, :], E.t[:, g * 8:(g + 1) * 8, :],
                         Gm.t[:, g, :].unsqueeze(1).broadcast_to([128, 8, 128]), ALU.mult, [E.d, Gm.d], [M.d])
                psgr = [pbank(), pbank()]
                for h in range(8):
                    rows = slice((h % 2) * 64, (h % 2) * 64 + 64)
                    K.mm(psgr[h % 2].t[:, (h // 2) * 128:(h // 2 + 1) * 128], rk_c.t[rows, h // 2, :], rq_c.t[rows, h // 2, :], True, True,
                         [rk_c.d, rq_c.d], [psgr[h % 2].d])
                for par in range(2):
                    K.tt(DVE, MR.t[:, :, :].rearrange("p (b two) i -> p b two i", two=2)[:, :, par, :],
                         psgr[par].t[:, :].rearrange("p (h i) -> p h i", i=128),
                         LM.t[:, :, :].rearrange("p (b two) i -> p b two i", two=2)[:, :, par, :],
                         ALU.mult, [psgr[par].d, LM.d], [MR.d])
                K.tt(DVE, RVd.t[:, :].rearrange("p (h e) -> p h e", e=128), rv_c.t[:, :].rearrange("p (h e) -> p h e", e=128),
                     bc3(DEr.t[:, :], 128), ALU.mult, [rv_c.d, DEr.d], [RVd.d])

            def partB(ci):
                if fin and ci + 1 < NO:
                    loads_fin(ci + 1)
                c = order[ci]
                t0 = c * 128
                i = ci % NB
                j2 = ci % 2
                xs_c, bt_c, cf_c = xsb[i], btb[i], cfb[i]
                rq_c, rkt_c, rv_c = rqb[i], rktb[i], rvb[i]
                sm = smb[j2]
                M, MR, Vd, Vdec, RVd = Mb[j2], MRb[j2], Vdb[j2], Vdecb[j2], RVdb[j2]
                sp, tmpv, la, Acol, Atot, expA, dece, cd = [sm.t[:, j, :] for j in range(8)]
                smd = [sm.d]
                yd = ydb[ci % 2]
                for g in range(2):
                    pso = pbank()
                    K.mm(pso.t[:, :], cf_c.t[:, g, :], Sbf[g].t[:, :], True, True, [cf_c.d, Sbf[g].d], [pso.d])
                    K.tt(DVE, yo.t[:, g * 512:(g + 1) * 512].rearrange("p (h e) -> p h e", e=64),
                         pso.t[:, :].rearrange("p (h e) -> p h e", e=64), bc3(expA[:, g * 8:(g + 1) * 8], 64), ALU.mult,
                         [pso.d] + smd, [yo.d])
                for g in range(2):
                    psy = pbank()
                    for hl in range(8):
                        h = g * 8 + hl
                        K.mm(psy.t[:, hl * 64:(hl + 1) * 64], M.t[:, h, :], Vd.t[:, h * 64:(h + 1) * 64], True, True, [M.d, Vd.d], [psy.d])
                    K.tt(DVE, yd.t[:, g * 512:(g + 1) * 512], psy.t[:, :], yo.t[:, g * 512:(g + 1) * 512], ALU.add, [psy.d, yo.d], [yd.d])
                for g in range(2):
                    psd2 = pbank()
                    K.mm(psd2.t[:, :], bt_c.t[:, g * 128:(g + 1) * 128], Vdec.t[:, g * 512:(g + 1) * 512], True, True, [bt_c.d, Vdec.d], [psd2.d])
                    s3 = S[g].t[:, :].rearrange("p (h e) -> p h e", e=64)
                    K.tt(POOL, s3, s3, bc3(cd[:, g * 8:(g + 1) * 8], 64), ALU.mult, [S[g].d] + smd, [S[g].d])
                    K.tt(DVE, S[g].t[:, :], S[g].t[:, :], psd2.t[:, :], ALU.add, [S[g].d, psd2.d], [S[g].d])
                    K.act(Sbf[g].t[:, :], S[g].t[:, :], AF.Copy, [S[g].d], [Sbf[g].d])
                psro = [pbank(), pbank()]
                for h in range(8):
                    rows = slice((h % 2) * 64, (h % 2) * 64 + 64)
                    K.mm(psro[h % 2].t[:, (h // 2) * 128:(h // 2 + 1) * 128], rq_c.t[rows, h // 2, :], SRbf.t[rows, h // 2, :], True, True,
                         [rq_c.d, SRbf.d], [psro[h % 2].d])
                for par in range(2):
                    K.tt(DVE, yo.t[:, :].rearrange("p (b two e) -> p b two e", two=2, e=128)[:, :, par, :],
                         psro[par].t[:, :].rearrange("p (h e) -> p h e", e=128),
                         bc3(EAr.t[:, :].rearrange("p (b two) -> p b two", two=2)[:, :, par], 128), ALU.mult,
                         [psro[par].d, EAr.d], [yo.d])
                psyr = [pbank(), pbank()]
                for h in range(8):
                    K.mm(psyr[h // 4].t[:, (h % 4) * 128:(h % 4 + 1) * 128], MR.t[:, h, :], rv_c.t[:, h * 128:(h + 1) * 128], True, True,
                         [MR.d, rv_c.d], [psyr[h // 4].d])
                for q in range(2):
                    K.tt(DVE, yd.t[:, 1024 + q * 512:1024 + (q + 1) * 512], psyr[q].t[:, :], yo.t[:, q * 512:(q + 1) * 512], ALU.add,
                         [psyr[q].d, yo.d], [yd.d])
                psdr = [pbank(), pbank()]
                for h in range(8):
                    pr = h // 2
                    K.mm(psdr[h // 4].t[:, (h % 4) * 128:(h % 4 + 1) * 128], rkt_c.t[:, pr * 128:(pr + 1) * 128], RVd.t[:, h * 128:(h + 1) * 128],
                         True, True, [rkt_c.d, RVd.d], [psdr[h // 4].d])
                K.tt(POOL, SR.t[:, :, :], SR.t[:, :, :], bc3(CDR.t[:, :], 128), ALU.mult, [SR.d, CDR.d], [SR.d])
                for half in range(2):
                    rows = slice(half * 64, half * 64 + 64)
                    for q in range(2):
                        src = psdr[q].t[:, :].rearrange("p (pp two e) -> p pp two e", two=2, e=128)[rows, :, half, :]
                        K.tt(DVE, SR.t[rows, 2 * q:2 * q + 2, :], SR.t[rows, 2 * q:2 * q + 2, :], src, ALU.add, [SR.d, psdr[q].d], [SR.d])
                K.act(SRbf.t[:, :, :], SR.t[:, :, :], AF.Copy, [SR.d], [SRbf.d])
                if not fin:
                    K.st(YF[t0:t0 + 128, :], yd.t[:, :], [yd.d])
                    return
                yf_c, sz_c, sg_c = yfb[ci % NF], szb[ci % NF], sgb[ci % NF]
                K.tt(POOL, yd.t[:, :], yd.t[:, :], yf_c.t[:, :], ALU.add, [yd.d, yf_c.d], [yd.d])
                K.tt(POOL, junk.t[:, :], xs_c.t[:, :], DSK.t[:, :], ALU.mult, [xs_c.d, DSK.d], [junk.d])
                K.tt(DVE, yd.t[:, 0:1024], yd.t[:, 0:1024], junk.t[:, :], ALU.add, [yd.d, junk.d], [yd.d])
                K.tt(DVE, yd.t[:, 0:1024], yd.t[:, 0:1024], sz_c.t[:, :], ALU.mult, [yd.d, sz_c.d], [yd.d])
                ssg = st8.t[:, 0, 0:2]
                for g in range(2):
                    K.act(junk.t[:, 0:512], yd.t[:, g * 512:(g + 1) * 512], AF.Square, [yd.d], [junk.d, st8.d], accum_out=st8.t[:, 0, g:g + 1])
                K.ts(DVE, ssg, ssg, 1.0 / 512, ALU.mult, [st8.d], [st8.d], s2=EPS, op1=ALU.add)
                K.act(ssg, ssg, AF.Sqrt, [st8.d], [st8.d])
                K.recip(ssg, ssg, [st8.d], [st8.d])
                for g in range(2):
                    K.stt(ynb.t[:, g * 512:(g + 1) * 512], yd.t[:, g * 512:(g + 1) * 512], st8.t[:, 0, g:g + 1], WN.t[:, g * 512:(g + 1) * 512],
                          ALU.mult, ALU.mult, [yd.d, st8.d, WN.d], [ynb.d])
                yr3 = yd.t[:, 1024:2048].rearrange("p (h e) -> p h e", e=128)
                s1, s2, mean, m2 = st8.t[:, 1, :], st8.t[:, 2, :], st8.t[:, 3, :], st8.t[:, 4, :]
                em.op(DVE, lambda: nc.vector.tensor_reduce(out=s1, in_=yr3, axis=AX.X, op=ALU.add), [yd.d], [st8.d])
                K.act(junk.t[:, :], yd.t[:, 1024:2048], AF.Square, [yd.d], [junk.d])
                em.op(DVE, lambda: nc.vector.tensor_reduce(out=s2, in_=junk.t[:, :].rearrange("p (h e) -> p h e", e=128), axis=AX.X, op=ALU.add),
                      [junk.d], [st8.d])
                K.ts(DVE, mean, s1, 1.0 / 128, ALU.mult, [st8.d], [st8.d])
                K.tt(DVE, m2, mean, mean, ALU.mult, [st8.d], [st8.d])
                K.stt(s2, s2, 1.0 / 128, m2, ALU.mult, ALU.subtract, [st8.d], [st8.d])
                K.ts(DVE, s2, s2, EPS, ALU.add, [st8.d], [st8.d])
                K.act(s2, s2, AF.Sqrt, [st8.d], [st8.d])
                K.recip(s2, s2, [st8.d], [st8.d])
                yn3 = ynb.t[:, 1024:2048].rearrange("p (h e) -> p h e", e=128)
                K.tt(DVE, yn3, yr3, bc3(mean, 128), ALU.subtract, [yd.d, st8.d], [ynb.d])
                K.tt(POOL, yn3, yn3, bc3(s2, 128), ALU.mult, [ynb.d, st8.d], [ynb.d])
                K.tt(DVE, ynb.t[:, 1024:2048], ynb.t[:, 1024:2048], RN.t[:, :], ALU.mult, [ynb.d, RN.d], [ynb.d])
                K.tt(POOL, ynb.t[:, 1024:2048], ynb.t[:, 1024:2048], sg_c.t[:, :], ALU.mult, [ynb.d, sg_c.d], [ynb.d])
                for b4 in range(4):
                    pst = pbank()
                    for kk in range(4):
                        k = b4 * 4 + kk
                        K.tr(pst.t[:, kk * 128:(kk + 1) * 128], ynb.t[:, k * 128:(k + 1) * 128], cst(CI_IDENT), [ynb.d, CS.d], [pst.d])
                    ys = ysb[b4 % 2]
                    K.act(ys.t[:, :, :], pst.t[:, :].rearrange("p (k t) -> p k t", t=128), AF.Copy, [pst.d], [ys.d])
                    K.st(YS[b4 * 512:(b4 + 1) * 512, t0:t0 + 128].rearrange("(k p) t -> p k t", p=128), ys.t[:, :, :], [ys.d])

            loads(0)
            if fin:
                loads_fin(0)
            partA(0)
            for ci in range(NO):
                if ci + 1 < NO:
                    partA(ci + 1)
                partB(ci)

    EPSB = K.gsb("epsb", [128, 1], F32)
    K.memset(DVE, EPSB.t[:, :], EPS, [EPSB.d])

    K.prefetch_w13 = True
    l0_phaseA()
    l0_phaseB()
    K.c2_args = (K.blocks, X2, 0, nlayers == 1)
    phaseC1(0, K.blocks, AO, 8, awout, xin, 0)
    if nlayers == 1:
        pass
    else:
        import os
        stop = int(os.environ.get("KSTOP", "99"))
        lat = [b for b in K.blocks if b[2] == 0]
        def l1_c():
            K.c2_args = (lat, outD, CTX, True)
            phaseC1(1, lat, YS, 16, swout, X2, 0)
        steps = [l1_phaseA, l1_phaseV, lambda: l1_scan(0), lambda: l1_scan(1), l1_c]
        for si, fn in enumerate(steps):
            if si < stop:
                fn()
    em.finish()
    K.stats = em.stats()
    return K


def prep_inputs(inp, L):
    f = lambda a: np.ascontiguousarray(np.asarray(a, dtype=np.float32))
    rope, sq = host_consts(L)
    col = lambda v, n: f(v).reshape(n, 128).T
    shared = {
        "mod_w": f(inp["mod_w"]),
        "mod_b": f(inp["mod_b"]),
        "modb_col": np.ascontiguousarray(np.stack([col(inp["mod_b"][i], 48) for i in range(2)], axis=1)),
        "ncol": np.ascontiguousarray(np.stack([np.stack([col(inp["norm1_w"][i], 8), col(inp["norm2_w"][i], 8)], axis=1) for i in range(2)], axis=1)),
        "rope": rope, "sqc": sq,
        "ffn_w13": f(inp["ffn_w13"]), "ffn_w2": f(inp["ffn_w2"]),
        "attn_w_in": f(inp["attn_w_in"][0]), "mla_wq_b": f(inp["mla_wq_b"][0]), "mla_wkv_b": f(inp["mla_wkv_b"][0]),
        "attn_w_out": f(inp["attn_w_out"][0]),
    }
    if "ssm_w_in" in inp:
        shared["ssm_w_in"] = f(inp["ssm_w_in"][0])
        shared["ssm_w_out"] = f(inp["ssm_w_out"][0])
        cw = f(inp["ssd_conv_w"][0])
        shared["convw_col"] = np.ascontiguousarray(cw.reshape(5, 12, 128).transpose(2, 0, 1).reshape(128, 60))
        shared["convb_col"] = col(inp["ssd_conv_b"][0], 12)
        rows = np.zeros((NR1,), np.float32)
        rows[R_CONVB:R_CONVB + 1536] = f(inp["ssd_conv_b"][0])
        rows[R_DTB:R_DTB + 32] = f(inp["ssd_dt_bias"][0]).reshape(-1)
        rows[R_ALOG:R_ALOG + 32] = f(inp["ssd_a_log"][0]).reshape(-1)
        rows[R_RLOG:R_RLOG + 16] = f(inp["ret_decay_logit"][0]).reshape(-1)
        rows[R_DSK:R_DSK + 1024] = np.repeat(f(inp["ssd_d"][0]), 64)
        rows[R_SSDN:R_SSDN + 1024] = f(inp["ssd_norm"][0])
        rows[R_RETN:R_RETN + 1024] = f(inp["ret_norm"][0])
        shared["rows1"] = rows
        jj, ii = np.meshgrid(np.arange(128, dtype=np.float32), np.arange(128, dtype=np.float32), indexing="ij")
        c2 = np.zeros((128, 260), np.float32)
        c2[:, 0:128] = np.maximum(ii - jj, 0)
        c2[:, 128:256] = np.maximum(jj - ii, 0)
        j1 = np.arange(128, dtype=np.float32)
        c2[:, 256], c2[:, 257], c2[:, 258], c2[:, 259] = j1 + 1, 128 - j1, 127 - j1, j1
        shared["sqc2"] = c2
        sel = np.zeros((16, 16, 128), np.float32)
        for h in range(16):
            sel[h, h, :] = 1.0
        shared["selc"] = np.ascontiguousarray(sel.reshape(16, 2048))
    vec = np.zeros((128, NV), np.float32)
    vec[:, V_GQ] = np.tile(f(inp["gqa_qn"][0]), 2)
    vec[:, V_GK] = np.tile(f(inp["gqa_kn"][0]), 2)
    vec[:, V_QA:V_QA + 3] = col(inp["mla_qa_norm"][0], 3)
    vec[:, V_KVA:V_KVA + 2] = col(inp["mla_kva_norm"][0], 2)
    vec[:96, V_MQ] = f(inp["mla_qn"][0])
    vec[:96, V_MK] = f(inp["mla_kn"][0])
    shared["vecs"] = vec
    maps = []
    x, c, ctx, c_ctx = f(inp["x"]), f(inp["c"]), f(inp["ctx"]), f(inp["c_ctx"])
    for b in range(x.shape[0]):
        m = dict(shared)
        m["xin"] = np.ascontiguousarray(np.concatenate([ctx[b], x[b]], axis=0))
        cc = np.stack([col(c[b], 8), col(c_ctx, 8)], axis=2)
        m["cc"] = np.ascontiguousarray(cc)
        maps.append(m)
    return maps


_CACHE = {}


def kernel(**inputs):
    L = int(np.asarray(inputs["x"]).shape[1])
    B = int(np.asarray(inputs["x"]).shape[0])
    if L not in _CACHE:
        _CACHE[L] = build(L, 2)
    K = _CACHE[L]
    maps = prep_inputs(inputs, L)
    res = run_bass_kernel_spmd(K.nc, maps, core_ids=list(range(B)))
    return np.stack([np.asarray(res.results[b]["out"]) for b in range(B)], axis=0).astype(np.float32)
```

```python
import math
from contextlib import ExitStack, contextmanager
import numpy as np
import concourse.bass as bass
import concourse.mybir as mybir
from concourse.bass_utils import run_bass_kernel_spmd

F32 = mybir.dt.float32
BF16 = mybir.dt.bfloat16
AF = mybir.ActivationFunctionType
ALU = mybir.AluOpType
AX = mybir.AxisListType

EPOCH = 30000
EPS = 1e-6
D = 1024
CTX = 256
FFH = 2816
GRID_W = 64
THETA = 10000.0


class Dep:
    __slots__ = ("w", "r")

    def __init__(self):
        self.w = None
        self.r = {}


class Eng:
    def __init__(self, em, name, h, self_sync=True):
        self.em, self.name, self.h, self.self_sync = em, name, h, self_sync
        self.sem = None
        self.cnt = 0
        self.waited = {}
        self.own = set()
        self.ninst = 0
        self.nwait = 0
        self.last = None

    def next_event(self):
        if self.sem is None or self.cnt >= EPOCH:
            self.sem = self.em.new_sem(self.name)
            self.own.add(self.sem.num)
            self.cnt = 0
        self.cnt += 1
        self.last = (self.sem.num, self.cnt)
        return self.last


class Emitter:
    def __init__(self, nc, es, n_dma_sp=24, n_dma_pool=16):
        self.nc, self.es = nc, es
        self.sems = {}
        self.nsem = 0
        self.pe = Eng(self, "pe", nc.tensor, self_sync=False)
        self.act = Eng(self, "act", nc.scalar)
        self.dve = Eng(self, "dve", nc.vector)
        self.pool = Eng(self, "pool", nc.gpsimd)
        self.sp = Eng(self, "sp", nc.sync)
        self.engines = [self.pe, self.act, self.dve, self.pool, self.sp]
        self.dma_pools = {}
        for e, n in ((self.sp, n_dma_sp), (self.pool, n_dma_pool)):
            self.dma_pools[e.name] = [[self.new_sem("d" + e.name), 0] for _ in range(n)]
        self.dma_rr = {k: 0 for k in self.dma_pools}
        self.out_events = []

    def new_sem(self, tag):
        s = self.es.enter_context(self.nc.semaphore("%s_%d" % (tag, self.nsem)))
        self.nsem += 1
        self.sems[s.num] = s
        return s

    def _wait(self, eng, evs):
        best = {}
        for (s, v) in evs:
            if v > best.get(s, 0):
                best[s] = v
        for s, v in best.items():
            if (not eng.self_sync) and s in eng.own:
                continue
            if eng.waited.get(s, 0) >= v:
                continue
            eng.h.wait_ge(self.sems[s], v)
            eng.waited[s] = v
            eng.nwait += 1

    @staticmethod
    def _collect(reads, writes):
        evs = []
        for d in reads:
            if d.w is not None:
                evs.append(d.w)
        for d in writes:
            if d.w is not None:
                evs.append(d.w)
            evs.extend(d.r.items())
        return evs

    @staticmethod
    def _commit(ev, reads, writes):
        for d in reads:
            if ev[1] > d.r.get(ev[0], 0):
                d.r[ev[0]] = ev[1]
        for d in writes:
            d.w = ev
            d.r = {}

    def op(self, eng, fn, reads=(), writes=()):
        self._wait(eng, self._collect(reads, writes))
        inst = fn()
        ev = eng.next_event()
        inst.then_inc(self.sems[ev[0]], 1)
        eng.ninst += 1
        self._commit(ev, reads, writes)
        return ev

    def dma(self, eng, out, in_, reads=(), writes=(), is_output=False, **kw):
        pool = self.dma_pools[eng.name]
        i = self.dma_rr[eng.name]
        self.dma_rr[eng.name] = (i + 1) % len(pool)
        slot = pool[i]
        evs = self._collect(reads, writes)
        if slot[1] > 0:
            evs.append((slot[0].num, slot[1]))
        self._wait(eng, evs)
        if slot[1] + 16 > EPOCH:
            slot[0] = self.new_sem("d" + eng.name)
            slot[1] = 0
        slot[1] += 16
        eng.h.dma_start(out=out, in_=in_, **kw).then_inc(slot[0], 16)
        ev = (slot[0].num, slot[1])
        eng.ninst += 1
        self._commit(ev, reads, writes)
        if is_output:
            self.out_events.append(ev)
        return ev

    def all_events(self):
        evs = []
        for e in self.engines:
            if e.last is not None:
                evs.append(e.last)
        for pool in self.dma_pools.values():
            for s, v in pool:
                if v > 0:
                    evs.append((s.num, v))
        return evs

    def barrier(self):
        evs = self.all_events()
        for e in self.engines:
            self._wait(e, evs)

    def finish(self):
        self._wait(self.sp, list(self.out_events) + self.all_events())

    def stats(self):
        return {e.name: (e.ninst, e.nwait) for e in self.engines}, self.nsem


class Buf:
    __slots__ = ("t", "d")

    def __init__(self, t):
        self.t = t
        self.d = Dep()


class KB:
    def __init__(self, L):
        self.L = L
        self.NT = CTX + L
        self.NCH = self.NT // 128
        self.nc = bass.Bass("TRN2", target_bir_lowering=False)
        self.es = ExitStack()
        self.em = Emitter(self.nc, self.es)
        self.cur = self.es
        self.uid = 0
        self.blocks = [(0, CTX, 1)] + [(CTX + i * 512, 512, 0) for i in range(L // 512)]
        self.PS = [Buf(self.es.enter_context(self.nc.psum_tensor("psb%d" % i, [128, 512], F32))) for i in range(8)]

    def sb(self, name, shape, dtype):
        self.uid += 1
        return Buf(self.cur.enter_context(self.nc.sbuf_tensor("s_%s_%d" % (name, self.uid), shape, dtype)))

    def gsb(self, name, shape, dtype):
        return Buf(self.es.enter_context(self.nc.sbuf_tensor("g_" + name, shape, dtype)))

    def dram(self, name, shape, dtype, kind="Internal"):
        return self.nc.dram_tensor(name, shape, dtype, kind=kind).ap()

    @contextmanager
    def phase(self):
        st = ExitStack()
        prev = self.cur
        self.cur = st
        try:
            yield
        finally:
            self.em.barrier()
            st.close()
            self.cur = prev

    def mm(self, out, lhsT, rhs, start, stop, R, W):
        nc = self.nc
        return self.em.op(self.em.pe, lambda: nc.tensor.matmul(out, lhsT=lhsT, rhs=rhs, start=start, stop=stop), R, W)

    def tr(self, out, in_, ident, R, W):
        nc = self.nc
        return self.em.op(self.em.pe, lambda: nc.tensor.transpose(out=out, in_=in_, identity=ident), R, W)

    def act(self, out, in_, func, R, W, scale=None, bias=None, accum_out=None):
        nc = self.nc
        kw = {}
        if scale is not None:
            kw["scale"] = scale
        if bias is not None:
            kw["bias"] = bias
        if accum_out is not None:
            kw["accum_out"] = accum_out
        return self.em.op(self.em.act, lambda: nc.scalar.activation(out=out, in_=in_, func=func, **kw), R, W)

    def _ve(self, eng):
        return self.nc.vector if eng is self.em.dve else self.nc.gpsimd

    def tt(self, eng, out, in0, in1, op, R, W):
        h = self._ve(eng)
        return self.em.op(eng, lambda: h.tensor_tensor(out=out, in0=in0, in1=in1, op=op), R, W)

    def ts(self, eng, out, in0, s1, op0, R, W, s2=None, op1=None):
        h = self._ve(eng)
        if op1 is None:
            return self.em.op(eng, lambda: h.tensor_scalar(out=out, in0=in0, scalar1=s1, scalar2=None, op0=op0), R, W)
        return self.em.op(eng, lambda: h.tensor_scalar(out=out, in0=in0, scalar1=s1, scalar2=s2, op0=op0, op1=op1), R, W)

    def stt(self, out, in0, scalar, in1, op0, op1, R, W):
        nc = self.nc
        return self.em.op(self.em.dve, lambda: nc.vector.scalar_tensor_tensor(out=out, in0=in0, scalar=scalar, in1=in1, op0=op0, op1=op1), R, W)

    def recip(self, out, in_, R, W):
        nc = self.nc
        return self.em.op(self.em.dve, lambda: nc.vector.reciprocal(out=out, in_=in_), R, W)

    def memset(self, eng, ap, val, W):
        h = self._ve(eng)
        return self.em.op(eng, lambda: h.memset(ap, val), (), W)

    def copy(self, eng, out, in_, R, W):
        h = self._ve(eng)
        return self.em.op(eng, lambda: h.tensor_copy(out=out, in_=in_), R, W)

    def ld(self, out, in_, W, R=(), **kw):
        return self.em.dma(self.em.sp, out, in_, reads=R, writes=W, **kw)

    def st(self, out, in_, R, W=(), **kw):
        return self.em.dma(self.em.pool, out, in_, reads=R, writes=W, **kw)

    def ldcast(self, out, in_, W, R=()):
        return self.em.dma(self.em.pool, out, in_, reads=R, writes=W, max_dma_last_dim=8192)


def rope_tables(L, dim):
    rows = L // GRID_W
    rr, cc = np.meshgrid(np.arange(rows, dtype=np.float32), np.arange(GRID_W, dtype=np.float32), indexing="ij")
    quarter = dim // 4
    inv = (np.float32(THETA) ** (-np.arange(quarter, dtype=np.float32) / np.float32(quarter))).astype(np.float32)
    ang = np.concatenate([rr.reshape(-1)[:, None] * inv, cc.reshape(-1)[:, None] * inv], axis=-1).astype(np.float32)
    cos = np.concatenate([np.ones((CTX, dim // 2), np.float32), np.cos(ang)], axis=0)
    sin = np.concatenate([np.zeros((CTX, dim // 2), np.float32), np.sin(ang)], axis=0)
    return cos.astype(np.float32), sin.astype(np.float32)


def host_consts(L):
    NT = CTX + L
    c64, s64 = rope_tables(L, 64)
    c32, s32 = rope_tables(L, 32)
    cos64 = np.repeat(c64.T, 2, axis=0)
    sin64 = np.repeat(s64.T, 2, axis=0)
    cos128 = np.concatenate([cos64, cos64], 0)
    sin128 = np.concatenate([sin64, sin64], 0)
    cosm = np.concatenate([np.ones((64, NT), np.float32), np.repeat(c32.T, 2, axis=0)], 0)
    sinm = np.concatenate([np.zeros((64, NT), np.float32), np.repeat(s32.T, 2, axis=0)], 0)
    rope = np.zeros((4, 128, NT), np.float32)
    rope[0], rope[1] = cos128, sin128
    rope[2, :96], rope[3, :96] = cosm, sinm
    ident = np.eye(128, dtype=np.float32)
    rot = np.zeros((128, 128), np.float32)
    for i in range(64):
        rot[2 * i + 1, 2 * i] = -1.0
        rot[2 * i, 2 * i + 1] = 1.0
    rotm = np.zeros((128, 128), np.float32)
    rotm[64:96, 64:96] = rot[64:96, 64:96]
    shift = np.zeros((128, 128), np.float32)
    for k in range(32):
        shift[k, 64 + k] = 1.0
    bd64 = np.zeros((128, 128), np.float32)
    bd64[:64, :64] = 1.0
    bd64[64:, 64:] = 1.0
    ones = np.ones((128, 128), np.float32)
    tri = np.triu(np.ones((128, 128), np.float32))
    sq = np.concatenate([ident, rot, rotm, shift, bd64, ones, tri, tri.T.copy()], axis=1)
    return rope, sq


CI_IDENT, CI_ROT, CI_ROTM, CI_SHIFT, CI_BD64, CI_ONES, CI_TRI, CI_TRIT = range(8)
NCONST = 8

V_GQ, V_GK, V_QA, V_KVA, V_MQ, V_MK = 0, 1, 2, 5, 7, 8
NV = 16
R_CONVB, R_DTB, R_ALOG, R_RLOG, R_DSK, R_SSDN, R_RETN = 0, 1536, 1568, 1600, 1616, 2640, 3664
NR1 = 4688


def build(L=4096, nlayers=2, debug=False):
    K = KB(L)
    nc, em = K.nc, K.em
    NT, NCH = K.NT, K.NCH
    PS = K.PS
    DVE, POOL = em.dve, em.pool

    xin = K.dram("xin", [NT, D], F32, "ExternalInput")
    ccd = K.dram("cc", [128, 8, 2], F32, "ExternalInput")
    mod_w = K.dram("mod_w", [2, D, 6 * D], F32, "ExternalInput")
    modb_col = K.dram("modb_col", [128, 2, 48], F32, "ExternalInput")
    mod_b = K.dram("mod_b", [2, 6 * D], F32, "ExternalInput")
    ncol = K.dram("ncol", [128, 2, 2, 8], F32, "ExternalInput")
    vecs = K.dram("vecs", [128, NV], F32, "ExternalInput")
    ropeD = K.dram("rope", [4, 128, NT], F32, "ExternalInput")
    sqc = K.dram("sqc", [128, NCONST * 128], F32, "ExternalInput")
    w13D = K.dram("ffn_w13", [2, D, 2 * FFH], F32, "ExternalInput")
    w2D = K.dram("ffn_w2", [2, FFH, D], F32, "ExternalInput")
    awin = K.dram("attn_w_in", [D, 1440], F32, "ExternalInput")
    wqb = K.dram("mla_wq_b", [384, 768], F32, "ExternalInput")
    wkvb = K.dram("mla_wkv_b", [256, 1024], F32, "ExternalInput")
    awout = K.dram("attn_w_out", [D, D], F32, "ExternalInput")
    outD = K.dram("out", [L, D], F32, "ExternalOutput")

    QA = K.dram("QA", [4, 128, NT], BF16)
    KA = K.dram("KA", [2, 128, NT], BF16)
    VA = K.dram("VA", [128, NCH, 130], BF16)
    QM = K.dram("QM", [8, 96, NT], BF16)
    KM = K.dram("KM", [8, 96, NT], BF16)
    VM = K.dram("VM", [128, NCH, 520], BF16)
    AO = K.dram("AO", [D, NT], BF16)
    X1 = K.dram("X1", [NT, D], F32)
    U2 = K.dram("U2", [D, NT], BF16)
    X2 = K.dram("X2", [NT, D], F32, "ExternalOutput" if debug else "Internal")
    MODROW = K.dram("MODROW", [2, 2, 2, D], F32)

    CS = K.gsb("consts", [128, NCONST * 128], F32)
    K.ld(CS.t[:, :], sqc[:, :], [CS.d])

    def cst(i, r=128, c=128):
        return CS.t[0:r, i * 128:i * 128 + c]

    CB = K.gsb("constsb", [128, 5 * 128], BF16)
    K.copy(DVE, CB.t[:, 0:128], cst(CI_BD64), [CS.d], [CB.d])
    K.copy(DVE, CB.t[:, 128:256], cst(CI_ONES), [CS.d], [CB.d])
    K.copy(DVE, CB.t[:, 256:384], cst(CI_ROT), [CS.d], [CB.d])
    K.copy(DVE, CB.t[:, 384:512], cst(CI_ROTM), [CS.d], [CB.d])
    K.copy(DVE, CB.t[:, 512:640], cst(CI_SHIFT), [CS.d], [CB.d])
    ROTb = CB.t[:, 256:384]
    ROTMb = CB.t[0:96, 384:480]
    SHIFTb = CB.t[0:32, 512:608]
    BD64b = CB.t[:, 0:128]
    ONESb = CB.t[:, 128:256]
    VEC = K.gsb("vecs", [128, NV], F32)
    K.ld(VEC.t[:, :], vecs[:, :], [VEC.d])
    NCOL = K.gsb("ncol", [128, 2, 2, 8], F32)
    K.ld(NCOL.t[:, :, :, :], ncol[:, :, :, :], [NCOL.d])
    MODT = K.gsb("modT", [128, 2, 48, 2], F32)
    MUL = K.gsb("mulT", [128, 2, 2, 8, 2], F32)

    with K.phase():
        cc = K.sb("cc", [128, 8, 2], F32)
        K.ld(cc.t[:, :, :], ccd[:, :, :], [cc.d])
        K.act(cc.t[:, :, :], cc.t[:, :, :], AF.Silu, [cc.d], [cc.d])
        modb = K.sb("modb", [128, 2, 48], F32)
        K.ld(modb.t[:, :, :], modb_col[:, :, :], [modb.d])
        mbrow = K.sb("mbrow", [2, 2, 6 * D], F32)
        K.ld(mbrow.t[:, :, :], mod_b.partition_broadcast(2), [mbrow.d])
        wbuf = [K.sb("mw", [128, 8, 1024], F32) for _ in range(2)]
        rowb = [K.sb("rowb", [2, 1024], F32) for _ in range(2)]
        it = 0
        for i in range(nlayers):
            for v in range(6):
                wb = wbuf[it % 2]
                K.ld(wb.t[:, :, :], mod_w[i, :, v * 1024:(v + 1) * 1024].rearrange("(k p) n -> p k n", p=128), [wb.d])
                ps = PS[it % 2]
                for j in range(8):
                    for k in range(8):
                        K.mm(ps.t[:, j * 2:(j + 1) * 2], wb.t[:, k, j * 128:(j + 1) * 128], cc.t[:, k, :], k == 0, k == 7,
                             [wb.d, cc.d], [ps.d])
                K.tt(DVE, MODT.t[:, i, v * 8:(v + 1) * 8, :], ps.t[:, 0:16].rearrange("p (j s) -> p j s", s=2),
                     modb.t[:, i, v * 8:(v + 1) * 8].unsqueeze(2).broadcast_to([128, 8, 2]), ALU.add,
                     [ps.d, modb.d], [MODT.d])
                if v in (2, 5):
                    gi = 0 if v == 2 else 1
                    rb = rowb[gi]
                    for half in range(2):
                        ps2 = PS[2 + half]
                        for k in range(8):
                            K.mm(ps2.t[0:2, :], cc.t[:, k, :], wb.t[:, k, half * 512:(half + 1) * 512], k == 0, k == 7,
                                 [wb.d, cc.d], [ps2.d])
                        K.tt(DVE, rb.t[:, half * 512:(half + 1) * 512], ps2.t[0:2, :],
                             mbrow.t[:, i, v * 1024 + half * 512: v * 1024 + (half + 1) * 512], ALU.add,
                             [ps2.d, mbrow.d], [rb.d])
                    K.st(MODROW[i, gi, :, :], rb.t[:, :], [rb.d])
                it += 1
            for nrm in range(2):
                K.stt(MUL.t[:, i, nrm, :, :], MODT.t[:, i, (3 * nrm + 1) * 8:(3 * nrm + 2) * 8, :], 1.0,
                      NCOL.t[:, i, nrm, :].unsqueeze(2).broadcast_to([128, 8, 2]), ALU.add, ALU.mult,
                      [MODT.d, NCOL.d], [MUL.d])

    def mul_ap(layer, nrm, k, s):
        return MUL.t[:, layer, nrm, k, s:s + 1]

    def add_ap(layer, nrm, k, s):
        return MODT.t[:, layer, 3 * nrm * 8 + k, s:s + 1]

    def norm_to_uT(xt, nch, T, layer, nrm, s, uT, scr, psbanks, use_pool=True):
        junk, ss, xs = scr
        for c in range(nch):
            K.act(junk.t[:, :], xt.t[:, c, :], AF.Square, [xt.d], [junk.d, ss.d], accum_out=ss.t[:, c:c + 1])
        K.ts(DVE, ss.t[:, 0:nch], ss.t[:, 0:nch], 1.0 / D, ALU.mult, [ss.d], [ss.d], s2=EPS, op1=ALU.add)
        K.act(ss.t[:, 0:nch], ss.t[:, 0:nch], AF.Sqrt, [ss.d], [ss.d])
        K.recip(ss.t[:, 0:nch], ss.t[:, 0:nch], [ss.d], [ss.d])
        for c in range(nch):
            K.ts(DVE if (c % 2 == 0 or not use_pool) else POOL, xs.t[:, c, :], xt.t[:, c, :], ss.t[:, c:c + 1], ALU.mult, [xt.d, ss.d], [xs.d])
        for k in range(8):
            ps = psbanks[k % len(psbanks)]
            for c in range(nch):
                K.tr(ps.t[:, c * 128:(c + 1) * 128], xs.t[:, c, k * 128:(k + 1) * 128], cst(CI_IDENT), [xs.d, CS.d], [ps.d])
            K.act(uT.t[:, k, 0:T], ps.t[:, 0:T], AF.Identity, [ps.d, MUL.d, MODT.d], [uT.d],
                  scale=mul_ap(layer, nrm, k, s), bias=add_ap(layer, nrm, k, s))

    def l0_phaseA():
        with K.phase():
            W = K.sb("win", [128, 8, 1440], BF16)
            for k in range(8):
                K.ldcast(W.t[:, k, :], awin[k * 128:(k + 1) * 128, :], [W.d])
            Wkd = K.sb("wkd", [128, 8, 2, 128], BF16)
            for g in range(2):
                for dup in range(2):
                    K.ldcast(Wkd.t[:, :, g, dup * 64:(dup + 1) * 64],
                             awin[:, 512 + g * 64:512 + (g + 1) * 64].rearrange("(k p) n -> p k n", p=128), [Wkd.d])
            Wq = K.sb("wqb", [128, 3, 768], BF16)
            K.ldcast(Wq.t[:, :, :], wqb.rearrange("(j p) n -> p j n", p=128), [Wq.d])
            Wkp = K.sb("wkvp", [128, 2, 8, 96], BF16)
            K.memset(DVE, Wkp.t[:, :, :, :], 0.0, [Wkp.d])
            wkv4 = wkvb.rearrange("(j p) (h t e) -> p j h t e", p=128, t=2, e=64)
            for j in range(2):
                K.ldcast(Wkp.t[:, j, :, 0:64], wkv4[:, j, :, 0, :], [Wkp.d])
            Wv = K.sb("wkvv", [128, 2, 8, 64], BF16)
            for j in range(2):
                K.ldcast(Wv.t[:, j, :, :], wkv4[:, j, :, 1, :], [Wv.d])

            xb = [K.sb("xa", [128, 4, D], F32) for _ in range(2)]
            scr = (K.sb("junk", [128, D], BF16), K.sb("ss", [128, 4], F32), K.sb("xs", [128, 4, D], F32))
            uTb = [K.sb("uT", [128, 8, 512], BF16) for _ in range(2)]
            ropeb = [K.sb("rope", [128, 4, 512], F32) for _ in range(2)]
            sqb = [K.sb("sq", [128, 512], BF16) for _ in range(2)]
            r1b = [K.sb("r1", [128, 512], F32) for _ in range(2)]
            qnb = [K.sb("qn", [128, 512], BF16) for _ in range(2)]
            t1b = [K.sb("t1", [128, 512], F32) for _ in range(2)]
            t2b = [K.sb("t2", [128, 512], F32) for _ in range(2)]
            outb = [K.sb("ob", [128, 512], BF16) for _ in range(3)]
            qln = K.sb("qln", [128, 3, 512], BF16)
            kvn = K.sb("kvn", [128, 2, 512], BF16)
            krp = K.sb("krp", [32, 512], BF16)
            vat = [K.sb("vat", [128, 4, 130], BF16) for _ in range(2)]
            vmt = [K.sb("vmt", [128, 4, 520], BF16) for _ in range(2)]
            for b in vat + vmt:
                K.memset(DVE, b.t[:, :, :], 1.0, [b.d])
            cnt = {"n": 0, "o": 0, "p": 0}
            PROJ = [PS[2], PS[3], PS[4]]

            def proj_bank():
                cnt["p"] += 1
                return PROJ[cnt["p"] % 3]

            pq = {"a": None, "b": None}
            SSB = [PS[5], PS[6]]
            ROTB = [PS[7], PS[0]]

            def flush():
                a, b = pq["a"], pq["b"]
                if a is not None:
                    a[0]()
                if b is not None:
                    b[1]()
                if a is not None:
                    a[1]()
                pq["a"] = pq["b"] = None

            def post(ps_src, R_, T, gain, ss_lhsT, inv_n, rot_lhsT, ci, rp, dst):
                i = cnt["n"] % 2
                cnt["n"] += 1
                sq, r1, qn, t1, t2 = sqb[i], r1b[i], qnb[i], t1b[i], t2b[i]
                pss, psr = SSB[i], ROTB[i]
                K.act(sq.t[0:R_, 0:T], ps_src.t[0:R_, 0:T], AF.Square, [ps_src.d], [sq.d])
                K.mm(pss.t[0:R_, 0:T], ss_lhsT, sq.t[0:R_, 0:T], True, True, [sq.d, CB.d], [pss.d])

                def s2a():
                    K.act(r1.t[0:R_, 0:T], pss.t[0:R_, 0:T], AF.Ln, [pss.d], [r1.d], scale=inv_n, bias=EPSB.t[0:R_, 0:1])
                    K.act(r1.t[0:R_, 0:T], r1.t[0:R_, 0:T], AF.Exp, [r1.d], [r1.d], scale=-0.5)
                    K.stt(qn.t[0:R_, 0:T], ps_src.t[0:R_, 0:T], gain, r1.t[0:R_, 0:T], ALU.mult, ALU.mult,
                          [ps_src.d, r1.d, VEC.d], [qn.d])
                    K.mm(psr.t[0:R_, 0:T], rot_lhsT, qn.t[0:R_, 0:T], True, True, [qn.d, CB.d], [psr.d])

                def s2b():
                    K.tt(DVE, t1.t[0:R_, 0:T], qn.t[0:R_, 0:T], rp.t[0:R_, ci, 0:T], ALU.mult, [qn.d, rp.d], [t1.d])
                    K.tt(DVE, t2.t[0:R_, 0:T], psr.t[0:R_, 0:T], rp.t[0:R_, ci + 1, 0:T], ALU.mult, [psr.d, rp.d], [t2.d])
                    ob = outb[cnt["o"] % 3]
                    cnt["o"] += 1
                    K.tt(POOL, ob.t[0:R_, 0:T], t1.t[0:R_, 0:T], t2.t[0:R_, 0:T], ALU.add, [t1.d, t2.d], [ob.d])
                    K.st(dst, ob.t[0:R_, 0:T], [ob.d])

                a, b = pq["a"], pq["b"]
                if a is not None:
                    a[0]()
                if b is not None:
                    b[1]()
                pq["b"] = a
                pq["a"] = (s2a, s2b)

            def lora_norm(ps_list, nj, gcol0, inv_n, dstb, T):
                i = cnt["n"] % 2
                cnt["n"] += 1
                r1 = r1b[i]
                pss = PS[5]
                for j in range(nj):
                    sq = sqb[(cnt["n"] + j) % 2]
                    K.act(sq.t[:, 0:T], ps_list[j].t[:, 0:T], AF.Square, [ps_list[j].d], [sq.d])
                    K.mm(pss.t[:, 0:T], ONESb, sq.t[:, 0:T], j == 0, j == nj - 1, [sq.d, CB.d], [pss.d])
                K.act(r1.t[:, 0:T], pss.t[:, 0:T], AF.Ln, [pss.d], [r1.d], scale=inv_n, bias=EPSB.t[:, 0:1])
                K.act(r1.t[:, 0:T], r1.t[:, 0:T], AF.Exp, [r1.d], [r1.d], scale=-0.5)
                for j in range(nj):
                    K.stt(dstb.t[:, j, 0:T], ps_list[j].t[:, 0:T], VEC.t[:, gcol0 + j:gcol0 + j + 1], r1.t[:, 0:T],
                          ALU.mult, ALU.mult, [ps_list[j].d, r1.d, VEC.d], [dstb.d])

            for bi, (t0, T, s) in enumerate(K.blocks):
                nch = T // 128
                c0 = t0 // 128
                xt = xb[bi % 2]
                K.ld(xt.t[:, 0:nch, :], xin[t0:t0 + T, :].rearrange("(c p) d -> p c d", p=128), [xt.d])
                rp = ropeb[bi % 2]
                K.ld(rp.t[:, :, 0:T], ropeD[:, :, t0:t0 + T].rearrange("f p t -> p f t"), [rp.d])
                uT = uTb[bi % 2]
                norm_to_uT(xt, nch, T, 0, 0, s, uT, scr, [PS[0], PS[1]])
                for ch in range(4):
                    ps = proj_bank()
                    for k in range(8):
                        K.mm(ps.t[:, 0:T], W.t[:, k, ch * 128:(ch + 1) * 128], uT.t[:, k, 0:T], k == 0, k == 7, [W.d, uT.d], [ps.d])
                    post(ps, 128, T, VEC.t[:, V_GQ:V_GQ + 1], BD64b, 1.0 / 64, ROTb, 0, rp, QA[ch, :, t0:t0 + T])
                for g in range(2):
                    ps = proj_bank()
                    for k in range(8):
                        K.mm(ps.t[:, 0:T], Wkd.t[:, k, g, :], uT.t[:, k, 0:T], k == 0, k == 7, [Wkd.d, uT.d], [ps.d])
                    post(ps, 128, T, VEC.t[:, V_GK:V_GK + 1], BD64b, 1.0 / 64, ROTb, 0, rp, KA[g, :, t0:t0 + T])
                flush()
                va_t = vat[bi % 2]
                for c in range(nch):
                    ps = proj_bank()
                    for k in range(8):
                        K.mm(ps.t[:, 0:128], uT.t[:, k, c * 128:(c + 1) * 128], W.t[:, k, 640:768], k == 0, k == 7, [W.d, uT.d], [ps.d])
                    K.act(va_t.t[:, c, :].rearrange("p (g e) -> p g e", e=65)[:, :, 0:64],
                          ps.t[:, 0:128].rearrange("p (g e) -> p g e", e=64), AF.Copy, [ps.d], [va_t.d])
                K.st(VA[:, c0:c0 + nch, :], va_t.t[:, 0:nch, :], [va_t.d])
                pl = []
                for j in range(3):
                    ps = proj_bank()
                    for k in range(8):
                        K.mm(ps.t[:, 0:T], W.t[:, k, 768 + j * 128:768 + (j + 1) * 128], uT.t[:, k, 0:T], k == 0, k == 7, [W.d, uT.d], [ps.d])
                    pl.append(ps)
                lora_norm(pl, 3, V_QA, 1.0 / 384, qln, T)
                for h in range(8):
                    ps = proj_bank()
                    for j in range(3):
                        K.mm(ps.t[0:96, 0:T], Wq.t[:, j, h * 96:(h + 1) * 96], qln.t[:, j, 0:T], j == 0, j == 2, [Wq.d, qln.d], [ps.d])
                    post(ps, 96, T, VEC.t[0:96, V_MQ:V_MQ + 1], ONESb[0:96, 0:96], 1.0 / 96, ROTMb, 2, rp, QM[h, :, t0:t0 + T])
                flush()
                pl = []
                for j in range(2):
                    ps = proj_bank()
                    for k in range(8):
                        K.mm(ps.t[:, 0:T], W.t[:, k, 1152 + j * 128:1152 + (j + 1) * 128], uT.t[:, k, 0:T], k == 0, k == 7, [W.d, uT.d], [ps.d])
                    pl.append(ps)
                lora_norm(pl, 2, V_KVA, 1.0 / 256, kvn, T)
                ps = proj_bank()
                for k in range(8):
                    K.mm(ps.t[0:32, 0:T], W.t[:, k, 1408:1440], uT.t[:, k, 0:T], k == 0, k == 7, [W.d, uT.d], [ps.d])
                K.act(krp.t[0:32, 0:T], ps.t[0:32, 0:T], AF.Copy, [ps.d], [krp.d])
                for h in range(8):
                    ps = proj_bank()
                    for j in range(2):
                        K.mm(ps.t[0:96, 0:T], Wkp.t[:, j, h, :], kvn.t[:, j, 0:T], j == 0, False, [Wkp.d, kvn.d], [ps.d])
                    K.mm(ps.t[0:96, 0:T], SHIFTb, krp.t[0:32, 0:T], False, True, [krp.d, CB.d], [ps.d])
                    post(ps, 96, T, VEC.t[0:96, V_MK:V_MK + 1], ONESb[0:96, 0:96], 1.0 / 96, ROTMb, 2, rp, KM[h, :, t0:t0 + T])
                flush()
                vm_t = vmt[bi % 2]
                for c in range(nch):
                    ps = proj_bank()
                    for j in range(2):
                        K.mm(ps.t[:, 0:512], kvn.t[:, j, c * 128:(c + 1) * 128], Wv.t[:, j, :, :].rearrange("p h e -> p (h e)"),
                             j == 0, j == 1, [Wv.d, kvn.d], [ps.d])
                    K.act(vm_t.t[:, c, :].rearrange("p (g e) -> p g e", e=65)[:, :, 0:64],
                          ps.t[:, 0:512].rearrange("p (g e) -> p g e", e=64), AF.Copy, [ps.d], [vm_t.d])
                K.st(VM[:, c0:c0 + nch, :], vm_t.t[:, 0:nch, :], [vm_t.d])

    def l0_phaseB():
        with K.phase():
            va = K.sb("va", [128, NCH, 130], BF16)
            vm = K.sb("vm", [128, NCH, 520], BF16)
            K.ld(va.t[:, :, :], VA[:, :, :], [va.d])
            K.ld(vm.t[:, :, :], VM[:, :, :], [vm.d])
            kb = [K.sb("kb", [128, NT], BF16) for _ in range(2)]
            kzT = [K.sb("kzT", [128, NT], BF16) for _ in range(2)]
            kzB = [K.sb("kzB", [128, NT], BF16) for _ in range(2)]
            for b in kzT:
                K.memset(DVE, b.t[64:128, :], 0.0, [b.d])
            for b in kzB:
                K.memset(DVE, b.t[0:64, :], 0.0, [b.d])
            qb = [K.sb("qb", [128, 512], BF16) for _ in range(3)]
            pb = [K.sb("pb", [128, 512], BF16) for _ in range(4)]
            osb = [K.sb("osb", [64, 512], F32) for _ in range(2)]
            rsb = [K.sb("rsb", [128, 512], F32) for _ in range(2)]
            aob = [K.sb("aob", [64, 512], BF16) for _ in range(2)]
            SB_ = [PS[0], PS[1], PS[2]]
            OB_ = [PS[3], PS[4]]
            BCB = PS[5]
            tasks = []
            groups = []
            ui = 0
            qi = 0
            units = [("a", c) for c in range(4)] + [("m", h) for h in range(8)]
            for kind, idx in units:
                if kind == "a":
                    kT, kB = kzT[ui % 2], kzB[ui % 2]
                    kloads = [(kT, KA[idx // 2, 0:64, :], slice(0, 64)), (kB, KA[idx // 2, 64:128, :], slice(64, 128))]
                else:
                    kbf = kb[ui % 2]
                    kloads = [(kbf, KM[idx, :, :], slice(0, 96))]
                ui += 1
                first_in_unit = True
                for (t0, T, s) in K.blocks:
                    qbf = qb[qi % 3]
                    qi += 1
                    if kind == "a":
                        qload = (qbf, QA[idx, :, t0:t0 + T], 128, T)
                        subs = [(kT, slice(0, 128), 2 * idx, va, idx // 2, 64 ** -0.5), (kB, slice(0, 128), 2 * idx + 1, va, idx // 2, 64 ** -0.5)]
                    else:
                        qload = (qbf, QM[idx, :, t0:t0 + T], 96, T)
                        subs = [(kbf, slice(0, 96), 8 + idx, vm, idx, 96 ** -0.5)]
                    kcs = list(range(2)) if s == 1 else list(range(NCH))
                    for si, (kbuf_, rows, hg, vbuf, vcol, sc) in enumerate(subs):
                        g = len(groups)
                        groups.append((hg, t0, T))
                        for n, kc in enumerate(kcs):
                            tasks.append(dict(kb=kbuf_, rows=rows, kc=kc, qb=qbf, T=T, vb=vbuf, vcol=vcol, first=(n == 0),
                                              last=(n == len(kcs) - 1), grp=g, sc=sc,
                                              kload=kloads if (first_in_unit and si == 0 and n == 0) else None,
                                              qload=qload if (si == 0 and n == 0) else None))
                    first_in_unit = False

            def emit_qk(i):
                t = tasks[i]
                if t["kload"] is not None:
                    for (kbf, src, rws) in t["kload"]:
                        K.ld(kbf.t[rws, :], src, [kbf.d])
                if t["qload"] is not None:
                    qbf, src, R_, T = t["qload"]
                    K.ld(qbf.t[0:R_, 0:T], src[0:R_, :], [qbf.d])
                ps = SB_[i % 3]
                kc, T = t["kc"], t["T"]
                K.mm(ps.t[:, 0:T], t["kb"].t[t["rows"], kc * 128:(kc + 1) * 128], t["qb"].t[t["rows"], 0:T], True, True,
                     [t["kb"].d, t["qb"].d], [ps.d])

            n = len(tasks)
            for i in range(min(2, n)):
                emit_qk(i)
            for i in range(n):
                t = tasks[i]
                T = t["T"]
                ps = SB_[i % 3]
                p = pb[i % 4]
                K.act(p.t[:, 0:T], ps.t[:, 0:T], AF.Exp, [ps.d], [p.d], scale=t["sc"])
                if i + 2 < n:
                    emit_qk(i + 2)
                po = OB_[t["grp"] % 2]
                vc = t["vcol"]
                K.mm(po.t[0:65, 0:T], t["vb"].t[:, t["kc"], vc * 65:(vc + 1) * 65], p.t[:, 0:T], t["first"], t["last"],
                     [t["vb"].d, p.d], [po.d])
                if t["last"]:
                    g = t["grp"]
                    hg, t0, _ = groups[g]
                    rs, osb_, ao_ = rsb[g % 2], osb[g % 2], aob[g % 2]
                    K.recip(rs.t[64:65, 0:T], po.t[64:65, 0:T], [po.d], [rs.d])
                    K.mm(BCB.t[0:64, 0:T], cst(CI_ONES)[64:65, 0:64], rs.t[64:65, 0:T], True, True, [rs.d, CS.d], [BCB.d])
                    K.act(osb_.t[0:64, 0:T], po.t[0:64, 0:T], AF.Copy, [po.d], [osb_.d])
                    K.tt(DVE, ao_.t[0:64, 0:T], osb_.t[0:64, 0:T], BCB.t[0:64, 0:T], ALU.mult, [osb_.d, BCB.d], [ao_.d])
                    K.st(AO[hg * 64:(hg + 1) * 64, t0:t0 + T], ao_.t[0:64, 0:T], [ao_.d])

    def load_w13(layer, w13=None):
        if w13 is None:
            w13 = K.sb("w13", [128, 8, 2 * FFH], BF16)
        for k in range(8):
            for hh in range(2):
                K.ldcast(w13.t[:, k, hh * FFH:(hh + 1) * FFH], w13D[layer, k * 128:(k + 1) * 128, hh * FFH:(hh + 1) * FFH], [w13.d])
        return w13

    def phaseC1(layer, blocks, AOsrc, nK, woutD, Xsrc, xoff):
        with K.phase():
            w13 = K.sb("w13", [128, 8, 2 * FFH], BF16)
            with K.phase():
                wo = K.sb("wo", [128, nK, D], BF16)
                for k in range(nK):
                    K.ldcast(wo.t[:, k, :], woutD[k * 128:(k + 1) * 128, :], [wo.d])
                load_w13(layer, w13)
                phaseC1_body(layer, blocks, AOsrc, nK, wo, Xsrc, xoff)
            phaseC2(layer, K.c2_args[0], K.c2_args[1], K.c2_args[2], K.c2_args[3], w13)

    def phaseC1_body(layer, blocks, AOsrc, nK, wo, Xsrc, xoff):
        if True:
            gts = {}
            for s in set(b[2] for b in blocks):
                gts[s] = K.sb("gt", [128, D], F32)
                K.ld(gts[s].t[:, :], MODROW[layer, 0, s, :].partition_broadcast(128), [gts[s].d])
            aob = [K.sb("ao", [128, nK, 512], BF16) for _ in range(1)]
            xb = [K.sb("xc", [128, 4, D], F32) for _ in range(1)]
            tmpb = [K.sb("tmp", [128, 512], F32) for _ in range(2)]
            scr = (K.sb("junk", [128, D], BF16), K.sb("ss", [128, 4], F32), K.sb("xs", [128, 4, D], F32))
            uTb = [K.sb("uT", [128, 8, 512], BF16) for _ in range(1)]
            it = 0
            for bi, (t0, T, s) in enumerate(blocks):
                nch = T // 128
                ao = aob[0]
                K.ld(ao.t[:, :, 0:T], AOsrc[:, t0:t0 + T].rearrange("(k p) t -> p k t", p=128), [ao.d])
                xt = xb[0]
                K.ld(xt.t[:, 0:nch, :], Xsrc[t0 - xoff:t0 - xoff + T, :].rearrange("(c p) d -> p c d", p=128), [xt.d])
                for c in range(nch):
                    for n2 in range(2):
                        ps = PS[2 + it % 4]
                        tmp = tmpb[it % 2]
                        it += 1
                        for k in range(nK):
                            K.mm(ps.t[:, :], ao.t[:, k, c * 128:(c + 1) * 128], wo.t[:, k, n2 * 512:(n2 + 1) * 512], k == 0, k == nK - 1,
                                 [ao.d, wo.d], [ps.d])
                        K.tt(DVE, tmp.t[:, :], ps.t[:, :], gts[s].t[:, n2 * 512:(n2 + 1) * 512], ALU.mult, [ps.d, gts[s].d], [tmp.d])
                        K.tt(DVE, xt.t[:, c, n2 * 512:(n2 + 1) * 512], xt.t[:, c, n2 * 512:(n2 + 1) * 512], tmp.t[:, :], ALU.add,
                             [xt.d, tmp.d], [xt.d])
                K.ld(X1[t0:t0 + T, :].rearrange("(c p) d -> p c d", p=128), xt.t[:, 0:nch, :], (), R=[xt.d])
                uT = uTb[0]
                norm_to_uT(xt, nch, T, layer, 1, s, uT, scr, [PS[0], PS[1]], use_pool=False)
                K.ld(U2[:, t0:t0 + T].rearrange("(k p) t -> p k t", p=128), uT.t[:, :, 0:T], (), R=[uT.d])

    def phaseC2(layer, blocks, Xdst, xoff, is_out, w13=None):
        with K.phase():
            if w13 is None:
                w13 = load_w13(layer)
            w2 = K.sb("w2", [128, 22, D], BF16)
            for j in range(22):
                K.ldcast(w2.t[:, j, :], w2D[layer, j * 128:(j + 1) * 128, :], [w2.d])
            gt1 = K.sb("gt", [128, D], F32)
            gts = {0: gt1, 1: gt1}
            ub = [K.sb("u", [128, 8, 512], BF16) for _ in range(1)]
            xb = [K.sb("xf", [128, 4, D], F32) for _ in range(1)]
            hb = K.sb("h", [128, 22, 512], BF16)
            sgb = [K.sb("sg", [128, 512], F32) for _ in range(2)]
            tmpb = [K.sb("tmp", [128, 512], F32) for _ in range(2)]
            it = 0
            for bi, (t0, T, s) in enumerate(blocks):
                nch = T // 128
                u = ub[0]
                K.ld(u.t[:, :, 0:T], U2[:, t0:t0 + T].rearrange("(k p) t -> p k t", p=128), [u.d])
                if bi == 0 or blocks[bi - 1][2] != s:
                    K.ld(gt1.t[:, :], MODROW[layer, 1, s, :].partition_broadcast(128), [gt1.d])
                xt = xb[0]
                K.ld(xt.t[:, 0:nch, :], X1[t0:t0 + T, :].rearrange("(c p) d -> p c d", p=128), [xt.d])
                for j in range(22):
                    psg = PS[(2 * j) % 4]
                    psu = PS[(2 * j + 1) % 4]
                    for k in range(8):
                        K.mm(psg.t[:, 0:T], w13.t[:, k, j * 128:(j + 1) * 128], u.t[:, k, 0:T], k == 0, k == 7, [w13.d, u.d], [psg.d])
                    for k in range(8):
                        K.mm(psu.t[:, 0:T], w13.t[:, k, FFH + j * 128:FFH + (j + 1) * 128], u.t[:, k, 0:T], k == 0, k == 7, [w13.d, u.d], [psu.d])
                    sg = sgb[j % 2]
                    K.act(sg.t[:, 0:T], psg.t[:, 0:T], AF.Silu, [psg.d], [sg.d])
                    K.tt(DVE, hb.t[:, j, 0:T], sg.t[:, 0:T], psu.t[:, 0:T], ALU.mult, [sg.d, psu.d], [hb.d])
                for c in range(nch):
                    for n2 in range(2):
                        ps = PS[4 + it % 4]
                        tmp = tmpb[it % 2]
                        it += 1
                        for j in range(22):
                            K.mm(ps.t[:, :], hb.t[:, j, c * 128:(c + 1) * 128], w2.t[:, j, n2 * 512:(n2 + 1) * 512], j == 0, j == 21,
                                 [hb.d, w2.d], [ps.d])
                        K.tt(DVE, tmp.t[:, :], ps.t[:, :], gts[s].t[:, n2 * 512:(n2 + 1) * 512], ALU.mult, [ps.d, gts[s].d], [tmp.d])
                        K.tt(POOL, xt.t[:, c, n2 * 512:(n2 + 1) * 512], xt.t[:, c, n2 * 512:(n2 + 1) * 512], tmp.t[:, :], ALU.add,
                             [xt.d, tmp.d], [xt.d])
                K.em.dma(em.pool, Xdst[t0 - xoff:t0 - xoff + T, :].rearrange("(c p) d -> p c d", p=128), xt.t[:, 0:nch, :],
                         reads=[xt.d], is_output=is_out)

    swin = K.dram("ssm_w_in", [D, 5664], F32, "ExternalInput")
    swout = K.dram("ssm_w_out", [2048, D], F32, "ExternalInput")
    convw_col = K.dram("convw_col", [128, 60], F32, "ExternalInput")
    convb_col = K.dram("convb_col", [128, 12], F32, "ExternalInput")
    rows1 = K.dram("rows1", [NR1], F32, "ExternalInput")
    sqc2 = K.dram("sqc2", [128, 2 * 128 + 4], F32, "ExternalInput")
    selc = K.dram("selc", [16, 2048], F32, "ExternalInput")
    SZ = K.dram("SZ", [NT, D], F32)
    SRG = K.dram("SRG", [NT, D], F32)
    DTs = K.dram("DTs", [NT, 32], F32)
    RVs = K.dram("RVs", [NT, D], BF16)
    XBC = K.dram("XBC", [1536, NT], BF16)
    RQ = K.dram("RQ", [4, 128, NT], BF16)
    RK = K.dram("RK", [4, 128, NT], BF16)
    RKT = K.dram("RKT", [NT, 512], BF16)
    XS = K.dram("XS", [NT, D], F32)
    BT = K.dram("BT", [NT, 256], BF16)
    BFs = K.dram("BFs", [256, NT], BF16)
    CFs = K.dram("CFs", [256, NT], BF16)
    YF = K.dram("YF", [NT, 2048], F32)
    YS = K.dram("YS", [2048, NT], BF16)
    ONEB = K.gsb("oneb", [128, 1], F32)
    K.memset(DVE, ONEB.t[:, :], 1.0, [ONEB.d])

    def softplus_small(x, tmp, R, W_):
        K.ts(DVE, tmp, x, -1.0, ALU.mult, R, W_)
        K.tt(DVE, tmp, tmp, x, ALU.max, R + W_, W_)
        K.act(tmp, tmp, AF.Exp, W_, W_, scale=-1.0)
        K.act(tmp, tmp, AF.Ln, W_, W_, bias=ONEB.t[0:x.shape[0], 0:1])
        K.stt(x, x, 0.0, tmp, ALU.max, ALU.add, R + W_, R)

    def l1_phaseA():
        with K.phase():
            W1 = K.sb("w1", [128, 8, 5664], BF16)
            for k in range(8):
                for (a, b) in ((0, 1888), (1888, 3776), (3776, 5664)):
                    K.ldcast(W1.t[:, k, a:b], swin[k * 128:(k + 1) * 128, a:b], [W1.d])
            xb = [K.sb("xa", [128, 4, D], F32)]
            scr = (K.sb("junk", [128, D], BF16), K.sb("ss", [128, 4], F32), K.sb("xs", [128, 4, D], F32))
            uTb = [K.sb("uT", [128, 8, 512], BF16) for _ in range(2)]
            ropeb = [K.sb("rope", [128, 2, 512], F32) for _ in range(2)]
            qnb = [K.sb("qn", [128, 512], BF16) for _ in range(2)]
            t1b = [K.sb("t1", [128, 512], F32) for _ in range(2)]
            t2b = [K.sb("t2", [128, 512], F32) for _ in range(2)]
            ofb = [K.sb("of", [128, 512], F32) for _ in range(2)]
            obb = [K.sb("ob", [128, 512], BF16) for _ in range(3)]
            tmz = [K.sb("tmz", [128, D], F32) for _ in range(3)]
            rvt = [K.sb("rvt", [128, D], BF16) for _ in range(2)]
            rktb = [K.sb("rkt", [128, 512], BF16) for _ in range(2)]
            dtt = [K.sb("dtt", [128, 32], F32) for _ in range(2)]
            cnt = {"p": 0, "n": 0, "o": 0, "z": 0, "t": 0}

            def pbank(banks):
                cnt["p"] += 1
                return banks[cnt["p"] % len(banks)]

            for bi, (t0, T, s) in enumerate(K.blocks):
                nch = T // 128
                xt = xb[0]
                K.ld(xt.t[:, 0:nch, :], X2[t0:t0 + T, :].rearrange("(c p) d -> p c d", p=128), [xt.d])
                rp = ropeb[bi % 2]
                K.ld(rp.t[:, :, 0:T], ropeD[0:2, :, t0:t0 + T].rearrange("f p t -> p f t"), [rp.d])
                uT = uTb[bi % 2]
                norm_to_uT(xt, nch, T, 1, 0, s, uT, scr, [PS[0], PS[1]])
                for ch in range(12):
                    ps = pbank([PS[2], PS[3]])
                    for k in range(8):
                        K.mm(ps.t[:, 0:T], W1.t[:, k, 1024 + ch * 128:1024 + (ch + 1) * 128], uT.t[:, k, 0:T], k == 0, k == 7, [W1.d, uT.d], [ps.d])
                    ob = obb[cnt["o"] % 3]
                    cnt["o"] += 1
                    K.act(ob.t[:, 0:T], ps.t[:, 0:T], AF.Copy, [ps.d], [ob.d])
                    K.st(XBC[ch * 128:(ch + 1) * 128, t0:t0 + T], ob.t[:, 0:T], [ob.d])
                for kind in range(2):
                    for ch in range(4):
                        col0 = 2592 + kind * 512 + ch * 128
                        ps = pbank([PS[2], PS[3]])
                        for k in range(8):
                            K.mm(ps.t[:, 0:T], W1.t[:, k, col0:col0 + 128], uT.t[:, k, 0:T], k == 0, k == 7, [W1.d, uT.d], [ps.d])
                        i = cnt["n"] % 2
                        cnt["n"] += 1
                        qn, t1, t2, of = qnb[i], t1b[i], t2b[i], ofb[i]
                        K.act(qn.t[:, 0:T], ps.t[:, 0:T], AF.Copy, [ps.d], [qn.d], scale=(1.0 if kind == 0 else 0.125))
                        psr = PS[0]
                        K.mm(psr.t[:, 0:T], ROTb, qn.t[:, 0:T], True, True, [qn.d, CB.d], [psr.d])
                        K.tt(DVE, t1.t[:, 0:T], qn.t[:, 0:T], rp.t[:, 0, 0:T], ALU.mult, [qn.d, rp.d], [t1.d])
                        K.tt(DVE, t2.t[:, 0:T], psr.t[:, 0:T], rp.t[:, 1, 0:T], ALU.mult, [psr.d, rp.d], [t2.d])
                        K.tt(POOL, of.t[:, 0:T], t1.t[:, 0:T], t2.t[:, 0:T], ALU.add, [t1.d, t2.d], [of.d])
                        ob = obb[cnt["o"] % 3]
                        cnt["o"] += 1
                        K.act(ob.t[:, 0:T], of.t[:, 0:T], AF.Copy, [of.d], [ob.d])
                        K.st((RQ if kind == 0 else RK)[ch, :, t0:t0 + T], ob.t[:, 0:T], [ob.d])
                        if kind == 1:
                            for c in range(nch):
                                K.tr(PS[4 + c].t[:, ch * 128:(ch + 1) * 128], of.t[:, c * 128:(c + 1) * 128], cst(CI_IDENT), [of.d, CS.d], [PS[4 + c].d])
                for c in range(nch):
                    rkt = rktb[c % 2]
                    K.copy(DVE, rkt.t[:, :], PS[4 + c].t[:, :], [PS[4 + c].d], [rkt.d])
                    K.st(RKT[t0 + c * 128:t0 + (c + 1) * 128, :], rkt.t[:, :], [rkt.d])
                TMB = [PS[2], PS[3], PS[4], PS[5], PS[6], PS[7]]
                for c in range(nch):
                    r0 = t0 + c * 128
                    for (col0, kind) in ((0, "z"), (4640, "g"), (3616, "v")):
                        if kind == "v":
                            dst = rvt[cnt["t"] % 2]
                            cnt["t"] += 1
                        else:
                            dst = tmz[cnt["z"] % 3]
                            cnt["z"] += 1
                        for half in range(2):
                            ps = pbank(TMB)
                            for k in range(8):
                                K.mm(ps.t[:, :], uT.t[:, k, c * 128:(c + 1) * 128], W1.t[:, k, col0 + half * 512:col0 + (half + 1) * 512],
                                     k == 0, k == 7, [W1.d, uT.d], [ps.d])
                            if kind == "v":
                                K.copy(DVE, dst.t[:, half * 512:(half + 1) * 512], ps.t[:, :], [ps.d], [dst.d])
                            else:
                                K.act(dst.t[:, half * 512:(half + 1) * 512], ps.t[:, :], AF.Silu, [ps.d], [dst.d])
                        K.st({"z": SZ, "g": SRG, "v": RVs}[kind][r0:r0 + 128, :], dst.t[:, :], [dst.d])
                    ps = pbank(TMB)
                    for k in range(8):
                        K.mm(ps.t[:, 0:32], uT.t[:, k, c * 128:(c + 1) * 128], W1.t[:, k, 2560:2592], k == 0, k == 7, [W1.d, uT.d], [ps.d])
                    dd = dtt[c % 2]
                    K.copy(DVE, dd.t[:, :], ps.t[:, 0:32], [ps.d], [dd.d])
                    K.st(DTs[r0:r0 + 128, :], dd.t[:, :], [dd.d])

    def l1_phaseV():
        with K.phase():
            cwc = K.sb("cwc", [128, 60], F32)
            K.ld(cwc.t[:, :], convw_col[:, :], [cwc.d])
            cbc = K.sb("cbc", [128, 12], F32)
            K.ld(cbc.t[:, :], convb_col[:, :], [cbc.d])
            cbr = K.sb("cbr", [1, 1536], F32)
            K.ld(cbr.t[:, :], rows1[R_CONVB:R_CONVB + 1536].partition_broadcast(1), [cbr.d])
            DG = K.sb("dg", [128, 60, 128], BF16)
            for idx in range(60):
                K.ts(DVE if idx % 2 == 0 else POOL, DG.t[:, idx, :], cst(CI_IDENT), cwc.t[:, idx:idx + 1], ALU.mult, [CS.d, cwc.d], [DG.d])
            xwb = [K.sb("xw", [128, 12, 516], BF16) for _ in range(2)]
            obb = [K.sb("ob", [128, 512], BF16) for _ in range(2)]
            xst = [K.sb("xst", [128, D], F32) for _ in range(2)]
            btt = [K.sb("btt", [128, 256], BF16) for _ in range(2)]
            ones_row = cst(CI_ONES)[0:1, 0:128]
            cnt = {"p": 0}
            BK = [PS[0], PS[1], PS[2], PS[3], PS[4], PS[5], PS[6], PS[7]]

            def pbank():
                cnt["p"] += 1
                return BK[cnt["p"] % 8]

            for bi, (t0, T, s) in enumerate(K.blocks):
                nch = T // 128
                seg0, seg1 = (0, CTX) if s == 1 else (CTX, NT)
                lo, hi = max(t0 - 2, seg0), min(t0 + T + 2, seg1)
                xw = xwb[bi % 2]
                K.memset(POOL, xw.t[:, :, :], 0.0, [xw.d])
                K.ld(xw.t[:, :, lo - (t0 - 2):hi - (t0 - 2)], XBC[:, lo:hi].rearrange("(c p) t -> p c t", p=128), [xw.d])
                for ch in range(8, 12):
                    ps = pbank()
                    for k in range(5):
                        K.mm(ps.t[:, 0:T], DG.t[:, k * 12 + ch, :], xw.t[:, ch, k:k + T], k == 0, k == 4, [DG.d, xw.d], [ps.d])
                    ob = obb[ch % 2]
                    K.act(ob.t[:, 0:T], ps.t[:, 0:T], AF.Silu, [ps.d, cbc.d], [ob.d], bias=cbc.t[:, ch:ch + 1])
                    dstD = BFs if ch < 10 else CFs
                    r = (ch - 8) % 2
                    K.st(dstD[r * 128:(r + 1) * 128, t0:t0 + T], ob.t[:, 0:T], [ob.d])
                for c in range(nch):
                    r0 = t0 + c * 128
                    banks = [pbank(), pbank(), pbank()]
                    for ch in range(10):
                        tgt = banks[ch // 4]
                        o_ap = tgt.t[:, (ch % 4) * 128:(ch % 4 + 1) * 128]
                        for k in range(5):
                            K.mm(o_ap, xw.t[:, ch, c * 128 + k:c * 128 + k + 128], DG.t[:, k * 12 + ch, :], k == 0, False, [DG.d, xw.d], [tgt.d])
                        K.mm(o_ap, ones_row, cbr.t[0:1, ch * 128:(ch + 1) * 128], False, True, [CS.d, cbr.d], [tgt.d])
                    xo = xst[c % 2]
                    for hh in range(2):
                        K.act(xo.t[:, hh * 512:(hh + 1) * 512], banks[hh].t[:, :], AF.Silu, [banks[hh].d], [xo.d])
                    K.st(XS[r0:r0 + 128, :], xo.t[:, :], [xo.d])
                    bo = btt[c % 2]
                    K.act(bo.t[:, :], banks[2].t[:, 0:256], AF.Silu, [banks[2].d], [bo.d])
                    K.st(BT[r0:r0 + 128, :], bo.t[:, :], [bo.d])

    def l1_scan(dirn):
        fin = dirn == 1
        with K.phase():
            C2 = K.sb("c2", [128, 2 * 128 + 4], F32)
            K.ld(C2.t[:, :], sqc2[:, :], [C2.d])
            SEL = K.sb("sel", [16, 2048], F32)
            K.ld(SEL.t[:, :], selc[:, :], [SEL.d])
            IDX = C2.t[:, dirn * 128:(dirn + 1) * 128]
            colA = C2.t[:, 256 + dirn:256 + dirn + 1]
            colE = C2.t[:, 258 + dirn:258 + dirn + 1]
            MASK = cst(CI_TRI) if dirn == 0 else cst(CI_TRIT)

            def brow(name, off, n):
                b = K.sb(name, [128, n], F32)
                K.ld(b.t[:, :], rows1[off:off + n].partition_broadcast(128), [b.d])
                return b

            DTB = brow("dtb", R_DTB + dirn * 16, 16)
            ANEG = brow("aneg", R_ALOG + dirn * 16, 16)
            K.act(ANEG.t[:, :], ANEG.t[:, :], AF.Exp, [ANEG.d], [ANEG.d])
            K.ts(DVE, ANEG.t[:, :], ANEG.t[:, :], -1.0, ALU.mult, [ANEG.d], [ANEG.d])
            LG = brow("lg", R_RLOG + dirn * 8, 8)
            lgt = K.sb("lgt", [128, 8], F32)
            K.ts(DVE, LG.t[:, :], LG.t[:, :], -1.0, ALU.mult, [LG.d], [LG.d])
            softplus_small(LG.t[:, :], lgt.t[:, :], [LG.d], [lgt.d])
            K.ts(DVE, LG.t[:, :], LG.t[:, :], -1.0, ALU.mult, [LG.d], [LG.d])
            LM = K.sb("lm", [128, 8, 128], F32)
            for h in range(8):
                K.ts(DVE, LM.t[:, h, :], IDX, LG.t[:, h:h + 1], ALU.mult, [C2.d, LG.d], [LM.d])
            K.act(LM.t[:, :, :], LM.t[:, :, :], AF.Exp, [LM.d], [LM.d])
            K.tt(DVE, LM.t[:, :, :], LM.t[:, :, :], MASK.unsqueeze(1).broadcast_to([128, 8, 128]), ALU.mult, [LM.d, CS.d], [LM.d])
            EAr = K.sb("ear", [128, 8], F32)
            K.ts(DVE, EAr.t[:, :], LG.t[:, :], colA, ALU.mult, [LG.d, C2.d], [EAr.d])
            K.act(EAr.t[:, :], EAr.t[:, :], AF.Exp, [EAr.d], [EAr.d])
            DEr = K.sb("der", [128, 8], F32)
            K.ts(DVE, DEr.t[:, :], LG.t[:, :], colE, ALU.mult, [LG.d, C2.d], [DEr.d])
            K.act(DEr.t[:, :], DEr.t[:, :], AF.Exp, [DEr.d], [DEr.d])
            CDR = K.sb("cdr", [128, 4], F32)
            lg2 = LG.t[:, :].rearrange("p (q two) -> p q two", two=2)
            K.ts(DVE, CDR.t[0:64, :], lg2[0:64, :, 0], 128.0, ALU.mult, [LG.d], [CDR.d])
            K.ts(DVE, CDR.t[64:128, :], lg2[64:128, :, 1], 128.0, ALU.mult, [LG.d], [CDR.d])
            K.act(CDR.t[:, :], CDR.t[:, :], AF.Exp, [CDR.d], [CDR.d])
            if fin:
                DSK = brow("dsk", R_DSK, 1024)
                WN = brow("wn", R_SSDN, 1024)
                RN = brow("rn", R_RETN, 1024)
            S = [K.sb("S", [128, 512], F32) for _ in range(2)]
            Sbf = [K.sb("Sbf", [128, 512], BF16) for _ in range(2)]
            SR = K.sb("SR", [128, 4, 128], F32)
            SRbf = K.sb("SRbf", [128, 4, 128], BF16)
            for b in S + Sbf:
                K.memset(DVE, b.t[:, :], 0.0, [b.d])
            K.memset(DVE, SR.t[:, :, :], 0.0, [SR.d])
            K.memset(DVE, SRbf.t[:, :, :], 0.0, [SRbf.d])
            NB = 3
            NF = 2
            xsb = [K.sb("xs", [128, D], F32) for _ in range(NB)]
            btb = [K.sb("bt", [128, 256], BF16) for _ in range(NB)]
            bfb = [K.sb("bf", [128, 2, 128], BF16) for _ in range(NB)]
            cfb = [K.sb("cf", [128, 2, 128], BF16) for _ in range(NB)]
            dtb_ = [K.sb("dt", [128, 16], F32) for _ in range(NB)]
            rqb = [K.sb("rq", [128, 4, 128], BF16) for _ in range(NB)]
            rkb = [K.sb("rk", [128, 4, 128], BF16) for _ in range(NB)]
            rktb = [K.sb("rkt", [128, 512], BF16) for _ in range(NB)]
            rvb = [K.sb("rv", [128, D], BF16) for _ in range(NB)]
            if fin:
                yfb = [K.sb("yf", [128, 2048], F32) for _ in range(NF)]
                szb = [K.sb("sz", [128, D], F32) for _ in range(NF)]
                sgb = [K.sb("srg", [128, D], F32) for _ in range(NF)]
            smb = [K.sb("sm", [128, 8, 16], F32) for _ in range(2)]
            at = K.sb("at", [16, 256], F32)
            E = K.sb("E", [128, 16, 128], F32)
            Gm = K.sb("Gm", [128, 2, 128], F32)
            Mb = [K.sb("M", [128, 16, 128], BF16) for _ in range(2)]
            MRb = [K.sb("MR", [128, 8, 128], BF16) for _ in range(2)]
            Vdb = [K.sb("Vd", [128, D], BF16) for _ in range(2)]
            Vdecb = [K.sb("Vdec", [128, D], BF16) for _ in range(2)]
            RVdb = [K.sb("RVd", [128, D], BF16) for _ in range(2)]
            yo = K.sb("yo", [128, D], F32)
            ydb = [K.sb("yd", [128, 2048], F32) for _ in range(2)]
            if fin:
                junk = K.sb("junk", [128, D], F32)
                st8 = K.sb("st8", [128, 8, 8], F32)
                ynb = K.sb("yn", [128, 2048], F32)
                ysb = [K.sb("ys", [128, 4, 128], BF16) for _ in range(2)]
            cnt = {"p": 0}

            def pbank():
                cnt["p"] += 1
                return PS[cnt["p"] % 8]

            order = list(range(NCH)) if dirn == 0 else [1, 0] + list(range(NCH - 1, 1, -1))
            NO = len(order)

            def loads(ci):
                c = order[ci]
                t0 = c * 128
                i = ci % NB
                K.ld(xsb[i].t[:, :], XS[t0:t0 + 128, :], [xsb[i].d])
                K.ld(btb[i].t[:, :], BT[t0:t0 + 128, :], [btb[i].d])
                K.ld(bfb[i].t[:, :, :], BFs[:, t0:t0 + 128].rearrange("(g n) t -> n g t", n=128), [bfb[i].d])
                K.ld(cfb[i].t[:, :, :], CFs[:, t0:t0 + 128].rearrange("(g n) t -> n g t", n=128), [cfb[i].d])
                K.ld(dtb_[i].t[:, :], DTs[t0:t0 + 128, dirn * 16:(dirn + 1) * 16], [dtb_[i].d])
                K.ld(rqb[i].t[:, :, :], RQ[:, :, t0:t0 + 128].rearrange("c p t -> p c t"), [rqb[i].d])
                K.ld(rkb[i].t[:, :, :], RK[:, :, t0:t0 + 128].rearrange("c p t -> p c t"), [rkb[i].d])
                K.ld(rktb[i].t[:, :], RKT[t0:t0 + 128, :], [rktb[i].d])
                K.ld(rvb[i].t[:, :], RVs[t0:t0 + 128, :], [rvb[i].d])

            def loads_fin(ci):
                c = order[ci]
                t0 = c * 128
                i = ci % NF
                K.ld(yfb[i].t[:, :], YF[t0:t0 + 128, :], [yfb[i].d])
                K.ld(szb[i].t[:, :], SZ[t0:t0 + 128, :], [szb[i].d])
                K.ld(sgb[i].t[:, :], SRG[t0:t0 + 128, :], [sgb[i].d])

            def bc3(ap2, n):
                return ap2.unsqueeze(2).broadcast_to([128, ap2.shape[1], n])

            def partA(ci):
                if ci + 1 < NO:
                    loads(ci + 1)
                i = ci % NB
                j2 = ci % 2
                xs_c, bf_c, cf_c, dt_c = xsb[i], bfb[i], cfb[i], dtb_[i]
                rq_c, rk_c, rv_c = rqb[i], rkb[i], rvb[i]
                sm = smb[j2]
                M, MR, Vd, Vdec, RVd = Mb[j2], MRb[j2], Vdb[j2], Vdecb[j2], RVdb[j2]
                sp, tmpv, la, Acol, Atot, expA, dece, cd = [sm.t[:, j, :] for j in range(8)]
                smd = [sm.d]
                K.tt(DVE, sp, dt_c.t[:, :], DTB.t[:, :], ALU.add, [dt_c.d, DTB.d], smd)
                softplus_small(sp, tmpv, smd, smd)
                K.tt(DVE, la, sp, ANEG.t[:, :], ALU.mult, smd + [ANEG.d], smd)
                psc = pbank()
                K.mm(psc.t[:, 0:16], MASK, la, True, True, smd + [CS.d], [psc.d])
                K.mm(psc.t[:, 16:32], cst(CI_ONES), la, True, True, smd + [CS.d], [psc.d])
                K.mm(psc.t[0:16, 128:256], la, MASK, True, True, smd + [CS.d], [psc.d])
                K.copy(DVE, sm.t[:, 3:5, :], psc.t[:, 0:32].rearrange("p (a b) -> p a b", b=16), [psc.d], smd)
                K.copy(DVE, at.t[:, 0:128], psc.t[0:16, 128:256], [psc.d], [at.d])
                K.ts(DVE, at.t[:, 128:256], at.t[:, 0:128], -1.0, ALU.mult, [at.d], [at.d])
                K.act(expA, Acol, AF.Exp, smd, smd)
                K.tt(DVE, dece, Atot, Acol, ALU.subtract, smd, smd)
                K.act(dece, dece, AF.Exp, smd, smd)
                K.act(cd, Atot, AF.Exp, smd, smd)
                K.tt(DVE, tmpv, sp, dece, ALU.mult, smd, smd)
                xs3 = xs_c.t[:, :].rearrange("p (h e) -> p h e", e=64)
                K.tt(DVE, Vd.t[:, :].rearrange("p (h e) -> p h e", e=64), xs3, bc3(sp, 64), ALU.mult, [xs_c.d] + smd, [Vd.d])
                K.tt(POOL, Vdec.t[:, :].rearrange("p (h e) -> p h e", e=64), xs3, bc3(tmpv, 64), ALU.mult, [xs_c.d] + smd, [Vdec.d])
                for q in range(4):
                    psd = pbank()
                    for hh in range(4):
                        h = q * 4 + hh
                        o_ap = psd.t[:, hh * 128:(hh + 1) * 128]
                        K.mm(o_ap, SEL.t[0:16, h * 128:(h + 1) * 128], at.t[0:16, 0:128], True, False, [SEL.d, at.d], [psd.d])
                        K.mm(o_ap, at.t[0:16, 128:256], SEL.t[0:16, h * 128:(h + 1) * 128], False, True, [SEL.d, at.d], [psd.d])
                    K.tt(DVE, E.t[:, q * 4:(q + 1) * 4, :], psd.t[:, :].rearrange("p (h i) -> p h i", i=128),
                         MASK.unsqueeze(1).broadcast_to([128, 4, 128]), ALU.mult, [psd.d, CS.d], [E.d])
                K.act(E.t[:, :, :], E.t[:, :, :], AF.Exp, [E.d], [E.d])
                psg = pbank()
                for g in range(2):
                    K.mm(psg.t[:, g * 128:(g + 1) * 128], bf_c.t[:, g, :], cf_c.t[:, g, :], True, True, [bf_c.d, cf_c.d], [psg.d])
                K.tt(DVE, Gm.t[:, :, :], psg.t[:, 0:256].rearrange("p (g i) -> p g i", i=128),
                     MASK.unsqueeze(1).broadcast_to([128, 2, 128]), ALU.mult, [psg.d, CS.d], [Gm.d])
                for g in range(2):
                    K.tt(DVE if g == 0 else POOL, M.t[:, g * 8:(g + 1) * 8, :], E.t[:, g * 8:(g + 1) * 8, :],
                         Gm.t[:, g, :].unsqueeze(1).broadcast_to([128, 8, 128]), ALU.mult, [E.d, Gm.d], [M.d])
                psgr = [pbank(), pbank()]
                for h in range(8):
                    rows = slice((h % 2) * 64, (h % 2) * 64 + 64)
                    K.mm(psgr[h % 2].t[:, (h // 2) * 128:(h // 2 + 1) * 128], rk_c.t[rows, h // 2, :], rq_c.t[rows, h // 2, :], True, True,
                         [rk_c.d, rq_c.d], [psgr[h % 2].d])
                for par in range(2):
                    K.tt(DVE, MR.t[:, :, :].rearrange("p (b two) i -> p b two i", two=2)[:, :, par, :],
                         psgr[par].t[:, :].rearrange("p (h i) -> p h i", i=128),
                         LM.t[:, :, :].rearrange("p (b two) i -> p b two i", two=2)[:, :, par, :],
                         ALU.mult, [psgr[par].d, LM.d], [MR.d])
                K.tt(DVE, RVd.t[:, :].rearrange("p (h e) -> p h e", e=128), rv_c.t[:, :].rearrange("p (h e) -> p h e", e=128),
                     bc3(DEr.t[:, :], 128), ALU.mult, [rv_c.d, DEr.d], [RVd.d])

            def partB(ci):
                if fin and ci + 1 < NO:
                    loads_fin(ci + 1)
                c = order[ci]
                t0 = c * 128
                i = ci % NB
                j2 = ci % 2
                xs_c, bt_c, cf_c = xsb[i], btb[i], cfb[i]
                rq_c, rkt_c, rv_c = rqb[i], rktb[i], rvb[i]
                sm = smb[j2]
                M, MR, Vd, Vdec, RVd = Mb[j2], MRb[j2], Vdb[j2], Vdecb[j2], RVdb[j2]
                sp, tmpv, la, Acol, Atot, expA, dece, cd = [sm.t[:, j, :] for j in range(8)]
                smd = [sm.d]
                yd = ydb[ci % 2]
                for g in range(2):
                    pso = pbank()
                    K.mm(pso.t[:, :], cf_c.t[:, g, :], Sbf[g].t[:, :], True, True, [cf_c.d, Sbf[g].d], [pso.d])
                    K.tt(DVE, yo.t[:, g * 512:(g + 1) * 512].rearrange("p (h e) -> p h e", e=64),
                         pso.t[:, :].rearrange("p (h e) -> p h e", e=64), bc3(expA[:, g * 8:(g + 1) * 8], 64), ALU.mult,
                         [pso.d] + smd, [yo.d])
                for g in range(2):
                    psy = pbank()
                    for hl in range(8):
                        h = g * 8 + hl
                        K.mm(psy.t[:, hl * 64:(hl + 1) * 64], M.t[:, h, :], Vd.t[:, h * 64:(h + 1) * 64], True, True, [M.d, Vd.d], [psy.d])
                    K.tt(DVE, yd.t[:, g * 512:(g + 1) * 512], psy.t[:, :], yo.t[:, g * 512:(g + 1) * 512], ALU.add, [psy.d, yo.d], [yd.d])
                for g in range(2):
                    psd2 = pbank()
                    K.mm(psd2.t[:, :], bt_c.t[:, g * 128:(g + 1) * 128], Vdec.t[:, g * 512:(g + 1) * 512], True, True, [bt_c.d, Vdec.d], [psd2.d])
                    s3 = S[g].t[:, :].rearrange("p (h e) -> p h e", e=64)
                    K.tt(POOL, s3, s3, bc3(cd[:, g * 8:(g + 1) * 8], 64), ALU.mult, [S[g].d] + smd, [S[g].d])
                    K.tt(DVE, S[g].t[:, :], S[g].t[:, :], psd2.t[:, :], ALU.add, [S[g].d, psd2.d], [S[g].d])
                    K.act(Sbf[g].t[:, :], S[g].t[:, :], AF.Copy, [S[g].d], [Sbf[g].d])
                psro = [pbank(), pbank()]
                for h in range(8):
                    rows = slice((h % 2) * 64, (h % 2) * 64 + 64)
                    K.mm(psro[h % 2].t[:, (h // 2) * 128:(h // 2 + 1) * 128], rq_c.t[rows, h // 2, :], SRbf.t[rows, h // 2, :], True, True,
                         [rq_c.d, SRbf.d], [psro[h % 2].d])
                for par in range(2):
                    K.tt(DVE, yo.t[:, :].rearrange("p (b two e) -> p b two e", two=2, e=128)[:, :, par, :],
                         psro[par].t[:, :].rearrange("p (h e) -> p h e", e=128),
                         bc3(EAr.t[:, :].rearrange("p (b two) -> p b two", two=2)[:, :, par], 128), ALU.mult,
                         [psro[par].d, EAr.d], [yo.d])
                psyr = [pbank(), pbank()]
                for h in range(8):
                    K.mm(psyr[h // 4].t[:, (h % 4) * 128:(h % 4 + 1) * 128], MR.t[:, h, :], rv_c.t[:, h * 128:(h + 1) * 128], True, True,
                         [MR.d, rv_c.d], [psyr[h // 4].d])
                for q in range(2):
                    K.tt(DVE, yd.t[:, 1024 + q * 512:1024 + (q + 1) * 512], psyr[q].t[:, :], yo.t[:, q * 512:(q + 1) * 512], ALU.add,
                         [psyr[q].d, yo.d], [yd.d])
                psdr = [pbank(), pbank()]
                for h in range(8):
                    pr = h // 2
                    K.mm(psdr[h // 4].t[:, (h % 4) * 128:(h % 4 + 1) * 128], rkt_c.t[:, pr * 128:(pr + 1) * 128], RVd.t[:, h * 128:(h + 1) * 128],
                         True, True, [rkt_c.d, RVd.d], [psdr[h // 4].d])
                K.tt(POOL, SR.t[:, :, :], SR.t[:, :, :], bc3(CDR.t[:, :], 128), ALU.mult, [SR.d, CDR.d], [SR.d])
                for half in range(2):
                    rows = slice(half * 64, half * 64 + 64)
                    for q in range(2):
                        src = psdr[q].t[:, :].rearrange("p (pp two e) -> p pp two e", two=2, e=128)[rows, :, half, :]
                        K.tt(DVE, SR.t[rows, 2 * q:2 * q + 2, :], SR.t[rows, 2 * q:2 * q + 2, :], src, ALU.add, [SR.d, psdr[q].d], [SR.d])
                K.act(SRbf.t[:, :, :], SR.t[:, :, :], AF.Copy, [SR.d], [SRbf.d])
                if not fin:
                    K.st(YF[t0:t0 + 128, :], yd.t[:, :], [yd.d])
                    return
                yf_c, sz_c, sg_c = yfb[ci % NF], szb[ci % NF], sgb[ci % NF]
                K.tt(POOL, yd.t[:, :], yd.t[:, :], yf_c.t[:, :], ALU.add, [yd.d, yf_c.d], [yd.d])
                K.tt(POOL, junk.t[:, :], xs_c.t[:, :], DSK.t[:, :], ALU.mult, [xs_c.d, DSK.d], [junk.d])
                K.tt(DVE, yd.t[:, 0:1024], yd.t[:, 0:1024], junk.t[:, :], ALU.add, [yd.d, junk.d], [yd.d])
                K.tt(DVE, yd.t[:, 0:1024], yd.t[:, 0:1024], sz_c.t[:, :], ALU.mult, [yd.d, sz_c.d], [yd.d])
                ssg = st8.t[:, 0, 0:2]
                for g in range(2):
                    K.act(junk.t[:, 0:512], yd.t[:, g * 512:(g + 1) * 512], AF.Square, [yd.d], [junk.d, st8.d], accum_out=st8.t[:, 0, g:g + 1])
                K.ts(DVE, ssg, ssg, 1.0 / 512, ALU.mult, [st8.d], [st8.d], s2=EPS, op1=ALU.add)
                K.act(ssg, ssg, AF.Sqrt, [st8.d], [st8.d])
                K.recip(ssg, ssg, [st8.d], [st8.d])
                for g in range(2):
                    K.stt(ynb.t[:, g * 512:(g + 1) * 512], yd.t[:, g * 512:(g + 1) * 512], st8.t[:, 0, g:g + 1], WN.t[:, g * 512:(g + 1) * 512],
                          ALU.mult, ALU.mult, [yd.d, st8.d, WN.d], [ynb.d])
                yr3 = yd.t[:, 1024:2048].rearrange("p (h e) -> p h e", e=128)
                s1, s2, mean, m2 = st8.t[:, 1, :], st8.t[:, 2, :], st8.t[:, 3, :], st8.t[:, 4, :]
                em.op(DVE, lambda: nc.vector.tensor_reduce(out=s1, in_=yr3, axis=AX.X, op=ALU.add), [yd.d], [st8.d])
                K.act(junk.t[:, :], yd.t[:, 1024:2048], AF.Square, [yd.d], [junk.d])
                em.op(DVE, lambda: nc.vector.tensor_reduce(out=s2, in_=junk.t[:, :].rearrange("p (h e) -> p h e", e=128), axis=AX.X, op=ALU.add),
                      [junk.d], [st8.d])
                K.ts(DVE, mean, s1, 1.0 / 128, ALU.mult, [st8.d], [st8.d])
                K.tt(DVE, m2, mean, mean, ALU.mult, [st8.d], [st8.d])
                K.stt(s2, s2, 1.0 / 128, m2, ALU.mult, ALU.subtract, [st8.d], [st8.d])
                K.ts(DVE, s2, s2, EPS, ALU.add, [st8.d], [st8.d])
                K.act(s2, s2, AF.Sqrt, [st8.d], [st8.d])
                K.recip(s2, s2, [st8.d], [st8.d])
                yn3 = ynb.t[:, 1024:2048].rearrange("p (h e) -> p h e", e=128)
                K.tt(DVE, yn3, yr3, bc3(mean, 128), ALU.subtract, [yd.d, st8.d], [ynb.d])
                K.tt(POOL, yn3, yn3, bc3(s2, 128), ALU.mult, [ynb.d, st8.d], [ynb.d])
                K.tt(DVE, ynb.t[:, 1024:2048], ynb.t[:, 1024:2048], RN.t[:, :], ALU.mult, [ynb.d, RN.d], [ynb.d])
                K.tt(POOL, ynb.t[:, 1024:2048], ynb.t[:, 1024:2048], sg_c.t[:, :], ALU.mult, [ynb.d, sg_c.d], [ynb.d])
                for b4 in range(4):
                    pst = pbank()
                    for kk in range(4):
                        k = b4 * 4 + kk
                        K.tr(pst.t[:, kk * 128:(kk + 1) * 128], ynb.t[:, k * 128:(k + 1) * 128], cst(CI_IDENT), [ynb.d, CS.d], [pst.d])
                    ys = ysb[b4 % 2]
                    K.act(ys.t[:, :, :], pst.t[:, :].rearrange("p (k t) -> p k t", t=128), AF.Copy, [pst.d], [ys.d])
                    K.st(YS[b4 * 512:(b4 + 1) * 512, t0:t0 + 128].rearrange("(k p) t -> p k t", p=128), ys.t[:, :, :], [ys.d])

            loads(0)
            if fin:
                loads_fin(0)
            partA(0)
            for ci in range(NO):
                if ci + 1 < NO:
                    partA(ci + 1)
                partB(ci)

    EPSB = K.gsb("epsb", [128, 1], F32)
    K.memset(DVE, EPSB.t[:, :], EPS, [EPSB.d])

    K.prefetch_w13 = True
    l0_phaseA()
    l0_phaseB()
    K.c2_args = (K.blocks, X2, 0, nlayers == 1)
    phaseC1(0, K.blocks, AO, 8, awout, xin, 0)
    if nlayers == 1:
        pass
    else:
        import os
        stop = int(os.environ.get("KSTOP", "99"))
        lat = [b for b in K.blocks if b[2] == 0]
        def l1_c():
            K.c2_args = (lat, outD, CTX, True)
            phaseC1(1, lat, YS, 16, swout, X2, 0)
        steps = [l1_phaseA, l1_phaseV, lambda: l1_scan(0), lambda: l1_scan(1), l1_c]
        for si, fn in enumerate(steps):
            if si < stop:
                fn()
    em.finish()
    K.stats = em.stats()
    return K


def prep_inputs(inp, L):
    f = lambda a: np.ascontiguousarray(np.asarray(a, dtype=np.float32))
    rope, sq = host_consts(L)
    col = lambda v, n: f(v).reshape(n, 128).T
    shared = {
        "mod_w": f(inp["mod_w"]),
        "mod_b": f(inp["mod_b"]),
        "modb_col": np.ascontiguousarray(np.stack([col(inp["mod_b"][i], 48) for i in range(2)], axis=1)),
        "ncol": np.ascontiguousarray(np.stack([np.stack([col(inp["norm1_w"][i], 8), col(inp["norm2_w"][i], 8)], axis=1) for i in range(2)], axis=1)),
        "rope": rope, "sqc": sq,
        "ffn_w13": f(inp["ffn_w13"]), "ffn_w2": f(inp["ffn_w2"]),
        "attn_w_in": f(inp["attn_w_in"][0]), "mla_wq_b": f(inp["mla_wq_b"][0]), "mla_wkv_b": f(inp["mla_wkv_b"][0]),
        "attn_w_out": f(inp["attn_w_out"][0]),
    }
    if "ssm_w_in" in inp:
        shared["ssm_w_in"] = f(inp["ssm_w_in"][0])
        shared["ssm_w_out"] = f(inp["ssm_w_out"][0])
        cw = f(inp["ssd_conv_w"][0])
        shared["convw_col"] = np.ascontiguousarray(cw.reshape(5, 12, 128).transpose(2, 0, 1).reshape(128, 60))
        shared["convb_col"] = col(inp["ssd_conv_b"][0], 12)
        rows = np.zeros((NR1,), np.float32)
        rows[R_CONVB:R_CONVB + 1536] = f(inp["ssd_conv_b"][0])
        rows[R_DTB:R_DTB + 32] = f(inp["ssd_dt_bias"][0]).reshape(-1)
        rows[R_ALOG:R_ALOG + 32] = f(inp["ssd_a_log"][0]).reshape(-1)
        rows[R_RLOG:R_RLOG + 16] = f(inp["ret_decay_logit"][0]).reshape(-1)
        rows[R_DSK:R_DSK + 1024] = np.repeat(f(inp["ssd_d"][0]), 64)
        rows[R_SSDN:R_SSDN + 1024] = f(inp["ssd_norm"][0])
        rows[R_RETN:R_RETN + 1024] = f(inp["ret_norm"][0])
        shared["rows1"] = rows
        jj, ii = np.meshgrid(np.arange(128, dtype=np.float32), np.arange(128, dtype=np.float32), indexing="ij")
        c2 = np.zeros((128, 260), np.float32)
        c2[:, 0:128] = np.maximum(ii - jj, 0)
        c2[:, 128:256] = np.maximum(jj - ii, 0)
        j1 = np.arange(128, dtype=np.float32)
        c2[:, 256], c2[:, 257], c2[:, 258], c2[:, 259] = j1 + 1, 128 - j1, 127 - j1, j1
        shared["sqc2"] = c2
        sel = np.zeros((16, 16, 128), np.float32)
        for h in range(16):
            sel[h, h, :] = 1.0
        shared["selc"] = np.ascontiguousarray(sel.reshape(16, 2048))
    vec = np.zeros((128, NV), np.float32)
    vec[:, V_GQ] = np.tile(f(inp["gqa_qn"][0]), 2)
    vec[:, V_GK] = np.tile(f(inp["gqa_kn"][0]), 2)
    vec[:, V_QA:V_QA + 3] = col(inp["mla_qa_norm"][0], 3)
    vec[:, V_KVA:V_KVA + 2] = col(inp["mla_kva_norm"][0], 2)
    vec[:96, V_MQ] = f(inp["mla_qn"][0])
    vec[:96, V_MK] = f(inp["mla_kn"][0])
    shared["vecs"] = vec
    maps = []
    x, c, ctx, c_ctx = f(inp["x"]), f(inp["c"]), f(inp["ctx"]), f(inp["c_ctx"])
    for b in range(x.shape[0]):
        m = dict(shared)
        m["xin"] = np.ascontiguousarray(np.concatenate([ctx[b], x[b]], axis=0))
        cc = np.stack([col(c[b], 8), col(c_ctx, 8)], axis=2)
        m["cc"] = np.ascontiguousarray(cc)
        maps.append(m)
    return maps


_CACHE = {}


def kernel(**inputs):
    L = int(np.asarray(inputs["x"]).shape[1])
    B = int(np.asarray(inputs["x"]).shape[0])
    if L not in _CACHE:
        _CACHE[L] = build(L, 2)
    K = _CACHE[L]
    maps = prep_inputs(inputs, L)
    res = run_bass_kernel_spmd(K.nc, maps, core_ids=list(range(B)))
    return np.stack([np.asarray(res.results[b]["out"]) for b in range(B)], axis=0).astype(np.float32)
```

```python
import math
from contextlib import ExitStack, contextmanager
import numpy as np
import concourse.bass as bass
import concourse.mybir as mybir
from concourse.bass_utils import run_bass_kernel_spmd

F32 = mybir.dt.float32
BF16 = mybir.dt.bfloat16
AF = mybir.ActivationFunctionType
ALU = mybir.AluOpType
AX = mybir.AxisListType

EPOCH = 30000
EPS = 1e-6
D = 1024
CTX = 256
FFH = 2816
GRID_W = 64
THETA = 10000.0


class Dep:
    __slots__ = ("w", "r")

    def __init__(self):
        self.w = None
        self.r = {}


class Eng:
    def __init__(self, em, name, h, self_sync=True):
        self.em, self.name, self.h, self.self_sync = em, name, h, self_sync
        self.sem = None
        self.cnt = 0
        self.waited = {}
        self.own = set()
        self.ninst = 0
        self.nwait = 0
        self.last = None

    def next_event(self):
        if self.sem is None or self.cnt >= EPOCH:
            self.sem = self.em.new_sem(self.name)
            self.own.add(self.sem.num)
            self.cnt = 0
        self.cnt += 1
        self.last = (self.sem.num, self.cnt)
        return self.last


class Emitter:
    def __init__(self, nc, es, n_dma_sp=24, n_dma_pool=16):
        self.nc, self.es = nc, es
        self.sems = {}
        self.nsem = 0
        self.pe = Eng(self, "pe", nc.tensor, self_sync=False)
        self.act = Eng(self, "act", nc.scalar)
        self.dve = Eng(self, "dve", nc.vector)
        self.pool = Eng(self, "pool", nc.gpsimd)
        self.sp = Eng(self, "sp", nc.sync)
        self.engines = [self.pe, self.act, self.dve, self.pool, self.sp]
        self.dma_pools = {}
        for e, n in ((self.sp, n_dma_sp), (self.pool, n_dma_pool)):
            self.dma_pools[e.name] = [[self.new_sem("d" + e.name), 0] for _ in range(n)]
        self.dma_rr = {k: 0 for k in self.dma_pools}
        self.out_events = []

    def new_sem(self, tag):
        s = self.es.enter_context(self.nc.semaphore("%s_%d" % (tag, self.nsem)))
        self.nsem += 1
        self.sems[s.num] = s
        return s

    def _wait(self, eng, evs):
        best = {}
        for (s, v) in evs:
            if v > best.get(s, 0):
                best[s] = v
        for s, v in best.items():
            if (not eng.self_sync) and s in eng.own:
                continue
            if eng.waited.get(s, 0) >= v:
                continue
            eng.h.wait_ge(self.sems[s], v)
            eng.waited[s] = v
            eng.nwait += 1

    @staticmethod
    def _collect(reads, writes):
        evs = []
        for d in reads:
            if d.w is not None:
                evs.append(d.w)
        for d in writes:
            if d.w is not None:
                evs.append(d.w)
            evs.extend(d.r.items())
        return evs

    @staticmethod
    def _commit(ev, reads, writes):
        for d in reads:
            if ev[1] > d.r.get(ev[0], 0):
                d.r[ev[0]] = ev[1]
        for d in writes:
            d.w = ev
            d.r = {}

    def op(self, eng, fn, reads=(), writes=()):
        self._wait(eng, self._collect(reads, writes))
        inst = fn()
        ev = eng.next_event()
        inst.then_inc(self.sems[ev[0]], 1)
        eng.ninst += 1
        self._commit(ev, reads, writes)
        return ev

    def dma(self, eng, out, in_, reads=(), writes=(), is_output=False, **kw):
        pool = self.dma_pools[eng.name]
        i = self.dma_rr[eng.name]
        self.dma_rr[eng.name] = (i + 1) % len(pool)
        slot = pool[i]
        evs = self._collect(reads, writes)
        if slot[1] > 0:
            evs.append((slot[0].num, slot[1]))
        self._wait(eng, evs)
        if slot[1] + 16 > EPOCH:
            slot[0] = self.new_sem("d" + eng.name)
            slot[1] = 0
        slot[1] += 16
        eng.h.dma_start(out=out, in_=in_, **kw).then_inc(slot[0], 16)
        ev = (slot[0].num, slot[1])
        eng.ninst += 1
        self._commit(ev, reads, writes)
        if is_output:
            self.out_events.append(ev)
        return ev

    def all_events(self):
        evs = []
        for e in self.engines:
            if e.last is not None:
                evs.append(e.last)
        for pool in self.dma_pools.values():
            for s, v in pool:
                if v > 0:
                    evs.append((s.num, v))
        return evs

    def barrier(self):
        evs = self.all_events()
        for e in self.engines:
            self._wait(e, evs)

    def finish(self):
        self._wait(self.sp, list(self.out_events) + self.all_events())

    def stats(self):
        return {e.name: (e.ninst, e.nwait) for e in self.engines}, self.nsem


class Buf:
    __slots__ = ("t", "d")

    def __init__(self, t):
        self.t = t
        self.d = Dep()


class KB:
    def __init__(self, L):
        self.L = L
        self.NT = CTX + L
        self.NCH = self.NT // 128
        self.nc = bass.Bass("TRN2", target_bir_lowering=False)
        self.es = ExitStack()
        self.em = Emitter(self.nc, self.es)
        self.cur = self.es
        self.uid = 0
        self.blocks = [(0, CTX, 1)] + [(CTX + i * 512, 512, 0) for i in range(L // 512)]
        self.PS = [Buf(self.es.enter_context(self.nc.psum_tensor("psb%d" % i, [128, 512], F32))) for i in range(8)]

    def sb(self, name, shape, dtype):
        self.uid += 1
        return Buf(self.cur.enter_context(self.nc.sbuf_tensor("s_%s_%d" % (name, self.uid), shape, dtype)))

    def gsb(self, name, shape, dtype):
        return Buf(self.es.enter_context(self.nc.sbuf_tensor("g_" + name, shape, dtype)))

    def dram(self, name, shape, dtype, kind="Internal"):
        return self.nc.dram_tensor(name, shape, dtype, kind=kind).ap()

    @contextmanager
    def phase(self):
        st = ExitStack()
        prev = self.cur
        self.cur = st
        try:
            yield
        finally:
            self.em.barrier()
            st.close()
            self.cur = prev

    def mm(self, out, lhsT, rhs, start, stop, R, W):
        nc = self.nc
        return self.em.op(self.em.pe, lambda: nc.tensor.matmul(out, lhsT=lhsT, rhs=rhs, start=start, stop=stop), R, W)

    def tr(self, out, in_, ident, R, W):
        nc = self.nc
        return self.em.op(self.em.pe, lambda: nc.tensor.transpose(out=out, in_=in_, identity=ident), R, W)

    def act(self, out, in_, func, R, W, scale=None, bias=None, accum_out=None):
        nc = self.nc
        kw = {}
        if scale is not None:
            kw["scale"] = scale
        if bias is not None:
            kw["bias"] = bias
        if accum_out is not None:
            kw["accum_out"] = accum_out
        return self.em.op(self.em.act, lambda: nc.scalar.activation(out=out, in_=in_, func=func, **kw), R, W)

    def _ve(self, eng):
        return self.nc.vector if eng is self.em.dve else self.nc.gpsimd

    def tt(self, eng, out, in0, in1, op, R, W):
        h = self._ve(eng)
        return self.em.op(eng, lambda: h.tensor_tensor(out=out, in0=in0, in1=in1, op=op), R, W)

    def ts(self, eng, out, in0, s1, op0, R, W, s2=None, op1=None):
        h = self._ve(eng)
        if op1 is None:
            return self.em.op(eng, lambda: h.tensor_scalar(out=out, in0=in0, scalar1=s1, scalar2=None, op0=op0), R, W)
        return self.em.op(eng, lambda: h.tensor_scalar(out=out, in0=in0, scalar1=s1, scalar2=s2, op0=op0, op1=op1), R, W)

    def stt(self, out, in0, scalar, in1, op0, op1, R, W):
        nc = self.nc
        return self.em.op(self.em.dve, lambda: nc.vector.scalar_tensor_tensor(out=out, in0=in0, scalar=scalar, in1=in1, op0=op0, op1=op1), R, W)

    def recip(self, out, in_, R, W):
        nc = self.nc
        return self.em.op(self.em.dve, lambda: nc.vector.reciprocal(out=out, in_=in_), R, W)

    def memset(self, eng, ap, val, W):
        h = self._ve(eng)
        return self.em.op(eng, lambda: h.memset(ap, val), (), W)

    def copy(self, eng, out, in_, R, W):
        h = self._ve(eng)
        return self.em.op(eng, lambda: h.tensor_copy(out=out, in_=in_), R, W)

    def ld(self, out, in_, W, R=(), **kw):
        return self.em.dma(self.em.sp, out, in_, reads=R, writes=W, **kw)

    def st(self, out, in_, R, W=(), **kw):
        return self.em.dma(self.em.pool, out, in_, reads=R, writes=W, **kw)

    def ldcast(self, out, in_, W, R=()):
        return self.em.dma(self.em.pool, out, in_, reads=R, writes=W, max_dma_last_dim=8192)


def rope_tables(L, dim):
    rows = L // GRID_W
    rr, cc = np.meshgrid(np.arange(rows, dtype=np.float32), np.arange(GRID_W, dtype=np.float32), indexing="ij")
    quarter = dim // 4
    inv = (np.float32(THETA) ** (-np.arange(quarter, dtype=np.float32) / np.float32(quarter))).astype(np.float32)
    ang = np.concatenate([rr.reshape(-1)[:, None] * inv, cc.reshape(-1)[:, None] * inv], axis=-1).astype(np.float32)
    cos = np.concatenate([np.ones((CTX, dim // 2), np.float32), np.cos(ang)], axis=0)
    sin = np.concatenate([np.zeros((CTX, dim // 2), np.float32), np.sin(ang)], axis=0)
    return cos.astype(np.float32), sin.astype(np.float32)


def host_consts(L):
    NT = CTX + L
    c64, s64 = rope_tables(L, 64)
    c32, s32 = rope_tables(L, 32)
    cos64 = np.repeat(c64.T, 2, axis=0)
    sin64 = np.repeat(s64.T, 2, axis=0)
    cos128 = np.concatenate([cos64, cos64], 0)
    sin128 = np.concatenate([sin64, sin64], 0)
    cosm = np.concatenate([np.ones((64, NT), np.float32), np.repeat(c32.T, 2, axis=0)], 0)
    sinm = np.concatenate([np.zeros((64, NT), np.float32), np.repeat(s32.T, 2, axis=0)], 0)
    rope = np.zeros((4, 128, NT), np.float32)
    rope[0], rope[1] = cos128, sin128
    rope[2, :96], rope[3, :96] = cosm, sinm
    ident = np.eye(128, dtype=np.float32)
    rot = np.zeros((128, 128), np.float32)
    for i in range(64):
        rot[2 * i + 1, 2 * i] = -1.0
        rot[2 * i, 2 * i + 1] = 1.0
    rotm = np.zeros((128, 128), np.float32)
    rotm[64:96, 64:96] = rot[64:96, 64:96]
    shift = np.zeros((128, 128), np.float32)
    for k in range(32):
        shift[k, 64 + k] = 1.0
    bd64 = np.zeros((128, 128), np.float32)
    bd64[:64, :64] = 1.0
    bd64[64:, 64:] = 1.0
    ones = np.ones((128, 128), np.float32)
    tri = np.triu(np.ones((128, 128), np.float32))
    sq = np.concatenate([ident, rot, rotm, shift, bd64, ones, tri, tri.T.copy()], axis=1)
    return rope, sq


CI_IDENT, CI_ROT, CI_ROTM, CI_SHIFT, CI_BD64, CI_ONES, CI_TRI, CI_TRIT = range(8)
NCONST = 8

V_GQ, V_GK, V_QA, V_KVA, V_MQ, V_MK = 0, 1, 2, 5, 7, 8
NV = 16
R_CONVB, R_DTB, R_ALOG, R_RLOG, R_DSK, R_SSDN, R_RETN = 0, 1536, 1568, 1600, 1616, 2640, 3664
NR1 = 4688


def build(L=4096, nlayers=2, debug=False):
    K = KB(L)
    nc, em = K.nc, K.em
    NT, NCH = K.NT, K.NCH
    PS = K.PS
    DVE, POOL = em.dve, em.pool

    xin = K.dram("xin", [NT, D], F32, "ExternalInput")
    ccd = K.dram("cc", [128, 8, 2], F32, "ExternalInput")
    mod_w = K.dram("mod_w", [2, D, 6 * D], F32, "ExternalInput")
    modb_col = K.dram("modb_col", [128, 2, 48], F32, "ExternalInput")
    mod_b = K.dram("mod_b", [2, 6 * D], F32, "ExternalInput")
    ncol = K.dram("ncol", [128, 2, 2, 8], F32, "ExternalInput")
    vecs = K.dram("vecs", [128, NV], F32, "ExternalInput")
    ropeD = K.dram("rope", [4, 128, NT], F32, "ExternalInput")
    sqc = K.dram("sqc", [128, NCONST * 128], F32, "ExternalInput")
    w13D = K.dram("ffn_w13", [2, D, 2 * FFH], F32, "ExternalInput")
    w2D = K.dram("ffn_w2", [2, FFH, D], F32, "ExternalInput")
    awin = K.dram("attn_w_in", [D, 1440], F32, "ExternalInput")
    wqb = K.dram("mla_wq_b", [384, 768], F32, "ExternalInput")
    wkvb = K.dram("mla_wkv_b", [256, 1024], F32, "ExternalInput")
    awout = K.dram("attn_w_out", [D, D], F32, "ExternalInput")
    outD = K.dram("out", [L, D], F32, "ExternalOutput")

    QA = K.dram("QA", [4, 128, NT], BF16)
    KA = K.dram("KA", [2, 128, NT], BF16)
    VA = K.dram("VA", [128, NCH, 130], BF16)
    QM = K.dram("QM", [8, 96, NT], BF16)
    KM = K.dram("KM", [8, 96, NT], BF16)
    VM = K.dram("VM", [128, NCH, 520], BF16)
    AO = K.dram("AO", [D, NT], BF16)
    X1 = K.dram("X1", [NT, D], F32)
    U2 = K.dram("U2", [D, NT], BF16)
    X2 = K.dram("X2", [NT, D], F32, "ExternalOutput" if debug else "Internal")
    MODROW = K.dram("MODROW", [2, 2, 2, D], F32)

    CS = K.gsb("consts", [128, NCONST * 128], F32)
    K.ld(CS.t[:, :], sqc[:, :], [CS.d])

    def cst(i, r=128, c=128):
        return CS.t[0:r, i * 128:i * 128 + c]

    CB = K.gsb("constsb", [128, 5 * 128], BF16)
    K.copy(DVE, CB.t[:, 0:128], cst(CI_BD64), [CS.d], [CB.d])
    K.copy(DVE, CB.t[:, 128:256], cst(CI_ONES), [CS.d], [CB.d])
    K.copy(DVE, CB.t[:, 256:384], cst(CI_ROT), [CS.d], [CB.d])
    K.copy(DVE, CB.t[:, 384:512], cst(CI_ROTM), [CS.d], [CB.d])
    K.copy(DVE, CB.t[:, 512:640], cst(CI_SHIFT), [CS.d], [CB.d])
    ROTb = CB.t[:, 256:384]
    ROTMb = CB.t[0:96, 384:480]
    SHIFTb = CB.t[0:32, 512:608]
    BD64b = CB.t[:, 0:128]
    ONESb = CB.t[:, 128:256]
    VEC = K.gsb("vecs", [128, NV], F32)
    K.ld(VEC.t[:, :], vecs[:, :], [VEC.d])
    NCOL = K.gsb("ncol", [128, 2, 2, 8], F32)
    K.ld(NCOL.t[:, :, :, :], ncol[:, :, :, :], [NCOL.d])
    MODT = K.gsb("modT", [128, 2, 48, 2], F32)
    MUL = K.gsb("mulT", [128, 2, 2, 8, 2], F32)

    with K.phase():
        cc = K.sb("cc", [128, 8, 2], F32)
        K.ld(cc.t[:, :, :], ccd[:, :, :], [cc.d])
        K.act(cc.t[:, :, :], cc.t[:, :, :], AF.Silu, [cc.d], [cc.d])
        modb = K.sb("modb", [128, 2, 48], F32)
        K.ld(modb.t[:, :, :], modb_col[:, :, :], [modb.d])
        mbrow = K.sb("mbrow", [2, 2, 6 * D], F32)
        K.ld(mbrow.t[:, :, :], mod_b.partition_broadcast(2), [mbrow.d])
        wbuf = [K.sb("mw", [128, 8, 1024], F32) for _ in range(2)]
        rowb = [K.sb("rowb", [2, 1024], F32) for _ in range(2)]
        it = 0
        for i in range(nlayers):
            for v in range(6):
                wb = wbuf[it % 2]
                K.ld(wb.t[:, :, :], mod_w[i, :, v * 1024:(v + 1) * 1024].rearrange("(k p) n -> p k n", p=128), [wb.d])
                ps = PS[it % 2]
                for j in range(8):
                    for k in range(8):
                        K.mm(ps.t[:, j * 2:(j + 1) * 2], wb.t[:, k, j * 128:(j + 1) * 128], cc.t[:, k, :], k == 0, k == 7,
                             [wb.d, cc.d], [ps.d])
                K.tt(DVE, MODT.t[:, i, v * 8:(v + 1) * 8, :], ps.t[:, 0:16].rearrange("p (j s) -> p j s", s=2),
                     modb.t[:, i, v * 8:(v + 1) * 8].unsqueeze(2).broadcast_to([128, 8, 2]), ALU.add,
                     [ps.d, modb.d], [MODT.d])
                if v in (2, 5):
                    gi = 0 if v == 2 else 1
                    rb = rowb[gi]
                    for half in range(2):
                        ps2 = PS[2 + half]
                        for k in range(8):
                            K.mm(ps2.t[0:2, :], cc.t[:, k, :], wb.t[:, k, half * 512:(half + 1) * 512], k == 0, k == 7,
                                 [wb.d, cc.d], [ps2.d])
                        K.tt(DVE, rb.t[:, half * 512:(half + 1) * 512], ps2.t[0:2, :],
                             mbrow.t[:, i, v * 1024 + half * 512: v * 1024 + (half + 1) * 512], ALU.add,
                             [ps2.d, mbrow.d], [rb.d])
                    K.st(MODROW[i, gi, :, :], rb.t[:, :], [rb.d])
                it += 1
            for nrm in range(2):
                K.stt(MUL.t[:, i, nrm, :, :], MODT.t[:, i, (3 * nrm + 1) * 8:(3 * nrm + 2) * 8, :], 1.0,
                      NCOL.t[:, i, nrm, :].unsqueeze(2).broadcast_to([128, 8, 2]), ALU.add, ALU.mult,
                      [MODT.d, NCOL.d], [MUL.d])

    def mul_ap(layer, nrm, k, s):
        return MUL.t[:, layer, nrm, k, s:s + 1]

    def add_ap(layer, nrm, k, s):
        return MODT.t[:, layer, 3 * nrm * 8 + k, s:s + 1]

    def norm_to_uT(xt, nch, T, layer, nrm, s, uT, scr, psbanks, use_pool=True):
        junk, ss, xs = scr
        for c in range(nch):
            K.act(junk.t[:, :], xt.t[:, c, :], AF.Square, [xt.d], [junk.d, ss.d], accum_out=ss.t[:, c:c + 1])
        K.ts(DVE, ss.t[:, 0:nch], ss.t[:, 0:nch], 1.0 / D, ALU.mult, [ss.d], [ss.d], s2=EPS, op1=ALU.add)
        K.act(ss.t[:, 0:nch], ss.t[:, 0:nch], AF.Sqrt, [ss.d], [ss.d])
        K.recip(ss.t[:, 0:nch], ss.t[:, 0:nch], [ss.d], [ss.d])
        for c in range(nch):
            K.ts(DVE if (c % 2 == 0 or not use_pool) else POOL, xs.t[:, c, :], xt.t[:, c, :], ss.t[:, c:c + 1], ALU.mult, [xt.d, ss.d], [xs.d])
        for k in range(8):
            ps = psbanks[k % len(psbanks)]
            for c in range(nch):
                K.tr(ps.t[:, c * 128:(c + 1) * 128], xs.t[:, c, k * 128:(k + 1) * 128], cst(CI_IDENT), [xs.d, CS.d], [ps.d])
            K.act(uT.t[:, k, 0:T], ps.t[:, 0:T], AF.Identity, [ps.d, MUL.d, MODT.d], [uT.d],
                  scale=mul_ap(layer, nrm, k, s), bias=add_ap(layer, nrm, k, s))

    def l0_phaseA():
        with K.phase():
            W = K.sb("win", [128, 8, 1440], BF16)
            for k in range(8):
                K.ldcast(W.t[:, k, :], awin[k * 128:(k + 1) * 128, :], [W.d])
            Wkd = K.sb("wkd", [128, 8, 2, 128], BF16)
            for g in range(2):
                for dup in range(2):
                    K.ldcast(Wkd.t[:, :, g, dup * 64:(dup + 1) * 64],
                             awin[:, 512 + g * 64:512 + (g + 1) * 64].rearrange("(k p) n -> p k n", p=128), [Wkd.d])
            Wq = K.sb("wqb", [128, 3, 768], BF16)
            K.ldcast(Wq.t[:, :, :], wqb.rearrange("(j p) n -> p j n", p=128), [Wq.d])
            Wkp = K.sb("wkvp", [128, 2, 8, 96], BF16)
            K.memset(DVE, Wkp.t[:, :, :, :], 0.0, [Wkp.d])
            wkv4 = wkvb.rearrange("(j p) (h t e) -> p j h t e", p=128, t=2, e=64)
            for j in range(2):
                K.ldcast(Wkp.t[:, j, :, 0:64], wkv4[:, j, :, 0, :], [Wkp.d])
            Wv = K.sb("wkvv", [128, 2, 8, 64], BF16)
            for j in range(2):
                K.ldcast(Wv.t[:, j, :, :], wkv4[:, j, :, 1, :], [Wv.d])

            xb = [K.sb("xa", [128, 4, D], F32) for _ in range(2)]
            scr = (K.sb("junk", [128, D], BF16), K.sb("ss", [128, 4], F32), K.sb("xs", [128, 4, D], F32))
            uTb = [K.sb("uT", [128, 8, 512], BF16) for _ in range(2)]
            ropeb = [K.sb("rope", [128, 4, 512], F32) for _ in range(2)]
            sqb = [K.sb("sq", [128, 512], BF16) for _ in range(2)]
            r1b = [K.sb("r1", [128, 512], F32) for _ in range(2)]
            qnb = [K.sb("qn", [128, 512], BF16) for _ in range(2)]
            t1b = [K.sb("t1", [128, 512], F32) for _ in range(2)]
            t2b = [K.sb("t2", [128, 512], F32) for _ in range(2)]
            outb = [K.sb("ob", [128, 512], BF16) for _ in range(3)]
            qln = K.sb("qln", [128, 3, 512], BF16)
            kvn = K.sb("kvn", [128, 2, 512], BF16)
            krp = K.sb("krp", [32, 512], BF16)
            vat = [K.sb("vat", [128, 4, 130], BF16) for _ in range(2)]
            vmt = [K.sb("vmt", [128, 4, 520], BF16) for _ in range(2)]
            for b in vat + vmt:
                K.memset(DVE, b.t[:, :, :], 1.0, [b.d])
            cnt = {"n": 0, "o": 0, "p": 0}
            PROJ = [PS[2], PS[3], PS[4]]

            def proj_bank():
                cnt["p"] += 1
                return PROJ[cnt["p"] % 3]

            pq = {"a": None, "b": None}
            SSB = [PS[5], PS[6]]
            ROTB = [PS[7], PS[0]]

            def flush():
                a, b = pq["a"], pq["b"]
                if a is not None:
                    a[0]()
                if b is not None:
                    b[1]()
                if a is not None:
                    a[1]()
                pq["a"] = pq["b"] = None

            def post(ps_src, R_, T, gain, ss_lhsT, inv_n, rot_lhsT, ci, rp, dst):
                i = cnt["n"] % 2
                cnt["n"] += 1
                sq, r1, qn, t1, t2 = sqb[i], r1b[i], qnb[i], t1b[i], t2b[i]
                pss, psr = SSB[i], ROTB[i]
                K.act(sq.t[0:R_, 0:T], ps_src.t[0:R_, 0:T], AF.Square, [ps_src.d], [sq.d])
                K.mm(pss.t[0:R_, 0:T], ss_lhsT, sq.t[0:R_, 0:T], True, True, [sq.d, CB.d], [pss.d])

                def s2a():
                    K.act(r1.t[0:R_, 0:T], pss.t[0:R_, 0:T], AF.Ln, [pss.d], [r1.d], scale=inv_n, bias=EPSB.t[0:R_, 0:1])
                    K.act(r1.t[0:R_, 0:T], r1.t[0:R_, 0:T], AF.Exp, [r1.d], [r1.d], scale=-0.5)
                    K.stt(qn.t[0:R_, 0:T], ps_src.t[0:R_, 0:T], gain, r1.t[0:R_, 0:T], ALU.mult, ALU.mult,
                          [ps_src.d, r1.d, VEC.d], [qn.d])
                    K.mm(psr.t[0:R_, 0:T], rot_lhsT, qn.t[0:R_, 0:T], True, True, [qn.d, CB.d], [psr.d])

                def s2b():
                    K.tt(DVE, t1.t[0:R_, 0:T], qn.t[0:R_, 0:T], rp.t[0:R_, ci, 0:T], ALU.mult, [qn.d, rp.d], [t1.d])
                    K.tt(DVE, t2.t[0:R_, 0:T], psr.t[0:R_, 0:T], rp.t[0:R_, ci + 1, 0:T], ALU.mult, [psr.d, rp.d], [t2.d])
                    ob = outb[cnt["o"] % 3]
                    cnt["o"] += 1
                    K.tt(DVE, ob.t[0:R_, 0:T], t1.t[0:R_, 0:T], t2.t[0:R_, 0:T], ALU.add, [t1.d, t2.d], [ob.d])
                    K.ld(dst, ob.t[0:R_, 0:T], (), R=[ob.d])

                a, b = pq["a"], pq["b"]
                if a is not None:
                    a[0]()
                if b is not None:
                    b[1]()
                pq["b"] = a
                pq["a"] = (s2a, s2b)

            def lora_norm(ps_list, nj, gcol0, inv_n, dstb, T):
                i = cnt["n"] % 2
                cnt["n"] += 1
                r1 = r1b[i]
                pss = PS[5]
                for j in range(nj):
                    sq = sqb[(cnt["n"] + j) % 2]
                    K.act(sq.t[:, 0:T], ps_list[j].t[:, 0:T], AF.Square, [ps_list[j].d], [sq.d])
                    K.mm(pss.t[:, 0:T], ONESb, sq.t[:, 0:T], j == 0, j == nj - 1, [sq.d, CB.d], [pss.d])
                K.act(r1.t[:, 0:T], pss.t[:, 0:T], AF.Ln, [pss.d], [r1.d], scale=inv_n, bias=EPSB.t[:, 0:1])
                K.act(r1.t[:, 0:T], r1.t[:, 0:T], AF.Exp, [r1.d], [r1.d], scale=-0.5)
                for j in range(nj):
                    K.stt(dstb.t[:, j, 0:T], ps_list[j].t[:, 0:T], VEC.t[:, gcol0 + j:gcol0 + j + 1], r1.t[:, 0:T],
                          ALU.mult, ALU.mult, [ps_list[j].d, r1.d, VEC.d], [dstb.d])

            def ldblk(bi):
                t0, T, s = K.blocks[bi]
                K.ld(xb[bi % 2].t[:, 0:T // 128, :], xin[t0:t0 + T, :].rearrange("(c p) d -> p c d", p=128), [xb[bi % 2].d])
                K.ld(ropeb[bi % 2].t[:, :, 0:T], ropeD[:, :, t0:t0 + T].rearrange("f p t -> p f t"), [ropeb[bi % 2].d])

            ldblk(0)
            for bi, (t0, T, s) in enumerate(K.blocks):
                nch = T // 128
                c0 = t0 // 128
                if bi + 1 < len(K.blocks):
                    ldblk(bi + 1)
                xt = xb[bi % 2]
                rp = ropeb[bi % 2]
                uT = uTb[bi % 2]
                norm_to_uT(xt, nch, T, 0, 0, s, uT, scr, [PS[0], PS[1]])
                for ch in range(4):
                    ps = proj_bank()
                    for k in range(8):
                        K.mm(ps.t[:, 0:T], W.t[:, k, ch * 128:(ch + 1) * 128], uT.t[:, k, 0:T], k == 0, k == 7, [W.d, uT.d], [ps.d])
                    post(ps, 128, T, VEC.t[:, V_GQ:V_GQ + 1], BD64b, 1.0 / 64, ROTb, 0, rp, QA[ch, :, t0:t0 + T])
                for g in range(2):
                    ps = proj_bank()
                    for k in range(8):
                        K.mm(ps.t[:, 0:T], Wkd.t[:, k, g, :], uT.t[:, k, 0:T], k == 0, k == 7, [Wkd.d, uT.d], [ps.d])
                    post(ps, 128, T, VEC.t[:, V_GK:V_GK + 1], BD64b, 1.0 / 64, ROTb, 0, rp, KA[g, :, t0:t0 + T])
                flush()
                va_t = vat[bi % 2]
                for c in range(nch):
                    ps = proj_bank()
                    for k in range(8):
                        K.mm(ps.t[:, 0:128], uT.t[:, k, c * 128:(c + 1) * 128], W.t[:, k, 640:768], k == 0, k == 7, [W.d, uT.d], [ps.d])
                    K.act(va_t.t[:, c, :].rearrange("p (g e) -> p g e", e=65)[:, :, 0:64],
                          ps.t[:, 0:128].rearrange("p (g e) -> p g e", e=64), AF.Copy, [ps.d], [va_t.d])
                K.st(VA[:, c0:c0 + nch, :], va_t.t[:, 0:nch, :], [va_t.d])
                pl = []
                for j in range(3):
                    ps = proj_bank()
                    for k in range(8):
                        K.mm(ps.t[:, 0:T], W.t[:, k, 768 + j * 128:768 + (j + 1) * 128], uT.t[:, k, 0:T], k == 0, k == 7, [W.d, uT.d], [ps.d])
                    pl.append(ps)
                lora_norm(pl, 3, V_QA, 1.0 / 384, qln, T)
                for h in range(8):
                    ps = proj_bank()
                    for j in range(3):
                        K.mm(ps.t[0:96, 0:T], Wq.t[:, j, h * 96:(h + 1) * 96], qln.t[:, j, 0:T], j == 0, j == 2, [Wq.d, qln.d], [ps.d])
                    post(ps, 96, T, VEC.t[0:96, V_MQ:V_MQ + 1], ONESb[0:96, 0:96], 1.0 / 96, ROTMb, 2, rp, QM[h, :, t0:t0 + T])
                flush()
                pl = []
                for j in range(2):
                    ps = proj_bank()
                    for k in range(8):
                        K.mm(ps.t[:, 0:T], W.t[:, k, 1152 + j * 128:1152 + (j + 1) * 128], uT.t[:, k, 0:T], k == 0, k == 7, [W.d, uT.d], [ps.d])
                    pl.append(ps)
                lora_norm(pl, 2, V_KVA, 1.0 / 256, kvn, T)
                ps = proj_bank()
                for k in range(8):
                    K.mm(ps.t[0:32, 0:T], W.t[:, k, 1408:1440], uT.t[:, k, 0:T], k == 0, k == 7, [W.d, uT.d], [ps.d])
                K.act(krp.t[0:32, 0:T], ps.t[0:32, 0:T], AF.Copy, [ps.d], [krp.d])
                for h in range(8):
                    ps = proj_bank()
                    for j in range(2):
                        K.mm(ps.t[0:96, 0:T], Wkp.t[:, j, h, :], kvn.t[:, j, 0:T], j == 0, False, [Wkp.d, kvn.d], [ps.d])
                    K.mm(ps.t[0:96, 0:T], SHIFTb, krp.t[0:32, 0:T], False, True, [krp.d, CB.d], [ps.d])
                    post(ps, 96, T, VEC.t[0:96, V_MK:V_MK + 1], ONESb[0:96, 0:96], 1.0 / 96, ROTMb, 2, rp, KM[h, :, t0:t0 + T])
                flush()
                vm_t = vmt[bi % 2]
                for c in range(nch):
                    ps = proj_bank()
                    for j in range(2):
                        K.mm(ps.t[:, 0:512], kvn.t[:, j, c * 128:(c + 1) * 128], Wv.t[:, j, :, :].rearrange("p h e -> p (h e)"),
                             j == 0, j == 1, [Wv.d, kvn.d], [ps.d])
                    K.act(vm_t.t[:, c, :].rearrange("p (g e) -> p g e", e=65)[:, :, 0:64],
                          ps.t[:, 0:512].rearrange("p (g e) -> p g e", e=64), AF.Copy, [ps.d], [vm_t.d])
                K.st(VM[:, c0:c0 + nch, :], vm_t.t[:, 0:nch, :], [vm_t.d])

    def l0_phaseB():
        with K.phase():
            va = K.sb("va", [128, NCH, 130], BF16)
            vm = K.sb("vm", [128, NCH, 520], BF16)
            K.ld(va.t[:, :, :], VA[:, :, :], [va.d])
            K.ld(vm.t[:, :, :], VM[:, :, :], [vm.d])
            kb = [K.sb("kb", [128, NT], BF16) for _ in range(2)]
            kzT = [K.sb("kzT", [128, NT], BF16) for _ in range(2)]
            kzB = [K.sb("kzB", [128, NT], BF16) for _ in range(2)]
            for b in kzT:
                K.memset(DVE, b.t[64:128, :], 0.0, [b.d])
            for b in kzB:
                K.memset(DVE, b.t[0:64, :], 0.0, [b.d])
            qb = [K.sb("qb", [128, 512], BF16) for _ in range(3)]
            pb = [K.sb("pb", [128, 512], BF16) for _ in range(4)]
            osb = [K.sb("osb", [64, 512], F32) for _ in range(2)]
            rsb = [K.sb("rsb", [128, 512], F32) for _ in range(2)]
            aob = [K.sb("aob", [64, 512], BF16) for _ in range(2)]
            SB_ = [PS[0], PS[1], PS[2]]
            OB_ = [PS[3], PS[4]]
            BCB = PS[5]
            tasks = []
            groups = []
            ui = 0
            qi = 0
            units = [("a", c) for c in range(4)] + [("m", h) for h in range(8)]
            for kind, idx in units:
                if kind == "a":
                    kT, kB = kzT[ui % 2], kzB[ui % 2]
                    kloads = [(kT, KA[idx // 2, 0:64, :], slice(0, 64)), (kB, KA[idx // 2, 64:128, :], slice(64, 128))]
                else:
                    kbf = kb[ui % 2]
                    kloads = [(kbf, KM[idx, :, :], slice(0, 96))]
                ui += 1
                first_in_unit = True
                for (t0, T, s) in K.blocks:
                    qbf = qb[qi % 3]
                    qi += 1
                    if kind == "a":
                        qload = (qbf, QA[idx, :, t0:t0 + T], 128, T)
                        subs = [(kT, slice(0, 128), 2 * idx, va, idx // 2, 64 ** -0.5), (kB, slice(0, 128), 2 * idx + 1, va, idx // 2, 64 ** -0.5)]
                    else:
                        qload = (qbf, QM[idx, :, t0:t0 + T], 96, T)
                        subs = [(kbf, slice(0, 96), 8 + idx, vm, idx, 96 ** -0.5)]
                    kcs = list(range(2)) if s == 1 else list(range(NCH))
                    for si, (kbuf_, rows, hg, vbuf, vcol, sc) in enumerate(subs):
                        g = len(groups)
                        groups.append((hg, t0, T))
                        for n, kc in enumerate(kcs):
                            tasks.append(dict(kb=kbuf_, rows=rows, kc=kc, qb=qbf, T=T, vb=vbuf, vcol=vcol, first=(n == 0),
                                              last=(n == len(kcs) - 1), grp=g, sc=sc,
                                              kload=kloads if (first_in_unit and si == 0 and n == 0) else None,
                                              qload=qload if (si == 0 and n == 0) else None))
                    first_in_unit = False

            def emit_qk(i):
                t = tasks[i]
                if t["kload"] is not None:
                    for (kbf, src, rws) in t["kload"]:
                        K.ld(kbf.t[rws, :], src, [kbf.d])
                if t["qload"] is not None:
                    qbf, src, R_, T = t["qload"]
                    K.ld(qbf.t[0:R_, 0:T], src[0:R_, :], [qbf.d])
                ps = SB_[i % 3]
                kc, T = t["kc"], t["T"]
                K.mm(ps.t[:, 0:T], t["kb"].t[t["rows"], kc * 128:(kc + 1) * 128], t["qb"].t[t["rows"], 0:T], True, True,
                     [t["kb"].d, t["qb"].d], [ps.d])

            n = len(tasks)
            for i in range(min(2, n)):
                emit_qk(i)
            for i in range(n):
                t = tasks[i]
                T = t["T"]
                ps = SB_[i % 3]
                p = pb[i % 4]
                K.act(p.t[:, 0:T], ps.t[:, 0:T], AF.Exp, [ps.d], [p.d], scale=t["sc"])
                if i + 2 < n:
                    emit_qk(i + 2)
                po = OB_[t["grp"] % 2]
                vc = t["vcol"]
                K.mm(po.t[0:65, 0:T], t["vb"].t[:, t["kc"], vc * 65:(vc + 1) * 65], p.t[:, 0:T], t["first"], t["last"],
                     [t["vb"].d, p.d], [po.d])
                if t["last"]:
                    g = t["grp"]
                    hg, t0, _ = groups[g]
                    rs, osb_, ao_ = rsb[g % 2], osb[g % 2], aob[g % 2]
                    K.recip(rs.t[64:65, 0:T], po.t[64:65, 0:T], [po.d], [rs.d])
                    K.mm(BCB.t[0:64, 0:T], cst(CI_ONES)[64:65, 0:64], rs.t[64:65, 0:T], True, True, [rs.d, CS.d], [BCB.d])
                    K.act(osb_.t[0:64, 0:T], po.t[0:64, 0:T], AF.Copy, [po.d], [osb_.d])
                    K.tt(DVE, ao_.t[0:64, 0:T], osb_.t[0:64, 0:T], BCB.t[0:64, 0:T], ALU.mult, [osb_.d, BCB.d], [ao_.d])
                    K.st(AO[hg * 64:(hg + 1) * 64, t0:t0 + T], ao_.t[0:64, 0:T], [ao_.d])

    def load_w13(layer, w13=None):
        if w13 is None:
            w13 = K.sb("w13", [128, 8, 2 * FFH], BF16)
        for k in range(8):
            for hh in range(2):
                K.ldcast(w13.t[:, k, hh * FFH:(hh + 1) * FFH], w13D[layer, k * 128:(k + 1) * 128, hh * FFH:(hh + 1) * FFH], [w13.d])
        return w13

    def phaseC1(layer, blocks, AOsrc, nK, woutD, Xsrc, xoff):
        with K.phase():
            w13 = K.sb("w13", [128, 8, 2 * FFH], BF16)
            with K.phase():
                wo = K.sb("wo", [128, nK, D], BF16)
                for k in range(nK):
                    K.ldcast(wo.t[:, k, :], woutD[k * 128:(k + 1) * 128, :], [wo.d])
                load_w13(layer, w13)
                phaseC1_body(layer, blocks, AOsrc, nK, wo, Xsrc, xoff)
            phaseC2(layer, K.c2_args[0], K.c2_args[1], K.c2_args[2], K.c2_args[3], w13)

    def phaseC1_body(layer, blocks, AOsrc, nK, wo, Xsrc, xoff):
        if True:
            gts = {}
            for s in set(b[2] for b in blocks):
                gts[s] = K.sb("gt", [128, D], F32)
                K.ld(gts[s].t[:, :], MODROW[layer, 0, s, :].partition_broadcast(128), [gts[s].d])
            aob = [K.sb("ao", [128, nK, 512], BF16) for _ in range(1)]
            xb = [K.sb("xc", [128, 4, D], F32) for _ in range(1)]
            tmpb = [K.sb("tmp", [128, 512], F32) for _ in range(2)]
            scr = (K.sb("junk", [128, D], BF16), K.sb("ss", [128, 4], F32), K.sb("xs", [128, 4, D], F32))
            uTb = [K.sb("uT", [128, 8, 512], BF16) for _ in range(1)]
            it = 0
            for bi, (t0, T, s) in enumerate(blocks):
                nch = T // 128
                ao = aob[0]
                K.ld(ao.t[:, :, 0:T], AOsrc[:, t0:t0 + T].rearrange("(k p) t -> p k t", p=128), [ao.d])
                xt = xb[0]
                K.ld(xt.t[:, 0:nch, :], Xsrc[t0 - xoff:t0 - xoff + T, :].rearrange("(c p) d -> p c d", p=128), [xt.d])
                for c in range(nch):
                    for n2 in range(2):
                        ps = PS[2 + it % 4]
                        tmp = tmpb[it % 2]
                        it += 1
                        for k in range(nK):
                            K.mm(ps.t[:, :], ao.t[:, k, c * 128:(c + 1) * 128], wo.t[:, k, n2 * 512:(n2 + 1) * 512], k == 0, k == nK - 1,
                                 [ao.d, wo.d], [ps.d])
                        K.tt(DVE, tmp.t[:, :], ps.t[:, :], gts[s].t[:, n2 * 512:(n2 + 1) * 512], ALU.mult, [ps.d, gts[s].d], [tmp.d])
                        K.tt(DVE, xt.t[:, c, n2 * 512:(n2 + 1) * 512], xt.t[:, c, n2 * 512:(n2 + 1) * 512], tmp.t[:, :], ALU.add,
                             [xt.d, tmp.d], [xt.d])
                K.ld(X1[t0:t0 + T, :].rearrange("(c p) d -> p c d", p=128), xt.t[:, 0:nch, :], (), R=[xt.d])
                uT = uTb[0]
                norm_to_uT(xt, nch, T, layer, 1, s, uT, scr, [PS[0], PS[1]], use_pool=False)
                K.ld(U2[:, t0:t0 + T].rearrange("(k p) t -> p k t", p=128), uT.t[:, :, 0:T], (), R=[uT.d])

    def phaseC2(layer, blocks, Xdst, xoff, is_out, w13=None):
        with K.phase():
            if w13 is None:
                w13 = load_w13(layer)
            w2 = K.sb("w2", [128, 22, D], BF16)
            for j in range(22):
                K.ldcast(w2.t[:, j, :], w2D[layer, j * 128:(j + 1) * 128, :], [w2.d])
            gt1 = K.sb("gt", [128, D], F32)
            gts = {0: gt1, 1: gt1}
            ub = [K.sb("u", [128, 8, 512], BF16) for _ in range(1)]
            xb = [K.sb("xf", [128, 4, D], F32) for _ in range(1)]
            hb = K.sb("h", [128, 22, 512], BF16)
            sgb = [K.sb("sg", [128, 512], F32) for _ in range(2)]
            tmpb = [K.sb("tmp", [128, 512], F32) for _ in range(2)]
            it = 0
            for bi, (t0, T, s) in enumerate(blocks):
                nch = T // 128
                u = ub[0]
                K.ld(u.t[:, :, 0:T], U2[:, t0:t0 + T].rearrange("(k p) t -> p k t", p=128), [u.d])
                if bi == 0 or blocks[bi - 1][2] != s:
                    K.ld(gt1.t[:, :], MODROW[layer, 1, s, :].partition_broadcast(128), [gt1.d])
                xt = xb[0]
                K.ld(xt.t[:, 0:nch, :], X1[t0:t0 + T, :].rearrange("(c p) d -> p c d", p=128), [xt.d])
                for j in range(22):
                    psg = PS[(2 * j) % 4]
                    psu = PS[(2 * j + 1) % 4]
                    for k in range(8):
                        K.mm(psg.t[:, 0:T], w13.t[:, k, j * 128:(j + 1) * 128], u.t[:, k, 0:T], k == 0, k == 7, [w13.d, u.d], [psg.d])
                    for k in range(8):
                        K.mm(psu.t[:, 0:T], w13.t[:, k, FFH + j * 128:FFH + (j + 1) * 128], u.t[:, k, 0:T], k == 0, k == 7, [w13.d, u.d], [psu.d])
                    sg = sgb[j % 2]
                    K.act(sg.t[:, 0:T], psg.t[:, 0:T], AF.Silu, [psg.d], [sg.d])
                    K.tt(DVE, hb.t[:, j, 0:T], sg.t[:, 0:T], psu.t[:, 0:T], ALU.mult, [sg.d, psu.d], [hb.d])
                for c in range(nch):
                    for n2 in range(2):
                        ps = PS[4 + it % 4]
                        tmp = tmpb[it % 2]
                        it += 1
                        for j in range(22):
                            K.mm(ps.t[:, :], hb.t[:, j, c * 128:(c + 1) * 128], w2.t[:, j, n2 * 512:(n2 + 1) * 512], j == 0, j == 21,
                                 [hb.d, w2.d], [ps.d])
                        K.tt(DVE, tmp.t[:, :], ps.t[:, :], gts[s].t[:, n2 * 512:(n2 + 1) * 512], ALU.mult, [ps.d, gts[s].d], [tmp.d])
                        K.tt(POOL, xt.t[:, c, n2 * 512:(n2 + 1) * 512], xt.t[:, c, n2 * 512:(n2 + 1) * 512], tmp.t[:, :], ALU.add,
                             [xt.d, tmp.d], [xt.d])
                K.em.dma(em.pool, Xdst[t0 - xoff:t0 - xoff + T, :].rearrange("(c p) d -> p c d", p=128), xt.t[:, 0:nch, :],
                         reads=[xt.d], is_output=is_out)

    swin = K.dram("ssm_w_in", [D, 5664], F32, "ExternalInput")
    swout = K.dram("ssm_w_out", [2048, D], F32, "ExternalInput")
    convw_col = K.dram("convw_col", [128, 60], F32, "ExternalInput")
    convb_col = K.dram("convb_col", [128, 12], F32, "ExternalInput")
    rows1 = K.dram("rows1", [NR1], F32, "ExternalInput")
    sqc2 = K.dram("sqc2", [128, 2 * 128 + 4], F32, "ExternalInput")
    selc = K.dram("selc", [16, 2048], F32, "ExternalInput")
    SZ = K.dram("SZ", [NT, D], F32)
    SRG = K.dram("SRG", [NT, D], F32)
    DTs = K.dram("DTs", [NT, 32], F32)
    RVs = K.dram("RVs", [NT, D], BF16)
    XBC = K.dram("XBC", [1536, NT], BF16)
    RQ = K.dram("RQ", [4, 128, NT], BF16)
    RK = K.dram("RK", [4, 128, NT], BF16)
    RKT = K.dram("RKT", [NT, 512], BF16)
    XS = K.dram("XS", [NT, D], F32)
    BT = K.dram("BT", [NT, 256], BF16)
    BFs = K.dram("BFs", [256, NT], BF16)
    CFs = K.dram("CFs", [256, NT], BF16)
    YF = K.dram("YF", [NT, 2048], F32)
    YS = K.dram("YS", [2048, NT], BF16)
    ONEB = K.gsb("oneb", [128, 1], F32)
    K.memset(DVE, ONEB.t[:, :], 1.0, [ONEB.d])

    def softplus_small(x, tmp, R, W_):
        K.ts(DVE, tmp, x, -1.0, ALU.mult, R, W_)
        K.tt(DVE, tmp, tmp, x, ALU.max, R + W_, W_)
        K.act(tmp, tmp, AF.Exp, W_, W_, scale=-1.0)
        K.act(tmp, tmp, AF.Ln, W_, W_, bias=ONEB.t[0:x.shape[0], 0:1])
        K.stt(x, x, 0.0, tmp, ALU.max, ALU.add, R + W_, R)

    def l1_phaseA():
        with K.phase():
            W1 = K.sb("w1", [128, 8, 5664], BF16)
            for k in range(8):
                for (a, b) in ((0, 1888), (1888, 3776), (3776, 5664)):
                    K.ldcast(W1.t[:, k, a:b], swin[k * 128:(k + 1) * 128, a:b], [W1.d])
            xb = [K.sb("xa", [128, 4, D], F32)]
            scr = (K.sb("junk", [128, D], BF16), K.sb("ss", [128, 4], F32), K.sb("xs", [128, 4, D], F32))
            uTb = [K.sb("uT", [128, 8, 512], BF16) for _ in range(2)]
            ropeb = [K.sb("rope", [128, 2, 512], F32) for _ in range(2)]
            qnb = [K.sb("qn", [128, 512], BF16) for _ in range(2)]
            t1b = [K.sb("t1", [128, 512], F32) for _ in range(2)]
            t2b = [K.sb("t2", [128, 512], F32) for _ in range(2)]
            ofb = [K.sb("of", [128, 512], F32) for _ in range(2)]
            obb = [K.sb("ob", [128, 512], BF16) for _ in range(3)]
            tmz = [K.sb("tmz", [128, D], F32) for _ in range(3)]
            rvt = [K.sb("rvt", [128, D], BF16) for _ in range(2)]
            rktb = [K.sb("rkt", [128, 512], BF16) for _ in range(2)]
            dtt = [K.sb("dtt", [128, 32], F32) for _ in range(2)]
            cnt = {"p": 0, "n": 0, "o": 0, "z": 0, "t": 0}

            def pbank(banks):
                cnt["p"] += 1
                return banks[cnt["p"] % len(banks)]

            for bi, (t0, T, s) in enumerate(K.blocks):
                nch = T // 128
                xt = xb[0]
                K.ld(xt.t[:, 0:nch, :], X2[t0:t0 + T, :].rearrange("(c p) d -> p c d", p=128), [xt.d])
                rp = ropeb[bi % 2]
                K.ld(rp.t[:, :, 0:T], ropeD[0:2, :, t0:t0 + T].rearrange("f p t -> p f t"), [rp.d])
                uT = uTb[bi % 2]
                norm_to_uT(xt, nch, T, 1, 0, s, uT, scr, [PS[0], PS[1]])
                for ch in range(12):
                    ps = pbank([PS[2], PS[3]])
                    for k in range(8):
                        K.mm(ps.t[:, 0:T], W1.t[:, k, 1024 + ch * 128:1024 + (ch + 1) * 128], uT.t[:, k, 0:T], k == 0, k == 7, [W1.d, uT.d], [ps.d])
                    ob = obb[cnt["o"] % 3]
                    cnt["o"] += 1
                    K.act(ob.t[:, 0:T], ps.t[:, 0:T], AF.Copy, [ps.d], [ob.d])
                    K.st(XBC[ch * 128:(ch + 1) * 128, t0:t0 + T], ob.t[:, 0:T], [ob.d])
                for kind in range(2):
                    for ch in range(4):
                        col0 = 2592 + kind * 512 + ch * 128
                        ps = pbank([PS[2], PS[3]])
                        for k in range(8):
                            K.mm(ps.t[:, 0:T], W1.t[:, k, col0:col0 + 128], uT.t[:, k, 0:T], k == 0, k == 7, [W1.d, uT.d], [ps.d])
                        i = cnt["n"] % 2
                        cnt["n"] += 1
                        qn, t1, t2, of = qnb[i], t1b[i], t2b[i], ofb[i]
                        K.act(qn.t[:, 0:T], ps.t[:, 0:T], AF.Copy, [ps.d], [qn.d], scale=(1.0 if kind == 0 else 0.125))
                        psr = PS[0]
                        K.mm(psr.t[:, 0:T], ROTb, qn.t[:, 0:T], True, True, [qn.d, CB.d], [psr.d])
                        K.tt(DVE, t1.t[:, 0:T], qn.t[:, 0:T], rp.t[:, 0, 0:T], ALU.mult, [qn.d, rp.d], [t1.d])
                        K.tt(DVE, t2.t[:, 0:T], psr.t[:, 0:T], rp.t[:, 1, 0:T], ALU.mult, [psr.d, rp.d], [t2.d])
                        K.tt(DVE, of.t[:, 0:T], t1.t[:, 0:T], t2.t[:, 0:T], ALU.add, [t1.d, t2.d], [of.d])
                        ob = obb[cnt["o"] % 3]
                        cnt["o"] += 1
                        K.act(ob.t[:, 0:T], of.t[:, 0:T], AF.Copy, [of.d], [ob.d])
                        K.st((RQ if kind == 0 else RK)[ch, :, t0:t0 + T], ob.t[:, 0:T], [ob.d])
                        if kind == 1:
                            for c in range(nch):
                                K.tr(PS[4 + c].t[:, ch * 128:(ch + 1) * 128], of.t[:, c * 128:(c + 1) * 128], cst(CI_IDENT), [of.d, CS.d], [PS[4 + c].d])
                for c in range(nch):
                    rkt = rktb[c % 2]
                    K.copy(DVE, rkt.t[:, :], PS[4 + c].t[:, :], [PS[4 + c].d], [rkt.d])
                    K.st(RKT[t0 + c * 128:t0 + (c + 1) * 128, :], rkt.t[:, :], [rkt.d])
                TMB = [PS[2], PS[3], PS[4], PS[5], PS[6], PS[7]]
                for c in range(nch):
                    r0 = t0 + c * 128
                    for (col0, kind) in ((0, "z"), (4640, "g"), (3616, "v")):
                        if kind == "v":
                            dst = rvt[cnt["t"] % 2]
                            cnt["t"] += 1
                        else:
                            dst = tmz[cnt["z"] % 3]
                            cnt["z"] += 1
                        for half in range(2):
                            ps = pbank(TMB)
                            for k in range(8):
                                K.mm(ps.t[:, :], uT.t[:, k, c * 128:(c + 1) * 128], W1.t[:, k, col0 + half * 512:col0 + (half + 1) * 512],
                                     k == 0, k == 7, [W1.d, uT.d], [ps.d])
                            if kind == "v":
                                K.copy(DVE, dst.t[:, half * 512:(half + 1) * 512], ps.t[:, :], [ps.d], [dst.d])
                            else:
                                K.act(dst.t[:, half * 512:(half + 1) * 512], ps.t[:, :], AF.Silu, [ps.d], [dst.d])
                        K.st({"z": SZ, "g": SRG, "v": RVs}[kind][r0:r0 + 128, :], dst.t[:, :], [dst.d])
                    ps = pbank(TMB)
                    for k in range(8):
                        K.mm(ps.t[:, 0:32], uT.t[:, k, c * 128:(c + 1) * 128], W1.t[:, k, 2560:2592], k == 0, k == 7, [W1.d, uT.d], [ps.d])
                    dd = dtt[c % 2]
                    K.copy(DVE, dd.t[:, :], ps.t[:, 0:32], [ps.d], [dd.d])
                    K.st(DTs[r0:r0 + 128, :], dd.t[:, :], [dd.d])

    def l1_phaseV():
        with K.phase():
            cwc = K.sb("cwc", [128, 60], F32)
            K.ld(cwc.t[:, :], convw_col[:, :], [cwc.d])
            cbc = K.sb("cbc", [128, 12], F32)
            K.ld(cbc.t[:, :], convb_col[:, :], [cbc.d])
            cbr = K.sb("cbr", [1, 1536], F32)
            K.ld(cbr.t[:, :], rows1[R_CONVB:R_CONVB + 1536].partition_broadcast(1), [cbr.d])
            DG = K.sb("dg", [128, 60, 128], BF16)
            for idx in range(60):
                K.ts(DVE if idx % 2 == 0 else POOL, DG.t[:, idx, :], cst(CI_IDENT), cwc.t[:, idx:idx + 1], ALU.mult, [CS.d, cwc.d], [DG.d])
            xwb = [K.sb("xw", [128, 12, 516], BF16) for _ in range(2)]
            obb = [K.sb("ob", [128, 512], BF16) for _ in range(2)]
            xst = [K.sb("xst", [128, D], F32) for _ in range(2)]
            btt = [K.sb("btt", [128, 256], BF16) for _ in range(2)]
            ones_row = cst(CI_ONES)[0:1, 0:128]
            cnt = {"p": 0}
            BK = [PS[0], PS[1], PS[2], PS[3], PS[4], PS[5], PS[6], PS[7]]

            def pbank():
                cnt["p"] += 1
                return BK[cnt["p"] % 8]

            for bi, (t0, T, s) in enumerate(K.blocks):
                nch = T // 128
                seg0, seg1 = (0, CTX) if s == 1 else (CTX, NT)
                lo, hi = max(t0 - 2, seg0), min(t0 + T + 2, seg1)
                xw = xwb[bi % 2]
                K.memset(POOL, xw.t[:, :, :], 0.0, [xw.d])
                K.ld(xw.t[:, :, lo - (t0 - 2):hi - (t0 - 2)], XBC[:, lo:hi].rearrange("(c p) t -> p c t", p=128), [xw.d])
                for ch in range(8, 12):
                    ps = pbank()
                    for k in range(5):
                        K.mm(ps.t[:, 0:T], DG.t[:, k * 12 + ch, :], xw.t[:, ch, k:k + T], k == 0, k == 4, [DG.d, xw.d], [ps.d])
                    ob = obb[ch % 2]
                    K.act(ob.t[:, 0:T], ps.t[:, 0:T], AF.Silu, [ps.d, cbc.d], [ob.d], bias=cbc.t[:, ch:ch + 1])
                    dstD = BFs if ch < 10 else CFs
                    r = (ch - 8) % 2
                    K.st(dstD[r * 128:(r + 1) * 128, t0:t0 + T], ob.t[:, 0:T], [ob.d])
                for c in range(nch):
                    r0 = t0 + c * 128
                    banks = [pbank(), pbank(), pbank()]
                    for ch in range(10):
                        tgt = banks[ch // 4]
                        o_ap = tgt.t[:, (ch % 4) * 128:(ch % 4 + 1) * 128]
                        for k in range(5):
                            K.mm(o_ap, xw.t[:, ch, c * 128 + k:c * 128 + k + 128], DG.t[:, k * 12 + ch, :], k == 0, False, [DG.d, xw.d], [tgt.d])
                        K.mm(o_ap, ones_row, cbr.t[0:1, ch * 128:(ch + 1) * 128], False, True, [CS.d, cbr.d], [tgt.d])
                    xo = xst[c % 2]
                    for hh in range(2):
                        K.act(xo.t[:, hh * 512:(hh + 1) * 512], banks[hh].t[:, :], AF.Silu, [banks[hh].d], [xo.d])
                    K.st(XS[r0:r0 + 128, :], xo.t[:, :], [xo.d])
                    bo = btt[c % 2]
                    K.act(bo.t[:, :], banks[2].t[:, 0:256], AF.Silu, [banks[2].d], [bo.d])
                    K.st(BT[r0:r0 + 128, :], bo.t[:, :], [bo.d])

    def l1_scan(dirn):
        fin = dirn == 1
        with K.phase():
            C2 = K.sb("c2", [128, 2 * 128 + 4], F32)
            K.ld(C2.t[:, :], sqc2[:, :], [C2.d])
            SEL = K.sb("sel", [16, 2048], F32)
            K.ld(SEL.t[:, :], selc[:, :], [SEL.d])
            IDX = C2.t[:, dirn * 128:(dirn + 1) * 128]
            colA = C2.t[:, 256 + dirn:256 + dirn + 1]
            colE = C2.t[:, 258 + dirn:258 + dirn + 1]
            MASK = cst(CI_TRI) if dirn == 0 else cst(CI_TRIT)

            def brow(name, off, n):
                b = K.sb(name, [128, n], F32)
                K.ld(b.t[:, :], rows1[off:off + n].partition_broadcast(128), [b.d])
                return b

            DTB = brow("dtb", R_DTB + dirn * 16, 16)
            ANEG = brow("aneg", R_ALOG + dirn * 16, 16)
            K.act(ANEG.t[:, :], ANEG.t[:, :], AF.Exp, [ANEG.d], [ANEG.d])
            K.ts(DVE, ANEG.t[:, :], ANEG.t[:, :], -1.0, ALU.mult, [ANEG.d], [ANEG.d])
            LG = brow("lg", R_RLOG + dirn * 8, 8)
            lgt = K.sb("lgt", [128, 8], F32)
            K.ts(DVE, LG.t[:, :], LG.t[:, :], -1.0, ALU.mult, [LG.d], [LG.d])
            softplus_small(LG.t[:, :], lgt.t[:, :], [LG.d], [lgt.d])
            K.ts(DVE, LG.t[:, :], LG.t[:, :], -1.0, ALU.mult, [LG.d], [LG.d])
            LM = K.sb("lm", [128, 8, 128], F32)
            for h in range(8):
                K.ts(DVE, LM.t[:, h, :], IDX, LG.t[:, h:h + 1], ALU.mult, [C2.d, LG.d], [LM.d])
            K.act(LM.t[:, :, :], LM.t[:, :, :], AF.Exp, [LM.d], [LM.d])
            K.tt(DVE, LM.t[:, :, :], LM.t[:, :, :], MASK.unsqueeze(1).broadcast_to([128, 8, 128]), ALU.mult, [LM.d, CS.d], [LM.d])
            EAr = K.sb("ear", [128, 8], F32)
            K.ts(DVE, EAr.t[:, :], LG.t[:, :], colA, ALU.mult, [LG.d, C2.d], [EAr.d])
            K.act(EAr.t[:, :], EAr.t[:, :], AF.Exp, [EAr.d], [EAr.d])
            DEr = K.sb("der", [128, 8], F32)
            K.ts(DVE, DEr.t[:, :], LG.t[:, :], colE, ALU.mult, [LG.d, C2.d], [DEr.d])
            K.act(DEr.t[:, :], DEr.t[:, :], AF.Exp, [DEr.d], [DEr.d])
            CDR = K.sb("cdr", [128, 4], F32)
            lg2 = LG.t[:, :].rearrange("p (q two) -> p q two", two=2)
            K.ts(DVE, CDR.t[0:64, :], lg2[0:64, :, 0], 128.0, ALU.mult, [LG.d], [CDR.d])
            K.ts(DVE, CDR.t[64:128, :], lg2[64:128, :, 1], 128.0, ALU.mult, [LG.d], [CDR.d])
            K.act(CDR.t[:, :], CDR.t[:, :], AF.Exp, [CDR.d], [CDR.d])
            if fin:
                DSK = brow("dsk", R_DSK, 1024)
                WN = brow("wn", R_SSDN, 1024)
                RN = brow("rn", R_RETN, 1024)
            S = [K.sb("S", [128, 512], F32) for _ in range(2)]
            Sbf = [K.sb("Sbf", [128, 512], BF16) for _ in range(2)]
            SR = K.sb("SR", [128, 4, 128], F32)
            SRbf = K.sb("SRbf", [128, 4, 128], BF16)
            for b in S + Sbf:
                K.memset(DVE, b.t[:, :], 0.0, [b.d])
            K.memset(DVE, SR.t[:, :, :], 0.0, [SR.d])
            K.memset(DVE, SRbf.t[:, :, :], 0.0, [SRbf.d])
            NB = 3
            NF = 2
            xsb = [K.sb("xs", [128, D], F32) for _ in range(NB)]
            btb = [K.sb("bt", [128, 256], BF16) for _ in range(NB)]
            bfb = [K.sb("bf", [128, 2, 128], BF16) for _ in range(NB)]
            cfb = [K.sb("cf", [128, 2, 128], BF16) for _ in range(NB)]
            dtb_ = [K.sb("dt", [128, 16], F32) for _ in range(NB)]
            rqb = [K.sb("rq", [128, 4, 128], BF16) for _ in range(NB)]
            rkb = [K.sb("rk", [128, 4, 128], BF16) for _ in range(NB)]
            rktb = [K.sb("rkt", [128, 512], BF16) for _ in range(NB)]
            rvb = [K.sb("rv", [128, D], BF16) for _ in range(NB)]
            if fin:
                yfb = [K.sb("yf", [128, 2048], F32) for _ in range(NF)]
                szb = [K.sb("sz", [128, D], F32) for _ in range(NF)]
                sgb = [K.sb("srg", [128, D], F32) for _ in range(NF)]
            smb = [K.sb("sm", [128, 8, 16], F32) for _ in range(2)]
            at = K.sb("at", [16, 256], F32)
            E = K.sb("E", [128, 16, 128], F32)
            Gm = K.sb("Gm", [128, 2, 128], F32)
            Mb = [K.sb("M", [128, 16, 128], BF16) for _ in range(2)]
            MRb = [K.sb("MR", [128, 8, 128], BF16) for _ in range(2)]
            Vdb = [K.sb("Vd", [128, D], BF16) for _ in range(2)]
            Vdecb = [K.sb("Vdec", [128, D], BF16) for _ in range(2)]
            RVdb = [K.sb("RVd", [128, D], BF16) for _ in range(2)]
            yo = K.sb("yo", [128, D], F32)
            ydb = [K.sb("yd", [128, 2048], F32) for _ in range(2)]
            if fin:
                junk = K.sb("junk", [128, D], F32)
                st8 = K.sb("st8", [128, 8, 8], F32)
                ynb = K.sb("yn", [128, 2048], F32)
                ysb = [K.sb("ys", [128, 4, 128], BF16) for _ in range(2)]
            cnt = {"p": 0}

            def pbank():
                cnt["p"] += 1
                return PS[cnt["p"] % 8]

            order = list(range(NCH)) if dirn == 0 else [1, 0] + list(range(NCH - 1, 1, -1))
            NO = len(order)

            def loads(ci):
                c = order[ci]
                t0 = c * 128
                i = ci % NB
                K.ld(xsb[i].t[:, :], XS[t0:t0 + 128, :], [xsb[i].d])
                K.ld(btb[i].t[:, :], BT[t0:t0 + 128, :], [btb[i].d])
                K.ld(bfb[i].t[:, :, :], BFs[:, t0:t0 + 128].rearrange("(g n) t -> n g t", n=128), [bfb[i].d])
                K.ld(cfb[i].t[:, :, :], CFs[:, t0:t0 + 128].rearrange("(g n) t -> n g t", n=128), [cfb[i].d])
                K.ld(dtb_[i].t[:, :], DTs[t0:t0 + 128, dirn * 16:(dirn + 1) * 16], [dtb_[i].d])
                K.ld(rqb[i].t[:, :, :], RQ[:, :, t0:t0 + 128].rearrange("c p t -> p c t"), [rqb[i].d])
                K.ld(rkb[i].t[:, :, :], RK[:, :, t0:t0 + 128].rearrange("c p t -> p c t"), [rkb[i].d])
                K.ld(rktb[i].t[:, :], RKT[t0:t0 + 128, :], [rktb[i].d])
                K.ld(rvb[i].t[:, :], RVs[t0:t0 + 128, :], [rvb[i].d])

            def loads_fin(ci):
                c = order[ci]
                t0 = c * 128
                i = ci % NF
                K.ld(yfb[i].t[:, :], YF[t0:t0 + 128, :], [yfb[i].d])
                K.ld(szb[i].t[:, :], SZ[t0:t0 + 128, :], [szb[i].d])
                K.ld(sgb[i].t[:, :], SRG[t0:t0 + 128, :], [sgb[i].d])

            def bc3(ap2, n):
                return ap2.unsqueeze(2).broadcast_to([128, ap2.shape[1], n])

            def partA(ci):
                if ci + 1 < NO:
                    loads(ci + 1)
                i = ci % NB
                j2 = ci % 2
                xs_c, bf_c, cf_c, dt_c = xsb[i], bfb[i], cfb[i], dtb_[i]
                rq_c, rk_c, rv_c = rqb[i], rkb[i], rvb[i]
                sm = smb[j2]
                M, MR, Vd, Vdec, RVd = Mb[j2], MRb[j2], Vdb[j2], Vdecb[j2], RVdb[j2]
                sp, tmpv, la, Acol, Atot, expA, dece, cd = [sm.t[:, j, :] for j in range(8)]
                smd = [sm.d]
                K.tt(DVE, sp, dt_c.t[:, :], DTB.t[:, :], ALU.add, [dt_c.d, DTB.d], smd)
                softplus_small(sp, tmpv, smd, smd)
                K.tt(DVE, la, sp, ANEG.t[:, :], ALU.mult, smd + [ANEG.d], smd)
                psc = pbank()
                K.mm(psc.t[:, 0:16], MASK, la, True, True, smd + [CS.d], [psc.d])
                K.mm(psc.t[:, 16:32], cst(CI_ONES), la, True, True, smd + [CS.d], [psc.d])
                K.mm(psc.t[0:16, 128:256], la, MASK, True, True, smd + [CS.d], [psc.d])
                K.copy(DVE, sm.t[:, 3:5, :], psc.t[:, 0:32].rearrange("p (a b) -> p a b", b=16), [psc.d], smd)
                K.copy(DVE, at.t[:, 0:128], psc.t[0:16, 128:256], [psc.d], [at.d])
                K.ts(DVE, at.t[:, 128:256], at.t[:, 0:128], -1.0, ALU.mult, [at.d], [at.d])
                K.act(expA, Acol, AF.Exp, smd, smd)
                K.tt(DVE, dece, Atot, Acol, ALU.subtract, smd, smd)
                K.act(dece, dece, AF.Exp, smd, smd)
                K.act(cd, Atot, AF.Exp, smd, smd)
                K.tt(DVE, tmpv, sp, dece, ALU.mult, smd, smd)
                xs3 = xs_c.t[:, :].rearrange("p (h e) -> p h e", e=64)
                K.tt(DVE, Vd.t[:, :].rearrange("p (h e) -> p h e", e=64), xs3, bc3(sp, 64), ALU.mult, [xs_c.d] + smd, [Vd.d])
                K.tt(POOL, Vdec.t[:, :].rearrange("p (h e) -> p h e", e=64), xs3, bc3(tmpv, 64), ALU.mult, [xs_c.d] + smd, [Vdec.d])
                for q in range(4):
                    psd = pbank()
                    for hh in range(4):
                        h = q * 4 + hh
                        o_ap = psd.t[:, hh * 128:(hh + 1) * 128]
                        K.mm(o_ap, SEL.t[0:16, h * 128:(h + 1) * 128], at.t[0:16, 0:128], True, False, [SEL.d, at.d], [psd.d])
                        K.mm(o_ap, at.t[0:16, 128:256], SEL.t[0:16, h * 128:(h + 1) * 128], False, True, [SEL.d, at.d], [psd.d])
                    K.tt(DVE, E.t[:, q * 4:(q + 1) * 4, :], psd.t[:, :].rearrange("p (h i) -> p h i", i=128),
                         MASK.unsqueeze(1).broadcast_to([128, 4, 128]), ALU.mult, [psd.d, CS.d], [E.d])
                K.act(E.t[:, :, :], E.t[:, :, :], AF.Exp, [E.d], [E.d])
                psg = pbank()
                for g in range(2):
                    K.mm(psg.t[:, g * 128:(g + 1) * 128], bf_c.t[:, g, :], cf_c.t[:, g, :], True, True, [bf_c.d, cf_c.d], [psg.d])
                K.tt(DVE, Gm.t[:, :, :], psg.t[:, 0:256].rearrange("p (g i) -> p g i", i=128),
                     MASK.unsqueeze(1).broadcast_to([128, 2, 128]), ALU.mult, [psg.d, CS.d], [Gm.d])
                for g in range(2):
                    K.tt(DVE if g == 0 else POOL, M.t[:, g * 8:(g + 1) * 8, :], E.t[:, g * 8:(g + 1) * 8, :],
                         Gm.t[:, g, :].unsqueeze(1).broadcast_to([128, 8, 128]), ALU.mult, [E.d, Gm.d], [M.d])
                psgr = [pbank(), pbank()]
                for h in range(8):
                    rows = slice((h % 2) * 64, (h % 2) * 64 + 64)
                    K.mm(psgr[h % 2].t[:, (h // 2) * 128:(h // 2 + 1) * 128], rk_c.t[rows, h // 2, :], rq_c.t[rows, h // 2, :], True, True,
                         [rk_c.d, rq_c.d], [psgr[h % 2].d])
                for par in range(2):
                    K.tt(DVE, MR.t[:, :, :].rearrange("p (b two) i -> p b two i", two=2)[:, :, par, :],
                         psgr[par].t[:, :].rearrange("p (h i) -> p h i", i=128),
                         LM.t[:, :, :].rearrange("p (b two) i -> p b two i", two=2)[:, :, par, :],
                         ALU.mult, [psgr[par].d, LM.d], [MR.d])
                K.tt(DVE, RVd.t[:, :].rearrange("p (h e) -> p h e", e=128), rv_c.t[:, :].rearrange("p (h e) -> p h e", e=128),
                     bc3(DEr.t[:, :], 128), ALU.mult, [rv_c.d, DEr.d], [RVd.d])

            def partB(ci):
                if fin and ci + 1 < NO:
                    loads_fin(ci + 1)
                c = order[ci]
                t0 = c * 128
                i = ci % NB
                j2 = ci % 2
                xs_c, bt_c, cf_c = xsb[i], btb[i], cfb[i]
                rq_c, rkt_c, rv_c = rqb[i], rktb[i], rvb[i]
                sm = smb[j2]
                M, MR, Vd, Vdec, RVd = Mb[j2], MRb[j2], Vdb[j2], Vdecb[j2], RVdb[j2]
                sp, tmpv, la, Acol, Atot, expA, dece, cd = [sm.t[:, j, :] for j in range(8)]
                smd = [sm.d]
                yd = ydb[ci % 2]
                for g in range(2):
                    pso = pbank()
                    K.mm(pso.t[:, :], cf_c.t[:, g, :], Sbf[g].t[:, :], True, True, [cf_c.d, Sbf[g].d], [pso.d])
                    K.tt(DVE, yo.t[:, g * 512:(g + 1) * 512].rearrange("p (h e) -> p h e", e=64),
                         pso.t[:, :].rearrange("p (h e) -> p h e", e=64), bc3(expA[:, g * 8:(g + 1) * 8], 64), ALU.mult,
                         [pso.d] + smd, [yo.d])
                for g in range(2):
                    psy = pbank()
                    for hl in range(8):
                        h = g * 8 + hl
                        K.mm(psy.t[:, hl * 64:(hl + 1) * 64], M.t[:, h, :], Vd.t[:, h * 64:(h + 1) * 64], True, True, [M.d, Vd.d], [psy.d])
                    K.tt(DVE, yd.t[:, g * 512:(g + 1) * 512], psy.t[:, :], yo.t[:, g * 512:(g + 1) * 512], ALU.add, [psy.d, yo.d], [yd.d])
                for g in range(2):
                    psd2 = pbank()
                    K.mm(psd2.t[:, :], bt_c.t[:, g * 128:(g + 1) * 128], Vdec.t[:, g * 512:(g + 1) * 512], True, True, [bt_c.d, Vdec.d], [psd2.d])
                    s3 = S[g].t[:, :].rearrange("p (h e) -> p h e", e=64)
                    K.tt(POOL, s3, s3, bc3(cd[:, g * 8:(g + 1) * 8], 64), ALU.mult, [S[g].d] + smd, [S[g].d])
                    K.tt(DVE, S[g].t[:, :], S[g].t[:, :], psd2.t[:, :], ALU.add, [S[g].d, psd2.d], [S[g].d])
                    K.act(Sbf[g].t[:, :], S[g].t[:, :], AF.Copy, [S[g].d], [Sbf[g].d])
                psro = [pbank(), pbank()]
                for h in range(8):
                    rows = slice((h % 2) * 64, (h % 2) * 64 + 64)
                    K.mm(psro[h % 2].t[:, (h // 2) * 128:(h // 2 + 1) * 128], rq_c.t[rows, h // 2, :], SRbf.t[rows, h // 2, :], True, True,
                         [rq_c.d, SRbf.d], [psro[h % 2].d])
                for par in range(2):
                    K.tt(DVE, yo.t[:, :].rearrange("p (b two e) -> p b two e", two=2, e=128)[:, :, par, :],
                         psro[par].t[:, :].rearrange("p (h e) -> p h e", e=128),
                         bc3(EAr.t[:, :].rearrange("p (b two) -> p b two", two=2)[:, :, par], 128), ALU.mult,
                         [psro[par].d, EAr.d], [yo.d])
                psyr = [pbank(), pbank()]
                for h in range(8):
                    K.mm(psyr[h // 4].t[:, (h % 4) * 128:(h % 4 + 1) * 128], MR.t[:, h, :], rv_c.t[:, h * 128:(h + 1) * 128], True, True,
                         [MR.d, rv_c.d], [psyr[h // 4].d])
                for q in range(2):
                    K.tt(DVE, yd.t[:, 1024 + q * 512:1024 + (q + 1) * 512], psyr[q].t[:, :], yo.t[:, q * 512:(q + 1) * 512], ALU.add,
                         [psyr[q].d, yo.d], [yd.d])
                psdr = [pbank(), pbank()]
                for h in range(8):
                    pr = h // 2
                    K.mm(psdr[h // 4].t[:, (h % 4) * 128:(h % 4 + 1) * 128], rkt_c.t[:, pr * 128:(pr + 1) * 128], RVd.t[:, h * 128:(h + 1) * 128],
                         True, True, [rkt_c.d, RVd.d], [psdr[h // 4].d])
                K.tt(POOL, SR.t[:, :, :], SR.t[:, :, :], bc3(CDR.t[:, :], 128), ALU.mult, [SR.d, CDR.d], [SR.d])
                for half in range(2):
                    rows = slice(half * 64, half * 64 + 64)
                    for q in range(2):
                        src = psdr[q].t[:, :].rearrange("p (pp two e) -> p pp two e", two=2, e=128)[rows, :, half, :]
                        K.tt(DVE, SR.t[rows, 2 * q:2 * q + 2, :], SR.t[rows, 2 * q:2 * q + 2, :], src, ALU.add, [SR.d, psdr[q].d], [SR.d])
                K.act(SRbf.t[:, :, :], SR.t[:, :, :], AF.Copy, [SR.d], [SRbf.d])
                if not fin:
                    K.st(YF[t0:t0 + 128, :], yd.t[:, :], [yd.d])
                    return
                yf_c, sz_c, sg_c = yfb[ci % NF], szb[ci % NF], sgb[ci % NF]
                K.tt(POOL, yd.t[:, :], yd.t[:, :], yf_c.t[:, :], ALU.add, [yd.d, yf_c.d], [yd.d])
                K.tt(POOL, junk.t[:, :], xs_c.t[:, :], DSK.t[:, :], ALU.mult, [xs_c.d, DSK.d], [junk.d])
                K.tt(DVE, yd.t[:, 0:1024], yd.t[:, 0:1024], junk.t[:, :], ALU.add, [yd.d, junk.d], [yd.d])
                K.tt(DVE, yd.t[:, 0:1024], yd.t[:, 0:1024], sz_c.t[:, :], ALU.mult, [yd.d, sz_c.d], [yd.d])
                ssg = st8.t[:, 0, 0:2]
                for g in range(2):
                    K.act(junk.t[:, 0:512], yd.t[:, g * 512:(g + 1) * 512], AF.Square, [yd.d], [junk.d, st8.d], accum_out=st8.t[:, 0, g:g + 1])
                K.ts(DVE, ssg, ssg, 1.0 / 512, ALU.mult, [st8.d], [st8.d], s2=EPS, op1=ALU.add)
                K.act(ssg, ssg, AF.Sqrt, [st8.d], [st8.d])
                K.recip(ssg, ssg, [st8.d], [st8.d])
                for g in range(2):
                    K.stt(ynb.t[:, g * 512:(g + 1) * 512], yd.t[:, g * 512:(g + 1) * 512], st8.t[:, 0, g:g + 1], WN.t[:, g * 512:(g + 1) * 512],
                          ALU.mult, ALU.mult, [yd.d, st8.d, WN.d], [ynb.d])
                yr3 = yd.t[:, 1024:2048].rearrange("p (h e) -> p h e", e=128)
                s1, s2, mean, m2 = st8.t[:, 1, :], st8.t[:, 2, :], st8.t[:, 3, :], st8.t[:, 4, :]
                em.op(DVE, lambda: nc.vector.tensor_reduce(out=s1, in_=yr3, axis=AX.X, op=ALU.add), [yd.d], [st8.d])
                K.act(junk.t[:, :], yd.t[:, 1024:2048], AF.Square, [yd.d], [junk.d])
                em.op(DVE, lambda: nc.vector.tensor_reduce(out=s2, in_=junk.t[:, :].rearrange("p (h e) -> p h e", e=128), axis=AX.X, op=ALU.add),
                      [junk.d], [st8.d])
                K.ts(DVE, mean, s1, 1.0 / 128, ALU.mult, [st8.d], [st8.d])
                K.tt(DVE, m2, mean, mean, ALU.mult, [st8.d], [st8.d])
                K.stt(s2, s2, 1.0 / 128, m2, ALU.mult, ALU.subtract, [st8.d], [st8.d])
                K.ts(DVE, s2, s2, EPS, ALU.add, [st8.d], [st8.d])
                K.act(s2, s2, AF.Sqrt, [st8.d], [st8.d])
                K.recip(s2, s2, [st8.d], [st8.d])
                yn3 = ynb.t[:, 1024:2048].rearrange("p (h e) -> p h e", e=128)
                K.tt(DVE, yn3, yr3, bc3(mean, 128), ALU.subtract, [yd.d, st8.d], [ynb.d])
                K.tt(POOL, yn3, yn3, bc3(s2, 128), ALU.mult, [ynb.d, st8.d], [ynb.d])
                K.tt(DVE, ynb.t[:, 1024:2048], ynb.t[:, 1024:2048], RN.t[:, :], ALU.mult, [ynb.d, RN.d], [ynb.d])
                K.tt(POOL, ynb.t[:, 1024:2048], ynb.t[:, 1024:2048], sg_c.t[:, :], ALU.mult, [ynb.d, sg_c.d], [ynb.d])
                for b4 in range(4):
                    pst = pbank()
                    for kk in range(4):
                        k = b4 * 4 + kk
                        K.tr(pst.t[:, kk * 128:(kk + 1) * 128], ynb.t[:, k * 128:(k + 1) * 128], cst(CI_IDENT), [ynb.d, CS.d], [pst.d])
                    ys = ysb[b4 % 2]
                    K.act(ys.t[:, :, :], pst.t[:, :].rearrange("p (k t) -> p k t", t=128), AF.Copy, [pst.d], [ys.d])
                    K.st(YS[b4 * 512:(b4 + 1) * 512, t0:t0 + 128].rearrange("(k p) t -> p k t", p=128), ys.t[:, :, :], [ys.d])

            loads(0)
            if fin:
                loads_fin(0)
            partA(0)
            for ci in range(NO):
                if ci + 1 < NO:
                    partA(ci + 1)
                partB(ci)

    EPSB = K.gsb("epsb", [128, 1], F32)
    K.memset(DVE, EPSB.t[:, :], EPS, [EPSB.d])

    K.prefetch_w13 = True
    l0_phaseA()
    l0_phaseB()
    K.c2_args = (K.blocks, X2, 0, nlayers == 1)
    phaseC1(0, K.blocks, AO, 8, awout, xin, 0)
    if nlayers == 1:
        pass
    else:
        import os
        stop = int(os.environ.get("KSTOP", "99"))
        lat = [b for b in K.blocks if b[2] == 0]
        def l1_c():
            K.c2_args = (lat, outD, CTX, True)
            phaseC1(1, lat, YS, 16, swout, X2, 0)
        steps = [l1_phaseA, l1_phaseV, lambda: l1_scan(0), lambda: l1_scan(1), l1_c]
        for si, fn in enumerate(steps):
            if si < stop:
                fn()
    em.finish()
    K.stats = em.stats()
    return K


def prep_inputs(inp, L):
    f = lambda a: np.ascontiguousarray(np.asarray(a, dtype=np.float32))
    rope, sq = host_consts(L)
    col = lambda v, n: f(v).reshape(n, 128).T
    shared = {
        "mod_w": f(inp["mod_w"]),
        "mod_b": f(inp["mod_b"]),
        "modb_col": np.ascontiguousarray(np.stack([col(inp["mod_b"][i], 48) for i in range(2)], axis=1)),
        "ncol": np.ascontiguousarray(np.stack([np.stack([col(inp["norm1_w"][i], 8), col(inp["norm2_w"][i], 8)], axis=1) for i in range(2)], axis=1)),
        "rope": rope, "sqc": sq,
        "ffn_w13": f(inp["ffn_w13"]), "ffn_w2": f(inp["ffn_w2"]),
        "attn_w_in": f(inp["attn_w_in"][0]), "mla_wq_b": f(inp["mla_wq_b"][0]), "mla_wkv_b": f(inp["mla_wkv_b"][0]),
        "attn_w_out": f(inp["attn_w_out"][0]),
    }
    if "ssm_w_in" in inp:
        shared["ssm_w_in"] = f(inp["ssm_w_in"][0])
        shared["ssm_w_out"] = f(inp["ssm_w_out"][0])
        cw = f(inp["ssd_conv_w"][0])
        shared["convw_col"] = np.ascontiguousarray(cw.reshape(5, 12, 128).transpose(2, 0, 1).reshape(128, 60))
        shared["convb_col"] = col(inp["ssd_conv_b"][0], 12)
        rows = np.zeros((NR1,), np.float32)
        rows[R_CONVB:R_CONVB + 1536] = f(inp["ssd_conv_b"][0])
        rows[R_DTB:R_DTB + 32] = f(inp["ssd_dt_bias"][0]).reshape(-1)
        rows[R_ALOG:R_ALOG + 32] = f(inp["ssd_a_log"][0]).reshape(-1)
        rows[R_RLOG:R_RLOG + 16] = f(inp["ret_decay_logit"][0]).reshape(-1)
        rows[R_DSK:R_DSK + 1024] = np.repeat(f(inp["ssd_d"][0]), 64)
        rows[R_SSDN:R_SSDN + 1024] = f(inp["ssd_norm"][0])
        rows[R_RETN:R_RETN + 1024] = f(inp["ret_norm"][0])
        shared["rows1"] = rows
        jj, ii = np.meshgrid(np.arange(128, dtype=np.float32), np.arange(128, dtype=np.float32), indexing="ij")
        c2 = np.zeros((128, 260), np.float32)
        c2[:, 0:128] = np.maximum(ii - jj, 0)
        c2[:, 128:256] = np.maximum(jj - ii, 0)
        j1 = np.arange(128, dtype=np.float32)
        c2[:, 256], c2[:, 257], c2[:, 258], c2[:, 259] = j1 + 1, 128 - j1, 127 - j1, j1
        shared["sqc2"] = c2
        sel = np.zeros((16, 16, 128), np.float32)
        for h in range(16):
            sel[h, h, :] = 1.0
        shared["selc"] = np.ascontiguousarray(sel.reshape(16, 2048))
    vec = np.zeros((128, NV), np.float32)
    vec[:, V_GQ] = np.tile(f(inp["gqa_qn"][0]), 2)
    vec[:, V_GK] = np.tile(f(inp["gqa_kn"][0]), 2)
    vec[:, V_QA:V_QA + 3] = col(inp["mla_qa_norm"][0], 3)
    vec[:, V_KVA:V_KVA + 2] = col(inp["mla_kva_norm"][0], 2)
    vec[:96, V_MQ] = f(inp["mla_qn"][0])
    vec[:96, V_MK] = f(inp["mla_kn"][0])
    shared["vecs"] = vec
    maps = []
    x, c, ctx, c_ctx = f(inp["x"]), f(inp["c"]), f(inp["ctx"]), f(inp["c_ctx"])
    for b in range(x.shape[0]):
        m = dict(shared)
        m["xin"] = np.ascontiguousarray(np.concatenate([ctx[b], x[b]], axis=0))
        cc = np.stack([col(c[b], 8), col(c_ctx, 8)], axis=2)
        m["cc"] = np.ascontiguousarray(cc)
        maps.append(m)
    return maps


_CACHE = {}


def kernel(**inputs):
    L = int(np.asarray(inputs["x"]).shape[1])
    B = int(np.asarray(inputs["x"]).shape[0])
    if L not in _CACHE:
        _CACHE[L] = build(L, 2)
    K = _CACHE[L]
    maps = prep_inputs(inputs, L)
    res = run_bass_kernel_spmd(K.nc, maps, core_ids=list(range(B)))
    return np.stack([np.asarray(res.results[b]["out"]) for b in range(B)], axis=0).astype(np.float32)
```
